# Optimizing a Trainium2 kernel written in Bass

```python
import jax
import jax.numpy as jnp
from jax import lax
import numpy as np

D_MODEL = 2048
BATCH = 4
SEQ = 4096
DEPTH = 4

CHUNK = 64
HEAD_DIM = 128
N_MIX_HEADS = D_MODEL // HEAD_DIM
FOX_HEADS = N_MIX_HEADS // 2
RET_HEADS = N_MIX_HEADS // 4
GMLP_GROUPS = N_MIX_HEADS - FOX_HEADS - RET_HEADS
FOX_W = FOX_HEADS * HEAD_DIM
RET_W = RET_HEADS * HEAD_DIM
GMLP_W = GMLP_GROUPS * HEAD_DIM
MIX_W = FOX_W + RET_W + GMLP_W
IN_SPLIT_SIZES = (FOX_W, FOX_W, FOX_W, FOX_HEADS, RET_W, RET_W, RET_W, RET_W, GMLP_W, GMLP_W)
IN_COLS = sum(IN_SPLIT_SIZES)
FOX_Q_BLOCK = 128
GMLP_CHUNK = 128
D_FF = ((8 * D_MODEL // 3 + 255) // 256) * 256
ROPE_BASE = 10000.0
RET_GAMMA_BASE = 5.0
EPS = 1e-6

kernel_name = "hybrid_fox_retention_gmlp_macaron"


def rmsnorm(x, g):
    x32 = x.astype(jnp.float32)
    y = x32 * lax.rsqrt(jnp.mean(x32 * x32, axis=-1, keepdims=True) + EPS)
    return (y * g.astype(jnp.float32)).astype(x.dtype)


def swiglu(h, w_gate, w_up, w_down):
    return (jax.nn.silu(h @ w_gate) * (h @ w_up)) @ w_down


def rope(x, pos):
    half = x.shape[-1] // 2
    inv_freq = ROPE_BASE ** (-jnp.arange(half, dtype=jnp.float32) / half)
    ang = pos[:, None] * inv_freq[None, :]
    cos = jnp.cos(ang)[None, :, None, :]
    sin = jnp.sin(ang)[None, :, None, :]
    x1, x2 = x[..., :half], x[..., half:]
    return jnp.concatenate([x1 * cos - x2 * sin, x1 * sin + x2 * cos], axis=-1)


def forgetting_attention(q, k, v, log_f):
    S = q.shape[1]
    scale = q.shape[-1] ** -0.5
    c = jnp.transpose(jnp.cumsum(log_f, axis=1), (0, 2, 1))
    outs = []
    for i in range(S // FOX_Q_BLOCK):
        q0, q1 = i * FOX_Q_BLOCK, (i + 1) * FOX_Q_BLOCK
        s = jnp.einsum("bqhd,bkhd->bhqk", q[:, q0:q1], k[:, :q1]) * scale
        s = s + c[:, :, q0:q1, None] - c[:, :, None, :q1]
        causal = jnp.arange(q0, q1)[:, None] >= jnp.arange(q1)[None, :]
        s = jnp.where(causal[None, None], s, -jnp.inf)
        p = jax.nn.softmax(s, axis=-1)
        outs.append(jnp.einsum("bhqk,bkhd->bqhd", p, v[:, :q1]))
    return jnp.concatenate(outs, axis=1)


def retention(q, k, v):
    B, S, H, D = q.shape
    N = S // CHUNK
    log_g = jnp.log1p(-jnp.exp2(-(RET_GAMMA_BASE + jnp.arange(H, dtype=jnp.float32))))
    k = k * (D ** -0.5)
    qc = q.reshape(B, N, CHUNK, H, D)
    kc = k.reshape(B, N, CHUNK, H, D)
    vc = v.reshape(B, N, CHUNK, H, D)
    idx = jnp.arange(CHUNK, dtype=jnp.float32)
    diff = idx[:, None] - idx[None, :]
    decay_mask = jnp.where(diff[None] >= 0, jnp.exp(jnp.maximum(diff, 0.0)[None] * log_g[:, None, None]), 0.0)
    scores = jnp.einsum("bnchd,bnshd->bnhcs", qc, kc) * decay_mask[None, None]
    o_inner = jnp.einsum("bnhcs,bnshe->bnche", scores, vc)
    zeta = jnp.exp((CHUNK - 1 - idx)[:, None] * log_g[None, :])
    kv = jnp.einsum("bnshd,bnshe->nbhde", kc * zeta[None, None, :, :, None], vc)
    chunk_decay = jnp.exp(CHUNK * log_g)[None, :, None, None]

    def step(state, kv_n):
        return state * chunk_decay + kv_n, state

    _, state_prev = lax.scan(step, jnp.zeros((B, H, D, D), jnp.float32), kv)
    xi = jnp.exp((idx + 1.0)[:, None] * log_g[None, :])
    o_cross = jnp.einsum("bnchd,nbhde->bnche", qc, state_prev) * xi[None, None, :, :, None]
    return (o_inner + o_cross).reshape(B, S, H, D)


def chunk_gmlp(u, v, ln_g, ln_b, w_s, b_s):
    B, S, G, Dg = v.shape
    mu = jnp.mean(v, axis=-1, keepdims=True)
    var = jnp.mean(jnp.square(v - mu), axis=-1, keepdims=True)
    v = (v - mu) * lax.rsqrt(var + EPS) * ln_g.reshape(G, Dg) + ln_b.reshape(G, Dg)
    v = v.reshape(B, S // GMLP_CHUNK, GMLP_CHUNK, G, Dg)
    tril = jnp.tril(jnp.ones((GMLP_CHUNK, GMLP_CHUNK), jnp.float32))
    v_mix = jnp.einsum("gts,bnsgc->bntgc", w_s * tril[None], v) + jnp.transpose(b_s)[None, None, :, :, None]
    return u * v_mix.reshape(B, S, G, Dg)


def hybrid_mixer(h, w_in, fox_b_f, gmlp_ln_g, gmlp_ln_b, gmlp_w_s, gmlp_b_s, out_norm, w_out):
    B, S, _ = h.shape
    f32 = jnp.float32
    proj = (h @ w_in).astype(f32)
    splits = np.cumsum(IN_SPLIT_SIZES)[:-1].tolist()
    fq, fk, fv, fz, rq, rk, rv, rg, gu, gv = jnp.split(proj, splits, axis=-1)

    def heads(t):
        return t.reshape(B, S, -1, HEAD_DIM)

    log_f = jax.nn.log_sigmoid(fz + fox_b_f.astype(f32))
    y_a = forgetting_attention(heads(fq), heads(fk), heads(fv), log_f)
    pos = jnp.arange(S, dtype=f32)
    y_b = retention(rope(heads(rq), pos), rope(heads(rk), pos), heads(rv))
    y_c = chunk_gmlp(heads(jax.nn.gelu(gu)), heads(jax.nn.gelu(gv)), gmlp_ln_g.astype(f32),
                     gmlp_ln_b.astype(f32), gmlp_w_s.astype(f32), gmlp_b_s.astype(f32))
    y = jnp.concatenate([y_a, y_b, y_c], axis=2)
    y = y * lax.rsqrt(jnp.mean(y * y, axis=-1, keepdims=True) + EPS) * out_norm.astype(f32).reshape(N_MIX_HEADS, HEAD_DIM)
    y_a, y_b, y_c = jnp.split(y, [FOX_HEADS, FOX_HEADS + RET_HEADS], axis=2)
    y = jnp.concatenate([y_a, y_b * jax.nn.silu(heads(rg)), y_c], axis=2).reshape(B, S, MIX_W)
    return y.astype(h.dtype) @ w_out


def setup_inputs(seed: int = 0) -> dict:
    key = jax.random.key(seed)
    ks = jax.random.split(key, 20)
    f32 = jnp.float32

    def nrm(k, shape, fan_in):
        return jax.random.normal(k, shape, f32) * (fan_in ** -0.5)

    def gain(k, shape):
        return 1.0 + 0.02 * jax.random.normal(k, shape, f32)

    return {
        "x": jax.random.normal(ks[0], (BATCH, SEQ, D_MODEL), f32),
        "ffn1_norm": gain(ks[1], (DEPTH, D_MODEL)),
        "ffn1_w_gate": nrm(ks[2], (DEPTH, D_MODEL, D_FF), D_MODEL),
        "ffn1_w_up": nrm(ks[3], (DEPTH, D_MODEL, D_FF), D_MODEL),
        "ffn1_w_down": nrm(ks[4], (DEPTH, D_FF, D_MODEL), D_FF),
        "mix_norm": gain(ks[5], (DEPTH, D_MODEL)),
        "w_in": nrm(ks[6], (DEPTH, D_MODEL, IN_COLS), D_MODEL),
        "fox_b_f": jax.random.uniform(ks[7], (DEPTH, FOX_HEADS), f32, 1.0, 5.0),
        "gmlp_ln_g": gain(ks[8], (DEPTH, GMLP_W)),
        "gmlp_ln_b": 0.02 * jax.random.normal(ks[9], (DEPTH, GMLP_W), f32),
        "gmlp_w_s": nrm(ks[10], (DEPTH, GMLP_GROUPS, GMLP_CHUNK, GMLP_CHUNK), GMLP_CHUNK),
        "gmlp_b_s": 1.0 + 0.1 * jax.random.normal(ks[11], (DEPTH, GMLP_GROUPS, GMLP_CHUNK), f32),
        "out_norm": gain(ks[12], (DEPTH, MIX_W)),
        "w_out": nrm(ks[13], (DEPTH, MIX_W, D_MODEL), MIX_W),
        "ffn2_norm": gain(ks[14], (DEPTH, D_MODEL)),
        "ffn2_w_gate": nrm(ks[15], (DEPTH, D_MODEL, D_FF), D_MODEL),
        "ffn2_w_up": nrm(ks[16], (DEPTH, D_MODEL, D_FF), D_MODEL),
        "ffn2_w_down": nrm(ks[17], (DEPTH, D_FF, D_MODEL), D_FF),
        "final_norm": gain(ks[18], (D_MODEL,)),
    }


def reference(x, ffn1_norm, ffn1_w_gate, ffn1_w_up, ffn1_w_down, mix_norm, w_in, fox_b_f,
              gmlp_ln_g, gmlp_ln_b, gmlp_w_s, gmlp_b_s, out_norm, w_out, ffn2_norm,
              ffn2_w_gate, ffn2_w_up, ffn2_w_down, final_norm):
    for l in range(DEPTH):
        x = x + 0.5 * swiglu(rmsnorm(x, ffn1_norm[l]), ffn1_w_gate[l], ffn1_w_up[l], ffn1_w_down[l])
        x = x + hybrid_mixer(rmsnorm(x, mix_norm[l]), w_in[l], fox_b_f[l], gmlp_ln_g[l], gmlp_ln_b[l],
                             gmlp_w_s[l], gmlp_b_s[l], out_norm[l], w_out[l])
        x = x + 0.5 * swiglu(rmsnorm(x, ffn2_norm[l]), ffn2_w_gate[l], ffn2_w_up[l], ffn2_w_down[l])
    return rmsnorm(x, final_norm)
```

```python
import contextlib
import numpy as np
import concourse.bass as bass
import concourse.mybir as mybir
from concourse.bass_utils import run_bass_kernel_spmd

F32 = mybir.dt.float32
BF16 = mybir.dt.bfloat16
AF = mybir.ActivationFunctionType
ALU = mybir.AluOpType

D = 2048
NTOK = 2048
DFF = 5632
NCORES = 8
DEPTH = 4
EPS = 1e-6
KC = D // 128
TT = 1024
NTT = NTOK // TT
FH = 22
INCOLS = 6152


class Cnt:
    def __init__(self, h):
        self.h = h
        self.v = 0


class Prog:
    ENGS = ("sync", "scalar", "vector", "gpsimd", "tensor")

    def __init__(self, nc, stack):
        self.nc = nc
        self.stack = stack
        self.q = {e: [] for e in self.ENGS}
        self.waited = {e: {} for e in self.ENGS}
        self.cache = {}

    def sem(self, name):
        if name not in self.cache:
            self.cache[name] = Cnt(self.stack.enter_context(self.nc.semaphore(name)))
        return self.cache[name]

    def barrier(self):
        for eng in self.ENGS:
            self.op(eng, None, waits=[(c, c.v) for c in self.cache.values()])

    def sems(self, name, n):
        return [self.sem("%s%d" % (name, i)) for i in range(n)]

    def op(self, eng, fn, waits=(), inc=None, k=1):
        ws = []
        for (c, v) in waits:
            if v <= 0:
                continue
            key = id(c)
            if self.waited[eng].get(key, 0) >= v:
                continue
            self.waited[eng][key] = v
            ws.append((c.h, v))
        tgt = None
        if inc is not None:
            inc.v += k
            tgt = inc.v
        self.q[eng].append((ws, fn, inc.h if inc is not None else None, k))
        return tgt

    def dma(self, eng, out, in_, waits=(), inc=None):
        return self.op(eng, lambda e: e.dma_start(out=out, in_=in_), waits, inc, 16)

    def wait_only(self, eng, waits):
        self.q[eng].append(([(c.h, v) for (c, v) in waits if v > 0], None, None, 0))

    def emit(self):
        with self.nc.Block() as block:
            for name in self.ENGS:
                q = self.q[name]

                def body(e, q=q):
                    for ws, fn, inc, k in q:
                        for (h, v) in ws:
                            e.wait_ge(h, v)
                        if fn is None:
                            continue
                        ins = fn(e)
                        if inc is not None:
                            ins.then_inc(inc, k)

                getattr(block, name)(body)
        self.q = {e: [] for e in self.ENGS}


class Ctx:
    pass


_UID = [0]


def _uid():
    _UID[0] += 1
    return _UID[0]


def alloc_common(nc, stack, p, tt=TT, nps=8, stat_bank=6):
    c = Ctx()
    uid = _uid()
    c.nc = nc
    c.p = p
    c.TT = tt
    c.NSEG = tt // 512
    c.stat_bank = stat_bank
    sb = lambda name, shape, dt: stack.enter_context(nc.sbuf_tensor("sb%d_%s" % (uid, name), shape, dt))
    c.sb = sb
    c.uid = uid
    c.stack = stack
    c.ones = sb("ones", [128, 128], F32)
    c.xs = sb("xs", [128, 3, tt], F32)
    c.sq = sb("sq", [128, 2, tt], F32)
    c.rstd = sb("rstd", [128, tt], F32)
    c.h = sb("h", [128, KC, tt], BF16)
    c.ps = stack.enter_context(nc.psum_tensor("ps%d" % uid, [128, nps, 512], F32))
    c.xs_full = p.sems("xsfull", 3)
    c.xs_st = p.sems("xsst", 3)
    c.xs_cond = [[], [], []]
    c.xs_n = 0
    c.act_sq = p.sem("actsq")
    c.pe_st = p.sem("pest")
    c.dve_m = p.sem("dvem")
    c.act_m = p.sem("actm")
    c.dve_h = p.sem("dveh")
    c.setup_v = p.sem("setupv")
    c.setup_d = p.sem("setupd")
    p.op("vector", lambda e: e.memset(c.ones[:], 1.0), inc=c.setup_v)
    c.sq_n = 0
    c.h_free = []
    c.ps_free_waits = []
    return c


def xs_acquire(c):
    s = c.xs_n % 3
    c.xs_n += 1
    return s, list(c.xs_cond[s])


def norm_stats_and_h(c, xsrc, gcol, tt, out_fn=None):
    p = c.p
    TT = c.TT
    t0 = tt * TT
    SB = c.stat_bank
    for kc in range(KC):
        s, w = xs_acquire(c)
        full = p.dma("sync", c.xs[:, s, :], xsrc[kc * 128:(kc + 1) * 128, t0:t0 + TT], waits=w, inc=c.xs_full[s])
        q = c.sq_n % 2
        c.sq_n += 1
        a = p.op("scalar",
                 lambda e, s=s, q=q: e.activation(out=c.sq[:, q, :], in_=c.xs[:, s, :], func=AF.Square),
                 waits=[(c.xs_full[s], full), (c.pe_st, c.pe_st.v - 1)], inc=c.act_sq)
        c.xs_cond[s] = [(c.act_sq, a)]
        extra = list(c.ps_free_waits) if kc == 0 else []
        for sg_ in range(c.NSEG):
            p.op("tensor",
                 lambda e, q=q, kc=kc, sg_=sg_: e.matmul(c.ps[:, SB + sg_, :], lhsT=c.ones[:],
                                                        rhs=c.sq[:, q, sg_ * 512:(sg_ + 1) * 512],
                                                        start=(kc == 0), stop=(kc == KC - 1)),
                 waits=([(c.act_sq, a), (c.setup_v, c.setup_v.v)] + extra) if sg_ == 0 else [],
                 inc=(c.pe_st if sg_ == c.NSEG - 1 else None))
    st_done = c.pe_st.v
    psv = c.ps[:, SB:SB + c.NSEG, :]
    rv = c.rstd[:].rearrange("p (a b) -> p a b", a=c.NSEG)
    d1 = p.op("vector",
              lambda e: e.tensor_scalar(out=rv, in0=psv, scalar1=1.0 / D, scalar2=EPS, op0=ALU.mult, op1=ALU.add),
              waits=[(c.pe_st, st_done), (c.dve_h, c.dve_h.v)], inc=c.dve_m)
    c.ps_free_waits = [(c.dve_m, d1)]
    a1 = p.op("scalar", lambda e: e.activation(out=c.rstd[:], in_=c.rstd[:], func=AF.Sqrt),
              waits=[(c.dve_m, d1)], inc=c.act_m)
    d2 = p.op("vector", lambda e: e.reciprocal(out=c.rstd[:], in_=c.rstd[:]),
              waits=[(c.act_m, a1)], inc=c.dve_m)
    for kc in range(KC):
        s, w = xs_acquire(c)
        full = p.dma("sync", c.xs[:, s, :], xsrc[kc * 128:(kc + 1) * 128, t0:t0 + TT], waits=w, inc=c.xs_full[s])
        if out_fn is None:
            waits = [(c.xs_full[s], full), (c.dve_m, d2), (c.setup_d, c.setup_d.v)]
            if kc == 0:
                waits += c.h_free
            hv = p.op("vector",
                 lambda e, s=s, kc=kc: e.scalar_tensor_tensor(out=c.h[:, kc, :], in0=c.xs[:, s, :],
                                                              scalar=gcol[:, kc:kc + 1], in1=c.rstd[:],
                                                              op0=ALU.mult, op1=ALU.mult),
                 waits=waits, inc=c.dve_h)
            c.xs_cond[s] = [(c.dve_h, hv)]
        else:
            out_fn(kc, s, [(c.xs_full[s], full), (c.dve_m, d2), (c.setup_d, c.setup_d.v)])
    return c.dve_h.v


def alloc_ffn(c):
    p = c.p
    sb = c.sb
    c.hid = sb("hid", [128, FH, TT], BF16)
    c.sg = sb("sg", [128, 2, 512], F32)
    c.wgu = sb("wgu", [128, 2, 2, KC, 256], BF16)
    c.wd = sb("wd", [128, 2, FH, 256], BF16)
    c.wgu_full = p.sems("wgufull", 2)
    c.wd_full = p.sems("wdfull", 2)
    c.pe_gu = p.sem("pegu")
    c.act_sg = p.sem("actsg")
    c.dve_hid = p.sem("dvehid")
    c.pe_dn = p.sem("pedn")
    c.dve_res = p.sem("dveres")
    c.n_panel = 0
    c.n_gu = c.pe_gu.v
    c.n_dpanel = 0
    c.n_dn = c.pe_dn.v
    c.panel_done = {}
    c.dpanel_done = {}
    c.hid_free = []


def ffn_body(c, xin, xout, gcol, wg, wu, wd):
    p = c.p
    wgv = wg.rearrange("(kc p) f -> p kc f", p=128)
    wuv = wu.rearrange("(kc p) f -> p kc f", p=128)
    wdv = wd.rearrange("(fc p) d -> p fc d", p=128)
    for tt in range(NTT):
        t0 = tt * TT
        h_ready = norm_stats_and_h(c, xin, gcol, tt)
        for hf in range(2):
            for pn in range(FH // 2):
                col0 = (hf * FH + pn * 2) * 128
                npn = c.n_panel
                b = npn % 2
                c.n_panel += 1
                wfree = [(c.pe_gu, c.panel_done[npn - 2])] if npn >= 2 else []
                p.dma("gpsimd", c.wgu[:, b, 0, :, :], wgv[:, :, col0:col0 + 256], waits=wfree, inc=c.wgu_full[b])
                wl = p.dma("gpsimd", c.wgu[:, b, 1, :, :], wuv[:, :, col0:col0 + 256], waits=wfree, inc=c.wgu_full[b])
                for jj in range(2):
                    j = pn * 2 + jj
                    for th in range(2):
                        n = c.n_gu
                        c.n_gu += 1
                        gb = n % 2
                        ub = 2 + n % 2
                        for kc in range(KC):
                            waits = []
                            if kc == 0:
                                waits = [(c.wgu_full[b], wl), (c.dve_h, h_ready), (c.act_sg, n - 1)]
                            p.op("tensor",
                                 lambda e, b=b, kc=kc, jj=jj, th=th, gb=gb: e.matmul(
                                     c.ps[:, gb, :], lhsT=c.wgu[:, b, 0, kc, jj * 128:(jj + 1) * 128],
                                     rhs=c.h[:, kc, th * 512:(th + 1) * 512], start=(kc == 0), stop=(kc == KC - 1)),
                                 waits=waits)
                        for kc in range(KC):
                            waits = []
                            if kc == 0:
                                waits = [(c.dve_hid, n - 1)]
                            last = (kc == KC - 1)
                            p.op("tensor",
                                 lambda e, b=b, kc=kc, jj=jj, th=th, ub=ub: e.matmul(
                                     c.ps[:, ub, :], lhsT=c.wgu[:, b, 1, kc, jj * 128:(jj + 1) * 128],
                                     rhs=c.h[:, kc, th * 512:(th + 1) * 512], start=(kc == 0), stop=(kc == KC - 1)),
                                 waits=waits, inc=(c.pe_gu if last else None))
                        gu = c.pe_gu.v
                        a = p.op("scalar",
                                 lambda e, n=n, gb=gb: e.activation(out=c.sg[:, n % 2, :], in_=c.ps[:, gb, :], func=AF.Silu),
                                 waits=[(c.pe_gu, gu), (c.dve_hid, n - 1)], inc=c.act_sg)
                        waits = [(c.act_sg, a), (c.pe_gu, gu)]
                        if j == 0 and th == 0:
                            waits += c.hid_free
                        p.op("vector",
                             lambda e, n=n, ub=ub, j=j, th=th: e.tensor_tensor(
                                 out=c.hid[:, j, th * 512:(th + 1) * 512], in0=c.sg[:, n % 2, :], in1=c.ps[:, ub, :],
                                 op=ALU.mult),
                             waits=waits, inc=c.dve_hid)
                c.panel_done[npn] = c.pe_gu.v
            if hf == 1:
                c.h_free = [(c.pe_gu, c.pe_gu.v)]
            hid_ready = c.dve_hid.v
            xsrc = xin if hf == 0 else xout
            for pd in range(8):
                col0 = pd * 256
                npd = c.n_dpanel
                b = npd % 2
                c.n_dpanel += 1
                wl = p.dma("gpsimd", c.wd[:, b, :, :], wdv[:, hf * FH:(hf + 1) * FH, col0:col0 + 256],
                           waits=([(c.pe_dn, c.dpanel_done[npd - 2])] if npd >= 2 else []), inc=c.wd_full[b])
                for ii in range(2):
                    i = pd * 2 + ii
                    s, w = xs_acquire(c)
                    full = p.dma("sync", c.xs[:, s, :], xsrc[i * 128:(i + 1) * 128, t0:t0 + TT], waits=w,
                                 inc=c.xs_full[s])
                    for th in range(2):
                        n = c.n_dn
                        c.n_dn += 1
                        ob = 4 + n % 2
                        for f in range(FH):
                            waits = []
                            if f == 0:
                                waits = [(c.wd_full[b], wl), (c.dve_hid, hid_ready), (c.dve_res, n - 1)]
                            last = (f == FH - 1)
                            p.op("tensor",
                                 lambda e, b=b, f=f, ii=ii, th=th, ob=ob: e.matmul(
                                     c.ps[:, ob, :], lhsT=c.wd[:, b, f, ii * 128:(ii + 1) * 128],
                                     rhs=c.hid[:, f, th * 512:(th + 1) * 512], start=(f == 0), stop=(f == FH - 1)),
                                 waits=waits, inc=(c.pe_dn if last else None))
                        dn = c.pe_dn.v
                        r = p.op("vector",
                                 lambda e, s=s, th=th, ob=ob: e.scalar_tensor_tensor(
                                     out=c.xs[:, s, th * 512:(th + 1) * 512], in0=c.ps[:, ob, :], scalar=0.5,
                                     in1=c.xs[:, s, th * 512:(th + 1) * 512], op0=ALU.mult, op1=ALU.add),
                                 waits=[(c.pe_dn, dn), (c.xs_full[s], full)], inc=c.dve_res)
                    sv = p.dma("sync", xout[i * 128:(i + 1) * 128, t0:t0 + TT], c.xs[:, s, :],
                               waits=[(c.dve_res, r)], inc=c.xs_st[s])
                    c.xs_cond[s] = [(c.xs_st[s], sv)]
                c.dpanel_done[npd] = c.pe_dn.v
            c.hid_free = [(c.pe_dn, c.pe_dn.v)]


def finish(c):
    p = c.p
    p.wait_only("sync", [(c.xs_st[s], c.xs_st[s].v) for s in range(3)])


def build_ffn():
    nc = bass.Bass("TRN2", target_bir_lowering=False)
    xin = nc.dram_tensor("xin", [D, NTOK], F32, kind="ExternalInput").ap()
    g = nc.dram_tensor("g", [128, KC], F32, kind="ExternalInput").ap()
    wg = nc.dram_tensor("wg", [D, DFF], F32, kind="ExternalInput").ap()
    wu = nc.dram_tensor("wu", [D, DFF], F32, kind="ExternalInput").ap()
    wd = nc.dram_tensor("wd", [DFF, D], F32, kind="ExternalInput").ap()
    xout = nc.dram_tensor("xout", [D, NTOK], F32, kind="ExternalOutput").ap()
    with contextlib.ExitStack() as stack:
        p = Prog(nc, stack)
        c = alloc_common(nc, stack, p)
        alloc_ffn(c)
        gcol = c.sb("gcol", [128, KC], F32)
        p.dma("sync", gcol[:], g[:, :], inc=c.setup_d)
        ffn_body(c, xin, xout, gcol, wg, wu, wd)
        finish(c)
        p.emit()
    return nc


FOX_SCALE = 128.0 ** -0.5
GAM = [1.0 - 2.0 ** -(5 + h) for h in range(4)]
GAM64 = [g ** 64 for g in GAM]
TA = 512
NBLK = TA // 128
AX = mybir.AxisListType


class RowSplit:
    def __init__(self, parts):
        self.parts = parts
        self.h = parts[0].shape[0]

    def __getitem__(self, key):
        rs, cs = key
        i = rs.start // self.h
        assert (rs.stop - 1) // self.h == i
        return self.parts[i][rs.start - i * self.h:rs.stop - i * self.h, cs]


def _parts(x):
    return x.parts if isinstance(x, RowSplit) else [x]


class Slots:
    def __init__(self, p, name, n):
        self.sem = p.sems(name, n)
        self.cond = [[] for _ in range(n)]
        self.i = 0
        self.n = n

    def acquire(self):
        s = self.i % self.n
        self.i += 1
        return s, list(self.cond[s])


def mixa_body(c, xin, gcol, w_in, io):
    p = c.p
    nc = c.nc
    sb = c.sb
    winv = w_in.rearrange("(kc p) f -> p kc f", p=128)
    tabv = io["tab"].rearrange("(b p) f -> p b f", p=128)
    wp = sb("wp", [128, 2, 8192], BF16)
    wps = Slots(p, "wps", 2)
    wpfm = lambda b: wp[:, b, 0:4096].rearrange("p (k c) -> p k c", c=256)
    wptm = lambda b: wp[:, b, :].rearrange("p (k c) -> p k c", c=512)
    wpfz = lambda b: wp[:, b, 0:128].rearrange("p (k c) -> p k c", c=8)
    tabt = sb("tabt", [128, 2, NBLK, 268], F32)
    tabs = Slots(p, "tabs", 2)
    stg16 = sb("stg16", [128, 4, 512], BF16)
    st16 = Slots(p, "st16", 4)
    stg32 = sb("stg32", [128, 2, 512], F32)
    st32 = Slots(p, "st32", 2)
    u = sb("u", [128, 4, TA], F32)
    rv = sb("rv", [128, NBLK, 512], BF16)
    kr = sb("kr", [128, NBLK, 512], BF16)
    kz = sb("kz", [128, NBLK, 512], BF16)
    qx = sb("qx", [128, NBLK, 512], BF16)
    qg = sb("qg", [128, NBLK, 512], BF16)
    vln = sb("vln", [128, NBLK, 512], BF16)
    rot = sb("rot", [128, 2, 2, 512], F32)
    rots = Slots(p, "rots", 2)
    st = sb("lnst", [128, 24], F32)
    fzt = sb("fzt", [8, 512], F32)
    onesr = sb("onesr", [8, 512], F32)
    cneg = sb("cneg", [8, NTOK], F32)
    S32 = sb("S32", [128, 512], F32)
    Sb = sb("Sb", [128, 2 * NBLK, 512], BF16)
    krT = sb("krT", [128, 512], BF16)
    qxT = sb("qxT", [128, 512], BF16)
    sm = sb("sm", [128, 512], BF16)
    y1 = sb("y1", [128, 512], F32)
    y2 = sb("y2", [128, 512], F32)
    pst = c.stack.enter_context(nc.psum_tensor("pst%d" % c.uid, [128, 2, 1024], BF16))
    c.pe = p.sem("pe")
    c.act = p.sem("act")
    c.dve = p.sem("dve")
    misc = p.sem("miscst")
    pj_cond = [[], []]
    kv_cond = [[], []]
    b4_cond = []
    pst_cond = [[], []]
    pj_n = [0]
    kv_n = [0]
    u_free = []
    ret_free = []
    vln_free = []
    p.op("vector", lambda e: e.memset(onesr[:], 1.0), inc=c.setup_v)
    p.op("vector", lambda e: e.memset(S32[:], 0.0), inc=c.setup_v)
    setupw = [(c.setup_d, c.setup_d.v), (c.setup_v, c.setup_v.v)]

    def V4(ap):
        return ap.rearrange("p (a b) -> p a b", a=4)

    def bc(ap4):
        return ap4.unsqueeze(2).to_broadcast([128, 4, 128])

    def bh(ap128):
        return ap128.unsqueeze(1).to_broadcast([128, 4, 128])

    def bh64(ap64):
        return ap64.unsqueeze(1).to_broadcast([128, 4, 64])

    def proj(b, waits, lhs_fn, rhs_fn, out_fn):
        n = pj_n[0]
        pj_n[0] += 1
        bank = n % 2
        for kc in range(KC):
            p.op("tensor",
                 lambda e, kc=kc: e.matmul(out_fn(bank), lhsT=lhs_fn(kc), rhs=rhs_fn(kc), start=(kc == 0),
                                           stop=(kc == KC - 1)),
                 waits=(list(waits) + pj_cond[bank]) if kc == 0 else [], inc=(c.pe if kc == KC - 1 else None))
        return bank, c.pe.v

    def store16(src_fn, dst_ap, waits, eng_op):
        s, w = st16.acquire()
        a = p.op("scalar", lambda e: eng_op(e, stg16[:, s, :]), waits=list(waits) + w, inc=c.act)
        sv = p.dma("sync", dst_ap, src_fn(stg16[:, s, :]), waits=[(c.act, a)], inc=st16.sem[s])
        st16.cond[s] = [(st16.sem[s], sv)]
        return a

    for tt in range(NTOK // TA):
        t0 = tt * TA
        h_ready = norm_stats_and_h(c, xin, gcol, tt)
        hw = [(c.dve_h, h_ready)]
        ts_, w = tabs.acquire()
        tk = p.dma("sync", tabt[:, ts_], tabv[:, tt * NBLK:(tt + 1) * NBLK, :], waits=w, inc=tabs.sem[ts_])
        tabw = [(tabs.sem[ts_], tk)]
        for name, cbase, ncol in (("fq", 0, 1024), ("fk", 1024, 1024), ("rg", 4616, 512), ("gu", 5128, 512)):
            for pn in range(ncol // 256):
                col0 = cbase + pn * 256
                b, w = wps.acquire()
                t = p.dma("gpsimd", wpfm(b), winv[:, :, col0:col0 + 256], waits=w, inc=wps.sem[b])
                for jj in range(2):
                    ch = pn * 2 + jj
                    bank, pt = proj(b, [(wps.sem[b], t)] + hw,
                                    lambda kc, b=b, jj=jj: wpfm(b)[:, kc, jj * 128:(jj + 1) * 128],
                                    lambda kc: c.h[:, kc, :], lambda bank: c.ps[:, bank, :])
                    pw = [(c.pe, pt)]
                    if name == "fq":
                        a = store16(lambda s_: s_, io["qT"][ch * 128:(ch + 1) * 128, t0:t0 + TA], pw,
                                    lambda e, o, bank=bank: e.mul(out=o, in_=c.ps[:, bank, :], mul=FOX_SCALE))
                    elif name == "fk":
                        a = store16(lambda s_: s_, io["kT"][ch * 128:(ch + 1) * 128, t0:t0 + TA], pw,
                                    lambda e, o, bank=bank: e.copy(out=o, in_=c.ps[:, bank, :]))
                    elif name == "rg":
                        s, w2 = st32.acquire()
                        a = p.op("scalar", lambda e, s=s, bank=bank: e.activation(out=stg32[:, s, :], in_=c.ps[:, bank, :],
                                                                                func=AF.Silu),
                                 waits=pw + w2, inc=c.act)
                        sv = p.dma("sync", io["rgs"][ch * 128:(ch + 1) * 128, t0:t0 + TA], stg32[:, s, :],
                                   waits=[(c.act, a)], inc=st32.sem[s])
                        st32.cond[s] = [(st32.sem[s], sv)]
                    else:
                        a = p.op("scalar", lambda e, ch=ch, bank=bank: e.activation(out=u[:, ch, :], in_=c.ps[:, bank, :],
                                                                                  func=AF.Gelu_apprx_tanh),
                                 waits=pw + (u_free if ch == 0 else []), inc=c.act)
                    pj_cond[bank] = [(c.act, a)]
                wps.cond[b] = [(c.pe, pt)]
        b, w = wps.acquire()
        t = p.dma("gpsimd", wpfz(b), winv[:, :, 3072:3080], waits=w, inc=wps.sem[b])
        bank, pt = proj(b, [(wps.sem[b], t)] + hw, lambda kc, b=b: wpfz(b)[:, kc, :], lambda kc: c.h[:, kc, :],
                        lambda bank: c.ps[0:8, bank, :])
        wps.cond[b] = [(c.pe, pt)]
        a = p.op("scalar", lambda e, bank=bank: e.activation(out=fzt[:], in_=c.ps[0:8, bank, :], func=AF.Exp,
                                                             bias=io["negb"][:, 0:1], scale=-1.0),
                 waits=[(c.pe, pt), (c.dve, c.dve.v)] + setupw, inc=c.act)
        pj_cond[bank] = [(c.act, a)]
        a = p.op("scalar", lambda e: e.activation(out=fzt[:], in_=fzt[:], func=AF.Ln, bias=1.0), waits=[(c.act, a)],
                 inc=c.act)
        init = 0.0 if tt == 0 else cneg[:, t0 - 1:t0]
        p.op("vector", lambda e, init=init, t0=t0: e.tensor_tensor_scan(out=cneg[:, t0:t0 + TA], data0=onesr[:], data1=fzt[:],
                                                                 initial=init, op0=ALU.mult, op1=ALU.add),
             waits=[(c.act, a), (c.dve, c.dve.v)] + setupw, inc=c.dve)
        ready = {}
        for name, col0 in (("fv0", 2048), ("fv1", 2560), ("rv", 4104), ("rk", 3592), ("rq", 3080), ("gv", 5640)):
            b, w = wps.acquire()
            t = p.dma("gpsimd", wptm(b), winv[:, :, col0:col0 + 512], waits=w, inc=wps.sem[b])
            for tb in range(NBLK):
                bank, pt = proj(b, [(wps.sem[b], t)] + hw,
                                lambda kc, tb=tb: c.h[:, kc, tb * 128:(tb + 1) * 128],
                                lambda kc, b=b: wptm(b)[:, kc, :], lambda bank: c.ps[:, bank, :])
                pw = [(c.pe, pt)]
                psb = c.ps[:, bank, :]
                psv = V4(psb)
                r0 = t0 + tb * 128
                if name in ("fv0", "fv1"):
                    hc = 0 if name == "fv0" else 512
                    a = store16(lambda s_: s_, io["v"][r0:r0 + 128, hc:hc + 512], pw,
                                lambda e, o, psb=psb: e.copy(out=o, in_=psb))
                    pj_cond[bank] = [(c.act, a)]
                elif name == "rv":
                    a = p.op("scalar", lambda e, tb=tb, psb=psb: e.copy(out=rv[:, tb, :], in_=psb),
                             waits=pw + (ret_free if tb == 0 else []), inc=c.act)
                    pj_cond[bank] = [(c.act, a)]
                    ready[("rv", tb)] = [(c.act, a)]
                elif name in ("rk", "rq"):
                    rs, w2 = rots.acquire()
                    r1 = rot[:, rs, 0, :]
                    r2 = rot[:, rs, 1, :]
                    cosb = bh(tabt[:, ts_, tb, 0:128])
                    sina = bh64(tabt[:, ts_, tb, 128:192])
                    sinb = bh64(tabt[:, ts_, tb, 192:256])
                    p.op("vector", lambda e, psv=psv, r1=r1, cosb=cosb: e.tensor_tensor(out=V4(r1), in0=psv, in1=cosb,
                                                                                      op=ALU.mult),
                         waits=pw + w2 + tabw, inc=c.dve)
                    p.op("vector", lambda e, psv=psv, r2=r2, sina=sina: e.tensor_tensor(
                        out=V4(r2)[:, :, 0:64], in0=psv[:, :, 64:128], in1=sina, op=ALU.mult), inc=c.dve)
                    d = p.op("vector", lambda e, psv=psv, r2=r2, sinb=sinb: e.tensor_tensor(
                        out=V4(r2)[:, :, 64:128], in0=psv[:, :, 0:64], in1=sinb, op=ALU.mult), inc=c.dve)
                    pj_cond[bank] = [(c.dve, d)]
                    d = p.op("vector", lambda e, r1=r1, r2=r2: e.tensor_tensor(out=r1, in0=r1, in1=r2, op=ALU.add),
                             waits=[(c.dve, d)], inc=c.dve)
                    fw = ret_free if tb == 0 else []
                    if name == "rk":
                        a = p.op("scalar", lambda e, tb=tb, r1=r1: e.copy(out=kr[:, tb, :], in_=r1),
                                 waits=[(c.dve, d)] + fw, inc=c.act)
                        zb = bc(tabt[:, ts_, tb, 264:268])
                        d2 = p.op("vector", lambda e, tb=tb, r1=r1, zb=zb: e.tensor_tensor(out=V4(kz[:, tb, :]), in0=V4(r1),
                                                                                         in1=zb, op=ALU.mult),
                                  waits=[(c.dve, d)] + fw, inc=c.dve)
                        rots.cond[rs] = [(c.act, a), (c.dve, d2)]
                        ready[("kr", tb)] = [(c.act, a)]
                        ready[("kz", tb)] = [(c.dve, d2)]
                    else:
                        xb = bc(tabt[:, ts_, tb, 256:260])
                        gb_ = bc(tabt[:, ts_, tb, 260:264])
                        p.op("vector", lambda e, tb=tb, r1=r1, xb=xb: e.tensor_tensor(out=V4(qx[:, tb, :]), in0=V4(r1),
                                                                                    in1=xb, op=ALU.mult),
                             waits=[(c.dve, d)] + fw, inc=c.dve)
                        d2 = p.op("vector", lambda e, tb=tb, r1=r1, gb_=gb_: e.tensor_tensor(out=V4(qg[:, tb, :]),
                                                                                           in0=V4(r1), in1=gb_,
                                                                                           op=ALU.mult), inc=c.dve)
                        rots.cond[rs] = [(c.dve, d2)]
                        ready[("q", tb)] = [(c.dve, d2)]
                else:
                    rs, w2 = rots.acquire()
                    r1 = rot[:, rs, 0, :]
                    r2 = rot[:, rs, 1, :]
                    a = p.op("scalar", lambda e, psb=psb, r1=r1: e.activation(out=r1, in_=psb, func=AF.Gelu_apprx_tanh),
                             waits=pw + w2, inc=c.act)
                    pj_cond[bank] = [(c.act, a)]
                    a2 = p.op("scalar", lambda e, r1=r1, r2=r2: e.activation(out=r2, in_=r1, func=AF.Square),
                              waits=[(c.act, a)], inc=c.act)
                    d = p.op("vector", lambda e, r1=r1: e.tensor_reduce(out=st[:, 0:4], in_=V4(r1), axis=AX.X, op=ALU.add),
                             waits=[(c.act, a), (c.dve, c.dve.v)], inc=c.dve)
                    d = p.op("vector", lambda e, r2=r2: e.tensor_reduce(out=st[:, 4:8], in_=V4(r2), axis=AX.X, op=ALU.add),
                             waits=[(c.act, a2)], inc=c.dve)
                    d = p.op("vector", lambda e: e.tensor_scalar(out=st[:, 8:12], in0=st[:, 0:4], scalar1=1.0 / 128,
                                                                 scalar2=None, op0=ALU.mult),
                             waits=[(c.dve, d)], inc=c.dve)
                    d = p.op("vector", lambda e: e.tensor_tensor(out=st[:, 12:16], in0=st[:, 8:12], in1=st[:, 8:12],
                                                                 op=ALU.mult), waits=[(c.dve, d)], inc=c.dve)
                    d = p.op("vector", lambda e: e.scalar_tensor_tensor(out=st[:, 16:20], in0=st[:, 4:8], scalar=1.0 / 128,
                                                                        in1=st[:, 12:16], op0=ALU.mult,
                                                                        op1=ALU.subtract), waits=[(c.dve, d)], inc=c.dve)
                    d = p.op("vector", lambda e: e.tensor_scalar(out=st[:, 16:20], in0=st[:, 16:20], scalar1=EPS,
                                                                 scalar2=None, op0=ALU.add), waits=[(c.dve, d)], inc=c.dve)
                    a3 = p.op("scalar", lambda e: e.activation(out=st[:, 16:20], in_=st[:, 16:20], func=AF.Sqrt),
                              waits=[(c.dve, d)], inc=c.act)
                    d = p.op("vector", lambda e: e.reciprocal(out=st[:, 20:24], in_=st[:, 16:20]), waits=[(c.act, a3)],
                             inc=c.dve)
                    d = p.op("vector", lambda e, r1=r1: e.tensor_tensor(out=V4(r1), in0=V4(r1), in1=bc(st[:, 8:12]),
                                                                      op=ALU.subtract), waits=[(c.dve, d)], inc=c.dve)
                    d = p.op("vector", lambda e, r1=r1: e.tensor_tensor(out=V4(r1), in0=V4(r1), in1=bc(st[:, 20:24]),
                                                                      op=ALU.mult), waits=[(c.dve, d)], inc=c.dve)
                    d = p.op("vector", lambda e, r1=r1: e.tensor_tensor(out=r1, in0=r1, in1=io["lnb3"][:, 0, :],
                                                                      op=ALU.mult), waits=[(c.dve, d)] + setupw,
                             inc=c.dve)
                    d = p.op("vector", lambda e, r1=r1, tb=tb: e.tensor_tensor(out=vln[:, tb, :], in0=r1,
                                                                             in1=io["lnb3"][:, 1, :], op=ALU.add),
                             waits=[(c.dve, d)] + (vln_free if tb == 0 else []), inc=c.dve)
                    rots.cond[rs] = [(c.dve, d)]
                    ready[("vln", tb)] = [(c.dve, d)]
            wps.cond[b] = [(c.pe, pt)]
        for n in range(2 * NBLK):
            tb, a_ = n // 2, n % 2
            kb = 2 + kv_n[0] % 2
            ci = kv_n[0] % 2
            kv_n[0] += 1
            for hh in range(4):
                sl = slice(hh * 128, (hh + 1) * 128)
                p.op("tensor", lambda e, kb=kb, sl=sl, a_=a_, tb=tb: e.matmul(
                    c.ps[:, kb, sl], lhsT=kz[a_ * 64:(a_ + 1) * 64, tb, sl], rhs=rv[a_ * 64:(a_ + 1) * 64, tb, sl],
                    start=True, stop=True),
                     waits=(ready[("kz", tb)] + ready[("rv", tb)] + kv_cond[ci]) if hh == 0 else [],
                     inc=(c.pe if hh == 3 else None))
            pt = c.pe.v
            a = p.op("scalar", lambda e, n=n: e.copy(out=Sb[:, n, :], in_=S32[:]),
                     waits=[(c.dve, c.dve.v)] + (ret_free if n == 0 else []) + setupw, inc=c.act)
            ready[("Sb", n)] = [(c.act, a)]
            for hh in range(4):
                sl = slice(hh * 128, (hh + 1) * 128)
                d = p.op("vector", lambda e, kb=kb, sl=sl, hh=hh: e.scalar_tensor_tensor(
                    out=S32[:, sl], in0=S32[:, sl], scalar=GAM64[hh], in1=c.ps[:, kb, sl], op0=ALU.mult, op1=ALU.add),
                         waits=[(c.pe, pt), (c.act, a)] if hh == 0 else [], inc=c.dve)
            kv_cond[ci] = [(c.dve, d)]
        s32_done = [(c.dve, d)]
        for tb in range(NBLK):
            r0 = t0 + tb * 128
            for gg in range(4):
                sl = slice(gg * 128, (gg + 1) * 128)
                p.op("tensor", lambda e, sl=sl, tb=tb, gg=gg: e.matmul(c.ps[:, 4, sl], lhsT=vln[:, tb, sl],
                                                                     rhs=io["wsb"][:, gg, :], start=True, stop=True),
                     waits=(ready[("vln", tb)] + b4_cond + setupw) if gg == 0 else [], inc=(c.pe if gg == 3 else None))
            pt = c.pe.v
            d = p.op("vector", lambda e: e.tensor_tensor(out=y1[:], in0=c.ps[:, 4, :], in1=io["lnb3"][:, 2, :], op=ALU.add),
                     waits=[(c.pe, pt), (c.act, c.act.v), (c.dve, c.dve.v)], inc=c.dve)
            d = p.op("vector", lambda e, tb=tb: e.tensor_tensor(out=V4(y1[:]), in0=V4(y1[:]),
                                                               in1=u[:, :, tb * 128:(tb + 1) * 128], op=ALU.mult),
                     waits=[(c.dve, d)], inc=c.dve)
            a = p.op("scalar", lambda e: e.activation(out=y2[:], in_=y1[:], func=AF.Square), waits=[(c.dve, d)], inc=c.act)
            p.op("tensor", lambda e: e.matmul(c.ps[:, 4, :], lhsT=c.ones[:], rhs=y2[:], start=True, stop=True),
                 waits=[(c.act, a), (c.dve, d)], inc=c.pe)
            pt = c.pe.v
            d = p.op("vector", lambda e: e.tensor_scalar(out=y2[:], in0=c.ps[:, 4, :], scalar1=1.0 / 128, scalar2=EPS,
                                                         op0=ALU.mult, op1=ALU.add), waits=[(c.pe, pt)], inc=c.dve)
            b4_cond = [(c.dve, d)]
            a = p.op("scalar", lambda e: e.activation(out=y2[:], in_=y2[:], func=AF.Sqrt), waits=[(c.dve, d)], inc=c.act)
            d = p.op("vector", lambda e: e.reciprocal(out=y2[:], in_=y2[:]), waits=[(c.act, a)], inc=c.dve)
            d = p.op("vector", lambda e: e.tensor_tensor(out=y1[:], in0=y1[:], in1=y2[:], op=ALU.mult),
                     waits=[(c.dve, d)], inc=c.dve)
            s, w = st16.acquire()
            for gg in range(4):
                sl = slice(gg * 128, (gg + 1) * 128)
                a = p.op("vector", lambda e, s=s, sl=sl, gg=gg: e.tensor_scalar(out=stg16[:, s, sl], in0=y1[:, sl],
                                                                              scalar1=io["ong"][:, gg:gg + 1],
                                                                              scalar2=None, op0=ALU.mult),
                         waits=([(c.dve, d)] + w + setupw) if gg == 0 else [], inc=c.dve)
            sv = p.dma("sync", io["ytg"].rearrange("(g c) t -> c g t", c=128)[:, :, r0:r0 + 128], V4(stg16[:, s, :]),
                       waits=[(c.dve, a)], inc=st16.sem[s])
            st16.cond[s] = [(st16.sem[s], sv)]
        u_free = [(c.dve, c.dve.v)]
        vln_free = [(c.pe, c.pe.v)]
        for tb in range(NBLK):
            r0 = t0 + tb * 128
            for src, key, dstT, pb in ((kr, "kr", krT, 0), (qx, "q", qxT, 1), (qg, "q", None, 0)):
                for hh in range(4):
                    sl = slice(hh * 128, (hh + 1) * 128)
                    p.op("tensor", lambda e, src=src, sl=sl, tb=tb, pb=pb: e.transpose(out=pst[:, pb, sl], in_=src[:, tb, sl],
                                                                                     identity=io["identb"][:]),
                         waits=(ready[(key, tb)] + pst_cond[pb] + setupw) if hh == 0 else [],
                         inc=(c.pe if hh == 3 else None))
                pt = c.pe.v
                if dstT is not None:
                    d = p.op("vector", lambda e, dstT=dstT, pb=pb: e.tensor_copy(out=dstT[:], in_=pst[:, pb, 0:512]),
                             waits=[(c.pe, pt), (c.pe, c.pe.v)], inc=c.dve)
                    pst_cond[pb] = [(c.dve, d)]
                    ready[(id(dstT), tb)] = [(c.dve, d)]
                else:
                    s, w = st16.acquire()
                    a = p.op("scalar", lambda e, s=s, pb=pb: e.copy(out=stg16[:, s, :], in_=pst[:, pb, 0:512]),
                             waits=[(c.pe, pt)] + w, inc=c.act)
                    pst_cond[pb] = [(c.act, a)]
                    sv = p.dma("sync", io["ret_qg"].rearrange("(h d) t -> d h t", d=128)[:, :, r0:r0 + 128],
                               V4(stg16[:, s, :]), waits=[(c.act, a)], inc=st16.sem[s])
                    st16.cond[s] = [(st16.sem[s], sv)]
            for hh in range(4):
                sl = slice(hh * 128, (hh + 1) * 128)
                p.op("tensor", lambda e, sl=sl: e.matmul(c.ps[:, 4, sl], lhsT=krT[:, sl], rhs=qxT[:, sl], start=True,
                                                         stop=True),
                     waits=(ready[(id(krT), tb)] + ready[(id(qxT), tb)] + b4_cond) if hh == 0 else [],
                     inc=(c.pe if hh == 3 else None))
            pt = c.pe.v
            d = p.op("vector", lambda e: e.tensor_tensor(out=sm[:], in0=c.ps[:, 4, :], in1=io["maskr"][:], op=ALU.mult),
                     waits=[(c.pe, pt), (c.pe, c.pe.v)] + setupw, inc=c.dve)
            b4_cond = [(c.dve, d)]
            for hh in range(4):
                sl = slice(hh * 128, (hh + 1) * 128)
                p.op("tensor", lambda e, sl=sl, tb=tb: e.matmul(c.ps[:, 5, sl], lhsT=rv[:, tb, sl], rhs=sm[:, sl], start=True,
                                                               stop=False),
                     waits=([(c.dve, d)] + c.ps_free_waits + ready[("Sb", 2 * tb)] + ready[("Sb", 2 * tb + 1)])
                     if hh == 0 else [])
                for a_ in range(2):
                    cs = slice(hh * 128 + a_ * 64, hh * 128 + (a_ + 1) * 64)
                    p.op("tensor", lambda e, sl=sl, cs=cs, tb=tb, a_=a_: e.matmul(
                        c.ps[:, 5, cs], lhsT=Sb[:, 2 * tb + a_, sl], rhs=qxT[:, cs], start=False, stop=(a_ == 1)),
                         inc=(c.pe if (hh == 3 and a_ == 1) else None))
            pt = c.pe.v
            s, w = st32.acquire()
            a = p.op("scalar", lambda e, s=s: e.copy(out=stg32[:, s, :], in_=c.ps[:, 5, :]), waits=[(c.pe, pt)] + w,
                     inc=c.act)
            c.ps_free_waits = c.ps_free_waits + [(c.act, a)]
            sv = p.dma("sync", io["ret_o"].rearrange("(h e) t -> e h t", e=128)[:, :, r0:r0 + 128], V4(stg32[:, s, :]),
                       waits=[(c.act, a)], inc=st32.sem[s])
            st32.cond[s] = [(st32.sem[s], sv)]
        ret_free = [(c.pe, c.pe.v)]
    f1 = p.dma("sync", io["cneg_o"][:, :], cneg[:], waits=[(c.dve, c.dve.v)], inc=misc)
    f2 = p.dma("sync", io["ret_S"][:, :], S32[:], waits=s32_done, inc=misc)
    p.wait_only("sync", [(misc, f2)] + [(st16.sem[s], st16.sem[s].v) for s in range(4)] +
                [(st32.sem[s], st32.sem[s].v) for s in range(2)])


def mixa_setup(c, din, io):
    p = c.p
    sb = c.sb
    gcol = sb("gcol", [128, KC], F32)
    negb = sb("negb", [8, 1], F32)
    maskr = sb("maskr", [128, 512], F32)
    wst = sb("wst", [128, 512], F32)
    triu = sb("triu", [128, 128], F32)
    wsb = sb("wsb", [128, 4, 128], BF16)
    lnb3 = sb("lnb3", [128, 3, 512], F32)
    ong = sb("ong", [128, 4], F32)
    identb = sb("identb", [128, 128], BF16)
    io.update({"negb": negb, "maskr": maskr, "wsb": wsb, "lnb3": lnb3, "ong": ong, "identb": identb})
    for dst, src in ((gcol[:], din["g"]), (negb[:], din["bf"]), (maskr[:], din["maskr"]), (wst[:], din["wst"]),
                     (triu[:], din["triu"]), (lnb3[:].rearrange("p a b -> p (a b)"), din["lnb3"]),
                     (ong[:], din["ong"])):
        p.dma("sync", dst, src, inc=c.setup_d)
    p.dma("gpsimd", identb[:], din["ident"], inc=c.setup_d)
    dl = [(c.setup_d, c.setup_d.v)]
    p.op("vector", lambda e: e.tensor_scalar(out=negb[:], in0=negb[:], scalar1=-1.0, scalar2=None, op0=ALU.mult),
         waits=dl, inc=c.setup_v)
    p.op("vector", lambda e: e.tensor_tensor(out=wsb[:], in0=wst[:].rearrange("p (a b) -> p a b", a=4),
                                             in1=triu[:].unsqueeze(1).to_broadcast([128, 4, 128]), op=ALU.mult),
         inc=c.setup_v)
    return gcol


def build_mixa():
    nc = bass.Bass("TRN2", target_bir_lowering=False)
    di = lambda name, shape: nc.dram_tensor(name, shape, F32, kind="ExternalInput").ap()
    do = lambda name, shape, dt: nc.dram_tensor(name, shape, dt, kind="ExternalOutput").ap()
    xin = di("xin", [D, NTOK])
    w_in = di("w_in", [D, INCOLS])
    din = {"g": di("g", [128, KC])[:, :], "bf": di("bf", [8, 1])[:, :], "maskr": di("maskr", [128, 512])[:, :],
           "wst": di("wst", [128, 512])[:, :], "triu": di("triu", [128, 128])[:, :],
           "lnb3": di("lnb3", [128, 3 * 512])[:, :], "ong": di("ong", [128, 4])[:, :],
           "ident": di("ident", [128, 128])[:, :]}
    io = {
        "tab": di("tab", [NTOK, 268]),
        "qT": do("qT", [1024, NTOK], BF16), "kT": do("kT", [1024, NTOK], BF16), "v": do("v", [NTOK, 1024], BF16),
        "cneg_o": do("cneg", [8, NTOK], F32), "ret_o": do("ret_o", [512, NTOK], F32),
        "ret_qg": do("ret_qg", [512, NTOK], BF16), "ret_S": do("ret_S", [128, 512], F32),
        "rgs": do("rgs", [512, NTOK], F32), "ytg": do("ytg", [512, NTOK], BF16),
    }
    with contextlib.ExitStack() as stack:
        p = Prog(nc, stack)
        c = alloc_common(nc, stack, p, tt=TA, nps=6, stat_bank=5)
        c.stack = stack
        gcol = mixa_setup(c, din, io)
        mixa_body(c, xin, gcol, w_in, io)
        p.emit()
    return nc


def host_consts(half):
    t = np.arange(NTOK, dtype=np.float64)
    pos = (half * NTOK + np.arange(NTOK)).astype(np.float32)
    inv_freq = (np.float32(10000.0) ** (-np.arange(64, dtype=np.float32) / np.float32(64))).astype(np.float32)
    ang = (pos[:, None] * inv_freq[None, :]).astype(np.float32).astype(np.float64)
    cos, sin = np.cos(ang), np.sin(ang)
    gam = np.array(GAM, dtype=np.float64)
    cidx = (np.arange(NTOK) % 64).astype(np.float64)
    xi = gam[None, :] ** (cidx[:, None] + 1.0)
    gm = gam[None, :] ** (t[:, None] + 1.0)
    zeta = gam[None, :] ** (63.0 - cidx[:, None]) * (128.0 ** -0.5)
    tab = np.concatenate([cos, cos, -sin, sin, xi, gm, zeta], axis=1).astype(np.float32)
    s = np.arange(128)
    cc = np.arange(128)
    same = (s[:, None] // 64 == cc[None, :] // 64) & (cc[None, :] >= s[:, None])
    maskr = np.zeros((128, 4, 128), np.float64)
    for h in range(4):
        maskr[:, h, :] = np.where(same, gam[h] ** (-(s[:, None] % 64 + 1.0)), 0.0) * (128.0 ** -0.5)
    triu = (s[:, None] <= cc[None, :]).astype(np.float32)
    return {"tab": np.ascontiguousarray(tab), "maskr": maskr.reshape(128, 512).astype(np.float32), "triu": triu,
            "ident": np.eye(128, dtype=np.float32)}


def col16(vec):
    return np.ascontiguousarray(np.asarray(vec, np.float32).reshape(KC, 128).T)


def mixa_inputs(xT, half, l, P):
    hc = host_consts(half)
    wst = np.ascontiguousarray(np.transpose(P["gmlp_w_s"][l], (2, 0, 1)).reshape(128, 512))
    lnb3 = np.concatenate([np.broadcast_to(P["gmlp_ln_g"][l][None, :], (128, 512)),
                           np.broadcast_to(P["gmlp_ln_b"][l][None, :], (128, 512)),
                           np.broadcast_to(P["gmlp_b_s"][l].reshape(1, 512), (128, 512))], axis=1)
    ong = np.ascontiguousarray(P["out_norm"][l][1536:2048].reshape(4, 128).T)
    return {"xin": xT, "g": col16(P["mix_norm"][l]), "w_in": P["w_in"][l],
            "bf": np.ascontiguousarray(P["fox_b_f"][l].reshape(8, 1)), "tab": hc["tab"], "maskr": hc["maskr"],
            "wst": wst.astype(np.float32), "triu": hc["triu"], "lnb3": np.ascontiguousarray(lnb3, dtype=np.float32),
            "ong": ong.astype(np.float32), "ident": hc["ident"]}


QT = 512
NQT = NTOK // QT
MASKNEG = -30000.0


def mixb_body(nc, stack, p, io):
    uid = _uid()
    sb = lambda name, shape, dt: stack.enter_context(nc.sbuf_tensor("sb%d_%s" % (uid, name), shape, dt))
    ps = stack.enter_context(nc.psum_tensor("psb%d" % uid, [128, 7, 512], F32))
    onesf = sb("onesf", [128, 128], F32)
    onesb = sb("onesb", [128, 128], BF16)
    cmask = sb("cmask", [128, 896], F32)
    selh = sb("selh", [8, 8, 128], F32)
    ident8 = sb("ident8", [8, 8], F32)
    pmask = sb("pmask", [128, 1], F32)
    sflag = sb("sflag", [128, 1], F32)
    ona = sb("ona", [128, 12], F32)
    sinit = sb("sinit", [128, 512], F32)
    sinb = sb("sinb", [128, 512], BF16)
    cn = sb("cn", [8, 2 * NTOK], F32)
    ncl = sb("ncl", [8, NTOK], F32)
    ncp = sb("ncp", [8, NTOK], F32)
    biasT = sb("biasT", [128, 32, 8], F32)
    setup_d = p.sem("bsetupd")
    dve = p.sem("bdve")
    act = p.sem("bact")
    pe = p.sem("bpe")
    for dst, src in ((cmask[:], io["cmask"][:, :]), (selh[:].rearrange("k h m -> k (h m)"), io["selh"][:, :]),
                     (ident8[:], io["ident8"][:, :]), (pmask[:], io["pmask"][:, :]), (sflag[:], io["sflag"][:, :]),
                     (ona[:], io["ona"][:, :]), (sinit[:], io["s_init"][:, :]), (cn[:, 0:NTOK], io["cneg_prev"][:, :]),
                     (cn[:, NTOK:2 * NTOK], io["cneg_loc"][:, :])):
        p.dma("sync", dst, src, inc=setup_d)
    sd = [(setup_d, setup_d.v)]
    p.op("vector", lambda e: e.memset(onesf[:], 1.0), inc=dve)
    p.op("vector", lambda e: e.memset(onesb[:], 1.0), inc=dve)
    p.op("vector", lambda e: e.tensor_scalar(out=sinb[:], in0=sinit[:], scalar1=sflag[:, 0:1], scalar2=None, op0=ALU.mult),
         waits=sd, inc=dve)
    p.op("vector", lambda e: e.tensor_scalar(out=ncl[:], in0=cn[:, NTOK:2 * NTOK], scalar1=-1.0, scalar2=None, op0=ALU.mult),
         inc=dve)
    d = p.op("vector", lambda e: e.tensor_scalar(out=ncp[:], in0=ncl[:], scalar1=cn[:, NTOK - 1:NTOK], scalar2=None,
                                                 op0=ALU.subtract), waits=[(dve, dve.v)], inc=dve)
    for blk in range(32):
        p.op("tensor", lambda e, blk=blk: e.transpose(out=ps[:, 6, blk * 8:(blk + 1) * 8],
                                                      in_=cn[0:8, blk * 128:(blk + 1) * 128], identity=ident8[:]),
             waits=sd if blk == 0 else [], inc=(pe if blk == 31 else None))
    p.op("vector", lambda e: e.tensor_scalar(out=biasT[:, 0:16, :].rearrange("p a b -> p (a b)"), in0=ps[:, 6, 0:128],
                                             scalar1=pmask[:, 0:1], scalar2=None, op0=ALU.add),
         waits=[(pe, pe.v)] + sd, inc=dve)
    d = p.op("vector", lambda e: e.tensor_copy(out=biasT[:, 16:32, :].rearrange("p a b -> p (a b)"), in_=ps[:, 6, 128:256]),
             inc=dve)
    b6_cond = [(dve, d)]
    setup_done = [(dve, d)] + sd

    qh = sb("qh", [128, 2, NTOK], BF16)
    kh = sb("kh", [128, 2, 2 * NTOK], BF16)
    vh = sb("vh", [128, 2, 32, 128], BF16)
    hs = Slots(p, "hs", 2)
    cb = sb("cb", [128, 2, 2, QT], F32)
    cbs = Slots(p, "cbs", 2)
    tmp = sb("tmp", [128, 3, QT], F32)
    tmps = Slots(p, "tmps", 3)
    pt = sb("pt", [128, 3, QT], BF16)
    pts = Slots(p, "pts", 3)
    ya = sb("ya", [128, 2, QT], F32)
    yas = Slots(p, "yas", 2)
    yq = sb("yq", [128, QT], F32)
    stg = sb("stg", [128, 2, QT], BF16)
    stgs = Slots(p, "stgs", 2)
    ro = sb("ro", [128, 2, QT], F32)
    rg = sb("rg", [128, 2, QT], F32)
    qgt = sb("qgt", [128, 2, QT], BF16)
    rls = Slots(p, "rls", 2)
    s_cond = [[], []]
    acc_cond = [[], []]
    s_n = [0]
    acc_n = [0]
    y_stores = []

    def headnorm_store(yslot, yready, gain_col, dst_ap, mul_tile=None, mul_wait=()):
        nonlocal b6_cond
        a = p.op("scalar", lambda e: e.activation(out=yq[:], in_=ya[:, yslot, :], func=AF.Square),
                 waits=list(yready) + [(dve, dve.v)], inc=act)
        p.op("tensor", lambda e: e.matmul(ps[:, 6, :], lhsT=onesf[:], rhs=yq[:], start=True, stop=True),
             waits=[(act, a)] + b6_cond, inc=pe)
        d = p.op("vector", lambda e: e.tensor_scalar(out=yq[:], in0=ps[:, 6, :], scalar1=1.0 / 128, scalar2=EPS,
                                                     op0=ALU.mult, op1=ALU.add), waits=[(pe, pe.v)], inc=dve)
        b6_cond = [(dve, d)]
        a = p.op("scalar", lambda e: e.activation(out=yq[:], in_=yq[:], func=AF.Sqrt), waits=[(dve, d)], inc=act)
        d = p.op("vector", lambda e: e.reciprocal(out=yq[:], in_=yq[:]), waits=[(act, a)], inc=dve)
        s, w = stgs.acquire()
        if mul_tile is None:
            d = p.op("vector", lambda e: e.scalar_tensor_tensor(out=stg[:, s, :], in0=ya[:, yslot, :], scalar=gain_col,
                                                                in1=yq[:], op0=ALU.mult, op1=ALU.mult),
                     waits=[(dve, d)] + w, inc=dve)
        else:
            d = p.op("vector", lambda e: e.scalar_tensor_tensor(out=ya[:, yslot, :], in0=ya[:, yslot, :], scalar=gain_col,
                                                                in1=yq[:], op0=ALU.mult, op1=ALU.mult),
                     waits=[(dve, d)], inc=dve)
            d = p.op("vector", lambda e: e.tensor_tensor(out=stg[:, s, :], in0=ya[:, yslot, :], in1=mul_tile, op=ALU.mult),
                     waits=[(dve, d)] + w + list(mul_wait), inc=dve)
        sv = p.dma("sync", dst_ap, stg[:, s, :], waits=[(dve, d)], inc=stgs.sem[s])
        stgs.cond[s] = [(stgs.sem[s], sv)]
        y_stores.append((stgs.sem[s], sv))
        return [(dve, d)]

    vparts = []
    blk0 = 0
    for key in ("v_prev", "v_loc"):
        for part in _parts(io[key]):
            nb = part.shape[0] // 128
            vparts.append((blk0, nb, part.rearrange("(b p) c -> p b c", p=128)))
            blk0 += nb
    assert blk0 == 32
    for h in range(8):
        hsl, w = hs.acquire()
        rows = slice(h * 128, (h + 1) * 128)
        p.dma("sync", qh[:, hsl, :], io["qT"][rows, :], waits=w, inc=hs.sem[hsl])
        p.dma("sync", kh[:, hsl, 0:NTOK], io["kT_prev"][rows, :], inc=hs.sem[hsl])
        p.dma("sync", kh[:, hsl, NTOK:2 * NTOK], io["kT_loc"][rows, :], inc=hs.sem[hsl])
        for (b0_, nb_, vw_) in vparts:
            hl = p.dma("sync", vh[:, hsl, b0_:b0_ + nb_, :], vw_[:, :, rows], inc=hs.sem[hsl])
        hw = [(hs.sem[hsl], hl)]
        for qt in range(NQT):
            qs = slice(qt * QT, (qt + 1) * QT)
            cs_, w = cbs.acquire()
            for vi, src in ((0, ncp), (1, ncl)):
                p.op("tensor", lambda e, src=src, h=h, qs=qs: e.matmul(ps[:, 6, :], lhsT=selh[0:8, h, :], rhs=src[0:8, qs],
                                                                      start=True, stop=True),
                     waits=b6_cond + setup_done, inc=pe)
                a = p.op("scalar", lambda e, cs_=cs_, vi=vi: e.copy(out=cb[:, cs_, vi, :], in_=ps[:, 6, :]),
                         waits=[(pe, pe.v)] + w, inc=act)
                b6_cond = [(act, a)]
            cbw = [(act, a)]
            blocks = [(j, 0, None) for j in range(16)] + [(16 + j, 1, (j - 4 * qt) if j >= 4 * qt else None)
                                                          for j in range(4 * qt + 4)]
            an = acc_n[0]
            acc_n[0] += 1
            ob, db = 2 + an % 2, 4 + an % 2
            pend = None
            nblk = len(blocks)

            def emit_pv(pend, first, last):
                j, slot, pw_ = pend
                p.op("tensor", lambda e, j=j, slot=slot, ob=ob, hsl=hsl: e.matmul(ps[:, ob, :], lhsT=vh[:, hsl, j, :],
                                                                                 rhs=pt[:, slot, :], start=first, stop=last),
                     waits=pw_ + (acc_cond[an % 2] if first else []))
                p.op("tensor", lambda e, slot=slot, db=db: e.matmul(ps[:, db, :], lhsT=onesb[:], rhs=pt[:, slot, :],
                                                                   start=first, stop=last), inc=pe)
                pts.cond[slot] = [(pe, pe.v)]

            npv = 0
            for (j, vi, dg) in blocks:
                sn = s_n[0]
                s_n[0] += 1
                sbk = sn % 2
                p.op("tensor", lambda e, j=j, sbk=sbk, qs=qs, hsl=hsl: e.matmul(ps[:, sbk, :], lhsT=kh[:, hsl, j * 128:(j + 1) * 128],
                                                                      rhs=qh[:, hsl, qs], start=True, stop=True),
                     waits=hw + s_cond[sbk], inc=pe)
                st_ = pe.v
                if pend is not None:
                    emit_pv(pend, npv == 0, False)
                    npv += 1
                ts, w = tmps.acquire()
                d = p.op("vector", lambda e, ts=ts, sbk=sbk, vi=vi, cs_=cs_: e.scalar_tensor_tensor(
                    out=tmp[:, ts, :], in0=ps[:, sbk, :], scalar=1.0, in1=cb[:, cs_, vi, :], op0=ALU.mult, op1=ALU.add),
                         waits=[(pe, st_)] + w + cbw, inc=dve)
                s_cond[sbk] = [(dve, d)]
                if dg is not None:
                    off = 384 - dg * 128
                    d = p.op("vector", lambda e, ts=ts, off=off: e.tensor_tensor(out=tmp[:, ts, :], in0=tmp[:, ts, :],
                                                                                 in1=cmask[:, off:off + QT], op=ALU.add),
                             waits=[(dve, d)] + setup_done, inc=dve)
                slot, w = pts.acquire()
                a = p.op("scalar", lambda e, ts=ts, slot=slot, j=j, h=h: e.activation(
                    out=pt[:, slot, :], in_=tmp[:, ts, :], func=AF.Exp, bias=biasT[:, j, h:h + 1], scale=1.0),
                         waits=[(dve, d)] + w + setup_done, inc=act)
                tmps.cond[ts] = [(act, a)]
                pend = (j, slot, [(act, a)])
            emit_pv(pend, npv == 0, True)
            acc_done = pe.v
            cbs.cond[cs_] = [(dve, dve.v)]
            ys, w = yas.acquire()
            d = p.op("vector", lambda e, db=db: e.reciprocal(out=yq[:], in_=ps[:, db, :]), waits=[(pe, acc_done), (dve, dve.v),
                                                                                          (act, act.v)], inc=dve)
            d = p.op("vector", lambda e, ys=ys, ob=ob: e.tensor_tensor(out=ya[:, ys, :], in0=ps[:, ob, :], in1=yq[:], op=ALU.mult),
                     waits=[(dve, d)] + w, inc=dve)
            acc_cond[an % 2] = [(dve, d)]
            yw = headnorm_store(ys, [(dve, d)], ona[:, h:h + 1], io["yT"][rows, qs])
            yas.cond[ys] = yw
        hs.cond[hsl] = [(pe, pe.v)]
    for hh in range(4):
        rows = slice(hh * 128, (hh + 1) * 128)
        for qt in range(NQT):
            qs = slice(qt * QT, (qt + 1) * QT)
            rs, w = rls.acquire()
            p.dma("sync", ro[:, rs, :], io["ret_o"][rows, qs], waits=w, inc=rls.sem[rs])
            p.dma("sync", rg[:, rs, :], io["rgs"][rows, qs], inc=rls.sem[rs])
            rl = p.dma("sync", qgt[:, rs, :], io["ret_qg"][rows, qs], inc=rls.sem[rs])
            p.op("tensor", lambda e, rs=rs, rows=rows: e.matmul(ps[:, 6, :], lhsT=sinb[:, rows], rhs=qgt[:, rs, :], start=True,
                                                               stop=True),
                 waits=[(rls.sem[rs], rl)] + b6_cond + setup_done, inc=pe)
            ys, w = yas.acquire()
            d = p.op("vector", lambda e, ys=ys, rs=rs: e.tensor_tensor(out=ya[:, ys, :], in0=ps[:, 6, :], in1=ro[:, rs, :],
                                                                      op=ALU.add), waits=[(pe, pe.v)] + w, inc=dve)
            b6_cond = [(dve, d)]
            yw = headnorm_store(ys, [(dve, d)], ona[:, 8 + hh:9 + hh],
                                io["yT"][1024 + hh * 128:1024 + (hh + 1) * 128, qs], mul_tile=rg[:, rs, :])
            yas.cond[ys] = yw
            rls.cond[rs] = yw
    hb = sb("hb", [128, KC, TT], BF16)
    wo = sb("wo", [128, 2, KC, 256], BF16)
    wos = Slots(p, "wos", 2)
    xs = sb("xsb", [128, 3, TT], F32)
    xss = Slots(p, "xss", 3)
    hbl = p.sem("hbl")
    wov = io["w_out"].rearrange("(kc p) f -> p kc f", p=128)
    ytv = io["yT"].rearrange("(kc p) t -> p kc t", p=128)
    ygv = io["ytg"].rearrange("(kc p) t -> p kc t", p=128)
    hb_free = []
    on = 0
    ob_cond = [[], []]
    for tt in range(NTT):
        t0 = tt * TT
        p.dma("sync", hb[:, 0:12, :], ytv[:, :, t0:t0 + TT], waits=list(y_stores) + hb_free, inc=hbl)
        hl = p.dma("sync", hb[:, 12:16, :], ygv[:, :, t0:t0 + TT], inc=hbl)
        for pd in range(8):
            col0 = pd * 256
            b, w = wos.acquire()
            wl = p.dma("gpsimd", wo[:, b, :, :], wov[:, :, col0:col0 + 256], waits=w, inc=wos.sem[b])
            for ii in range(2):
                i = pd * 2 + ii
                s, w = xss.acquire()
                full = p.dma("sync", xs[:, s, :], io["xin"][i * 128:(i + 1) * 128, t0:t0 + TT], waits=w, inc=xss.sem[s])
                for th in range(2):
                    obk = on % 2
                    on += 1
                    for kc in range(KC):
                        p.op("tensor", lambda e, b=b, kc=kc, ii=ii, th=th, obk=obk: e.matmul(
                            ps[:, obk, :], lhsT=wo[:, b, kc, ii * 128:(ii + 1) * 128], rhs=hb[:, kc, th * 512:(th + 1) * 512],
                            start=(kc == 0), stop=(kc == KC - 1)),
                             waits=([(wos.sem[b], wl), (hbl, hl)] + ob_cond[obk]) if kc == 0 else [],
                             inc=(pe if kc == KC - 1 else None))
                    r = p.op("vector", lambda e, s=s, th=th, obk=obk: e.tensor_tensor(
                        out=xs[:, s, th * 512:(th + 1) * 512], in0=ps[:, obk, :], in1=xs[:, s, th * 512:(th + 1) * 512],
                        op=ALU.add), waits=[(pe, pe.v), (xss.sem[s], full)], inc=dve)
                    ob_cond[obk] = [(dve, r)]
                sv = p.dma("sync", io["xout"][i * 128:(i + 1) * 128, t0:t0 + TT], xs[:, s, :], waits=[(dve, r)],
                           inc=xss.sem[s])
                xss.cond[s] = [(xss.sem[s], sv)]
            wos.cond[b] = [(pe, pe.v)]
        hb_free = [(pe, pe.v)]
    p.wait_only("sync", [(xss.sem[s], xss.sem[s].v) for s in range(3)])


def build_mixb():
    nc = bass.Bass("TRN2", target_bir_lowering=False)
    di = lambda name, shape, dt=F32: nc.dram_tensor(name, shape, dt, kind="ExternalInput").ap()
    io = {
        "qT": di("qT", [1024, NTOK], BF16), "kT_loc": di("kT_loc", [1024, NTOK], BF16),
        "kT_prev": di("kT_prev", [1024, NTOK], BF16), "v_loc": di("v_loc", [NTOK, 1024], BF16),
        "v_prev": di("v_prev", [NTOK, 1024], BF16), "cneg_loc": di("cneg_loc", [8, NTOK]),
        "cneg_prev": di("cneg_prev", [8, NTOK]), "s_init": di("s_init", [128, 512]),
        "ret_o": di("ret_o", [512, NTOK]), "ret_qg": di("ret_qg", [512, NTOK], BF16), "rgs": di("rgs", [512, NTOK]),
        "ytg": di("ytg", [512, NTOK], BF16), "cmask": di("cmask", [128, 896]), "selh": di("selh", [8, 1024]),
        "ident8": di("ident8", [8, 8]), "pmask": di("pmask", [128, 1]), "sflag": di("sflag", [128, 1]),
        "ona": di("ona", [128, 12]), "w_out": di("w_out", [D, D]), "xin": di("xin", [D, NTOK]),
        "yT": nc.dram_tensor("yT", [1536, NTOK], BF16, kind="Internal").ap(),
        "xout": nc.dram_tensor("xout", [D, NTOK], F32, kind="ExternalOutput").ap(),
    }
    with contextlib.ExitStack() as stack:
        p = Prog(nc, stack)
        mixb_body(nc, stack, p, io)
        p.emit()
    return nc


def mixb_consts(half):
    s = np.arange(128)[:, None]
    u = np.arange(896)[None, :]
    cmask = np.where((u - 384) >= s, 0.0, MASKNEG).astype(np.float32)
    selh = np.zeros((8, 8, 128), np.float32)
    for h in range(8):
        selh[h, h, :] = 1.0
    return {"cmask": cmask, "selh": selh.reshape(8, 1024), "ident8": np.eye(8, dtype=np.float32),
            "pmask": np.full((128, 1), 0.0 if half == 1 else MASKNEG, np.float32),
            "sflag": np.full((128, 1), 1.0 if half == 1 else 0.0, np.float32)}


def build_norm():
    nc = bass.Bass("TRN2", target_bir_lowering=False)
    xin = nc.dram_tensor("xin", [D, NTOK], F32, kind="ExternalInput").ap()
    g = nc.dram_tensor("g", [128, KC], F32, kind="ExternalInput").ap()
    xout = nc.dram_tensor("xout", [D, NTOK], F32, kind="ExternalOutput").ap()
    with contextlib.ExitStack() as stack:
        p = Prog(nc, stack)
        c = alloc_common(nc, stack, p)
        gcol = c.sb("gcol", [128, KC], F32)
        p.dma("sync", gcol[:], g[:, :], inc=c.setup_d)
        for tt in range(NTT):
            t0 = tt * TT

            def out_fn(kc, s, waits, t0=t0):
                d = p.op("vector", lambda e: e.scalar_tensor_tensor(out=c.xs[:, s, :], in0=c.xs[:, s, :],
                                                                    scalar=gcol[:, kc:kc + 1], in1=c.rstd[:],
                                                                    op0=ALU.mult, op1=ALU.mult), waits=waits, inc=c.dve_h)
                sv = p.dma("sync", xout[kc * 128:(kc + 1) * 128, t0:t0 + TT], c.xs[:, s, :], waits=[(c.dve_h, d)],
                           inc=c.xs_st[s])
                c.xs_cond[s] = [(c.xs_st[s], sv)]

            norm_stats_and_h(c, xin, gcol, tt, out_fn=out_fn)
        finish(c)
        p.emit()
    return nc


_PROGS = {}


def _prog(name):
    if name not in _PROGS:
        _PROGS[name] = {"ffn": build_ffn, "mixa": build_mixa, "mixb": build_mixb, "norm": build_norm}[name]()
    return _PROGS[name]


def _run(name, in_maps):
    res = run_bass_kernel_spmd(_prog(name), in_maps, core_ids=list(range(NCORES)))
    return res.results


def run_ffn(xTs, l, P, pre):
    g = col16(P[pre + "_norm"][l])
    maps = [{"xin": xTs[c], "g": g, "wg": P[pre + "_w_gate"][l], "wu": P[pre + "_w_up"][l], "wd": P[pre + "_w_down"][l]}
            for c in range(NCORES)]
    return [r["xout"] for r in _run("ffn", maps)]


def run_mixer(xTs, l, P):
    ra = _run("mixa", [mixa_inputs(xTs[c], c % 2, l, P) for c in range(NCORES)])
    ona = np.ascontiguousarray(P["out_norm"][l][0:1536].reshape(12, 128).T).astype(np.float32)
    maps = []
    for c in range(NCORES):
        half = c % 2
        pc = c - 1 if half == 1 else c
        m = {"qT": ra[c]["qT"], "kT_loc": ra[c]["kT"], "kT_prev": ra[pc]["kT"], "v_loc": ra[c]["v"], "v_prev": ra[pc]["v"],
             "cneg_loc": ra[c]["cneg"], "cneg_prev": ra[pc]["cneg"], "s_init": ra[pc]["ret_S"], "ret_o": ra[c]["ret_o"],
             "ret_qg": ra[c]["ret_qg"], "rgs": ra[c]["rgs"], "ytg": ra[c]["ytg"], "ona": ona, "w_out": P["w_out"][l],
             "xin": xTs[c]}
        m.update(mixb_consts(half))
        maps.append(m)
    return [r["xout"] for r in _run("mixb", maps)]


def kernel_unfused(**inputs):
    P = {k: np.asarray(v) for k, v in inputs.items()}
    x = P["x"]
    xTs = [np.ascontiguousarray(x[c // 2, (c % 2) * NTOK:(c % 2 + 1) * NTOK, :].T) for c in range(NCORES)]
    for l in range(DEPTH):
        xTs = run_ffn(xTs, l, P, "ffn1")
        xTs = run_mixer(xTs, l, P)
        xTs = run_ffn(xTs, l, P, "ffn2")
    g = col16(P["final_norm"])
    outs = [r["xout"] for r in _run("norm", [{"xin": xTs[c], "g": g} for c in range(NCORES)])]
    out = np.empty_like(x)
    for c in range(NCORES):
        out[c // 2, (c % 2) * NTOK:(c % 2 + 1) * NTOK, :] = outs[c].T
    return out


PAIRS = [[0, 1], [2, 3], [4, 5], [6, 7]]
WSHAPES = {"ffn1_w_gate": [DEPTH, D, DFF], "ffn1_w_up": [DEPTH, D, DFF], "ffn1_w_down": [DEPTH, DFF, D],
           "w_in": [DEPTH, D, INCOLS], "w_out": [DEPTH, D, D],
           "ffn2_w_gate": [DEPTH, D, DFF], "ffn2_w_up": [DEPTH, D, DFF], "ffn2_w_down": [DEPTH, DFF, D]}
SMALL = {"g_ffn1": [DEPTH, 128, KC], "g_mix": [DEPTH, 128, KC], "g_ffn2": [DEPTH, 128, KC], "g_fin": [128, KC],
         "bf": [DEPTH, 8, 1], "wst": [DEPTH, 128, 512], "lnb3": [DEPTH, 128, 1536], "ong": [DEPTH, 128, 4],
         "ona": [DEPTH, 128, 12], "tab": [NTOK, 268], "maskr": [128, 512], "triu": [128, 128], "ident": [128, 128],
         "cmask": [128, 896], "selh": [8, 1024], "ident8": [8, 8], "pmask": [128, 1], "sflag": [128, 1]}


def build_fused(depth=DEPTH, phases="fmxbF"):
    nc = bass.Bass("TRN2", target_bir_lowering=False)
    di = lambda name, shape: nc.dram_tensor(name, shape, F32, kind="ExternalInput").ap()
    it = lambda name, shape, dt: nc.dram_tensor(name, shape, dt, kind="Internal").ap()
    x_in = di("x", [D, NTOK])
    W = {k: di(k, [depth] + s[1:]) for k, s in WSHAPES.items()}
    S = {k: di(k, ([depth] + s[1:]) if len(s) == 3 else s) for k, s in SMALL.items()}
    out = nc.dram_tensor("out", [D, NTOK], F32, kind="ExternalOutput").ap()
    xres = it("xres", [D, NTOK], F32)
    qT = it("qT", [1024, NTOK], BF16)
    xk = [it("xk%d" % i, [512, NTOK], BF16) for i in range(2)]
    xv = [it("xv%d" % i, [1024, 1024], BF16) for i in range(2)]
    xc = it("xc", [8, NTOK], F32)
    xs_ = it("xs_", [128, 512], F32)
    gk = [it("gk%d" % i, [1024, NTOK], BF16) for i in range(2)]
    gv_ = [it("gv%d" % i, [2048, 1024], BF16) for i in range(2)]
    gc = it("gc", [16, NTOK], F32)
    gs = it("gs", [256, 512], F32)
    ret_o = it("ret_o", [512, NTOK], F32)
    ret_qg = it("ret_qg", [512, NTOK], BF16)
    rgs = it("rgs", [512, NTOK], F32)
    ytg = it("ytg", [512, NTOK], BF16)
    yT = it("yT", [1536, NTOK], BF16)
    with contextlib.ExitStack() as gstack:
        p = Prog(nc, gstack)

        def ffn_phase(xin, xout, g_ap, wg, wu, wd):
            with contextlib.ExitStack() as st:
                c = alloc_common(nc, st, p)
                alloc_ffn(c)
                gcol = c.sb("gcol", [128, KC], F32)
                p.dma("sync", gcol[:], g_ap, inc=c.setup_d)
                ffn_body(c, xin, xout, gcol, wg, wu, wd)
                finish(c)
                p.barrier()
                p.emit()

        def mixa_phase(l):
            with contextlib.ExitStack() as st:
                c = alloc_common(nc, st, p, tt=TA, nps=6, stat_bank=5)
                din = {"g": S["g_mix"][l], "bf": S["bf"][l], "maskr": S["maskr"][:, :], "wst": S["wst"][l],
                       "triu": S["triu"][:, :], "lnb3": S["lnb3"][l], "ong": S["ong"][l], "ident": S["ident"][:, :]}
                io = {"tab": S["tab"], "qT": qT, "kT": RowSplit(xk), "v": RowSplit(xv), "cneg_o": xc,
                      "ret_o": ret_o, "ret_qg": ret_qg, "ret_S": xs_, "rgs": rgs, "ytg": ytg}
                gcol = mixa_setup(c, din, io)
                mixa_body(c, xres, gcol, W["w_in"][l], io)
                p.barrier()
                p.emit()

        def exchange_phase():
            cc = p.sem("ccsem")
            for a_, b_ in ((xk[0], gk[0]), (xk[1], gk[1]), (xv[0], gv_[0]), (xv[1], gv_[1]), (xc, gc), (xs_, gs)):
                p.op("gpsimd", lambda e, a_=a_, b_=b_: e.collective_compute("AllGather", ALU.bypass, replica_groups=PAIRS,
                                                                            ins=[a_], outs=[b_]), inc=cc, k=1)
            p.barrier()
            p.emit()

        def mixb_phase(l):
            with contextlib.ExitStack() as st:
                io = {"qT": qT, "kT_loc": RowSplit(xk), "kT_prev": RowSplit([gk[0][0:512, :], gk[1][0:512, :]]),
                      "v_loc": RowSplit(xv), "v_prev": RowSplit([gv_[0][0:1024, :], gv_[1][0:1024, :]]),
                      "cneg_loc": xc, "cneg_prev": gc[0:8, :], "s_init": gs[0:128, :], "ret_o": ret_o,
                      "ret_qg": ret_qg, "rgs": rgs, "ytg": ytg, "cmask": S["cmask"], "selh": S["selh"],
                      "ident8": S["ident8"], "pmask": S["pmask"], "sflag": S["sflag"], "ona": S["ona"][l],
                      "w_out": W["w_out"][l], "xin": xres, "yT": yT, "xout": xres}
                mixb_body(nc, st, p, io)
                p.barrier()
                p.emit()

        def norm_phase():
            with contextlib.ExitStack() as st:
                c = alloc_common(nc, st, p)
                gcol = c.sb("gcol", [128, KC], F32)
                p.dma("sync", gcol[:], S["g_fin"][:, :], inc=c.setup_d)
                for tt in range(NTT):
                    t0 = tt * TT

                    def out_fn(kc, s, waits, t0=t0):
                        d = p.op("vector", lambda e: e.scalar_tensor_tensor(out=c.xs[:, s, :], in0=c.xs[:, s, :],
                                                                            scalar=gcol[:, kc:kc + 1], in1=c.rstd[:],
                                                                            op0=ALU.mult, op1=ALU.mult), waits=waits,
                                 inc=c.dve_h)
                        sv = p.dma("sync", out[kc * 128:(kc + 1) * 128, t0:t0 + TT], c.xs[:, s, :], waits=[(c.dve_h, d)],
                                   inc=c.xs_st[s])
                        c.xs_cond[s] = [(c.xs_st[s], sv)]

                    norm_stats_and_h(c, xres, gcol, tt, out_fn=out_fn)
                finish(c)
                p.barrier()
                p.emit()

        for l in range(depth):
            if "f" in phases:
                ffn_phase(x_in if l == 0 else xres, xres, S["g_ffn1"][l], W["ffn1_w_gate"][l], W["ffn1_w_up"][l],
                          W["ffn1_w_down"][l])
            if "m" in phases:
                mixa_phase(l)
            if "x" in phases:
                exchange_phase()
            if "b" in phases:
                mixb_phase(l)
            if "F" in phases:
                ffn_phase(xres, xres, S["g_ffn2"][l], W["ffn2_w_gate"][l], W["ffn2_w_up"][l], W["ffn2_w_down"][l])
        norm_phase()
    return nc


def fused_inputs(P, core):
    half = core % 2
    x = P["x"]
    m = {"x": np.ascontiguousarray(x[core // 2, half * NTOK:(half + 1) * NTOK, :].T)}
    for k in WSHAPES:
        m[k] = P[k]
    return m


def fused_shared(P):
    sh = {}
    sh["g_ffn1"] = np.stack([col16(P["ffn1_norm"][l]) for l in range(DEPTH)])
    sh["g_mix"] = np.stack([col16(P["mix_norm"][l]) for l in range(DEPTH)])
    sh["g_ffn2"] = np.stack([col16(P["ffn2_norm"][l]) for l in range(DEPTH)])
    sh["g_fin"] = col16(P["final_norm"])
    sh["bf"] = np.ascontiguousarray(P["fox_b_f"].reshape(DEPTH, 8, 1)).astype(np.float32)
    sh["wst"] = np.ascontiguousarray(np.transpose(P["gmlp_w_s"], (0, 3, 1, 2)).reshape(DEPTH, 128, 512)).astype(np.float32)
    sh["lnb3"] = np.ascontiguousarray(np.stack([np.concatenate(
        [np.broadcast_to(P["gmlp_ln_g"][l][None, :], (128, 512)), np.broadcast_to(P["gmlp_ln_b"][l][None, :], (128, 512)),
         np.broadcast_to(P["gmlp_b_s"][l].reshape(1, 512), (128, 512))], axis=1) for l in range(DEPTH)])).astype(np.float32)
    sh["ong"] = np.ascontiguousarray(np.stack([P["out_norm"][l][1536:2048].reshape(4, 128).T for l in range(DEPTH)])).astype(np.float32)
    sh["ona"] = np.ascontiguousarray(np.stack([P["out_norm"][l][0:1536].reshape(12, 128).T for l in range(DEPTH)])).astype(np.float32)
    return sh


_FUSED = {}


def kernel(**inputs):
    P = {k: np.asarray(v) for k, v in inputs.items()}
    if "nc" not in _FUSED:
        _FUSED["nc"] = build_fused()
    sh = fused_shared(P)
    maps = []
    for c in range(NCORES):
        half = c % 2
        m = fused_inputs(P, c)
        m.update(sh)
        hc = host_consts(half)
        m.update({"tab": hc["tab"], "maskr": hc["maskr"], "triu": hc["triu"], "ident": hc["ident"]})
        m.update(mixb_consts(half))
        maps.append(m)
    res = run_bass_kernel_spmd(_FUSED["nc"], maps, core_ids=list(range(NCORES)))
    x = P["x"]
    outp = np.empty_like(x)
    for c in range(NCORES):
        outp[c // 2, (c % 2) * NTOK:(c % 2 + 1) * NTOK, :] = res.results[c]["out"].T
    return outp
```

```python
import contextlib
import numpy as np
import concourse.bass as bass
import concourse.mybir as mybir
from concourse.bass_utils import run_bass_kernel_spmd

F32 = mybir.dt.float32
BF16 = mybir.dt.bfloat16
AF = mybir.ActivationFunctionType
ALU = mybir.AluOpType

D = 2048
NTOK = 2048
DFF = 5632
NCORES = 8
DEPTH = 4
EPS = 1e-6
KC = D // 128
TT = 1024
NTT = NTOK // TT
FH = 22
INCOLS = 6152


class Cnt:
    def __init__(self, h):
        self.h = h
        self.v = 0


class Prog:
    ENGS = ("sync", "scalar", "vector", "gpsimd", "tensor")

    def __init__(self, nc, stack):
        self.nc = nc
        self.stack = stack
        self.q = {e: [] for e in self.ENGS}
        self.waited = {e: {} for e in self.ENGS}
        self.cache = {}

    def sem(self, name):
        if name not in self.cache:
            self.cache[name] = Cnt(self.stack.enter_context(self.nc.semaphore(name)))
        return self.cache[name]

    def barrier(self):
        for eng in self.ENGS:
            self.op(eng, None, waits=[(c, c.v) for c in self.cache.values()])

    def sems(self, name, n):
        return [self.sem("%s%d" % (name, i)) for i in range(n)]

    def op(self, eng, fn, waits=(), inc=None, k=1):
        ws = []
        for (c, v) in waits:
            if v <= 0:
                continue
            key = id(c)
            if self.waited[eng].get(key, 0) >= v:
                continue
            self.waited[eng][key] = v
            ws.append((c.h, v))
        tgt = None
        if inc is not None:
            inc.v += k
            tgt = inc.v
        self.q[eng].append((ws, fn, inc.h if inc is not None else None, k))
        return tgt

    def dma(self, eng, out, in_, waits=(), inc=None):
        return self.op(eng, lambda e: e.dma_start(out=out, in_=in_), waits, inc, 16)

    def wait_only(self, eng, waits):
        self.q[eng].append(([(c.h, v) for (c, v) in waits if v > 0], None, None, 0))

    def emit(self):
        with self.nc.Block() as block:
            for name in self.ENGS:
                q = self.q[name]

                def body(e, q=q):
                    for ws, fn, inc, k in q:
                        for (h, v) in ws:
                            e.wait_ge(h, v)
                        if fn is None:
                            continue
                        ins = fn(e)
                        if inc is not None:
                            ins.then_inc(inc, k)

                getattr(block, name)(body)
        self.q = {e: [] for e in self.ENGS}


class Ctx:
    pass


_UID = [0]


def _uid():
    _UID[0] += 1
    return _UID[0]


def alloc_common(nc, stack, p, tt=TT, nps=8, stat_bank=6):
    c = Ctx()
    uid = _uid()
    c.nc = nc
    c.p = p
    c.TT = tt
    c.NSEG = tt // 512
    c.stat_bank = stat_bank
    sb = lambda name, shape, dt: stack.enter_context(nc.sbuf_tensor("sb%d_%s" % (uid, name), shape, dt))
    c.sb = sb
    c.uid = uid
    c.stack = stack
    c.ones = sb("ones", [128, 128], F32)
    c.xs = sb("xs", [128, 3, tt], F32)
    c.sq = sb("sq", [128, 2, tt], F32)
    c.rstd = sb("rstd", [128, tt], F32)
    c.h = sb("h", [128, KC, tt], BF16)
    c.ps = stack.enter_context(nc.psum_tensor("ps%d" % uid, [128, nps, 512], F32))
    c.xs_full = p.sems("xsfull", 3)
    c.xs_st = p.sems("xsst", 3)
    c.xs_cond = [[], [], []]
    c.xs_n = 0
    c.act_sq = p.sem("actsq")
    c.pe_st = p.sem("pest")
    c.dve_m = p.sem("dvem")
    c.act_m = p.sem("actm")
    c.dve_h = p.sem("dveh")
    c.setup_v = p.sem("setupv")
    c.setup_d = p.sem("setupd")
    p.op("vector", lambda e: e.memset(c.ones[:], 1.0), inc=c.setup_v)
    c.sq_n = 0
    c.h_free = []
    c.ps_free_waits = []
    return c


def xs_acquire(c):
    s = c.xs_n % 3
    c.xs_n += 1
    return s, list(c.xs_cond[s])


def norm_stats_and_h(c, xsrc, gcol, tt, out_fn=None):
    p = c.p
    TT = c.TT
    t0 = tt * TT
    SB = c.stat_bank
    for kc in range(KC):
        s, w = xs_acquire(c)
        full = p.dma("sync", c.xs[:, s, :], xsrc[kc * 128:(kc + 1) * 128, t0:t0 + TT], waits=w, inc=c.xs_full[s])
        q = c.sq_n % 2
        c.sq_n += 1
        a = p.op("scalar",
                 lambda e, s=s, q=q: e.activation(out=c.sq[:, q, :], in_=c.xs[:, s, :], func=AF.Square),
                 waits=[(c.xs_full[s], full), (c.pe_st, c.pe_st.v - 1)], inc=c.act_sq)
        c.xs_cond[s] = [(c.act_sq, a)]
        extra = list(c.ps_free_waits) if kc == 0 else []
        for sg_ in range(c.NSEG):
            p.op("tensor",
                 lambda e, q=q, kc=kc, sg_=sg_: e.matmul(c.ps[:, SB + sg_, :], lhsT=c.ones[:],
                                                        rhs=c.sq[:, q, sg_ * 512:(sg_ + 1) * 512],
                                                        start=(kc == 0), stop=(kc == KC - 1)),
                 waits=([(c.act_sq, a), (c.setup_v, c.setup_v.v)] + extra) if sg_ == 0 else [],
                 inc=(c.pe_st if sg_ == c.NSEG - 1 else None))
    st_done = c.pe_st.v
    psv = c.ps[:, SB:SB + c.NSEG, :]
    rv = c.rstd[:].rearrange("p (a b) -> p a b", a=c.NSEG)
    d1 = p.op("vector",
              lambda e: e.tensor_scalar(out=rv, in0=psv, scalar1=1.0 / D, scalar2=EPS, op0=ALU.mult, op1=ALU.add),
              waits=[(c.pe_st, st_done), (c.dve_h, c.dve_h.v)], inc=c.dve_m)
    c.ps_free_waits = [(c.dve_m, d1)]
    a1 = p.op("scalar", lambda e: e.activation(out=c.rstd[:], in_=c.rstd[:], func=AF.Sqrt),
              waits=[(c.dve_m, d1)], inc=c.act_m)
    d2 = p.op("vector", lambda e: e.reciprocal(out=c.rstd[:], in_=c.rstd[:]),
              waits=[(c.act_m, a1)], inc=c.dve_m)
    for kc in range(KC):
        s, w = xs_acquire(c)
        full = p.dma("sync", c.xs[:, s, :], xsrc[kc * 128:(kc + 1) * 128, t0:t0 + TT], waits=w, inc=c.xs_full[s])
        if out_fn is None:
            waits = [(c.xs_full[s], full), (c.dve_m, d2), (c.setup_d, c.setup_d.v)]
            if kc == 0:
                waits += c.h_free
            hv = p.op("vector",
                 lambda e, s=s, kc=kc: e.scalar_tensor_tensor(out=c.h[:, kc, :], in0=c.xs[:, s, :],
                                                              scalar=gcol[:, kc:kc + 1], in1=c.rstd[:],
                                                              op0=ALU.mult, op1=ALU.mult),
                 waits=waits, inc=c.dve_h)
            c.xs_cond[s] = [(c.dve_h, hv)]
        else:
            out_fn(kc, s, [(c.xs_full[s], full), (c.dve_m, d2), (c.setup_d, c.setup_d.v)])
    return c.dve_h.v


def alloc_ffn(c):
    p = c.p
    sb = c.sb
    c.hid = sb("hid", [128, FH, TT], BF16)
    c.sg = sb("sg", [128, 2, 512], F32)
    c.wgu = sb("wgu", [128, 2, 2, KC, 256], BF16)
    c.wd = sb("wd", [128, 2, FH, 256], BF16)
    c.wgu_full = p.sems("wgufull", 2)
    c.wd_full = p.sems("wdfull", 2)
    c.pe_gu = p.sem("pegu")
    c.act_sg = p.sem("actsg")
    c.dve_hid = p.sem("dvehid")
    c.pe_dn = p.sem("pedn")
    c.dve_res = p.sem("dveres")
    c.n_panel = 0
    c.n_gu = c.pe_gu.v
    c.n_dpanel = 0
    c.n_dn = c.pe_dn.v
    c.panel_done = {}
    c.dpanel_done = {}
    c.hid_free = []


def ffn_body(c, xin, xout, gcol, wg, wu, wd):
    p = c.p
    wgv = wg.rearrange("(kc p) f -> p kc f", p=128)
    wuv = wu.rearrange("(kc p) f -> p kc f", p=128)
    wdv = wd.rearrange("(fc p) d -> p fc d", p=128)
    for tt in range(NTT):
        t0 = tt * TT
        h_ready = norm_stats_and_h(c, xin, gcol, tt)
        for hf in range(2):
            for pn in range(FH // 2):
                col0 = (hf * FH + pn * 2) * 128
                npn = c.n_panel
                b = npn % 2
                c.n_panel += 1
                wfree = [(c.pe_gu, c.panel_done[npn - 2])] if npn >= 2 else []
                p.dma("gpsimd", c.wgu[:, b, 0, :, :], wgv[:, :, col0:col0 + 256], waits=wfree, inc=c.wgu_full[b])
                wl = p.dma("gpsimd", c.wgu[:, b, 1, :, :], wuv[:, :, col0:col0 + 256], waits=wfree, inc=c.wgu_full[b])
                for jj in range(2):
                    j = pn * 2 + jj
                    for th in range(2):
                        n = c.n_gu
                        c.n_gu += 1
                        gb = n % 2
                        ub = 2 + n % 2
                        for kc in range(KC):
                            waits = []
                            if kc == 0:
                                waits = [(c.wgu_full[b], wl), (c.dve_h, h_ready), (c.act_sg, n - 1)]
                            p.op("tensor",
                                 lambda e, b=b, kc=kc, jj=jj, th=th, gb=gb: e.matmul(
                                     c.ps[:, gb, :], lhsT=c.wgu[:, b, 0, kc, jj * 128:(jj + 1) * 128],
                                     rhs=c.h[:, kc, th * 512:(th + 1) * 512], start=(kc == 0), stop=(kc == KC - 1)),
                                 waits=waits)
                        for kc in range(KC):
                            waits = []
                            if kc == 0:
                                waits = [(c.dve_hid, n - 1)]
                            last = (kc == KC - 1)
                            p.op("tensor",
                                 lambda e, b=b, kc=kc, jj=jj, th=th, ub=ub: e.matmul(
                                     c.ps[:, ub, :], lhsT=c.wgu[:, b, 1, kc, jj * 128:(jj + 1) * 128],
                                     rhs=c.h[:, kc, th * 512:(th + 1) * 512], start=(kc == 0), stop=(kc == KC - 1)),
                                 waits=waits, inc=(c.pe_gu if last else None))
                        gu = c.pe_gu.v
                        a = p.op("scalar",
                                 lambda e, n=n, gb=gb: e.activation(out=c.sg[:, n % 2, :], in_=c.ps[:, gb, :], func=AF.Silu),
                                 waits=[(c.pe_gu, gu), (c.dve_hid, n - 1)], inc=c.act_sg)
                        waits = [(c.act_sg, a), (c.pe_gu, gu)]
                        if j == 0 and th == 0:
                            waits += c.hid_free
                        p.op("vector",
                             lambda e, n=n, ub=ub, j=j, th=th: e.tensor_tensor(
                                 out=c.hid[:, j, th * 512:(th + 1) * 512], in0=c.sg[:, n % 2, :], in1=c.ps[:, ub, :],
                                 op=ALU.mult),
                             waits=waits, inc=c.dve_hid)
                c.panel_done[npn] = c.pe_gu.v
            if hf == 1:
                c.h_free = [(c.pe_gu, c.pe_gu.v)]
            hid_ready = c.dve_hid.v
            xsrc = xin if hf == 0 else xout
            for pd in range(8):
                col0 = pd * 256
                npd = c.n_dpanel
                b = npd % 2
                c.n_dpanel += 1
                wl = p.dma("gpsimd", c.wd[:, b, :, :], wdv[:, hf * FH:(hf + 1) * FH, col0:col0 + 256],
                           waits=([(c.pe_dn, c.dpanel_done[npd - 2])] if npd >= 2 else []), inc=c.wd_full[b])
                for ii in range(2):
                    i = pd * 2 + ii
                    s, w = xs_acquire(c)
                    full = p.dma("sync", c.xs[:, s, :], xsrc[i * 128:(i + 1) * 128, t0:t0 + TT], waits=w,
                                 inc=c.xs_full[s])
                    for th in range(2):
                        n = c.n_dn
                        c.n_dn += 1
                        ob = 4 + n % 2
                        for f in range(FH):
                            waits = []
                            if f == 0:
                                waits = [(c.wd_full[b], wl), (c.dve_hid, hid_ready), (c.dve_res, n - 1)]
                            last = (f == FH - 1)
                            p.op("tensor",
                                 lambda e, b=b, f=f, ii=ii, th=th, ob=ob: e.matmul(
                                     c.ps[:, ob, :], lhsT=c.wd[:, b, f, ii * 128:(ii + 1) * 128],
                                     rhs=c.hid[:, f, th * 512:(th + 1) * 512], start=(f == 0), stop=(f == FH - 1)),
                                 waits=waits, inc=(c.pe_dn if last else None))
                        dn = c.pe_dn.v
                        r = p.op("vector",
                                 lambda e, s=s, th=th, ob=ob: e.scalar_tensor_tensor(
                                     out=c.xs[:, s, th * 512:(th + 1) * 512], in0=c.ps[:, ob, :], scalar=0.5,
                                     in1=c.xs[:, s, th * 512:(th + 1) * 512], op0=ALU.mult, op1=ALU.add),
                                 waits=[(c.pe_dn, dn), (c.xs_full[s], full)], inc=c.dve_res)
                    sv = p.dma("sync", xout[i * 128:(i + 1) * 128, t0:t0 + TT], c.xs[:, s, :],
                               waits=[(c.dve_res, r)], inc=c.xs_st[s])
                    c.xs_cond[s] = [(c.xs_st[s], sv)]
                c.dpanel_done[npd] = c.pe_dn.v
            c.hid_free = [(c.pe_dn, c.pe_dn.v)]


def finish(c):
    p = c.p
    p.wait_only("sync", [(c.xs_st[s], c.xs_st[s].v) for s in range(3)])


def build_ffn():
    nc = bass.Bass("TRN2", target_bir_lowering=False)
    xin = nc.dram_tensor("xin", [D, NTOK], F32, kind="ExternalInput").ap()
    g = nc.dram_tensor("g", [128, KC], F32, kind="ExternalInput").ap()
    wg = nc.dram_tensor("wg", [D, DFF], F32, kind="ExternalInput").ap()
    wu = nc.dram_tensor("wu", [D, DFF], F32, kind="ExternalInput").ap()
    wd = nc.dram_tensor("wd", [DFF, D], F32, kind="ExternalInput").ap()
    xout = nc.dram_tensor("xout", [D, NTOK], F32, kind="ExternalOutput").ap()
    with contextlib.ExitStack() as stack:
        p = Prog(nc, stack)
        c = alloc_common(nc, stack, p)
        alloc_ffn(c)
        gcol = c.sb("gcol", [128, KC], F32)
        p.dma("sync", gcol[:], g[:, :], inc=c.setup_d)
        ffn_body(c, xin, xout, gcol, wg, wu, wd)
        finish(c)
        p.emit()
    return nc


FOX_SCALE = 128.0 ** -0.5
GAM = [1.0 - 2.0 ** -(5 + h) for h in range(4)]
GAM64 = [g ** 64 for g in GAM]
TA = 512
NBLK = TA // 128
AX = mybir.AxisListType


class RowSplit:
    def __init__(self, parts):
        self.parts = parts
        self.h = parts[0].shape[0]

    def __getitem__(self, key):
        rs, cs = key
        i = rs.start // self.h
        assert (rs.stop - 1) // self.h == i
        return self.parts[i][rs.start - i * self.h:rs.stop - i * self.h, cs]


def _parts(x):
    return x.parts if isinstance(x, RowSplit) else [x]


class Slots:
    def __init__(self, p, name, n):
        self.sem = p.sems(name, n)
        self.cond = [[] for _ in range(n)]
        self.i = 0
        self.n = n

    def acquire(self):
        s = self.i % self.n
        self.i += 1
        return s, list(self.cond[s])


def mixa_body(c, xin, gcol, w_in, io):
    p = c.p
    nc = c.nc
    sb = c.sb
    winv = w_in.rearrange("(kc p) f -> p kc f", p=128)
    tabv = io["tab"].rearrange("(b p) f -> p b f", p=128)
    wp = sb("wp", [128, 2, 8192], BF16)
    wps = Slots(p, "wps", 2)
    wpfm = lambda b: wp[:, b, 0:4096].rearrange("p (k c) -> p k c", c=256)
    wptm = lambda b: wp[:, b, :].rearrange("p (k c) -> p k c", c=512)
    wpfz = lambda b: wp[:, b, 0:128].rearrange("p (k c) -> p k c", c=8)
    tabt = sb("tabt", [128, 2, NBLK, 268], F32)
    tabs = Slots(p, "tabs", 2)
    stg16 = sb("stg16", [128, 4, 512], BF16)
    st16 = Slots(p, "st16", 4)
    stg32 = sb("stg32", [128, 2, 512], F32)
    st32 = Slots(p, "st32", 2)
    u = sb("u", [128, 4, TA], F32)
    rv = sb("rv", [128, NBLK, 512], BF16)
    kr = sb("kr", [128, NBLK, 512], BF16)
    kz = sb("kz", [128, NBLK, 512], BF16)
    qx = sb("qx", [128, NBLK, 512], BF16)
    qg = sb("qg", [128, NBLK, 512], BF16)
    vln = sb("vln", [128, NBLK, 512], BF16)
    rot = sb("rot", [128, 2, 2, 512], F32)
    rots = Slots(p, "rots", 2)
    st = sb("lnst", [128, 24], F32)
    fzt = sb("fzt", [8, 512], F32)
    onesr = sb("onesr", [8, 512], F32)
    cneg = sb("cneg", [8, NTOK], F32)
    S32 = sb("S32", [128, 512], F32)
    Sb = sb("Sb", [128, 2 * NBLK, 512], BF16)
    krT = sb("krT", [128, 512], BF16)
    qxT = sb("qxT", [128, 512], BF16)
    sm = sb("sm", [128, 512], BF16)
    y1 = sb("y1", [128, 512], F32)
    y2 = sb("y2", [128, 512], F32)
    pst = c.stack.enter_context(nc.psum_tensor("pst%d" % c.uid, [128, 2, 1024], BF16))
    c.pe = p.sem("pe")
    c.act = p.sem("act")
    c.dve = p.sem("dve")
    misc = p.sem("miscst")
    pj_cond = [[], []]
    kv_cond = [[], []]
    b4_cond = []
    pst_cond = [[], []]
    pj_n = [0]
    kv_n = [0]
    u_free = []
    ret_free = []
    vln_free = []
    p.op("vector", lambda e: e.memset(onesr[:], 1.0), inc=c.setup_v)
    p.op("vector", lambda e: e.memset(S32[:], 0.0), inc=c.setup_v)
    setupw = [(c.setup_d, c.setup_d.v), (c.setup_v, c.setup_v.v)]

    def V4(ap):
        return ap.rearrange("p (a b) -> p a b", a=4)

    def bc(ap4):
        return ap4.unsqueeze(2).to_broadcast([128, 4, 128])

    def bh(ap128):
        return ap128.unsqueeze(1).to_broadcast([128, 4, 128])

    def bh64(ap64):
        return ap64.unsqueeze(1).to_broadcast([128, 4, 64])

    def proj(b, waits, lhs_fn, rhs_fn, out_fn):
        n = pj_n[0]
        pj_n[0] += 1
        bank = n % 2
        for kc in range(KC):
            p.op("tensor",
                 lambda e, kc=kc: e.matmul(out_fn(bank), lhsT=lhs_fn(kc), rhs=rhs_fn(kc), start=(kc == 0),
                                           stop=(kc == KC - 1)),
                 waits=(list(waits) + pj_cond[bank]) if kc == 0 else [], inc=(c.pe if kc == KC - 1 else None))
        return bank, c.pe.v

    def store16(src_fn, dst_ap, waits, eng_op):
        s, w = st16.acquire()
        a = p.op("scalar", lambda e: eng_op(e, stg16[:, s, :]), waits=list(waits) + w, inc=c.act)
        sv = p.dma("sync", dst_ap, src_fn(stg16[:, s, :]), waits=[(c.act, a)], inc=st16.sem[s])
        st16.cond[s] = [(st16.sem[s], sv)]
        return a

    for tt in range(NTOK // TA):
        t0 = tt * TA
        h_ready = norm_stats_and_h(c, xin, gcol, tt)
        hw = [(c.dve_h, h_ready)]
        ts_, w = tabs.acquire()
        tk = p.dma("sync", tabt[:, ts_], tabv[:, tt * NBLK:(tt + 1) * NBLK, :], waits=w, inc=tabs.sem[ts_])
        tabw = [(tabs.sem[ts_], tk)]
        for name, cbase, ncol in (("fq", 0, 1024), ("fk", 1024, 1024), ("rg", 4616, 512), ("gu", 5128, 512)):
            for pn in range(ncol // 256):
                col0 = cbase + pn * 256
                b, w = wps.acquire()
                t = p.dma("gpsimd", wpfm(b), winv[:, :, col0:col0 + 256], waits=w, inc=wps.sem[b])
                for jj in range(2):
                    ch = pn * 2 + jj
                    bank, pt = proj(b, [(wps.sem[b], t)] + hw,
                                    lambda kc, b=b, jj=jj: wpfm(b)[:, kc, jj * 128:(jj + 1) * 128],
                                    lambda kc: c.h[:, kc, :], lambda bank: c.ps[:, bank, :])
                    pw = [(c.pe, pt)]
                    if name == "fq":
                        a = store16(lambda s_: s_, io["qT"][ch * 128:(ch + 1) * 128, t0:t0 + TA], pw,
                                    lambda e, o, bank=bank: e.mul(out=o, in_=c.ps[:, bank, :], mul=FOX_SCALE))
                    elif name == "fk":
                        a = store16(lambda s_: s_, io["kT"][ch * 128:(ch + 1) * 128, t0:t0 + TA], pw,
                                    lambda e, o, bank=bank: e.copy(out=o, in_=c.ps[:, bank, :]))
                    elif name == "rg":
                        s, w2 = st32.acquire()
                        a = p.op("scalar", lambda e, s=s, bank=bank: e.activation(out=stg32[:, s, :], in_=c.ps[:, bank, :],
                                                                                func=AF.Silu),
                                 waits=pw + w2, inc=c.act)
                        sv = p.dma("sync", io["rgs"][ch * 128:(ch + 1) * 128, t0:t0 + TA], stg32[:, s, :],
                                   waits=[(c.act, a)], inc=st32.sem[s])
                        st32.cond[s] = [(st32.sem[s], sv)]
                    else:
                        a = p.op("scalar", lambda e, ch=ch, bank=bank: e.activation(out=u[:, ch, :], in_=c.ps[:, bank, :],
                                                                                  func=AF.Gelu_apprx_tanh),
                                 waits=pw + (u_free if ch == 0 else []), inc=c.act)
                    pj_cond[bank] = [(c.act, a)]
                wps.cond[b] = [(c.pe, pt)]
        b, w = wps.acquire()
        t = p.dma("gpsimd", wpfz(b), winv[:, :, 3072:3080], waits=w, inc=wps.sem[b])
        bank, pt = proj(b, [(wps.sem[b], t)] + hw, lambda kc, b=b: wpfz(b)[:, kc, :], lambda kc: c.h[:, kc, :],
                        lambda bank: c.ps[0:8, bank, :])
        wps.cond[b] = [(c.pe, pt)]
        a = p.op("scalar", lambda e, bank=bank: e.activation(out=fzt[:], in_=c.ps[0:8, bank, :], func=AF.Exp,
                                                             bias=io["negb"][:, 0:1], scale=-1.0),
                 waits=[(c.pe, pt), (c.dve, c.dve.v)] + setupw, inc=c.act)
        pj_cond[bank] = [(c.act, a)]
        a = p.op("scalar", lambda e: e.activation(out=fzt[:], in_=fzt[:], func=AF.Ln, bias=1.0), waits=[(c.act, a)],
                 inc=c.act)
        init = 0.0 if tt == 0 else cneg[:, t0 - 1:t0]
        p.op("vector", lambda e, init=init, t0=t0: e.tensor_tensor_scan(out=cneg[:, t0:t0 + TA], data0=onesr[:], data1=fzt[:],
                                                                 initial=init, op0=ALU.mult, op1=ALU.add),
             waits=[(c.act, a), (c.dve, c.dve.v)] + setupw, inc=c.dve)
        ready = {}
        for name, col0 in (("fv0", 2048), ("fv1", 2560), ("rv", 4104), ("rk", 3592), ("rq", 3080), ("gv", 5640)):
            b, w = wps.acquire()
            t = p.dma("gpsimd", wptm(b), winv[:, :, col0:col0 + 512], waits=w, inc=wps.sem[b])
            for tb in range(NBLK):
                bank, pt = proj(b, [(wps.sem[b], t)] + hw,
                                lambda kc, tb=tb: c.h[:, kc, tb * 128:(tb + 1) * 128],
                                lambda kc, b=b: wptm(b)[:, kc, :], lambda bank: c.ps[:, bank, :])
                pw = [(c.pe, pt)]
                psb = c.ps[:, bank, :]
                psv = V4(psb)
                r0 = t0 + tb * 128
                if name in ("fv0", "fv1"):
                    hc = 0 if name == "fv0" else 512
                    a = store16(lambda s_: s_, io["v"][r0:r0 + 128, hc:hc + 512], pw,
                                lambda e, o, psb=psb: e.copy(out=o, in_=psb))
                    pj_cond[bank] = [(c.act, a)]
                elif name == "rv":
                    a = p.op("scalar", lambda e, tb=tb, psb=psb: e.copy(out=rv[:, tb, :], in_=psb),
                             waits=pw + (ret_free if tb == 0 else []), inc=c.act)
                    pj_cond[bank] = [(c.act, a)]
                    ready[("rv", tb)] = [(c.act, a)]
                elif name in ("rk", "rq"):
                    rs, w2 = rots.acquire()
                    r1 = rot[:, rs, 0, :]
                    r2 = rot[:, rs, 1, :]
                    cosb = bh(tabt[:, ts_, tb, 0:128])
                    sina = bh64(tabt[:, ts_, tb, 128:192])
                    sinb = bh64(tabt[:, ts_, tb, 192:256])
                    p.op("vector", lambda e, psv=psv, r1=r1, cosb=cosb: e.tensor_tensor(out=V4(r1), in0=psv, in1=cosb,
                                                                                      op=ALU.mult),
                         waits=pw + w2 + tabw, inc=c.dve)
                    p.op("vector", lambda e, psv=psv, r2=r2, sina=sina: e.tensor_tensor(
                        out=V4(r2)[:, :, 0:64], in0=psv[:, :, 64:128], in1=sina, op=ALU.mult), inc=c.dve)
                    d = p.op("vector", lambda e, psv=psv, r2=r2, sinb=sinb: e.tensor_tensor(
                        out=V4(r2)[:, :, 64:128], in0=psv[:, :, 0:64], in1=sinb, op=ALU.mult), inc=c.dve)
                    pj_cond[bank] = [(c.dve, d)]
                    d = p.op("vector", lambda e, r1=r1, r2=r2: e.tensor_tensor(out=r1, in0=r1, in1=r2, op=ALU.add),
                             waits=[(c.dve, d)], inc=c.dve)
                    fw = ret_free if tb == 0 else []
                    if name == "rk":
                        a = p.op("scalar", lambda e, tb=tb, r1=r1: e.copy(out=kr[:, tb, :], in_=r1),
                                 waits=[(c.dve, d)] + fw, inc=c.act)
                        zb = bc(tabt[:, ts_, tb, 264:268])
                        d2 = p.op("vector", lambda e, tb=tb, r1=r1, zb=zb: e.tensor_tensor(out=V4(kz[:, tb, :]), in0=V4(r1),
                                                                                         in1=zb, op=ALU.mult),
                                  waits=[(c.dve, d)] + fw, inc=c.dve)
                        rots.cond[rs] = [(c.act, a), (c.dve, d2)]
                        ready[("kr", tb)] = [(c.act, a)]
                        ready[("kz", tb)] = [(c.dve, d2)]
                    else:
                        xb = bc(tabt[:, ts_, tb, 256:260])
                        gb_ = bc(tabt[:, ts_, tb, 260:264])
                        p.op("vector", lambda e, tb=tb, r1=r1, xb=xb: e.tensor_tensor(out=V4(qx[:, tb, :]), in0=V4(r1),
                                                                                    in1=xb, op=ALU.mult),
                             waits=[(c.dve, d)] + fw, inc=c.dve)
                        d2 = p.op("vector", lambda e, tb=tb, r1=r1, gb_=gb_: e.tensor_tensor(out=V4(qg[:, tb, :]),
                                                                                           in0=V4(r1), in1=gb_,
                                                                                           op=ALU.mult), inc=c.dve)
                        rots.cond[rs] = [(c.dve, d2)]
                        ready[("q", tb)] = [(c.dve, d2)]
                else:
                    rs, w2 = rots.acquire()
                    r1 = rot[:, rs, 0, :]
                    r2 = rot[:, rs, 1, :]
                    a = p.op("scalar", lambda e, psb=psb, r1=r1: e.activation(out=r1, in_=psb, func=AF.Gelu_apprx_tanh),
                             waits=pw + w2, inc=c.act)
                    pj_cond[bank] = [(c.act, a)]
                    a2 = p.op("scalar", lambda e, r1=r1, r2=r2: e.activation(out=r2, in_=r1, func=AF.Square),
                              waits=[(c.act, a)], inc=c.act)
                    d = p.op("vector", lambda e, r1=r1: e.tensor_reduce(out=st[:, 0:4], in_=V4(r1), axis=AX.X, op=ALU.add),
                             waits=[(c.act, a), (c.dve, c.dve.v)], inc=c.dve)
                    d = p.op("vector", lambda e, r2=r2: e.tensor_reduce(out=st[:, 4:8], in_=V4(r2), axis=AX.X, op=ALU.add),
                             waits=[(c.act, a2)], inc=c.dve)
                    d = p.op("vector", lambda e: e.tensor_scalar(out=st[:, 8:12], in0=st[:, 0:4], scalar1=1.0 / 128,
                                                                 scalar2=None, op0=ALU.mult),
                             waits=[(c.dve, d)], inc=c.dve)
                    d = p.op("vector", lambda e: e.tensor_tensor(out=st[:, 12:16], in0=st[:, 8:12], in1=st[:, 8:12],
                                                                 op=ALU.mult), waits=[(c.dve, d)], inc=c.dve)
                    d = p.op("vector", lambda e: e.scalar_tensor_tensor(out=st[:, 16:20], in0=st[:, 4:8], scalar=1.0 / 128,
                                                                        in1=st[:, 12:16], op0=ALU.mult,
                                                                        op1=ALU.subtract), waits=[(c.dve, d)], inc=c.dve)
                    d = p.op("vector", lambda e: e.tensor_scalar(out=st[:, 16:20], in0=st[:, 16:20], scalar1=EPS,
                                                                 scalar2=None, op0=ALU.add), waits=[(c.dve, d)], inc=c.dve)
                    a3 = p.op("scalar", lambda e: e.activation(out=st[:, 16:20], in_=st[:, 16:20], func=AF.Sqrt),
                              waits=[(c.dve, d)], inc=c.act)
                    d = p.op("vector", lambda e: e.reciprocal(out=st[:, 20:24], in_=st[:, 16:20]), waits=[(c.act, a3)],
                             inc=c.dve)
                    d = p.op("vector", lambda e, r1=r1: e.tensor_tensor(out=V4(r1), in0=V4(r1), in1=bc(st[:, 8:12]),
                                                                      op=ALU.subtract), waits=[(c.dve, d)], inc=c.dve)
                    d = p.op("vector", lambda e, r1=r1: e.tensor_tensor(out=V4(r1), in0=V4(r1), in1=bc(st[:, 20:24]),
                                                                      op=ALU.mult), waits=[(c.dve, d)], inc=c.dve)
                    d = p.op("vector", lambda e, r1=r1: e.tensor_tensor(out=r1, in0=r1, in1=io["lnb3"][:, 0, :],
                                                                      op=ALU.mult), waits=[(c.dve, d)] + setupw,
                             inc=c.dve)
                    d = p.op("vector", lambda e, r1=r1, tb=tb: e.tensor_tensor(out=vln[:, tb, :], in0=r1,
                                                                             in1=io["lnb3"][:, 1, :], op=ALU.add),
                             waits=[(c.dve, d)] + (vln_free if tb == 0 else []), inc=c.dve)
                    rots.cond[rs] = [(c.dve, d)]
                    ready[("vln", tb)] = [(c.dve, d)]
            wps.cond[b] = [(c.pe, pt)]
        for n in range(2 * NBLK):
            tb, a_ = n // 2, n % 2
            kb = 2 + kv_n[0] % 2
            ci = kv_n[0] % 2
            kv_n[0] += 1
            for hh in range(4):
                sl = slice(hh * 128, (hh + 1) * 128)
                p.op("tensor", lambda e, kb=kb, sl=sl, a_=a_, tb=tb: e.matmul(
                    c.ps[:, kb, sl], lhsT=kz[a_ * 64:(a_ + 1) * 64, tb, sl], rhs=rv[a_ * 64:(a_ + 1) * 64, tb, sl],
                    start=True, stop=True),
                     waits=(ready[("kz", tb)] + ready[("rv", tb)] + kv_cond[ci]) if hh == 0 else [],
                     inc=(c.pe if hh == 3 else None))
            pt = c.pe.v
            a = p.op("scalar", lambda e, n=n: e.copy(out=Sb[:, n, :], in_=S32[:]),
                     waits=[(c.dve, c.dve.v)] + (ret_free if n == 0 else []) + setupw, inc=c.act)
            ready[("Sb", n)] = [(c.act, a)]
            for hh in range(4):
                sl = slice(hh * 128, (hh + 1) * 128)
                d = p.op("vector", lambda e, kb=kb, sl=sl, hh=hh: e.scalar_tensor_tensor(
                    out=S32[:, sl], in0=S32[:, sl], scalar=GAM64[hh], in1=c.ps[:, kb, sl], op0=ALU.mult, op1=ALU.add),
                         waits=[(c.pe, pt), (c.act, a)] if hh == 0 else [], inc=c.dve)
            kv_cond[ci] = [(c.dve, d)]
        s32_done = [(c.dve, d)]
        for tb in range(NBLK):
            r0 = t0 + tb * 128
            for gg in range(4):
                sl = slice(gg * 128, (gg + 1) * 128)
                p.op("tensor", lambda e, sl=sl, tb=tb, gg=gg: e.matmul(c.ps[:, 4, sl], lhsT=vln[:, tb, sl],
                                                                     rhs=io["wsb"][:, gg, :], start=True, stop=True),
                     waits=(ready[("vln", tb)] + b4_cond + setupw) if gg == 0 else [], inc=(c.pe if gg == 3 else None))
            pt = c.pe.v
            d = p.op("vector", lambda e: e.tensor_tensor(out=y1[:], in0=c.ps[:, 4, :], in1=io["lnb3"][:, 2, :], op=ALU.add),
                     waits=[(c.pe, pt), (c.act, c.act.v), (c.dve, c.dve.v)], inc=c.dve)
            d = p.op("vector", lambda e, tb=tb: e.tensor_tensor(out=V4(y1[:]), in0=V4(y1[:]),
                                                               in1=u[:, :, tb * 128:(tb + 1) * 128], op=ALU.mult),
                     waits=[(c.dve, d)], inc=c.dve)
            a = p.op("scalar", lambda e: e.activation(out=y2[:], in_=y1[:], func=AF.Square), waits=[(c.dve, d)], inc=c.act)
            p.op("tensor", lambda e: e.matmul(c.ps[:, 4, :], lhsT=c.ones[:], rhs=y2[:], start=True, stop=True),
                 waits=[(c.act, a), (c.dve, d)], inc=c.pe)
            pt = c.pe.v
            d = p.op("vector", lambda e: e.tensor_scalar(out=y2[:], in0=c.ps[:, 4, :], scalar1=1.0 / 128, scalar2=EPS,
                                                         op0=ALU.mult, op1=ALU.add), waits=[(c.pe, pt)], inc=c.dve)
            b4_cond = [(c.dve, d)]
            a = p.op("scalar", lambda e: e.activation(out=y2[:], in_=y2[:], func=AF.Sqrt), waits=[(c.dve, d)], inc=c.act)
            d = p.op("vector", lambda e: e.reciprocal(out=y2[:], in_=y2[:]), waits=[(c.act, a)], inc=c.dve)
            d = p.op("vector", lambda e: e.tensor_tensor(out=y1[:], in0=y1[:], in1=y2[:], op=ALU.mult),
                     waits=[(c.dve, d)], inc=c.dve)
            s, w = st16.acquire()
            for gg in range(4):
                sl = slice(gg * 128, (gg + 1) * 128)
                a = p.op("vector", lambda e, s=s, sl=sl, gg=gg: e.tensor_scalar(out=stg16[:, s, sl], in0=y1[:, sl],
                                                                              scalar1=io["ong"][:, gg:gg + 1],
                                                                              scalar2=None, op0=ALU.mult),
                         waits=([(c.dve, d)] + w + setupw) if gg == 0 else [], inc=c.dve)
            sv = p.dma("sync", io["ytg"].rearrange("(g c) t -> c g t", c=128)[:, :, r0:r0 + 128], V4(stg16[:, s, :]),
                       waits=[(c.dve, a)], inc=st16.sem[s])
            st16.cond[s] = [(st16.sem[s], sv)]
        u_free = [(c.dve, c.dve.v)]
        vln_free = [(c.pe, c.pe.v)]
        for tb in range(NBLK):
            r0 = t0 + tb * 128
            for src, key, dstT, pb in ((kr, "kr", krT, 0), (qx, "q", qxT, 1), (qg, "q", None, 0)):
                for hh in range(4):
                    sl = slice(hh * 128, (hh + 1) * 128)
                    p.op("tensor", lambda e, src=src, sl=sl, tb=tb, pb=pb: e.transpose(out=pst[:, pb, sl], in_=src[:, tb, sl],
                                                                                     identity=io["identb"][:]),
                         waits=(ready[(key, tb)] + pst_cond[pb] + setupw) if hh == 0 else [],
                         inc=(c.pe if hh == 3 else None))
                pt = c.pe.v
                if dstT is not None:
                    d = p.op("vector", lambda e, dstT=dstT, pb=pb: e.tensor_copy(out=dstT[:], in_=pst[:, pb, 0:512]),
                             waits=[(c.pe, pt), (c.pe, c.pe.v)], inc=c.dve)
                    pst_cond[pb] = [(c.dve, d)]
                    ready[(id(dstT), tb)] = [(c.dve, d)]
                else:
                    s, w = st16.acquire()
                    a = p.op("scalar", lambda e, s=s, pb=pb: e.copy(out=stg16[:, s, :], in_=pst[:, pb, 0:512]),
                             waits=[(c.pe, pt)] + w, inc=c.act)
                    pst_cond[pb] = [(c.act, a)]
                    sv = p.dma("sync", io["ret_qg"].rearrange("(h d) t -> d h t", d=128)[:, :, r0:r0 + 128],
                               V4(stg16[:, s, :]), waits=[(c.act, a)], inc=st16.sem[s])
                    st16.cond[s] = [(st16.sem[s], sv)]
            for hh in range(4):
                sl = slice(hh * 128, (hh + 1) * 128)
                p.op("tensor", lambda e, sl=sl: e.matmul(c.ps[:, 4, sl], lhsT=krT[:, sl], rhs=qxT[:, sl], start=True,
                                                         stop=True),
                     waits=(ready[(id(krT), tb)] + ready[(id(qxT), tb)] + b4_cond) if hh == 0 else [],
                     inc=(c.pe if hh == 3 else None))
            pt = c.pe.v
            d = p.op("vector", lambda e: e.tensor_tensor(out=sm[:], in0=c.ps[:, 4, :], in1=io["maskr"][:], op=ALU.mult),
                     waits=[(c.pe, pt), (c.pe, c.pe.v)] + setupw, inc=c.dve)
            b4_cond = [(c.dve, d)]
            for hh in range(4):
                sl = slice(hh * 128, (hh + 1) * 128)
                p.op("tensor", lambda e, sl=sl, tb=tb: e.matmul(c.ps[:, 5, sl], lhsT=rv[:, tb, sl], rhs=sm[:, sl], start=True,
                                                               stop=False),
                     waits=([(c.dve, d)] + c.ps_free_waits + ready[("Sb", 2 * tb)] + ready[("Sb", 2 * tb + 1)])
                     if hh == 0 else [])
                for a_ in range(2):
                    cs = slice(hh * 128 + a_ * 64, hh * 128 + (a_ + 1) * 64)
                    p.op("tensor", lambda e, sl=sl, cs=cs, tb=tb, a_=a_: e.matmul(
                        c.ps[:, 5, cs], lhsT=Sb[:, 2 * tb + a_, sl], rhs=qxT[:, cs], start=False, stop=(a_ == 1)),
                         inc=(c.pe if (hh == 3 and a_ == 1) else None))
            pt = c.pe.v
            s, w = st32.acquire()
            a = p.op("scalar", lambda e, s=s: e.copy(out=stg32[:, s, :], in_=c.ps[:, 5, :]), waits=[(c.pe, pt)] + w,
                     inc=c.act)
            c.ps_free_waits = c.ps_free_waits + [(c.act, a)]
            sv = p.dma("sync", io["ret_o"].rearrange("(h e) t -> e h t", e=128)[:, :, r0:r0 + 128], V4(stg32[:, s, :]),
                       waits=[(c.act, a)], inc=st32.sem[s])
            st32.cond[s] = [(st32.sem[s], sv)]
        ret_free = [(c.pe, c.pe.v)]
    f1 = p.dma("sync", io["cneg_o"][:, :], cneg[:], waits=[(c.dve, c.dve.v)], inc=misc)
    f2 = p.dma("sync", io["ret_S"][:, :], S32[:], waits=s32_done, inc=misc)
    p.wait_only("sync", [(misc, f2)] + [(st16.sem[s], st16.sem[s].v) for s in range(4)] +
                [(st32.sem[s], st32.sem[s].v) for s in range(2)])


def mixa_setup(c, din, io):
    p = c.p
    sb = c.sb
    gcol = sb("gcol", [128, KC], F32)
    negb = sb("negb", [8, 1], F32)
    maskr = sb("maskr", [128, 512], F32)
    wst = sb("wst", [128, 512], F32)
    triu = sb("triu", [128, 128], F32)
    wsb = sb("wsb", [128, 4, 128], BF16)
    lnb3 = sb("lnb3", [128, 3, 512], F32)
    ong = sb("ong", [128, 4], F32)
    identb = sb("identb", [128, 128], BF16)
    io.update({"negb": negb, "maskr": maskr, "wsb": wsb, "lnb3": lnb3, "ong": ong, "identb": identb})
    for dst, src in ((gcol[:], din["g"]), (negb[:], din["bf"]), (maskr[:], din["maskr"]), (wst[:], din["wst"]),
                     (triu[:], din["triu"]), (lnb3[:].rearrange("p a b -> p (a b)"), din["lnb3"]),
                     (ong[:], din["ong"])):
        p.dma("sync", dst, src, inc=c.setup_d)
    p.dma("gpsimd", identb[:], din["ident"], inc=c.setup_d)
    dl = [(c.setup_d, c.setup_d.v)]
    p.op("vector", lambda e: e.tensor_scalar(out=negb[:], in0=negb[:], scalar1=-1.0, scalar2=None, op0=ALU.mult),
         waits=dl, inc=c.setup_v)
    p.op("vector", lambda e: e.tensor_tensor(out=wsb[:], in0=wst[:].rearrange("p (a b) -> p a b", a=4),
                                             in1=triu[:].unsqueeze(1).to_broadcast([128, 4, 128]), op=ALU.mult),
         inc=c.setup_v)
    return gcol


def build_mixa():
    nc = bass.Bass("TRN2", target_bir_lowering=False)
    di = lambda name, shape: nc.dram_tensor(name, shape, F32, kind="ExternalInput").ap()
    do = lambda name, shape, dt: nc.dram_tensor(name, shape, dt, kind="ExternalOutput").ap()
    xin = di("xin", [D, NTOK])
    w_in = di("w_in", [D, INCOLS])
    din = {"g": di("g", [128, KC])[:, :], "bf": di("bf", [8, 1])[:, :], "maskr": di("maskr", [128, 512])[:, :],
           "wst": di("wst", [128, 512])[:, :], "triu": di("triu", [128, 128])[:, :],
           "lnb3": di("lnb3", [128, 3 * 512])[:, :], "ong": di("ong", [128, 4])[:, :],
           "ident": di("ident", [128, 128])[:, :]}
    io = {
        "tab": di("tab", [NTOK, 268]),
        "qT": do("qT", [1024, NTOK], BF16), "kT": do("kT", [1024, NTOK], BF16), "v": do("v", [NTOK, 1024], BF16),
        "cneg_o": do("cneg", [8, NTOK], F32), "ret_o": do("ret_o", [512, NTOK], F32),
        "ret_qg": do("ret_qg", [512, NTOK], BF16), "ret_S": do("ret_S", [128, 512], F32),
        "rgs": do("rgs", [512, NTOK], F32), "ytg": do("ytg", [512, NTOK], BF16),
    }
    with contextlib.ExitStack() as stack:
        p = Prog(nc, stack)
        c = alloc_common(nc, stack, p, tt=TA, nps=6, stat_bank=5)
        c.stack = stack
        gcol = mixa_setup(c, din, io)
        mixa_body(c, xin, gcol, w_in, io)
        p.emit()
    return nc


def host_consts(half):
    t = np.arange(NTOK, dtype=np.float64)
    pos = (half * NTOK + np.arange(NTOK)).astype(np.float32)
    inv_freq = (np.float32(10000.0) ** (-np.arange(64, dtype=np.float32) / np.float32(64))).astype(np.float32)
    ang = (pos[:, None] * inv_freq[None, :]).astype(np.float32).astype(np.float64)
    cos, sin = np.cos(ang), np.sin(ang)
    gam = np.array(GAM, dtype=np.float64)
    cidx = (np.arange(NTOK) % 64).astype(np.float64)
    xi = gam[None, :] ** (cidx[:, None] + 1.0)
    gm = gam[None, :] ** (t[:, None] + 1.0)
    zeta = gam[None, :] ** (63.0 - cidx[:, None]) * (128.0 ** -0.5)
    tab = np.concatenate([cos, cos, -sin, sin, xi, gm, zeta], axis=1).astype(np.float32)
    s = np.arange(128)
    cc = np.arange(128)
    same = (s[:, None] // 64 == cc[None, :] // 64) & (cc[None, :] >= s[:, None])
    maskr = np.zeros((128, 4, 128), np.float64)
    for h in range(4):
        maskr[:, h, :] = np.where(same, gam[h] ** (-(s[:, None] % 64 + 1.0)), 0.0) * (128.0 ** -0.5)
    triu = (s[:, None] <= cc[None, :]).astype(np.float32)
    return {"tab": np.ascontiguousarray(tab), "maskr": maskr.reshape(128, 512).astype(np.float32), "triu": triu,
            "ident": np.eye(128, dtype=np.float32)}


def col16(vec):
    return np.ascontiguousarray(np.asarray(vec, np.float32).reshape(KC, 128).T)


def mixa_inputs(xT, half, l, P):
    hc = host_consts(half)
    wst = np.ascontiguousarray(np.transpose(P["gmlp_w_s"][l], (2, 0, 1)).reshape(128, 512))
    lnb3 = np.concatenate([np.broadcast_to(P["gmlp_ln_g"][l][None, :], (128, 512)),
                           np.broadcast_to(P["gmlp_ln_b"][l][None, :], (128, 512)),
                           np.broadcast_to(P["gmlp_b_s"][l].reshape(1, 512), (128, 512))], axis=1)
    ong = np.ascontiguousarray(P["out_norm"][l][1536:2048].reshape(4, 128).T)
    return {"xin": xT, "g": col16(P["mix_norm"][l]), "w_in": P["w_in"][l],
            "bf": np.ascontiguousarray(P["fox_b_f"][l].reshape(8, 1)), "tab": hc["tab"], "maskr": hc["maskr"],
            "wst": wst.astype(np.float32), "triu": hc["triu"], "lnb3": np.ascontiguousarray(lnb3, dtype=np.float32),
            "ong": ong.astype(np.float32), "ident": hc["ident"]}


QT = 512
NQT = NTOK // QT
MASKNEG = -30000.0


def mixb_body(nc, stack, p, io):
    uid = _uid()
    sb = lambda name, shape, dt: stack.enter_context(nc.sbuf_tensor("sb%d_%s" % (uid, name), shape, dt))
    ps = stack.enter_context(nc.psum_tensor("psb%d" % uid, [128, 7, 512], F32))
    onesf = sb("onesf", [128, 128], F32)
    onesb = sb("onesb", [128, 128], BF16)
    cmask = sb("cmask", [128, 896], F32)
    selh = sb("selh", [8, 8, 128], F32)
    ident8 = sb("ident8", [8, 8], F32)
    pmask = sb("pmask", [128, 1], F32)
    sflag = sb("sflag", [128, 1], F32)
    ona = sb("ona", [128, 12], F32)
    sinit = sb("sinit", [128, 512], F32)
    sinb = sb("sinb", [128, 512], BF16)
    cn = sb("cn", [8, 2 * NTOK], F32)
    ncl = sb("ncl", [8, NTOK], F32)
    ncp = sb("ncp", [8, NTOK], F32)
    biasT = sb("biasT", [128, 32, 8], F32)
    setup_d = p.sem("bsetupd")
    dve = p.sem("bdve")
    act = p.sem("bact")
    pe = p.sem("bpe")
    for dst, src in ((cmask[:], io["cmask"][:, :]), (selh[:].rearrange("k h m -> k (h m)"), io["selh"][:, :]),
                     (ident8[:], io["ident8"][:, :]), (pmask[:], io["pmask"][:, :]), (sflag[:], io["sflag"][:, :]),
                     (ona[:], io["ona"][:, :]), (sinit[:], io["s_init"][:, :]), (cn[:, 0:NTOK], io["cneg_prev"][:, :]),
                     (cn[:, NTOK:2 * NTOK], io["cneg_loc"][:, :])):
        p.dma("sync", dst, src, inc=setup_d)
    sd = [(setup_d, setup_d.v)]
    p.op("vector", lambda e: e.memset(onesf[:], 1.0), inc=dve)
    p.op("vector", lambda e: e.memset(onesb[:], 1.0), inc=dve)
    p.op("vector", lambda e: e.tensor_scalar(out=sinb[:], in0=sinit[:], scalar1=sflag[:, 0:1], scalar2=None, op0=ALU.mult),
         waits=sd, inc=dve)
    p.op("vector", lambda e: e.tensor_scalar(out=ncl[:], in0=cn[:, NTOK:2 * NTOK], scalar1=-1.0, scalar2=None, op0=ALU.mult),
         inc=dve)
    d = p.op("vector", lambda e: e.tensor_scalar(out=ncp[:], in0=ncl[:], scalar1=cn[:, NTOK - 1:NTOK], scalar2=None,
                                                 op0=ALU.subtract), waits=[(dve, dve.v)], inc=dve)
    for blk in range(32):
        p.op("tensor", lambda e, blk=blk: e.transpose(out=ps[:, 6, blk * 8:(blk + 1) * 8],
                                                      in_=cn[0:8, blk * 128:(blk + 1) * 128], identity=ident8[:]),
             waits=sd if blk == 0 else [], inc=(pe if blk == 31 else None))
    p.op("vector", lambda e: e.tensor_scalar(out=biasT[:, 0:16, :].rearrange("p a b -> p (a b)"), in0=ps[:, 6, 0:128],
                                             scalar1=pmask[:, 0:1], scalar2=None, op0=ALU.add),
         waits=[(pe, pe.v)] + sd, inc=dve)
    d = p.op("vector", lambda e: e.tensor_copy(out=biasT[:, 16:32, :].rearrange("p a b -> p (a b)"), in_=ps[:, 6, 128:256]),
             inc=dve)
    b6_cond = [(dve, d)]
    setup_done = [(dve, d)] + sd

    qh = sb("qh", [128, 2, NTOK], BF16)
    kh = sb("kh", [128, 2, 2 * NTOK], BF16)
    vh = sb("vh", [128, 2, 32, 128], BF16)
    hs = Slots(p, "hs", 2)
    cb = sb("cb", [128, 2, 2, QT], F32)
    cbs = Slots(p, "cbs", 2)
    tmp = sb("tmp", [128, 3, QT], F32)
    tmps = Slots(p, "tmps", 3)
    pt = sb("pt", [128, 3, QT], BF16)
    pts = Slots(p, "pts", 3)
    ya = sb("ya", [128, 2, QT], F32)
    yas = Slots(p, "yas", 2)
    yq = sb("yq", [128, QT], F32)
    stg = sb("stg", [128, 2, QT], BF16)
    stgs = Slots(p, "stgs", 2)
    ro = sb("ro", [128, 2, QT], F32)
    rg = sb("rg", [128, 2, QT], F32)
    qgt = sb("qgt", [128, 2, QT], BF16)
    rls = Slots(p, "rls", 2)
    s_cond = [[], []]
    acc_cond = [[], []]
    s_n = [0]
    acc_n = [0]
    y_stores = []

    def headnorm_store(yslot, yready, gain_col, dst_ap, mul_tile=None, mul_wait=()):
        nonlocal b6_cond
        a = p.op("scalar", lambda e: e.activation(out=yq[:], in_=ya[:, yslot, :], func=AF.Square),
                 waits=list(yready) + [(dve, dve.v)], inc=act)
        p.op("tensor", lambda e: e.matmul(ps[:, 6, :], lhsT=onesf[:], rhs=yq[:], start=True, stop=True),
             waits=[(act, a)] + b6_cond, inc=pe)
        d = p.op("vector", lambda e: e.tensor_scalar(out=yq[:], in0=ps[:, 6, :], scalar1=1.0 / 128, scalar2=EPS,
                                                     op0=ALU.mult, op1=ALU.add), waits=[(pe, pe.v)], inc=dve)
        b6_cond = [(dve, d)]
        a = p.op("scalar", lambda e: e.activation(out=yq[:], in_=yq[:], func=AF.Sqrt), waits=[(dve, d)], inc=act)
        d = p.op("vector", lambda e: e.reciprocal(out=yq[:], in_=yq[:]), waits=[(act, a)], inc=dve)
        s, w = stgs.acquire()
        if mul_tile is None:
            d = p.op("vector", lambda e: e.scalar_tensor_tensor(out=stg[:, s, :], in0=ya[:, yslot, :], scalar=gain_col,
                                                                in1=yq[:], op0=ALU.mult, op1=ALU.mult),
                     waits=[(dve, d)] + w, inc=dve)
        else:
            d = p.op("vector", lambda e: e.scalar_tensor_tensor(out=ya[:, yslot, :], in0=ya[:, yslot, :], scalar=gain_col,
                                                                in1=yq[:], op0=ALU.mult, op1=ALU.mult),
                     waits=[(dve, d)], inc=dve)
            d = p.op("vector", lambda e: e.tensor_tensor(out=stg[:, s, :], in0=ya[:, yslot, :], in1=mul_tile, op=ALU.mult),
                     waits=[(dve, d)] + w + list(mul_wait), inc=dve)
        sv = p.dma("sync", dst_ap, stg[:, s, :], waits=[(dve, d)], inc=stgs.sem[s])
        stgs.cond[s] = [(stgs.sem[s], sv)]
        y_stores.append((stgs.sem[s], sv))
        return [(dve, d)]

    pending = [None]

    def run_pending():
        if pending[0] is not None:
            ys_, rdy_, gain_, dst_, mt_, rs_ = pending[0]
            pending[0] = None
            yw_ = headnorm_store(ys_, rdy_, gain_, dst_, mul_tile=mt_)
            yas.cond[ys_] = yw_
            if rs_ is not None:
                rls.cond[rs_] = yw_

    vparts = []
    blk0 = 0
    for key in ("v_prev", "v_loc"):
        for part in _parts(io[key]):
            nb = part.shape[0] // 128
            vparts.append((blk0, nb, part.rearrange("(b p) c -> p b c", p=128)))
            blk0 += nb
    assert blk0 == 32
    for h in range(8):
        hsl, w = hs.acquire()
        rows = slice(h * 128, (h + 1) * 128)
        p.dma("sync", qh[:, hsl, :], io["qT"][rows, :], waits=w, inc=hs.sem[hsl])
        p.dma("sync", kh[:, hsl, 0:NTOK], io["kT_prev"][rows, :], inc=hs.sem[hsl])
        p.dma("sync", kh[:, hsl, NTOK:2 * NTOK], io["kT_loc"][rows, :], inc=hs.sem[hsl])
        for (b0_, nb_, vw_) in vparts:
            hl = p.dma("sync", vh[:, hsl, b0_:b0_ + nb_, :], vw_[:, :, rows], inc=hs.sem[hsl])
        hw = [(hs.sem[hsl], hl)]
        for qt in range(NQT):
            qs = slice(qt * QT, (qt + 1) * QT)
            cs_, w = cbs.acquire()
            for vi, src in ((0, ncp), (1, ncl)):
                p.op("tensor", lambda e, src=src, h=h, qs=qs: e.matmul(ps[:, 6, :], lhsT=selh[0:8, h, :], rhs=src[0:8, qs],
                                                                      start=True, stop=True),
                     waits=b6_cond + setup_done, inc=pe)
                a = p.op("scalar", lambda e, cs_=cs_, vi=vi: e.copy(out=cb[:, cs_, vi, :], in_=ps[:, 6, :]),
                         waits=[(pe, pe.v)] + w, inc=act)
                b6_cond = [(act, a)]
            cbw = [(act, a)]
            blocks = [(j, 0, None) for j in range(16)] + [(16 + j, 1, (j - 4 * qt) if j >= 4 * qt else None)
                                                          for j in range(4 * qt + 4)]
            an = acc_n[0]
            acc_n[0] += 1
            ob, db = 2 + an % 2, 4 + an % 2
            pend = None
            nblk = len(blocks)

            def emit_pv(pend, first, last):
                j, slot, pw_ = pend
                p.op("tensor", lambda e, j=j, slot=slot, ob=ob, hsl=hsl: e.matmul(ps[:, ob, :], lhsT=vh[:, hsl, j, :],
                                                                                 rhs=pt[:, slot, :], start=first, stop=last),
                     waits=pw_ + (acc_cond[an % 2] if first else []))
                p.op("tensor", lambda e, slot=slot, db=db: e.matmul(ps[:, db, :], lhsT=onesb[:], rhs=pt[:, slot, :],
                                                                   start=first, stop=last), inc=pe)
                pts.cond[slot] = [(pe, pe.v)]

            npv = 0
            for bi, (j, vi, dg) in enumerate(blocks):
                if bi == 6:
                    run_pending()
                sn = s_n[0]
                s_n[0] += 1
                sbk = sn % 2
                p.op("tensor", lambda e, j=j, sbk=sbk, qs=qs, hsl=hsl: e.matmul(ps[:, sbk, :], lhsT=kh[:, hsl, j * 128:(j + 1) * 128],
                                                                      rhs=qh[:, hsl, qs], start=True, stop=True),
                     waits=hw + s_cond[sbk], inc=pe)
                st_ = pe.v
                if pend is not None:
                    emit_pv(pend, npv == 0, False)
                    npv += 1
                ts, w = tmps.acquire()
                d = p.op("vector", lambda e, ts=ts, sbk=sbk, vi=vi, cs_=cs_: e.scalar_tensor_tensor(
                    out=tmp[:, ts, :], in0=ps[:, sbk, :], scalar=1.0, in1=cb[:, cs_, vi, :], op0=ALU.mult, op1=ALU.add),
                         waits=[(pe, st_)] + w + cbw, inc=dve)
                s_cond[sbk] = [(dve, d)]
                if dg is not None:
                    off = 384 - dg * 128
                    d = p.op("vector", lambda e, ts=ts, off=off: e.tensor_tensor(out=tmp[:, ts, :], in0=tmp[:, ts, :],
                                                                                 in1=cmask[:, off:off + QT], op=ALU.add),
                             waits=[(dve, d)] + setup_done, inc=dve)
                slot, w = pts.acquire()
                a = p.op("scalar", lambda e, ts=ts, slot=slot, j=j, h=h: e.activation(
                    out=pt[:, slot, :], in_=tmp[:, ts, :], func=AF.Exp, bias=biasT[:, j, h:h + 1], scale=1.0),
                         waits=[(dve, d)] + w + setup_done, inc=act)
                tmps.cond[ts] = [(act, a)]
                pend = (j, slot, [(act, a)])
            emit_pv(pend, npv == 0, True)
            acc_done = pe.v
            cbs.cond[cs_] = [(dve, dve.v)]
            ys, w = yas.acquire()
            d = p.op("vector", lambda e, db=db: e.reciprocal(out=yq[:], in_=ps[:, db, :]), waits=[(pe, acc_done), (dve, dve.v),
                                                                                          (act, act.v)], inc=dve)
            d = p.op("vector", lambda e, ys=ys, ob=ob: e.tensor_tensor(out=ya[:, ys, :], in0=ps[:, ob, :], in1=yq[:], op=ALU.mult),
                     waits=[(dve, d)] + w, inc=dve)
            acc_cond[an % 2] = [(dve, d)]
            run_pending()
            pending[0] = (ys, [(dve, d)], ona[:, h:h + 1], io["yT"][rows, qs], None, None)
        hs.cond[hsl] = [(pe, pe.v)]
    for hh in range(4):
        rows = slice(hh * 128, (hh + 1) * 128)
        for qt in range(NQT):
            qs = slice(qt * QT, (qt + 1) * QT)
            rs, w = rls.acquire()
            p.dma("sync", ro[:, rs, :], io["ret_o"][rows, qs], waits=w, inc=rls.sem[rs])
            p.dma("sync", rg[:, rs, :], io["rgs"][rows, qs], inc=rls.sem[rs])
            rl = p.dma("sync", qgt[:, rs, :], io["ret_qg"][rows, qs], inc=rls.sem[rs])
            p.op("tensor", lambda e, rs=rs, rows=rows: e.matmul(ps[:, 6, :], lhsT=sinb[:, rows], rhs=qgt[:, rs, :], start=True,
                                                               stop=True),
                 waits=[(rls.sem[rs], rl)] + b6_cond + setup_done, inc=pe)
            ys, w = yas.acquire()
            d = p.op("vector", lambda e, ys=ys, rs=rs: e.tensor_tensor(out=ya[:, ys, :], in0=ps[:, 6, :], in1=ro[:, rs, :],
                                                                      op=ALU.add), waits=[(pe, pe.v)] + w, inc=dve)
            b6_cond = [(dve, d)]
            run_pending()
            pending[0] = (ys, [(dve, d)], ona[:, 8 + hh:9 + hh],
                          io["yT"][1024 + hh * 128:1024 + (hh + 1) * 128, qs], rg[:, rs, :], rs)
    run_pending()
    hb = sb("hb", [128, KC, TT], BF16)
    wo = sb("wo", [128, 2, KC, 256], BF16)
    wos = Slots(p, "wos", 2)
    xs = sb("xsb", [128, 3, TT], F32)
    xss = Slots(p, "xss", 3)
    hbl = p.sem("hbl")
    wov = io["w_out"].rearrange("(kc p) f -> p kc f", p=128)
    ytv = io["yT"].rearrange("(kc p) t -> p kc t", p=128)
    ygv = io["ytg"].rearrange("(kc p) t -> p kc t", p=128)
    hb_free = []
    on = 0
    ob_cond = [[], []]
    for tt in range(NTT):
        t0 = tt * TT
        p.dma("sync", hb[:, 0:12, :], ytv[:, :, t0:t0 + TT], waits=list(y_stores) + hb_free, inc=hbl)
        hl = p.dma("sync", hb[:, 12:16, :], ygv[:, :, t0:t0 + TT], inc=hbl)
        for pd in range(8):
            col0 = pd * 256
            b, w = wos.acquire()
            wl = p.dma("gpsimd", wo[:, b, :, :], wov[:, :, col0:col0 + 256], waits=w, inc=wos.sem[b])
            for ii in range(2):
                i = pd * 2 + ii
                s, w = xss.acquire()
                full = p.dma("sync", xs[:, s, :], io["xin"][i * 128:(i + 1) * 128, t0:t0 + TT], waits=w, inc=xss.sem[s])
                for th in range(2):
                    obk = on % 2
                    on += 1
                    for kc in range(KC):
                        p.op("tensor", lambda e, b=b, kc=kc, ii=ii, th=th, obk=obk: e.matmul(
                            ps[:, obk, :], lhsT=wo[:, b, kc, ii * 128:(ii + 1) * 128], rhs=hb[:, kc, th * 512:(th + 1) * 512],
                            start=(kc == 0), stop=(kc == KC - 1)),
                             waits=([(wos.sem[b], wl), (hbl, hl)] + ob_cond[obk]) if kc == 0 else [],
                             inc=(pe if kc == KC - 1 else None))
                    r = p.op("vector", lambda e, s=s, th=th, obk=obk: e.tensor_tensor(
                        out=xs[:, s, th * 512:(th + 1) * 512], in0=ps[:, obk, :], in1=xs[:, s, th * 512:(th + 1) * 512],
                        op=ALU.add), waits=[(pe, pe.v), (xss.sem[s], full)], inc=dve)
                    ob_cond[obk] = [(dve, r)]
                sv = p.dma("sync", io["xout"][i * 128:(i + 1) * 128, t0:t0 + TT], xs[:, s, :], waits=[(dve, r)],
                           inc=xss.sem[s])
                xss.cond[s] = [(xss.sem[s], sv)]
            wos.cond[b] = [(pe, pe.v)]
        hb_free = [(pe, pe.v)]
    p.wait_only("sync", [(xss.sem[s], xss.sem[s].v) for s in range(3)])


def build_mixb():
    nc = bass.Bass("TRN2", target_bir_lowering=False)
    di = lambda name, shape, dt=F32: nc.dram_tensor(name, shape, dt, kind="ExternalInput").ap()
    io = {
        "qT": di("qT", [1024, NTOK], BF16), "kT_loc": di("kT_loc", [1024, NTOK], BF16),
        "kT_prev": di("kT_prev", [1024, NTOK], BF16), "v_loc": di("v_loc", [NTOK, 1024], BF16),
        "v_prev": di("v_prev", [NTOK, 1024], BF16), "cneg_loc": di("cneg_loc", [8, NTOK]),
        "cneg_prev": di("cneg_prev", [8, NTOK]), "s_init": di("s_init", [128, 512]),
        "ret_o": di("ret_o", [512, NTOK]), "ret_qg": di("ret_qg", [512, NTOK], BF16), "rgs": di("rgs", [512, NTOK]),
        "ytg": di("ytg", [512, NTOK], BF16), "cmask": di("cmask", [128, 896]), "selh": di("selh", [8, 1024]),
        "ident8": di("ident8", [8, 8]), "pmask": di("pmask", [128, 1]), "sflag": di("sflag", [128, 1]),
        "ona": di("ona", [128, 12]), "w_out": di("w_out", [D, D]), "xin": di("xin", [D, NTOK]),
        "yT": nc.dram_tensor("yT", [1536, NTOK], BF16, kind="Internal").ap(),
        "xout": nc.dram_tensor("xout", [D, NTOK], F32, kind="ExternalOutput").ap(),
    }
    with contextlib.ExitStack() as stack:
        p = Prog(nc, stack)
        mixb_body(nc, stack, p, io)
        p.emit()
    return nc


def mixb_consts(half):
    s = np.arange(128)[:, None]
    u = np.arange(896)[None, :]
    cmask = np.where((u - 384) >= s, 0.0, MASKNEG).astype(np.float32)
    selh = np.zeros((8, 8, 128), np.float32)
    for h in range(8):
        selh[h, h, :] = 1.0
    return {"cmask": cmask, "selh": selh.reshape(8, 1024), "ident8": np.eye(8, dtype=np.float32),
            "pmask": np.full((128, 1), 0.0 if half == 1 else MASKNEG, np.float32),
            "sflag": np.full((128, 1), 1.0 if half == 1 else 0.0, np.float32)}


def build_norm():
    nc = bass.Bass("TRN2", target_bir_lowering=False)
    xin = nc.dram_tensor("xin", [D, NTOK], F32, kind="ExternalInput").ap()
    g = nc.dram_tensor("g", [128, KC], F32, kind="ExternalInput").ap()
    xout = nc.dram_tensor("xout", [D, NTOK], F32, kind="ExternalOutput").ap()
    with contextlib.ExitStack() as stack:
        p = Prog(nc, stack)
        c = alloc_common(nc, stack, p)
        gcol = c.sb("gcol", [128, KC], F32)
        p.dma("sync", gcol[:], g[:, :], inc=c.setup_d)
        for tt in range(NTT):
            t0 = tt * TT

            def out_fn(kc, s, waits, t0=t0):
                d = p.op("vector", lambda e: e.scalar_tensor_tensor(out=c.xs[:, s, :], in0=c.xs[:, s, :],
                                                                    scalar=gcol[:, kc:kc + 1], in1=c.rstd[:],
                                                                    op0=ALU.mult, op1=ALU.mult), waits=waits, inc=c.dve_h)
                sv = p.dma("sync", xout[kc * 128:(kc + 1) * 128, t0:t0 + TT], c.xs[:, s, :], waits=[(c.dve_h, d)],
                           inc=c.xs_st[s])
                c.xs_cond[s] = [(c.xs_st[s], sv)]

            norm_stats_and_h(c, xin, gcol, tt, out_fn=out_fn)
        finish(c)
        p.emit()
    return nc


_PROGS = {}


def _prog(name):
    if name not in _PROGS:
        _PROGS[name] = {"ffn": build_ffn, "mixa": build_mixa, "mixb": build_mixb, "norm": build_norm}[name]()
    return _PROGS[name]


def _run(name, in_maps):
    res = run_bass_kernel_spmd(_prog(name), in_maps, core_ids=list(range(NCORES)))
    return res.results


def run_ffn(xTs, l, P, pre):
    g = col16(P[pre + "_norm"][l])
    maps = [{"xin": xTs[c], "g": g, "wg": P[pre + "_w_gate"][l], "wu": P[pre + "_w_up"][l], "wd": P[pre + "_w_down"][l]}
            for c in range(NCORES)]
    return [r["xout"] for r in _run("ffn", maps)]


def run_mixer(xTs, l, P):
    ra = _run("mixa", [mixa_inputs(xTs[c], c % 2, l, P) for c in range(NCORES)])
    ona = np.ascontiguousarray(P["out_norm"][l][0:1536].reshape(12, 128).T).astype(np.float32)
    maps = []
    for c in range(NCORES):
        half = c % 2
        pc = c - 1 if half == 1 else c
        m = {"qT": ra[c]["qT"], "kT_loc": ra[c]["kT"], "kT_prev": ra[pc]["kT"], "v_loc": ra[c]["v"], "v_prev": ra[pc]["v"],
             "cneg_loc": ra[c]["cneg"], "cneg_prev": ra[pc]["cneg"], "s_init": ra[pc]["ret_S"], "ret_o": ra[c]["ret_o"],
             "ret_qg": ra[c]["ret_qg"], "rgs": ra[c]["rgs"], "ytg": ra[c]["ytg"], "ona": ona, "w_out": P["w_out"][l],
             "xin": xTs[c]}
        m.update(mixb_consts(half))
        maps.append(m)
    return [r["xout"] for r in _run("mixb", maps)]


def kernel_unfused(**inputs):
    P = {k: np.asarray(v) for k, v in inputs.items()}
    x = P["x"]
    xTs = [np.ascontiguousarray(x[c // 2, (c % 2) * NTOK:(c % 2 + 1) * NTOK, :].T) for c in range(NCORES)]
    for l in range(DEPTH):
        xTs = run_ffn(xTs, l, P, "ffn1")
        xTs = run_mixer(xTs, l, P)
        xTs = run_ffn(xTs, l, P, "ffn2")
    g = col16(P["final_norm"])
    outs = [r["xout"] for r in _run("norm", [{"xin": xTs[c], "g": g} for c in range(NCORES)])]
    out = np.empty_like(x)
    for c in range(NCORES):
        out[c // 2, (c % 2) * NTOK:(c % 2 + 1) * NTOK, :] = outs[c].T
    return out


PAIRS = [[0, 1], [2, 3], [4, 5], [6, 7]]
WSHAPES = {"ffn1_w_gate": [DEPTH, D, DFF], "ffn1_w_up": [DEPTH, D, DFF], "ffn1_w_down": [DEPTH, DFF, D],
           "w_in": [DEPTH, D, INCOLS], "w_out": [DEPTH, D, D],
           "ffn2_w_gate": [DEPTH, D, DFF], "ffn2_w_up": [DEPTH, D, DFF], "ffn2_w_down": [DEPTH, DFF, D]}
SMALL = {"g_ffn1": [DEPTH, 128, KC], "g_mix": [DEPTH, 128, KC], "g_ffn2": [DEPTH, 128, KC], "g_fin": [128, KC],
         "bf": [DEPTH, 8, 1], "wst": [DEPTH, 128, 512], "lnb3": [DEPTH, 128, 1536], "ong": [DEPTH, 128, 4],
         "ona": [DEPTH, 128, 12], "tab": [NTOK, 268], "maskr": [128, 512], "triu": [128, 128], "ident": [128, 128],
         "cmask": [128, 896], "selh": [8, 1024], "ident8": [8, 8], "pmask": [128, 1], "sflag": [128, 1]}


def build_fused(depth=DEPTH, phases="fmxbF"):
    nc = bass.Bass("TRN2", target_bir_lowering=False)
    di = lambda name, shape: nc.dram_tensor(name, shape, F32, kind="ExternalInput").ap()
    it = lambda name, shape, dt: nc.dram_tensor(name, shape, dt, kind="Internal").ap()
    x_in = di("x", [D, NTOK])
    W = {k: di(k, [depth] + s[1:]) for k, s in WSHAPES.items()}
    S = {k: di(k, ([depth] + s[1:]) if len(s) == 3 else s) for k, s in SMALL.items()}
    out = nc.dram_tensor("out", [D, NTOK], F32, kind="ExternalOutput").ap()
    xres = it("xres", [D, NTOK], F32)
    qT = it("qT", [1024, NTOK], BF16)
    xk = [it("xk%d" % i, [512, NTOK], BF16) for i in range(2)]
    xv = [it("xv%d" % i, [1024, 1024], BF16) for i in range(2)]
    xc = it("xc", [8, NTOK], F32)
    xs_ = it("xs_", [128, 512], F32)
    gk = [it("gk%d" % i, [1024, NTOK], BF16) for i in range(2)]
    gv_ = [it("gv%d" % i, [2048, 1024], BF16) for i in range(2)]
    gc = it("gc", [16, NTOK], F32)
    gs = it("gs", [256, 512], F32)
    ret_o = it("ret_o", [512, NTOK], F32)
    ret_qg = it("ret_qg", [512, NTOK], BF16)
    rgs = it("rgs", [512, NTOK], F32)
    ytg = it("ytg", [512, NTOK], BF16)
    yT = it("yT", [1536, NTOK], BF16)
    with contextlib.ExitStack() as gstack:
        p = Prog(nc, gstack)

        def ffn_phase(xin, xout, g_ap, wg, wu, wd):
            with contextlib.ExitStack() as st:
                c = alloc_common(nc, st, p)
                alloc_ffn(c)
                gcol = c.sb("gcol", [128, KC], F32)
                p.dma("sync", gcol[:], g_ap, inc=c.setup_d)
                ffn_body(c, xin, xout, gcol, wg, wu, wd)
                finish(c)
                p.barrier()
                p.emit()

        def mixa_phase(l):
            with contextlib.ExitStack() as st:
                c = alloc_common(nc, st, p, tt=TA, nps=6, stat_bank=5)
                din = {"g": S["g_mix"][l], "bf": S["bf"][l], "maskr": S["maskr"][:, :], "wst": S["wst"][l],
                       "triu": S["triu"][:, :], "lnb3": S["lnb3"][l], "ong": S["ong"][l], "ident": S["ident"][:, :]}
                io = {"tab": S["tab"], "qT": qT, "kT": RowSplit(xk), "v": RowSplit(xv), "cneg_o": xc,
                      "ret_o": ret_o, "ret_qg": ret_qg, "ret_S": xs_, "rgs": rgs, "ytg": ytg}
                gcol = mixa_setup(c, din, io)
                mixa_body(c, xres, gcol, W["w_in"][l], io)
                p.barrier()
                p.emit()

        def exchange_phase():
            cc = p.sem("ccsem")
            for a_, b_ in ((xk[0], gk[0]), (xk[1], gk[1]), (xv[0], gv_[0]), (xv[1], gv_[1]), (xc, gc), (xs_, gs)):
                p.op("gpsimd", lambda e, a_=a_, b_=b_: e.collective_compute("AllGather", ALU.bypass, replica_groups=PAIRS,
                                                                            ins=[a_], outs=[b_]), inc=cc, k=1)
            p.barrier()
            p.emit()

        def mixb_phase(l):
            with contextlib.ExitStack() as st:
                io = {"qT": qT, "kT_loc": RowSplit(xk), "kT_prev": RowSplit([gk[0][0:512, :], gk[1][0:512, :]]),
                      "v_loc": RowSplit(xv), "v_prev": RowSplit([gv_[0][0:1024, :], gv_[1][0:1024, :]]),
                      "cneg_loc": xc, "cneg_prev": gc[0:8, :], "s_init": gs[0:128, :], "ret_o": ret_o,
                      "ret_qg": ret_qg, "rgs": rgs, "ytg": ytg, "cmask": S["cmask"], "selh": S["selh"],
                      "ident8": S["ident8"], "pmask": S["pmask"], "sflag": S["sflag"], "ona": S["ona"][l],
                      "w_out": W["w_out"][l], "xin": xres, "yT": yT, "xout": xres}
                mixb_body(nc, st, p, io)
                p.barrier()
                p.emit()

        def norm_phase():
            with contextlib.ExitStack() as st:
                c = alloc_common(nc, st, p)
                gcol = c.sb("gcol", [128, KC], F32)
                p.dma("sync", gcol[:], S["g_fin"][:, :], inc=c.setup_d)
                for tt in range(NTT):
                    t0 = tt * TT

                    def out_fn(kc, s, waits, t0=t0):
                        d = p.op("vector", lambda e: e.scalar_tensor_tensor(out=c.xs[:, s, :], in0=c.xs[:, s, :],
                                                                            scalar=gcol[:, kc:kc + 1], in1=c.rstd[:],
                                                                            op0=ALU.mult, op1=ALU.mult), waits=waits,
                                 inc=c.dve_h)
                        sv = p.dma("sync", out[kc * 128:(kc + 1) * 128, t0:t0 + TT], c.xs[:, s, :], waits=[(c.dve_h, d)],
                                   inc=c.xs_st[s])
                        c.xs_cond[s] = [(c.xs_st[s], sv)]

                    norm_stats_and_h(c, xres, gcol, tt, out_fn=out_fn)
                finish(c)
                p.barrier()
                p.emit()

        for l in range(depth):
            if "f" in phases:
                ffn_phase(x_in if l == 0 else xres, xres, S["g_ffn1"][l], W["ffn1_w_gate"][l], W["ffn1_w_up"][l],
                          W["ffn1_w_down"][l])
            if "m" in phases:
                mixa_phase(l)
            if "x" in phases:
                exchange_phase()
            if "b" in phases:
                mixb_phase(l)
            if "F" in phases:
                ffn_phase(xres, xres, S["g_ffn2"][l], W["ffn2_w_gate"][l], W["ffn2_w_up"][l], W["ffn2_w_down"][l])
        norm_phase()
    return nc


def fused_inputs(P, core):
    half = core % 2
    x = P["x"]
    m = {"x": np.ascontiguousarray(x[core // 2, half * NTOK:(half + 1) * NTOK, :].T)}
    for k in WSHAPES:
        m[k] = P[k]
    return m


def fused_shared(P):
    sh = {}
    sh["g_ffn1"] = np.stack([col16(P["ffn1_norm"][l]) for l in range(DEPTH)])
    sh["g_mix"] = np.stack([col16(P["mix_norm"][l]) for l in range(DEPTH)])
    sh["g_ffn2"] = np.stack([col16(P["ffn2_norm"][l]) for l in range(DEPTH)])
    sh["g_fin"] = col16(P["final_norm"])
    sh["bf"] = np.ascontiguousarray(P["fox_b_f"].reshape(DEPTH, 8, 1)).astype(np.float32)
    sh["wst"] = np.ascontiguousarray(np.transpose(P["gmlp_w_s"], (0, 3, 1, 2)).reshape(DEPTH, 128, 512)).astype(np.float32)
    sh["lnb3"] = np.ascontiguousarray(np.stack([np.concatenate(
        [np.broadcast_to(P["gmlp_ln_g"][l][None, :], (128, 512)), np.broadcast_to(P["gmlp_ln_b"][l][None, :], (128, 512)),
         np.broadcast_to(P["gmlp_b_s"][l].reshape(1, 512), (128, 512))], axis=1) for l in range(DEPTH)])).astype(np.float32)
    sh["ong"] = np.ascontiguousarray(np.stack([P["out_norm"][l][1536:2048].reshape(4, 128).T for l in range(DEPTH)])).astype(np.float32)
    sh["ona"] = np.ascontiguousarray(np.stack([P["out_norm"][l][0:1536].reshape(12, 128).T for l in range(DEPTH)])).astype(np.float32)
    return sh


_FUSED = {}


def kernel(**inputs):
    P = {k: np.asarray(v) for k, v in inputs.items()}
    if "nc" not in _FUSED:
        _FUSED["nc"] = build_fused()
    sh = fused_shared(P)
    maps = []
    for c in range(NCORES):
        half = c % 2
        m = fused_inputs(P, c)
        m.update(sh)
        hc = host_consts(half)
        m.update({"tab": hc["tab"], "maskr": hc["maskr"], "triu": hc["triu"], "ident": hc["ident"]})
        m.update(mixb_consts(half))
        maps.append(m)
    res = run_bass_kernel_spmd(_FUSED["nc"], maps, core_ids=list(range(NCORES)))
    x = P["x"]
    outp = np.empty_like(x)
    for c in range(NCORES):
        outp[c // 2, (c % 2) * NTOK:(c % 2 + 1) * NTOK, :] = res.results[c]["out"].T
    return outp
```

```python
import contextlib
import numpy as np
import concourse.bass as bass
import concourse.mybir as mybir
from concourse.bass_utils import run_bass_kernel_spmd

F32 = mybir.dt.float32
BF16 = mybir.dt.bfloat16
AF = mybir.ActivationFunctionType
ALU = mybir.AluOpType

D = 2048
NTOK = 2048
DFF = 5632
NCORES = 8
DEPTH = 4
EPS = 1e-6
KC = D // 128
TT = 1024
NTT = NTOK // TT
FH = 22
INCOLS = 6152


class Cnt:
    def __init__(self, h):
        self.h = h
        self.v = 0


class Prog:
    ENGS = ("sync", "scalar", "vector", "gpsimd", "tensor")

    def __init__(self, nc, stack):
        self.nc = nc
        self.stack = stack
        self.q = {e: [] for e in self.ENGS}
        self.waited = {e: {} for e in self.ENGS}
        self.cache = {}

    def sem(self, name):
        if name not in self.cache:
            self.cache[name] = Cnt(self.stack.enter_context(self.nc.semaphore(name)))
        return self.cache[name]

    def barrier(self):
        for eng in self.ENGS:
            self.op(eng, None, waits=[(c, c.v) for c in self.cache.values()])

    def sems(self, name, n):
        return [self.sem("%s%d" % (name, i)) for i in range(n)]

    def op(self, eng, fn, waits=(), inc=None, k=1):
        ws = []
        for (c, v) in waits:
            if v <= 0:
                continue
            key = id(c)
            if self.waited[eng].get(key, 0) >= v:
                continue
            self.waited[eng][key] = v
            ws.append((c.h, v))
        tgt = None
        if inc is not None:
            inc.v += k
            tgt = inc.v
        self.q[eng].append((ws, fn, inc.h if inc is not None else None, k))
        return tgt

    def dma(self, eng, out, in_, waits=(), inc=None):
        return self.op(eng, lambda e: e.dma_start(out=out, in_=in_), waits, inc, 16)

    def wait_only(self, eng, waits):
        self.q[eng].append(([(c.h, v) for (c, v) in waits if v > 0], None, None, 0))

    def emit(self):
        with self.nc.Block() as block:
            for name in self.ENGS:
                q = self.q[name]

                def body(e, q=q):
                    for ws, fn, inc, k in q:
                        for (h, v) in ws:
                            e.wait_ge(h, v)
                        if fn is None:
                            continue
                        ins = fn(e)
                        if inc is not None:
                            ins.then_inc(inc, k)

                getattr(block, name)(body)
        self.q = {e: [] for e in self.ENGS}


class Ctx:
    pass


_UID = [0]


def _uid():
    _UID[0] += 1
    return _UID[0]


def alloc_common(nc, stack, p, tt=TT, nps=8, stat_bank=6):
    c = Ctx()
    uid = _uid()
    c.nc = nc
    c.p = p
    c.TT = tt
    c.NSEG = tt // 512
    c.stat_bank = stat_bank
    sb = lambda name, shape, dt: stack.enter_context(nc.sbuf_tensor("sb%d_%s" % (uid, name), shape, dt))
    c.sb = sb
    c.uid = uid
    c.stack = stack
    c.ones = sb("ones", [128, 128], F32)
    c.xs = sb("xs", [128, 3, tt], F32)
    c.sq = sb("sq", [128, 2, tt], F32)
    c.rstd = sb("rstd", [128, tt], F32)
    c.h = sb("h", [128, KC, tt], BF16)
    c.ps = stack.enter_context(nc.psum_tensor("ps%d" % uid, [128, nps, 512], F32))
    c.xs_full = p.sems("xsfull", 3)
    c.xs_st = p.sems("xsst", 3)
    c.xs_cond = [[], [], []]
    c.xs_n = 0
    c.act_sq = p.sem("actsq")
    c.pe_st = p.sem("pest")
    c.dve_m = p.sem("dvem")
    c.act_m = p.sem("actm")
    c.dve_h = p.sem("dveh")
    c.setup_v = p.sem("setupv")
    c.setup_d = p.sem("setupd")
    p.op("vector", lambda e: e.memset(c.ones[:], 1.0), inc=c.setup_v)
    c.sq_n = 0
    c.h_free = []
    c.ps_free_waits = []
    return c


def xs_acquire(c):
    s = c.xs_n % 3
    c.xs_n += 1
    return s, list(c.xs_cond[s])


def norm_stats_and_h(c, xsrc, gcol, tt, out_fn=None):
    p = c.p
    TT = c.TT
    t0 = tt * TT
    SB = c.stat_bank
    for kc in range(KC):
        s, w = xs_acquire(c)
        full = p.dma("sync", c.xs[:, s, :], xsrc[kc * 128:(kc + 1) * 128, t0:t0 + TT], waits=w, inc=c.xs_full[s])
        q = c.sq_n % 2
        c.sq_n += 1
        a = p.op("scalar",
                 lambda e, s=s, q=q: e.activation(out=c.sq[:, q, :], in_=c.xs[:, s, :], func=AF.Square),
                 waits=[(c.xs_full[s], full), (c.pe_st, c.pe_st.v - 1)], inc=c.act_sq)
        c.xs_cond[s] = [(c.act_sq, a)]
        extra = list(c.ps_free_waits) if kc == 0 else []
        for sg_ in range(c.NSEG):
            p.op("tensor",
                 lambda e, q=q, kc=kc, sg_=sg_: e.matmul(c.ps[:, SB + sg_, :], lhsT=c.ones[:],
                                                        rhs=c.sq[:, q, sg_ * 512:(sg_ + 1) * 512],
                                                        start=(kc == 0), stop=(kc == KC - 1)),
                 waits=([(c.act_sq, a), (c.setup_v, c.setup_v.v)] + extra) if sg_ == 0 else [],
                 inc=(c.pe_st if sg_ == c.NSEG - 1 else None))
    st_done = c.pe_st.v
    psv = c.ps[:, SB:SB + c.NSEG, :]
    rv = c.rstd[:].rearrange("p (a b) -> p a b", a=c.NSEG)
    d1 = p.op("vector",
              lambda e: e.tensor_scalar(out=rv, in0=psv, scalar1=1.0 / D, scalar2=EPS, op0=ALU.mult, op1=ALU.add),
              waits=[(c.pe_st, st_done), (c.dve_h, c.dve_h.v)], inc=c.dve_m)
    c.ps_free_waits = [(c.dve_m, d1)]
    a1 = p.op("scalar", lambda e: e.activation(out=c.rstd[:], in_=c.rstd[:], func=AF.Sqrt),
              waits=[(c.dve_m, d1)], inc=c.act_m)
    d2 = p.op("vector", lambda e: e.reciprocal(out=c.rstd[:], in_=c.rstd[:]),
              waits=[(c.act_m, a1)], inc=c.dve_m)
    for kc in range(KC):
        s, w = xs_acquire(c)
        full = p.dma("sync", c.xs[:, s, :], xsrc[kc * 128:(kc + 1) * 128, t0:t0 + TT], waits=w, inc=c.xs_full[s])
        if out_fn is None:
            waits = [(c.xs_full[s], full), (c.dve_m, d2), (c.setup_d, c.setup_d.v)]
            if kc == 0:
                waits += c.h_free
            hv = p.op("vector",
                 lambda e, s=s, kc=kc: e.scalar_tensor_tensor(out=c.h[:, kc, :], in0=c.xs[:, s, :],
                                                              scalar=gcol[:, kc:kc + 1], in1=c.rstd[:],
                                                              op0=ALU.mult, op1=ALU.mult),
                 waits=waits, inc=c.dve_h)
            c.xs_cond[s] = [(c.dve_h, hv)]
        else:
            out_fn(kc, s, [(c.xs_full[s], full), (c.dve_m, d2), (c.setup_d, c.setup_d.v)])
    return c.dve_h.v


def alloc_ffn(c):
    p = c.p
    sb = c.sb
    c.hid = sb("hid", [128, FH, TT], BF16)
    c.sg = sb("sg", [128, 2, 512], F32)
    c.wgu = sb("wgu", [128, 2, 2, KC, 256], BF16)
    c.wd = sb("wd", [128, 2, FH, 256], BF16)
    c.wgu_full = p.sems("wgufull", 2)
    c.wd_full = p.sems("wdfull", 2)
    c.pe_gu = p.sem("pegu")
    c.act_sg = p.sem("actsg")
    c.dve_hid = p.sem("dvehid")
    c.pe_dn = p.sem("pedn")
    c.dve_res = p.sem("dveres")
    c.n_panel = 0
    c.n_gu = c.pe_gu.v
    c.n_dpanel = 0
    c.n_dn = c.pe_dn.v
    c.panel_done = {}
    c.dpanel_done = {}
    c.hid_free = []


def ffn_body(c, xin, xout, gcol, wg, wu, wd):
    p = c.p
    wgv = wg.rearrange("(kc p) f -> p kc f", p=128)
    wuv = wu.rearrange("(kc p) f -> p kc f", p=128)
    wdv = wd.rearrange("(fc p) d -> p fc d", p=128)
    for tt in range(NTT):
        t0 = tt * TT
        h_ready = norm_stats_and_h(c, xin, gcol, tt)
        for hf in range(2):
            for pn in range(FH // 2):
                col0 = (hf * FH + pn * 2) * 128
                npn = c.n_panel
                b = npn % 2
                c.n_panel += 1
                wfree = [(c.pe_gu, c.panel_done[npn - 2])] if npn >= 2 else []
                p.dma("gpsimd", c.wgu[:, b, 0, :, :], wgv[:, :, col0:col0 + 256], waits=wfree, inc=c.wgu_full[b])
                wl = p.dma("gpsimd", c.wgu[:, b, 1, :, :], wuv[:, :, col0:col0 + 256], waits=wfree, inc=c.wgu_full[b])
                for jj in range(2):
                    j = pn * 2 + jj
                    for th in range(2):
                        n = c.n_gu
                        c.n_gu += 1
                        gb = n % 2
                        ub = 2 + n % 2
                        for kc in range(KC):
                            waits = []
                            if kc == 0:
                                waits = [(c.wgu_full[b], wl), (c.dve_h, h_ready), (c.act_sg, n - 1)]
                            p.op("tensor",
                                 lambda e, b=b, kc=kc, jj=jj, th=th, gb=gb: e.matmul(
                                     c.ps[:, gb, :], lhsT=c.wgu[:, b, 0, kc, jj * 128:(jj + 1) * 128],
                                     rhs=c.h[:, kc, th * 512:(th + 1) * 512], start=(kc == 0), stop=(kc == KC - 1)),
                                 waits=waits)
                        for kc in range(KC):
                            waits = []
                            if kc == 0:
                                waits = [(c.dve_hid, n - 1)]
                            last = (kc == KC - 1)
                            p.op("tensor",
                                 lambda e, b=b, kc=kc, jj=jj, th=th, ub=ub: e.matmul(
                                     c.ps[:, ub, :], lhsT=c.wgu[:, b, 1, kc, jj * 128:(jj + 1) * 128],
                                     rhs=c.h[:, kc, th * 512:(th + 1) * 512], start=(kc == 0), stop=(kc == KC - 1)),
                                 waits=waits, inc=(c.pe_gu if last else None))
                        gu = c.pe_gu.v
                        a = p.op("scalar",
                                 lambda e, n=n, gb=gb: e.activation(out=c.sg[:, n % 2, :], in_=c.ps[:, gb, :], func=AF.Silu),
                                 waits=[(c.pe_gu, gu), (c.dve_hid, n - 1)], inc=c.act_sg)
                        waits = [(c.act_sg, a), (c.pe_gu, gu)]
                        if j == 0 and th == 0:
                            waits += c.hid_free
                        p.op("vector",
                             lambda e, n=n, ub=ub, j=j, th=th: e.tensor_tensor(
                                 out=c.hid[:, j, th * 512:(th + 1) * 512], in0=c.sg[:, n % 2, :], in1=c.ps[:, ub, :],
                                 op=ALU.mult),
                             waits=waits, inc=c.dve_hid)
                c.panel_done[npn] = c.pe_gu.v
            if hf == 1:
                c.h_free = [(c.pe_gu, c.pe_gu.v)]
            hid_ready = c.dve_hid.v
            xsrc = xin if hf == 0 else xout
            for pd in range(8):
                col0 = pd * 256
                npd = c.n_dpanel
                b = npd % 2
                c.n_dpanel += 1
                wl = p.dma("gpsimd", c.wd[:, b, :, :], wdv[:, hf * FH:(hf + 1) * FH, col0:col0 + 256],
                           waits=([(c.pe_dn, c.dpanel_done[npd - 2])] if npd >= 2 else []), inc=c.wd_full[b])
                for ii in range(2):
                    i = pd * 2 + ii
                    s, w = xs_acquire(c)
                    full = p.dma("sync", c.xs[:, s, :], xsrc[i * 128:(i + 1) * 128, t0:t0 + TT], waits=w,
                                 inc=c.xs_full[s])
                    for th in range(2):
                        n = c.n_dn
                        c.n_dn += 1
                        ob = 4 + n % 2
                        for f in range(FH):
                            waits = []
                            if f == 0:
                                waits = [(c.wd_full[b], wl), (c.dve_hid, hid_ready), (c.dve_res, n - 1)]
                            last = (f == FH - 1)
                            p.op("tensor",
                                 lambda e, b=b, f=f, ii=ii, th=th, ob=ob: e.matmul(
                                     c.ps[:, ob, :], lhsT=c.wd[:, b, f, ii * 128:(ii + 1) * 128],
                                     rhs=c.hid[:, f, th * 512:(th + 1) * 512], start=(f == 0), stop=(f == FH - 1)),
                                 waits=waits, inc=(c.pe_dn if last else None))
                        dn = c.pe_dn.v
                        r = p.op("vector",
                                 lambda e, s=s, th=th, ob=ob: e.scalar_tensor_tensor(
                                     out=c.xs[:, s, th * 512:(th + 1) * 512], in0=c.ps[:, ob, :], scalar=0.5,
                                     in1=c.xs[:, s, th * 512:(th + 1) * 512], op0=ALU.mult, op1=ALU.add),
                                 waits=[(c.pe_dn, dn), (c.xs_full[s], full)], inc=c.dve_res)
                    sv = p.dma("sync", xout[i * 128:(i + 1) * 128, t0:t0 + TT], c.xs[:, s, :],
                               waits=[(c.dve_res, r)], inc=c.xs_st[s])
                    c.xs_cond[s] = [(c.xs_st[s], sv)]
                c.dpanel_done[npd] = c.pe_dn.v
            c.hid_free = [(c.pe_dn, c.pe_dn.v)]


def finish(c):
    p = c.p
    p.wait_only("sync", [(c.xs_st[s], c.xs_st[s].v) for s in range(3)])


def build_ffn():
    nc = bass.Bass("TRN2", target_bir_lowering=False)
    xin = nc.dram_tensor("xin", [D, NTOK], F32, kind="ExternalInput").ap()
    g = nc.dram_tensor("g", [128, KC], F32, kind="ExternalInput").ap()
    wg = nc.dram_tensor("wg", [D, DFF], F32, kind="ExternalInput").ap()
    wu = nc.dram_tensor("wu", [D, DFF], F32, kind="ExternalInput").ap()
    wd = nc.dram_tensor("wd", [DFF, D], F32, kind="ExternalInput").ap()
    xout = nc.dram_tensor("xout", [D, NTOK], F32, kind="ExternalOutput").ap()
    with contextlib.ExitStack() as stack:
        p = Prog(nc, stack)
        c = alloc_common(nc, stack, p)
        alloc_ffn(c)
        gcol = c.sb("gcol", [128, KC], F32)
        p.dma("sync", gcol[:], g[:, :], inc=c.setup_d)
        ffn_body(c, xin, xout, gcol, wg, wu, wd)
        finish(c)
        p.emit()
    return nc


FOX_SCALE = 128.0 ** -0.5
GAM = [1.0 - 2.0 ** -(5 + h) for h in range(4)]
GAM64 = [g ** 64 for g in GAM]
TA = 512
NBLK = TA // 128
AX = mybir.AxisListType


class RowSplit:
    def __init__(self, parts):
        self.parts = parts
        self.h = parts[0].shape[0]

    def __getitem__(self, key):
        rs, cs = key
        i = rs.start // self.h
        assert (rs.stop - 1) // self.h == i
        return self.parts[i][rs.start - i * self.h:rs.stop - i * self.h, cs]


def _parts(x):
    return x.parts if isinstance(x, RowSplit) else [x]


class Slots:
    def __init__(self, p, name, n):
        self.sem = p.sems(name, n)
        self.cond = [[] for _ in range(n)]
        self.i = 0
        self.n = n

    def acquire(self):
        s = self.i % self.n
        self.i += 1
        return s, list(self.cond[s])


def mixa_body(c, xin, gcol, w_in, io):
    p = c.p
    nc = c.nc
    sb = c.sb
    winv = w_in.rearrange("(kc p) f -> p kc f", p=128)
    tabv = io["tab"].rearrange("(b p) f -> p b f", p=128)
    wp = sb("wp", [128, 2, 8192], BF16)
    wps = Slots(p, "wps", 2)
    wpfm = lambda b: wp[:, b, 0:4096].rearrange("p (k c) -> p k c", c=256)
    wptm = lambda b: wp[:, b, :].rearrange("p (k c) -> p k c", c=512)
    wpfz = lambda b: wp[:, b, 0:128].rearrange("p (k c) -> p k c", c=8)
    tabt = sb("tabt", [128, 2, NBLK, 268], F32)
    tabs = Slots(p, "tabs", 2)
    stg16 = sb("stg16", [128, 4, 512], BF16)
    st16 = Slots(p, "st16", 4)
    stg32 = sb("stg32", [128, 2, 512], F32)
    st32 = Slots(p, "st32", 2)
    u = sb("u", [128, 4, TA], F32)
    rv = sb("rv", [128, NBLK, 512], BF16)
    kr = sb("kr", [128, NBLK, 512], BF16)
    kz = sb("kz", [128, NBLK, 512], BF16)
    qx = sb("qx", [128, NBLK, 512], BF16)
    qg = sb("qg", [128, NBLK, 512], BF16)
    vln = sb("vln", [128, NBLK, 512], BF16)
    rot = sb("rot", [128, 2, 2, 512], F32)
    rots = Slots(p, "rots", 2)
    st = sb("lnst", [128, 24], F32)
    fzt = sb("fzt", [8, 512], F32)
    onesr = sb("onesr", [8, 512], F32)
    cneg = sb("cneg", [8, NTOK], F32)
    S32 = sb("S32", [128, 512], F32)
    Sb = sb("Sb", [128, 2 * NBLK, 512], BF16)
    krT = sb("krT", [128, 512], BF16)
    qxT = sb("qxT", [128, 512], BF16)
    sm = sb("sm", [128, 512], BF16)
    y1 = sb("y1", [128, 512], F32)
    y2 = sb("y2", [128, 512], F32)
    pst = c.stack.enter_context(nc.psum_tensor("pst%d" % c.uid, [128, 2, 1024], BF16))
    c.pe = p.sem("pe")
    c.act = p.sem("act")
    c.dve = p.sem("dve")
    misc = p.sem("miscst")
    pj_cond = [[], []]
    kv_cond = [[], []]
    b4_cond = []
    pst_cond = [[], []]
    pj_n = [0]
    kv_n = [0]
    u_free = []
    ret_free = []
    vln_free = []
    p.op("vector", lambda e: e.memset(onesr[:], 1.0), inc=c.setup_v)
    p.op("vector", lambda e: e.memset(S32[:], 0.0), inc=c.setup_v)
    setupw = [(c.setup_d, c.setup_d.v), (c.setup_v, c.setup_v.v)]

    def V4(ap):
        return ap.rearrange("p (a b) -> p a b", a=4)

    def bc(ap4):
        return ap4.unsqueeze(2).to_broadcast([128, 4, 128])

    def bh(ap128):
        return ap128.unsqueeze(1).to_broadcast([128, 4, 128])

    def bh64(ap64):
        return ap64.unsqueeze(1).to_broadcast([128, 4, 64])

    def proj(b, waits, lhs_fn, rhs_fn, out_fn):
        n = pj_n[0]
        pj_n[0] += 1
        bank = n % 2
        for kc in range(KC):
            p.op("tensor",
                 lambda e, kc=kc: e.matmul(out_fn(bank), lhsT=lhs_fn(kc), rhs=rhs_fn(kc), start=(kc == 0),
                                           stop=(kc == KC - 1)),
                 waits=(list(waits) + pj_cond[bank]) if kc == 0 else [], inc=(c.pe if kc == KC - 1 else None))
        return bank, c.pe.v

    def store16(src_fn, dst_ap, waits, eng_op):
        s, w = st16.acquire()
        a = p.op("scalar", lambda e: eng_op(e, stg16[:, s, :]), waits=list(waits) + w, inc=c.act)
        sv = p.dma("sync", dst_ap, src_fn(stg16[:, s, :]), waits=[(c.act, a)], inc=st16.sem[s])
        st16.cond[s] = [(st16.sem[s], sv)]
        return a

    for tt in range(NTOK // TA):
        t0 = tt * TA
        h_ready = norm_stats_and_h(c, xin, gcol, tt)
        hw = [(c.dve_h, h_ready)]
        ts_, w = tabs.acquire()
        tk = p.dma("sync", tabt[:, ts_], tabv[:, tt * NBLK:(tt + 1) * NBLK, :], waits=w, inc=tabs.sem[ts_])
        tabw = [(tabs.sem[ts_], tk)]
        for name, cbase, ncol in (("fq", 0, 1024), ("fk", 1024, 1024), ("rg", 4616, 512), ("gu", 5128, 512)):
            for pn in range(ncol // 256):
                col0 = cbase + pn * 256
                b, w = wps.acquire()
                t = p.dma("gpsimd", wpfm(b), winv[:, :, col0:col0 + 256], waits=w, inc=wps.sem[b])
                for jj in range(2):
                    ch = pn * 2 + jj
                    bank, pt = proj(b, [(wps.sem[b], t)] + hw,
                                    lambda kc, b=b, jj=jj: wpfm(b)[:, kc, jj * 128:(jj + 1) * 128],
                                    lambda kc: c.h[:, kc, :], lambda bank: c.ps[:, bank, :])
                    pw = [(c.pe, pt)]
                    if name == "fq":
                        a = store16(lambda s_: s_, io["qT"][ch * 128:(ch + 1) * 128, t0:t0 + TA], pw,
                                    lambda e, o, bank=bank: e.mul(out=o, in_=c.ps[:, bank, :], mul=FOX_SCALE))
                    elif name == "fk":
                        a = store16(lambda s_: s_, io["kT"][ch * 128:(ch + 1) * 128, t0:t0 + TA], pw,
                                    lambda e, o, bank=bank: e.copy(out=o, in_=c.ps[:, bank, :]))
                    elif name == "rg":
                        s, w2 = st32.acquire()
                        a = p.op("scalar", lambda e, s=s, bank=bank: e.activation(out=stg32[:, s, :], in_=c.ps[:, bank, :],
                                                                                func=AF.Silu),
                                 waits=pw + w2, inc=c.act)
                        sv = p.dma("sync", io["rgs"][ch * 128:(ch + 1) * 128, t0:t0 + TA], stg32[:, s, :],
                                   waits=[(c.act, a)], inc=st32.sem[s])
                        st32.cond[s] = [(st32.sem[s], sv)]
                    else:
                        a = p.op("scalar", lambda e, ch=ch, bank=bank: e.activation(out=u[:, ch, :], in_=c.ps[:, bank, :],
                                                                                  func=AF.Gelu_apprx_tanh),
                                 waits=pw + (u_free if ch == 0 else []), inc=c.act)
                    pj_cond[bank] = [(c.act, a)]
                wps.cond[b] = [(c.pe, pt)]
        b, w = wps.acquire()
        t = p.dma("gpsimd", wpfz(b), winv[:, :, 3072:3080], waits=w, inc=wps.sem[b])
        bank, pt = proj(b, [(wps.sem[b], t)] + hw, lambda kc, b=b: wpfz(b)[:, kc, :], lambda kc: c.h[:, kc, :],
                        lambda bank: c.ps[0:8, bank, :])
        wps.cond[b] = [(c.pe, pt)]
        a = p.op("scalar", lambda e, bank=bank: e.activation(out=fzt[:], in_=c.ps[0:8, bank, :], func=AF.Exp,
                                                             bias=io["negb"][:, 0:1], scale=-1.0),
                 waits=[(c.pe, pt), (c.dve, c.dve.v)] + setupw, inc=c.act)
        pj_cond[bank] = [(c.act, a)]
        a = p.op("scalar", lambda e: e.activation(out=fzt[:], in_=fzt[:], func=AF.Ln, bias=1.0), waits=[(c.act, a)],
                 inc=c.act)
        init = 0.0 if tt == 0 else cneg[:, t0 - 1:t0]
        p.op("vector", lambda e, init=init, t0=t0: e.tensor_tensor_scan(out=cneg[:, t0:t0 + TA], data0=onesr[:], data1=fzt[:],
                                                                 initial=init, op0=ALU.mult, op1=ALU.add),
             waits=[(c.act, a), (c.dve, c.dve.v)] + setupw, inc=c.dve)
        ready = {}
        for name, col0 in (("fv0", 2048), ("fv1", 2560), ("rv", 4104), ("rk", 3592), ("rq", 3080), ("gv", 5640)):
            b, w = wps.acquire()
            t = p.dma("gpsimd", wptm(b), winv[:, :, col0:col0 + 512], waits=w, inc=wps.sem[b])
            for tb in range(NBLK):
                bank, pt = proj(b, [(wps.sem[b], t)] + hw,
                                lambda kc, tb=tb: c.h[:, kc, tb * 128:(tb + 1) * 128],
                                lambda kc, b=b: wptm(b)[:, kc, :], lambda bank: c.ps[:, bank, :])
                pw = [(c.pe, pt)]
                psb = c.ps[:, bank, :]
                psv = V4(psb)
                r0 = t0 + tb * 128
                if name in ("fv0", "fv1"):
                    hc = 0 if name == "fv0" else 512
                    a = store16(lambda s_: s_, io["v"][r0:r0 + 128, hc:hc + 512], pw,
                                lambda e, o, psb=psb: e.copy(out=o, in_=psb))
                    pj_cond[bank] = [(c.act, a)]
                elif name == "rv":
                    a = p.op("scalar", lambda e, tb=tb, psb=psb: e.copy(out=rv[:, tb, :], in_=psb),
                             waits=pw + (ret_free if tb == 0 else []), inc=c.act)
                    pj_cond[bank] = [(c.act, a)]
                    ready[("rv", tb)] = [(c.act, a)]
                elif name in ("rk", "rq"):
                    rs, w2 = rots.acquire()
                    r1 = rot[:, rs, 0, :]
                    r2 = rot[:, rs, 1, :]
                    cosb = bh(tabt[:, ts_, tb, 0:128])
                    sina = bh64(tabt[:, ts_, tb, 128:192])
                    sinb = bh64(tabt[:, ts_, tb, 192:256])
                    p.op("vector", lambda e, psv=psv, r1=r1, cosb=cosb: e.tensor_tensor(out=V4(r1), in0=psv, in1=cosb,
                                                                                      op=ALU.mult),
                         waits=pw + w2 + tabw, inc=c.dve)
                    p.op("vector", lambda e, psv=psv, r2=r2, sina=sina: e.tensor_tensor(
                        out=V4(r2)[:, :, 0:64], in0=psv[:, :, 64:128], in1=sina, op=ALU.mult), inc=c.dve)
                    d = p.op("vector", lambda e, psv=psv, r2=r2, sinb=sinb: e.tensor_tensor(
                        out=V4(r2)[:, :, 64:128], in0=psv[:, :, 0:64], in1=sinb, op=ALU.mult), inc=c.dve)
                    pj_cond[bank] = [(c.dve, d)]
                    d = p.op("vector", lambda e, r1=r1, r2=r2: e.tensor_tensor(out=r1, in0=r1, in1=r2, op=ALU.add),
                             waits=[(c.dve, d)], inc=c.dve)
                    fw = ret_free if tb == 0 else []
                    if name == "rk":
                        a = p.op("scalar", lambda e, tb=tb, r1=r1: e.copy(out=kr[:, tb, :], in_=r1),
                                 waits=[(c.dve, d)] + fw, inc=c.act)
                        zb = bc(tabt[:, ts_, tb, 264:268])
                        d2 = p.op("vector", lambda e, tb=tb, r1=r1, zb=zb: e.tensor_tensor(out=V4(kz[:, tb, :]), in0=V4(r1),
                                                                                         in1=zb, op=ALU.mult),
                                  waits=[(c.dve, d)] + fw, inc=c.dve)
                        rots.cond[rs] = [(c.act, a), (c.dve, d2)]
                        ready[("kr", tb)] = [(c.act, a)]
                        ready[("kz", tb)] = [(c.dve, d2)]
                    else:
                        xb = bc(tabt[:, ts_, tb, 256:260])
                        gb_ = bc(tabt[:, ts_, tb, 260:264])
                        p.op("vector", lambda e, tb=tb, r1=r1, xb=xb: e.tensor_tensor(out=V4(qx[:, tb, :]), in0=V4(r1),
                                                                                    in1=xb, op=ALU.mult),
                             waits=[(c.dve, d)] + fw, inc=c.dve)
                        d2 = p.op("vector", lambda e, tb=tb, r1=r1, gb_=gb_: e.tensor_tensor(out=V4(qg[:, tb, :]),
                                                                                           in0=V4(r1), in1=gb_,
                                                                                           op=ALU.mult), inc=c.dve)
                        rots.cond[rs] = [(c.dve, d2)]
                        ready[("q", tb)] = [(c.dve, d2)]
                else:
                    rs, w2 = rots.acquire()
                    r1 = rot[:, rs, 0, :]
                    r2 = rot[:, rs, 1, :]
                    a = p.op("scalar", lambda e, psb=psb, r1=r1: e.activation(out=r1, in_=psb, func=AF.Gelu_apprx_tanh),
                             waits=pw + w2, inc=c.act)
                    pj_cond[bank] = [(c.act, a)]
                    a2 = p.op("scalar", lambda e, r1=r1, r2=r2: e.activation(out=r2, in_=r1, func=AF.Square),
                              waits=[(c.act, a)], inc=c.act)
                    d = p.op("vector", lambda e, r1=r1: e.tensor_reduce(out=st[:, 0:4], in_=V4(r1), axis=AX.X, op=ALU.add),
                             waits=[(c.act, a), (c.dve, c.dve.v)], inc=c.dve)
                    d = p.op("vector", lambda e, r2=r2: e.tensor_reduce(out=st[:, 4:8], in_=V4(r2), axis=AX.X, op=ALU.add),
                             waits=[(c.act, a2)], inc=c.dve)
                    d = p.op("vector", lambda e: e.tensor_scalar(out=st[:, 8:12], in0=st[:, 0:4], scalar1=1.0 / 128,
                                                                 scalar2=None, op0=ALU.mult),
                             waits=[(c.dve, d)], inc=c.dve)
                    d = p.op("vector", lambda e: e.tensor_tensor(out=st[:, 12:16], in0=st[:, 8:12], in1=st[:, 8:12],
                                                                 op=ALU.mult), waits=[(c.dve, d)], inc=c.dve)
                    d = p.op("vector", lambda e: e.scalar_tensor_tensor(out=st[:, 16:20], in0=st[:, 4:8], scalar=1.0 / 128,
                                                                        in1=st[:, 12:16], op0=ALU.mult,
                                                                        op1=ALU.subtract), waits=[(c.dve, d)], inc=c.dve)
                    d = p.op("vector", lambda e: e.tensor_scalar(out=st[:, 16:20], in0=st[:, 16:20], scalar1=EPS,
                                                                 scalar2=None, op0=ALU.add), waits=[(c.dve, d)], inc=c.dve)
                    a3 = p.op("scalar", lambda e: e.activation(out=st[:, 16:20], in_=st[:, 16:20], func=AF.Sqrt),
                              waits=[(c.dve, d)], inc=c.act)
                    d = p.op("vector", lambda e: e.reciprocal(out=st[:, 20:24], in_=st[:, 16:20]), waits=[(c.act, a3)],
                             inc=c.dve)
                    d = p.op("vector", lambda e, r1=r1: e.tensor_tensor(out=V4(r1), in0=V4(r1), in1=bc(st[:, 8:12]),
                                                                      op=ALU.subtract), waits=[(c.dve, d)], inc=c.dve)
                    d = p.op("vector", lambda e, r1=r1: e.tensor_tensor(out=V4(r1), in0=V4(r1), in1=bc(st[:, 20:24]),
                                                                      op=ALU.mult), waits=[(c.dve, d)], inc=c.dve)
                    d = p.op("vector", lambda e, r1=r1: e.tensor_tensor(out=r1, in0=r1, in1=io["lnb3"][:, 0, :],
                                                                      op=ALU.mult), waits=[(c.dve, d)] + setupw,
                             inc=c.dve)
                    d = p.op("vector", lambda e, r1=r1, tb=tb: e.tensor_tensor(out=vln[:, tb, :], in0=r1,
                                                                             in1=io["lnb3"][:, 1, :], op=ALU.add),
                             waits=[(c.dve, d)] + (vln_free if tb == 0 else []), inc=c.dve)
                    rots.cond[rs] = [(c.dve, d)]
                    ready[("vln", tb)] = [(c.dve, d)]
            wps.cond[b] = [(c.pe, pt)]
        for n in range(2 * NBLK):
            tb, a_ = n // 2, n % 2
            kb = 2 + kv_n[0] % 2
            ci = kv_n[0] % 2
            kv_n[0] += 1
            for hh in range(4):
                sl = slice(hh * 128, (hh + 1) * 128)
                p.op("tensor", lambda e, kb=kb, sl=sl, a_=a_, tb=tb: e.matmul(
                    c.ps[:, kb, sl], lhsT=kz[a_ * 64:(a_ + 1) * 64, tb, sl], rhs=rv[a_ * 64:(a_ + 1) * 64, tb, sl],
                    start=True, stop=True),
                     waits=(ready[("kz", tb)] + ready[("rv", tb)] + kv_cond[ci]) if hh == 0 else [],
                     inc=(c.pe if hh == 3 else None))
            pt = c.pe.v
            a = p.op("scalar", lambda e, n=n: e.copy(out=Sb[:, n, :], in_=S32[:]),
                     waits=[(c.dve, c.dve.v)] + (ret_free if n == 0 else []) + setupw, inc=c.act)
            ready[("Sb", n)] = [(c.act, a)]
            for hh in range(4):
                sl = slice(hh * 128, (hh + 1) * 128)
                d = p.op("vector", lambda e, kb=kb, sl=sl, hh=hh: e.scalar_tensor_tensor(
                    out=S32[:, sl], in0=S32[:, sl], scalar=GAM64[hh], in1=c.ps[:, kb, sl], op0=ALU.mult, op1=ALU.add),
                         waits=[(c.pe, pt), (c.act, a)] if hh == 0 else [], inc=c.dve)
            kv_cond[ci] = [(c.dve, d)]
        s32_done = [(c.dve, d)]
        for tb in range(NBLK):
            r0 = t0 + tb * 128
            for gg in range(4):
                sl = slice(gg * 128, (gg + 1) * 128)
                p.op("tensor", lambda e, sl=sl, tb=tb, gg=gg: e.matmul(c.ps[:, 4, sl], lhsT=vln[:, tb, sl],
                                                                     rhs=io["wsb"][:, gg, :], start=True, stop=True),
                     waits=(ready[("vln", tb)] + b4_cond + setupw) if gg == 0 else [], inc=(c.pe if gg == 3 else None))
            pt = c.pe.v
            d = p.op("vector", lambda e: e.tensor_tensor(out=y1[:], in0=c.ps[:, 4, :], in1=io["lnb3"][:, 2, :], op=ALU.add),
                     waits=[(c.pe, pt), (c.act, c.act.v), (c.dve, c.dve.v)], inc=c.dve)
            d = p.op("vector", lambda e, tb=tb: e.tensor_tensor(out=V4(y1[:]), in0=V4(y1[:]),
                                                               in1=u[:, :, tb * 128:(tb + 1) * 128], op=ALU.mult),
                     waits=[(c.dve, d)], inc=c.dve)
            a = p.op("scalar", lambda e: e.activation(out=y2[:], in_=y1[:], func=AF.Square), waits=[(c.dve, d)], inc=c.act)
            p.op("tensor", lambda e: e.matmul(c.ps[:, 4, :], lhsT=c.ones[:], rhs=y2[:], start=True, stop=True),
                 waits=[(c.act, a), (c.dve, d)], inc=c.pe)
            pt = c.pe.v
            d = p.op("vector", lambda e: e.tensor_scalar(out=y2[:], in0=c.ps[:, 4, :], scalar1=1.0 / 128, scalar2=EPS,
                                                         op0=ALU.mult, op1=ALU.add), waits=[(c.pe, pt)], inc=c.dve)
            b4_cond = [(c.dve, d)]
            a = p.op("scalar", lambda e: e.activation(out=y2[:], in_=y2[:], func=AF.Sqrt), waits=[(c.dve, d)], inc=c.act)
            d = p.op("vector", lambda e: e.reciprocal(out=y2[:], in_=y2[:]), waits=[(c.act, a)], inc=c.dve)
            d = p.op("vector", lambda e: e.tensor_tensor(out=y1[:], in0=y1[:], in1=y2[:], op=ALU.mult),
                     waits=[(c.dve, d)], inc=c.dve)
            s, w = st16.acquire()
            for gg in range(4):
                sl = slice(gg * 128, (gg + 1) * 128)
                a = p.op("vector", lambda e, s=s, sl=sl, gg=gg: e.tensor_scalar(out=stg16[:, s, sl], in0=y1[:, sl],
                                                                              scalar1=io["ong"][:, gg:gg + 1],
                                                                              scalar2=None, op0=ALU.mult),
                         waits=([(c.dve, d)] + w + setupw) if gg == 0 else [], inc=c.dve)
            sv = p.dma("sync", io["ytg"].rearrange("(g c) t -> c g t", c=128)[:, :, r0:r0 + 128], V4(stg16[:, s, :]),
                       waits=[(c.dve, a)], inc=st16.sem[s])
            st16.cond[s] = [(st16.sem[s], sv)]
        u_free = [(c.dve, c.dve.v)]
        vln_free = [(c.pe, c.pe.v)]
        for tb in range(NBLK):
            r0 = t0 + tb * 128
            for src, key, dstT, pb in ((kr, "kr", krT, 0), (qx, "q", qxT, 1), (qg, "q", None, 0)):
                for hh in range(4):
                    sl = slice(hh * 128, (hh + 1) * 128)
                    p.op("tensor", lambda e, src=src, sl=sl, tb=tb, pb=pb: e.transpose(out=pst[:, pb, sl], in_=src[:, tb, sl],
                                                                                     identity=io["identb"][:]),
                         waits=(ready[(key, tb)] + pst_cond[pb] + setupw) if hh == 0 else [],
                         inc=(c.pe if hh == 3 else None))
                pt = c.pe.v
                if dstT is not None:
                    d = p.op("vector", lambda e, dstT=dstT, pb=pb: e.tensor_copy(out=dstT[:], in_=pst[:, pb, 0:512]),
                             waits=[(c.pe, pt), (c.pe, c.pe.v)], inc=c.dve)
                    pst_cond[pb] = [(c.dve, d)]
                    ready[(id(dstT), tb)] = [(c.dve, d)]
                else:
                    s, w = st16.acquire()
                    a = p.op("scalar", lambda e, s=s, pb=pb: e.copy(out=stg16[:, s, :], in_=pst[:, pb, 0:512]),
                             waits=[(c.pe, pt)] + w, inc=c.act)
                    pst_cond[pb] = [(c.act, a)]
                    sv = p.dma("sync", io["ret_qg"].rearrange("(h d) t -> d h t", d=128)[:, :, r0:r0 + 128],
                               V4(stg16[:, s, :]), waits=[(c.act, a)], inc=st16.sem[s])
                    st16.cond[s] = [(st16.sem[s], sv)]
            for hh in range(4):
                sl = slice(hh * 128, (hh + 1) * 128)
                p.op("tensor", lambda e, sl=sl: e.matmul(c.ps[:, 4, sl], lhsT=krT[:, sl], rhs=qxT[:, sl], start=True,
                                                         stop=True),
                     waits=(ready[(id(krT), tb)] + ready[(id(qxT), tb)] + b4_cond) if hh == 0 else [],
                     inc=(c.pe if hh == 3 else None))
            pt = c.pe.v
            d = p.op("vector", lambda e: e.tensor_tensor(out=sm[:], in0=c.ps[:, 4, :], in1=io["maskr"][:], op=ALU.mult),
                     waits=[(c.pe, pt), (c.pe, c.pe.v)] + setupw, inc=c.dve)
            b4_cond = [(c.dve, d)]
            for hh in range(4):
                sl = slice(hh * 128, (hh + 1) * 128)
                p.op("tensor", lambda e, sl=sl, tb=tb: e.matmul(c.ps[:, 5, sl], lhsT=rv[:, tb, sl], rhs=sm[:, sl], start=True,
                                                               stop=False),
                     waits=([(c.dve, d)] + c.ps_free_waits + ready[("Sb", 2 * tb)] + ready[("Sb", 2 * tb + 1)])
                     if hh == 0 else [])
                for a_ in range(2):
                    cs = slice(hh * 128 + a_ * 64, hh * 128 + (a_ + 1) * 64)
                    p.op("tensor", lambda e, sl=sl, cs=cs, tb=tb, a_=a_: e.matmul(
                        c.ps[:, 5, cs], lhsT=Sb[:, 2 * tb + a_, sl], rhs=qxT[:, cs], start=False, stop=(a_ == 1)),
                         inc=(c.pe if (hh == 3 and a_ == 1) else None))
            pt = c.pe.v
            s, w = st32.acquire()
            a = p.op("scalar", lambda e, s=s: e.copy(out=stg32[:, s, :], in_=c.ps[:, 5, :]), waits=[(c.pe, pt)] + w,
                     inc=c.act)
            c.ps_free_waits = c.ps_free_waits + [(c.act, a)]
            sv = p.dma("sync", io["ret_o"].rearrange("(h e) t -> e h t", e=128)[:, :, r0:r0 + 128], V4(stg32[:, s, :]),
                       waits=[(c.act, a)], inc=st32.sem[s])
            st32.cond[s] = [(st32.sem[s], sv)]
        ret_free = [(c.pe, c.pe.v)]
    f1 = p.dma("sync", io["cneg_o"][:, :], cneg[:], waits=[(c.dve, c.dve.v)], inc=misc)
    f2 = p.dma("sync", io["ret_S"][:, :], S32[:], waits=s32_done, inc=misc)
    p.wait_only("sync", [(misc, f2)] + [(st16.sem[s], st16.sem[s].v) for s in range(4)] +
                [(st32.sem[s], st32.sem[s].v) for s in range(2)])


def mixa_setup(c, din, io):
    p = c.p
    sb = c.sb
    gcol = sb("gcol", [128, KC], F32)
    negb = sb("negb", [8, 1], F32)
    maskr = sb("maskr", [128, 512], F32)
    wst = sb("wst", [128, 512], F32)
    triu = sb("triu", [128, 128], F32)
    wsb = sb("wsb", [128, 4, 128], BF16)
    lnb3 = sb("lnb3", [128, 3, 512], F32)
    ong = sb("ong", [128, 4], F32)
    identb = sb("identb", [128, 128], BF16)
    io.update({"negb": negb, "maskr": maskr, "wsb": wsb, "lnb3": lnb3, "ong": ong, "identb": identb})
    for dst, src in ((gcol[:], din["g"]), (negb[:], din["bf"]), (maskr[:], din["maskr"]), (wst[:], din["wst"]),
                     (triu[:], din["triu"]), (lnb3[:].rearrange("p a b -> p (a b)"), din["lnb3"]),
                     (ong[:], din["ong"])):
        p.dma("sync", dst, src, inc=c.setup_d)
    p.dma("gpsimd", identb[:], din["ident"], inc=c.setup_d)
    dl = [(c.setup_d, c.setup_d.v)]
    p.op("vector", lambda e: e.tensor_scalar(out=negb[:], in0=negb[:], scalar1=-1.0, scalar2=None, op0=ALU.mult),
         waits=dl, inc=c.setup_v)
    p.op("vector", lambda e: e.tensor_tensor(out=wsb[:], in0=wst[:].rearrange("p (a b) -> p a b", a=4),
                                             in1=triu[:].unsqueeze(1).to_broadcast([128, 4, 128]), op=ALU.mult),
         inc=c.setup_v)
    return gcol


def build_mixa():
    nc = bass.Bass("TRN2", target_bir_lowering=False)
    di = lambda name, shape: nc.dram_tensor(name, shape, F32, kind="ExternalInput").ap()
    do = lambda name, shape, dt: nc.dram_tensor(name, shape, dt, kind="ExternalOutput").ap()
    xin = di("xin", [D, NTOK])
    w_in = di("w_in", [D, INCOLS])
    din = {"g": di("g", [128, KC])[:, :], "bf": di("bf", [8, 1])[:, :], "maskr": di("maskr", [128, 512])[:, :],
           "wst": di("wst", [128, 512])[:, :], "triu": di("triu", [128, 128])[:, :],
           "lnb3": di("lnb3", [128, 3 * 512])[:, :], "ong": di("ong", [128, 4])[:, :],
           "ident": di("ident", [128, 128])[:, :]}
    io = {
        "tab": di("tab", [NTOK, 268]),
        "qT": do("qT", [1024, NTOK], BF16), "kT": do("kT", [1024, NTOK], BF16), "v": do("v", [NTOK, 1024], BF16),
        "cneg_o": do("cneg", [8, NTOK], F32), "ret_o": do("ret_o", [512, NTOK], F32),
        "ret_qg": do("ret_qg", [512, NTOK], BF16), "ret_S": do("ret_S", [128, 512], F32),
        "rgs": do("rgs", [512, NTOK], F32), "ytg": do("ytg", [512, NTOK], BF16),
    }
    with contextlib.ExitStack() as stack:
        p = Prog(nc, stack)
        c = alloc_common(nc, stack, p, tt=TA, nps=6, stat_bank=5)
        c.stack = stack
        gcol = mixa_setup(c, din, io)
        mixa_body(c, xin, gcol, w_in, io)
        p.emit()
    return nc


def host_consts(half):
    t = np.arange(NTOK, dtype=np.float64)
    pos = (half * NTOK + np.arange(NTOK)).astype(np.float32)
    inv_freq = (np.float32(10000.0) ** (-np.arange(64, dtype=np.float32) / np.float32(64))).astype(np.float32)
    ang = (pos[:, None] * inv_freq[None, :]).astype(np.float32).astype(np.float64)
    cos, sin = np.cos(ang), np.sin(ang)
    gam = np.array(GAM, dtype=np.float64)
    cidx = (np.arange(NTOK) % 64).astype(np.float64)
    xi = gam[None, :] ** (cidx[:, None] + 1.0)
    gm = gam[None, :] ** (t[:, None] + 1.0)
    zeta = gam[None, :] ** (63.0 - cidx[:, None]) * (128.0 ** -0.5)
    tab = np.concatenate([cos, cos, -sin, sin, xi, gm, zeta], axis=1).astype(np.float32)
    s = np.arange(128)
    cc = np.arange(128)
    same = (s[:, None] // 64 == cc[None, :] // 64) & (cc[None, :] >= s[:, None])
    maskr = np.zeros((128, 4, 128), np.float64)
    for h in range(4):
        maskr[:, h, :] = np.where(same, gam[h] ** (-(s[:, None] % 64 + 1.0)), 0.0) * (128.0 ** -0.5)
    triu = (s[:, None] <= cc[None, :]).astype(np.float32)
    return {"tab": np.ascontiguousarray(tab), "maskr": maskr.reshape(128, 512).astype(np.float32), "triu": triu,
            "ident": np.eye(128, dtype=np.float32)}


def col16(vec):
    return np.ascontiguousarray(np.asarray(vec, np.float32).reshape(KC, 128).T)


def mixa_inputs(xT, half, l, P):
    hc = host_consts(half)
    wst = np.ascontiguousarray(np.transpose(P["gmlp_w_s"][l], (2, 0, 1)).reshape(128, 512))
    lnb3 = np.concatenate([np.broadcast_to(P["gmlp_ln_g"][l][None, :], (128, 512)),
                           np.broadcast_to(P["gmlp_ln_b"][l][None, :], (128, 512)),
                           np.broadcast_to(P["gmlp_b_s"][l].reshape(1, 512), (128, 512))], axis=1)
    ong = np.ascontiguousarray(P["out_norm"][l][1536:2048].reshape(4, 128).T)
    return {"xin": xT, "g": col16(P["mix_norm"][l]), "w_in": P["w_in"][l],
            "bf": np.ascontiguousarray(P["fox_b_f"][l].reshape(8, 1)), "tab": hc["tab"], "maskr": hc["maskr"],
            "wst": wst.astype(np.float32), "triu": hc["triu"], "lnb3": np.ascontiguousarray(lnb3, dtype=np.float32),
            "ong": ong.astype(np.float32), "ident": hc["ident"]}


QT = 512
NQT = NTOK // QT
MASKNEG = -30000.0


def mixb_body(nc, stack, p, io):
    uid = _uid()
    sb = lambda name, shape, dt: stack.enter_context(nc.sbuf_tensor("sb%d_%s" % (uid, name), shape, dt))
    ps = stack.enter_context(nc.psum_tensor("psb%d" % uid, [128, 8, 512], F32))
    onesf = sb("onesf", [128, 128], F32)
    onesb = sb("onesb", [128, 128], BF16)
    cmask = sb("cmask", [128, 896], F32)
    selh = sb("selh", [8, 8, 128], F32)
    ident8 = sb("ident8", [8, 8], F32)
    pmask = sb("pmask", [128, 1], F32)
    sflag = sb("sflag", [128, 1], F32)
    ona = sb("ona", [128, 12], F32)
    sinit = sb("sinit", [128, 512], F32)
    sinb = sb("sinb", [128, 512], BF16)
    cn = sb("cn", [8, 2 * NTOK], F32)
    ncl = sb("ncl", [8, NTOK], F32)
    ncp = sb("ncp", [8, NTOK], F32)
    biasT = sb("biasT", [128, 32, 8], F32)
    setup_d = p.sem("bsetupd")
    dve = p.sem("bdve")
    act = p.sem("bact")
    pe = p.sem("bpe")
    for dst, src in ((cmask[:], io["cmask"][:, :]), (selh[:].rearrange("k h m -> k (h m)"), io["selh"][:, :]),
                     (ident8[:], io["ident8"][:, :]), (pmask[:], io["pmask"][:, :]), (sflag[:], io["sflag"][:, :]),
                     (ona[:], io["ona"][:, :]), (sinit[:], io["s_init"][:, :]), (cn[:, 0:NTOK], io["cneg_prev"][:, :]),
                     (cn[:, NTOK:2 * NTOK], io["cneg_loc"][:, :])):
        p.dma("sync", dst, src, inc=setup_d)
    sd = [(setup_d, setup_d.v)]
    p.op("vector", lambda e: e.memset(onesf[:], 1.0), inc=dve)
    p.op("vector", lambda e: e.memset(onesb[:], 1.0), inc=dve)
    p.op("vector", lambda e: e.tensor_scalar(out=sinb[:], in0=sinit[:], scalar1=sflag[:, 0:1], scalar2=None, op0=ALU.mult),
         waits=sd, inc=dve)
    p.op("vector", lambda e: e.tensor_scalar(out=ncl[:], in0=cn[:, NTOK:2 * NTOK], scalar1=-1.0, scalar2=None, op0=ALU.mult),
         inc=dve)
    d = p.op("vector", lambda e: e.tensor_scalar(out=ncp[:], in0=ncl[:], scalar1=cn[:, NTOK - 1:NTOK], scalar2=None,
                                                 op0=ALU.subtract), waits=[(dve, dve.v)], inc=dve)
    for blk in range(32):
        p.op("tensor", lambda e, blk=blk: e.transpose(out=ps[:, 6, blk * 8:(blk + 1) * 8],
                                                      in_=cn[0:8, blk * 128:(blk + 1) * 128], identity=ident8[:]),
             waits=sd if blk == 0 else [], inc=(pe if blk == 31 else None))
    p.op("vector", lambda e: e.tensor_scalar(out=biasT[:, 0:16, :].rearrange("p a b -> p (a b)"), in0=ps[:, 6, 0:128],
                                             scalar1=pmask[:, 0:1], scalar2=None, op0=ALU.add),
         waits=[(pe, pe.v)] + sd, inc=dve)
    d = p.op("vector", lambda e: e.tensor_copy(out=biasT[:, 16:32, :].rearrange("p a b -> p (a b)"), in_=ps[:, 6, 128:256]),
             inc=dve)
    b6_cond = [(dve, d)]
    setup_done = [(dve, d)] + sd

    qh = sb("qh", [128, 2, NTOK], BF16)
    kh = sb("kh", [128, 2, 2 * NTOK], BF16)
    vh = sb("vh", [128, 2, 32, 128], BF16)
    hs = Slots(p, "hs", 2)
    cb = sb("cb", [128, 2, 2, QT], F32)
    cbs = Slots(p, "cbs", 2)
    tmp = sb("tmp", [128, 4, QT], F32)
    tmps = Slots(p, "tmps", 4)
    pt = sb("pt", [128, 4, QT], BF16)
    pts = Slots(p, "pts", 4)
    ya = sb("ya", [128, 2, QT], F32)
    yas = Slots(p, "yas", 2)
    yq = sb("yq", [128, QT], F32)
    stg = sb("stg", [128, 2, QT], BF16)
    stgs = Slots(p, "stgs", 2)
    ro = sb("ro", [128, 2, QT], F32)
    rg = sb("rg", [128, 2, QT], F32)
    qgt = sb("qgt", [128, 2, QT], BF16)
    rls = Slots(p, "rls", 2)
    S_BANKS = (0, 1, 7)
    LOOK = 2
    s_cond = {0: [], 1: [], 7: []}
    acc_cond = [[], []]
    s_n = [0]
    acc_n = [0]
    y_stores = []

    def headnorm_store(yslot, yready, gain_col, dst_ap, mul_tile=None, mul_wait=()):
        nonlocal b6_cond
        a = p.op("scalar", lambda e: e.activation(out=yq[:], in_=ya[:, yslot, :], func=AF.Square),
                 waits=list(yready) + [(dve, dve.v)], inc=act)
        p.op("tensor", lambda e: e.matmul(ps[:, 6, :], lhsT=onesf[:], rhs=yq[:], start=True, stop=True),
             waits=[(act, a)] + b6_cond, inc=pe)
        d = p.op("vector", lambda e: e.tensor_scalar(out=yq[:], in0=ps[:, 6, :], scalar1=1.0 / 128, scalar2=EPS,
                                                     op0=ALU.mult, op1=ALU.add), waits=[(pe, pe.v)], inc=dve)
        b6_cond = [(dve, d)]
        a = p.op("scalar", lambda e: e.activation(out=yq[:], in_=yq[:], func=AF.Sqrt), waits=[(dve, d)], inc=act)
        d = p.op("vector", lambda e: e.reciprocal(out=yq[:], in_=yq[:]), waits=[(act, a)], inc=dve)
        s, w = stgs.acquire()
        if mul_tile is None:
            d = p.op("vector", lambda e: e.scalar_tensor_tensor(out=stg[:, s, :], in0=ya[:, yslot, :], scalar=gain_col,
                                                                in1=yq[:], op0=ALU.mult, op1=ALU.mult),
                     waits=[(dve, d)] + w, inc=dve)
        else:
            d = p.op("vector", lambda e: e.scalar_tensor_tensor(out=ya[:, yslot, :], in0=ya[:, yslot, :], scalar=gain_col,
                                                                in1=yq[:], op0=ALU.mult, op1=ALU.mult),
                     waits=[(dve, d)], inc=dve)
            d = p.op("vector", lambda e: e.tensor_tensor(out=stg[:, s, :], in0=ya[:, yslot, :], in1=mul_tile, op=ALU.mult),
                     waits=[(dve, d)] + w + list(mul_wait), inc=dve)
        sv = p.dma("sync", dst_ap, stg[:, s, :], waits=[(dve, d)], inc=stgs.sem[s])
        stgs.cond[s] = [(stgs.sem[s], sv)]
        y_stores.append((stgs.sem[s], sv))
        return [(dve, d)]

    pending = [None]

    def run_pending():
        if pending[0] is not None:
            ys_, rdy_, gain_, dst_, mt_, rs_ = pending[0]
            pending[0] = None
            yw_ = headnorm_store(ys_, rdy_, gain_, dst_, mul_tile=mt_)
            yas.cond[ys_] = yw_
            if rs_ is not None:
                rls.cond[rs_] = yw_

    vparts = []
    blk0 = 0
    for key in ("v_prev", "v_loc"):
        for part in _parts(io[key]):
            nb = part.shape[0] // 128
            vparts.append((blk0, nb, part.rearrange("(b p) c -> p b c", p=128)))
            blk0 += nb
    assert blk0 == 32
    for h in range(8):
        hsl, w = hs.acquire()
        rows = slice(h * 128, (h + 1) * 128)
        p.dma("sync", qh[:, hsl, :], io["qT"][rows, :], waits=w, inc=hs.sem[hsl])
        p.dma("sync", kh[:, hsl, 0:NTOK], io["kT_prev"][rows, :], inc=hs.sem[hsl])
        p.dma("sync", kh[:, hsl, NTOK:2 * NTOK], io["kT_loc"][rows, :], inc=hs.sem[hsl])
        for (b0_, nb_, vw_) in vparts:
            hl = p.dma("sync", vh[:, hsl, b0_:b0_ + nb_, :], vw_[:, :, rows], inc=hs.sem[hsl])
        hw = [(hs.sem[hsl], hl)]
        for qt in range(NQT):
            qs = slice(qt * QT, (qt + 1) * QT)
            cs_, w = cbs.acquire()
            for vi, src in ((0, ncp), (1, ncl)):
                p.op("tensor", lambda e, src=src, h=h, qs=qs: e.matmul(ps[:, 6, :], lhsT=selh[0:8, h, :], rhs=src[0:8, qs],
                                                                      start=True, stop=True),
                     waits=b6_cond + setup_done, inc=pe)
                a = p.op("scalar", lambda e, cs_=cs_, vi=vi: e.copy(out=cb[:, cs_, vi, :], in_=ps[:, 6, :]),
                         waits=[(pe, pe.v)] + w, inc=act)
                b6_cond = [(act, a)]
            cbw = [(act, a)]
            blocks = [(j, 0, None) for j in range(16)] + [(16 + j, 1, (j - 4 * qt) if j >= 4 * qt else None)
                                                          for j in range(4 * qt + 4)]
            an = acc_n[0]
            acc_n[0] += 1
            ob, db = 2 + an % 2, 4 + an % 2
            pend = None
            nblk = len(blocks)

            def emit_pv(pend, first, last):
                j, slot, pw_ = pend
                p.op("tensor", lambda e, j=j, slot=slot, ob=ob, hsl=hsl: e.matmul(ps[:, ob, :], lhsT=vh[:, hsl, j, :],
                                                                                 rhs=pt[:, slot, :], start=first, stop=last),
                     waits=pw_ + (acc_cond[an % 2] if first else []))
                p.op("tensor", lambda e, slot=slot, db=db: e.matmul(ps[:, db, :], lhsT=onesb[:], rhs=pt[:, slot, :],
                                                                   start=first, stop=last), inc=pe)
                pts.cond[slot] = [(pe, pe.v)]

            npv = 0
            queue = []
            for bi, (j, vi, dg) in enumerate(blocks):
                if bi == 6:
                    run_pending()
                sn = s_n[0]
                s_n[0] += 1
                sbk = S_BANKS[sn % 3]
                p.op("tensor", lambda e, j=j, sbk=sbk, qs=qs, hsl=hsl: e.matmul(ps[:, sbk, :], lhsT=kh[:, hsl, j * 128:(j + 1) * 128],
                                                                      rhs=qh[:, hsl, qs], start=True, stop=True),
                     waits=hw + s_cond[sbk], inc=pe)
                st_ = pe.v
                if len(queue) >= LOOK:
                    emit_pv(queue.pop(0), npv == 0, False)
                    npv += 1
                ts, w = tmps.acquire()
                d = p.op("vector", lambda e, ts=ts, sbk=sbk, vi=vi, cs_=cs_: e.scalar_tensor_tensor(
                    out=tmp[:, ts, :], in0=ps[:, sbk, :], scalar=1.0, in1=cb[:, cs_, vi, :], op0=ALU.mult, op1=ALU.add),
                         waits=[(pe, st_)] + w + cbw, inc=dve)
                s_cond[sbk] = [(dve, d)]
                if dg is not None:
                    off = 384 - dg * 128
                    d = p.op("vector", lambda e, ts=ts, off=off: e.tensor_tensor(out=tmp[:, ts, :], in0=tmp[:, ts, :],
                                                                                 in1=cmask[:, off:off + QT], op=ALU.add),
                             waits=[(dve, d)] + setup_done, inc=dve)
                slot, w = pts.acquire()
                a = p.op("scalar", lambda e, ts=ts, slot=slot, j=j, h=h: e.activation(
                    out=pt[:, slot, :], in_=tmp[:, ts, :], func=AF.Exp, bias=biasT[:, j, h:h + 1], scale=1.0),
                         waits=[(dve, d)] + w + setup_done, inc=act)
                tmps.cond[ts] = [(act, a)]
                queue.append((j, slot, [(act, a)]))
            while queue:
                pend = queue.pop(0)
                emit_pv(pend, npv == 0, len(queue) == 0)
                npv += 1
            acc_done = pe.v
            cbs.cond[cs_] = [(dve, dve.v)]
            ys, w = yas.acquire()
            d = p.op("vector", lambda e, db=db: e.reciprocal(out=yq[:], in_=ps[:, db, :]), waits=[(pe, acc_done), (dve, dve.v),
                                                                                          (act, act.v)], inc=dve)
            d = p.op("vector", lambda e, ys=ys, ob=ob: e.tensor_tensor(out=ya[:, ys, :], in0=ps[:, ob, :], in1=yq[:], op=ALU.mult),
                     waits=[(dve, d)] + w, inc=dve)
            acc_cond[an % 2] = [(dve, d)]
            run_pending()
            pending[0] = (ys, [(dve, d)], ona[:, h:h + 1], io["yT"][rows, qs], None, None)
        hs.cond[hsl] = [(pe, pe.v)]
    for hh in range(4):
        rows = slice(hh * 128, (hh + 1) * 128)
        for qt in range(NQT):
            qs = slice(qt * QT, (qt + 1) * QT)
            rs, w = rls.acquire()
            p.dma("sync", ro[:, rs, :], io["ret_o"][rows, qs], waits=w, inc=rls.sem[rs])
            p.dma("sync", rg[:, rs, :], io["rgs"][rows, qs], inc=rls.sem[rs])
            rl = p.dma("sync", qgt[:, rs, :], io["ret_qg"][rows, qs], inc=rls.sem[rs])
            p.op("tensor", lambda e, rs=rs, rows=rows: e.matmul(ps[:, 6, :], lhsT=sinb[:, rows], rhs=qgt[:, rs, :], start=True,
                                                               stop=True),
                 waits=[(rls.sem[rs], rl)] + b6_cond + setup_done, inc=pe)
            ys, w = yas.acquire()
            d = p.op("vector", lambda e, ys=ys, rs=rs: e.tensor_tensor(out=ya[:, ys, :], in0=ps[:, 6, :], in1=ro[:, rs, :],
                                                                      op=ALU.add), waits=[(pe, pe.v)] + w, inc=dve)
            b6_cond = [(dve, d)]
            run_pending()
            pending[0] = (ys, [(dve, d)], ona[:, 8 + hh:9 + hh],
                          io["yT"][1024 + hh * 128:1024 + (hh + 1) * 128, qs], rg[:, rs, :], rs)
    run_pending()
    hb = sb("hb", [128, KC, TT], BF16)
    wo = sb("wo", [128, 2, KC, 256], BF16)
    wos = Slots(p, "wos", 2)
    xs = sb("xsb", [128, 3, TT], F32)
    xss = Slots(p, "xss", 3)
    hbl = p.sem("hbl")
    wov = io["w_out"].rearrange("(kc p) f -> p kc f", p=128)
    ytv = io["yT"].rearrange("(kc p) t -> p kc t", p=128)
    ygv = io["ytg"].rearrange("(kc p) t -> p kc t", p=128)
    hb_free = []
    on = 0
    ob_cond = [[], []]
    for tt in range(NTT):
        t0 = tt * TT
        p.dma("sync", hb[:, 0:12, :], ytv[:, :, t0:t0 + TT], waits=list(y_stores) + hb_free, inc=hbl)
        hl = p.dma("sync", hb[:, 12:16, :], ygv[:, :, t0:t0 + TT], inc=hbl)
        for pd in range(8):
            col0 = pd * 256
            b, w = wos.acquire()
            wl = p.dma("gpsimd", wo[:, b, :, :], wov[:, :, col0:col0 + 256], waits=w, inc=wos.sem[b])
            for ii in range(2):
                i = pd * 2 + ii
                s, w = xss.acquire()
                full = p.dma("sync", xs[:, s, :], io["xin"][i * 128:(i + 1) * 128, t0:t0 + TT], waits=w, inc=xss.sem[s])
                for th in range(2):
                    obk = on % 2
                    on += 1
                    for kc in range(KC):
                        p.op("tensor", lambda e, b=b, kc=kc, ii=ii, th=th, obk=obk: e.matmul(
                            ps[:, obk, :], lhsT=wo[:, b, kc, ii * 128:(ii + 1) * 128], rhs=hb[:, kc, th * 512:(th + 1) * 512],
                            start=(kc == 0), stop=(kc == KC - 1)),
                             waits=([(wos.sem[b], wl), (hbl, hl)] + ob_cond[obk]) if kc == 0 else [],
                             inc=(pe if kc == KC - 1 else None))
                    r = p.op("vector", lambda e, s=s, th=th, obk=obk: e.tensor_tensor(
                        out=xs[:, s, th * 512:(th + 1) * 512], in0=ps[:, obk, :], in1=xs[:, s, th * 512:(th + 1) * 512],
                        op=ALU.add), waits=[(pe, pe.v), (xss.sem[s], full)], inc=dve)
                    ob_cond[obk] = [(dve, r)]
                sv = p.dma("sync", io["xout"][i * 128:(i + 1) * 128, t0:t0 + TT], xs[:, s, :], waits=[(dve, r)],
                           inc=xss.sem[s])
                xss.cond[s] = [(xss.sem[s], sv)]
            wos.cond[b] = [(pe, pe.v)]
        hb_free = [(pe, pe.v)]
    p.wait_only("sync", [(xss.sem[s], xss.sem[s].v) for s in range(3)])


def build_mixb():
    nc = bass.Bass("TRN2", target_bir_lowering=False)
    di = lambda name, shape, dt=F32: nc.dram_tensor(name, shape, dt, kind="ExternalInput").ap()
    io = {
        "qT": di("qT", [1024, NTOK], BF16), "kT_loc": di("kT_loc", [1024, NTOK], BF16),
        "kT_prev": di("kT_prev", [1024, NTOK], BF16), "v_loc": di("v_loc", [NTOK, 1024], BF16),
        "v_prev": di("v_prev", [NTOK, 1024], BF16), "cneg_loc": di("cneg_loc", [8, NTOK]),
        "cneg_prev": di("cneg_prev", [8, NTOK]), "s_init": di("s_init", [128, 512]),
        "ret_o": di("ret_o", [512, NTOK]), "ret_qg": di("ret_qg", [512, NTOK], BF16), "rgs": di("rgs", [512, NTOK]),
        "ytg": di("ytg", [512, NTOK], BF16), "cmask": di("cmask", [128, 896]), "selh": di("selh", [8, 1024]),
        "ident8": di("ident8", [8, 8]), "pmask": di("pmask", [128, 1]), "sflag": di("sflag", [128, 1]),
        "ona": di("ona", [128, 12]), "w_out": di("w_out", [D, D]), "xin": di("xin", [D, NTOK]),
        "yT": nc.dram_tensor("yT", [1536, NTOK], BF16, kind="Internal").ap(),
        "xout": nc.dram_tensor("xout", [D, NTOK], F32, kind="ExternalOutput").ap(),
    }
    with contextlib.ExitStack() as stack:
        p = Prog(nc, stack)
        mixb_body(nc, stack, p, io)
        p.emit()
    return nc


def mixb_consts(half):
    s = np.arange(128)[:, None]
    u = np.arange(896)[None, :]
    cmask = np.where((u - 384) >= s, 0.0, MASKNEG).astype(np.float32)
    selh = np.zeros((8, 8, 128), np.float32)
    for h in range(8):
        selh[h, h, :] = 1.0
    return {"cmask": cmask, "selh": selh.reshape(8, 1024), "ident8": np.eye(8, dtype=np.float32),
            "pmask": np.full((128, 1), 0.0 if half == 1 else MASKNEG, np.float32),
            "sflag": np.full((128, 1), 1.0 if half == 1 else 0.0, np.float32)}


def build_norm():
    nc = bass.Bass("TRN2", target_bir_lowering=False)
    xin = nc.dram_tensor("xin", [D, NTOK], F32, kind="ExternalInput").ap()
    g = nc.dram_tensor("g", [128, KC], F32, kind="ExternalInput").ap()
    xout = nc.dram_tensor("xout", [D, NTOK], F32, kind="ExternalOutput").ap()
    with contextlib.ExitStack() as stack:
        p = Prog(nc, stack)
        c = alloc_common(nc, stack, p)
        gcol = c.sb("gcol", [128, KC], F32)
        p.dma("sync", gcol[:], g[:, :], inc=c.setup_d)
        for tt in range(NTT):
            t0 = tt * TT

            def out_fn(kc, s, waits, t0=t0):
                d = p.op("vector", lambda e: e.scalar_tensor_tensor(out=c.xs[:, s, :], in0=c.xs[:, s, :],
                                                                    scalar=gcol[:, kc:kc + 1], in1=c.rstd[:],
                                                                    op0=ALU.mult, op1=ALU.mult), waits=waits, inc=c.dve_h)
                sv = p.dma("sync", xout[kc * 128:(kc + 1) * 128, t0:t0 + TT], c.xs[:, s, :], waits=[(c.dve_h, d)],
                           inc=c.xs_st[s])
                c.xs_cond[s] = [(c.xs_st[s], sv)]

            norm_stats_and_h(c, xin, gcol, tt, out_fn=out_fn)
        finish(c)
        p.emit()
    return nc


_PROGS = {}


def _prog(name):
    if name not in _PROGS:
        _PROGS[name] = {"ffn": build_ffn, "mixa": build_mixa, "mixb": build_mixb, "norm": build_norm}[name]()
    return _PROGS[name]


def _run(name, in_maps):
    res = run_bass_kernel_spmd(_prog(name), in_maps, core_ids=list(range(NCORES)))
    return res.results


def run_ffn(xTs, l, P, pre):
    g = col16(P[pre + "_norm"][l])
    maps = [{"xin": xTs[c], "g": g, "wg": P[pre + "_w_gate"][l], "wu": P[pre + "_w_up"][l], "wd": P[pre + "_w_down"][l]}
            for c in range(NCORES)]
    return [r["xout"] for r in _run("ffn", maps)]


def run_mixer(xTs, l, P):
    ra = _run("mixa", [mixa_inputs(xTs[c], c % 2, l, P) for c in range(NCORES)])
    ona = np.ascontiguousarray(P["out_norm"][l][0:1536].reshape(12, 128).T).astype(np.float32)
    maps = []
    for c in range(NCORES):
        half = c % 2
        pc = c - 1 if half == 1 else c
        m = {"qT": ra[c]["qT"], "kT_loc": ra[c]["kT"], "kT_prev": ra[pc]["kT"], "v_loc": ra[c]["v"], "v_prev": ra[pc]["v"],
             "cneg_loc": ra[c]["cneg"], "cneg_prev": ra[pc]["cneg"], "s_init": ra[pc]["ret_S"], "ret_o": ra[c]["ret_o"],
             "ret_qg": ra[c]["ret_qg"], "rgs": ra[c]["rgs"], "ytg": ra[c]["ytg"], "ona": ona, "w_out": P["w_out"][l],
             "xin": xTs[c]}
        m.update(mixb_consts(half))
        maps.append(m)
    return [r["xout"] for r in _run("mixb", maps)]


def kernel_unfused(**inputs):
    P = {k: np.asarray(v) for k, v in inputs.items()}
    x = P["x"]
    xTs = [np.ascontiguousarray(x[c // 2, (c % 2) * NTOK:(c % 2 + 1) * NTOK, :].T) for c in range(NCORES)]
    for l in range(DEPTH):
        xTs = run_ffn(xTs, l, P, "ffn1")
        xTs = run_mixer(xTs, l, P)
        xTs = run_ffn(xTs, l, P, "ffn2")
    g = col16(P["final_norm"])
    outs = [r["xout"] for r in _run("norm", [{"xin": xTs[c], "g": g} for c in range(NCORES)])]
    out = np.empty_like(x)
    for c in range(NCORES):
        out[c // 2, (c % 2) * NTOK:(c % 2 + 1) * NTOK, :] = outs[c].T
    return out


PAIRS = [[0, 1], [2, 3], [4, 5], [6, 7]]
WSHAPES = {"ffn1_w_gate": [DEPTH, D, DFF], "ffn1_w_up": [DEPTH, D, DFF], "ffn1_w_down": [DEPTH, DFF, D],
           "w_in": [DEPTH, D, INCOLS], "w_out": [DEPTH, D, D],
           "ffn2_w_gate": [DEPTH, D, DFF], "ffn2_w_up": [DEPTH, D, DFF], "ffn2_w_down": [DEPTH, DFF, D]}
SMALL = {"g_ffn1": [DEPTH, 128, KC], "g_mix": [DEPTH, 128, KC], "g_ffn2": [DEPTH, 128, KC], "g_fin": [128, KC],
         "bf": [DEPTH, 8, 1], "wst": [DEPTH, 128, 512], "lnb3": [DEPTH, 128, 1536], "ong": [DEPTH, 128, 4],
         "ona": [DEPTH, 128, 12], "tab": [NTOK, 268], "maskr": [128, 512], "triu": [128, 128], "ident": [128, 128],
         "cmask": [128, 896], "selh": [8, 1024], "ident8": [8, 8], "pmask": [128, 1], "sflag": [128, 1]}


def build_fused(depth=DEPTH, phases="fmxbF"):
    nc = bass.Bass("TRN2", target_bir_lowering=False)
    di = lambda name, shape: nc.dram_tensor(name, shape, F32, kind="ExternalInput").ap()
    it = lambda name, shape, dt: nc.dram_tensor(name, shape, dt, kind="Internal").ap()
    x_in = di("x", [D, NTOK])
    W = {k: di(k, [depth] + s[1:]) for k, s in WSHAPES.items()}
    S = {k: di(k, ([depth] + s[1:]) if len(s) == 3 else s) for k, s in SMALL.items()}
    out = nc.dram_tensor("out", [D, NTOK], F32, kind="ExternalOutput").ap()
    xres = it("xres", [D, NTOK], F32)
    qT = it("qT", [1024, NTOK], BF16)
    xk = [it("xk%d" % i, [512, NTOK], BF16) for i in range(2)]
    xv = [it("xv%d" % i, [1024, 1024], BF16) for i in range(2)]
    xc = it("xc", [8, NTOK], F32)
    xs_ = it("xs_", [128, 512], F32)
    gk = [it("gk%d" % i, [1024, NTOK], BF16) for i in range(2)]
    gv_ = [it("gv%d" % i, [2048, 1024], BF16) for i in range(2)]
    gc = it("gc", [16, NTOK], F32)
    gs = it("gs", [256, 512], F32)
    ret_o = it("ret_o", [512, NTOK], F32)
    ret_qg = it("ret_qg", [512, NTOK], BF16)
    rgs = it("rgs", [512, NTOK], F32)
    ytg = it("ytg", [512, NTOK], BF16)
    yT = it("yT", [1536, NTOK], BF16)
    with contextlib.ExitStack() as gstack:
        p = Prog(nc, gstack)

        def ffn_phase(xin, xout, g_ap, wg, wu, wd):
            with contextlib.ExitStack() as st:
                c = alloc_common(nc, st, p)
                alloc_ffn(c)
                gcol = c.sb("gcol", [128, KC], F32)
                p.dma("sync", gcol[:], g_ap, inc=c.setup_d)
                ffn_body(c, xin, xout, gcol, wg, wu, wd)
                finish(c)
                p.barrier()
                p.emit()

        def mixa_phase(l):
            with contextlib.ExitStack() as st:
                c = alloc_common(nc, st, p, tt=TA, nps=6, stat_bank=5)
                din = {"g": S["g_mix"][l], "bf": S["bf"][l], "maskr": S["maskr"][:, :], "wst": S["wst"][l],
                       "triu": S["triu"][:, :], "lnb3": S["lnb3"][l], "ong": S["ong"][l], "ident": S["ident"][:, :]}
                io = {"tab": S["tab"], "qT": qT, "kT": RowSplit(xk), "v": RowSplit(xv), "cneg_o": xc,
                      "ret_o": ret_o, "ret_qg": ret_qg, "ret_S": xs_, "rgs": rgs, "ytg": ytg}
                gcol = mixa_setup(c, din, io)
                mixa_body(c, xres, gcol, W["w_in"][l], io)
                p.barrier()
                p.emit()

        def exchange_phase():
            cc = p.sem("ccsem")
            for a_, b_ in ((xk[0], gk[0]), (xk[1], gk[1]), (xv[0], gv_[0]), (xv[1], gv_[1]), (xc, gc), (xs_, gs)):
                p.op("gpsimd", lambda e, a_=a_, b_=b_: e.collective_compute("AllGather", ALU.bypass, replica_groups=PAIRS,
                                                                            ins=[a_], outs=[b_]), inc=cc, k=1)
            p.barrier()
            p.emit()

        def mixb_phase(l):
            with contextlib.ExitStack() as st:
                io = {"qT": qT, "kT_loc": RowSplit(xk), "kT_prev": RowSplit([gk[0][0:512, :], gk[1][0:512, :]]),
                      "v_loc": RowSplit(xv), "v_prev": RowSplit([gv_[0][0:1024, :], gv_[1][0:1024, :]]),
                      "cneg_loc": xc, "cneg_prev": gc[0:8, :], "s_init": gs[0:128, :], "ret_o": ret_o,
                      "ret_qg": ret_qg, "rgs": rgs, "ytg": ytg, "cmask": S["cmask"], "selh": S["selh"],
                      "ident8": S["ident8"], "pmask": S["pmask"], "sflag": S["sflag"], "ona": S["ona"][l],
                      "w_out": W["w_out"][l], "xin": xres, "yT": yT, "xout": xres}
                mixb_body(nc, st, p, io)
                p.barrier()
                p.emit()

        def norm_phase():
            with contextlib.ExitStack() as st:
                c = alloc_common(nc, st, p)
                gcol = c.sb("gcol", [128, KC], F32)
                p.dma("sync", gcol[:], S["g_fin"][:, :], inc=c.setup_d)
                for tt in range(NTT):
                    t0 = tt * TT

                    def out_fn(kc, s, waits, t0=t0):
                        d = p.op("vector", lambda e: e.scalar_tensor_tensor(out=c.xs[:, s, :], in0=c.xs[:, s, :],
                                                                            scalar=gcol[:, kc:kc + 1], in1=c.rstd[:],
                                                                            op0=ALU.mult, op1=ALU.mult), waits=waits,
                                 inc=c.dve_h)
                        sv = p.dma("sync", out[kc * 128:(kc + 1) * 128, t0:t0 + TT], c.xs[:, s, :], waits=[(c.dve_h, d)],
                                   inc=c.xs_st[s])
                        c.xs_cond[s] = [(c.xs_st[s], sv)]

                    norm_stats_and_h(c, xres, gcol, tt, out_fn=out_fn)
                finish(c)
                p.barrier()
                p.emit()

        for l in range(depth):
            if "f" in phases:
                ffn_phase(x_in if l == 0 else xres, xres, S["g_ffn1"][l], W["ffn1_w_gate"][l], W["ffn1_w_up"][l],
                          W["ffn1_w_down"][l])
            if "m" in phases:
                mixa_phase(l)
            if "x" in phases:
                exchange_phase()
            if "b" in phases:
                mixb_phase(l)
            if "F" in phases:
                ffn_phase(xres, xres, S["g_ffn2"][l], W["ffn2_w_gate"][l], W["ffn2_w_up"][l], W["ffn2_w_down"][l])
        norm_phase()
    return nc


def fused_inputs(P, core):
    half = core % 2
    x = P["x"]
    m = {"x": np.ascontiguousarray(x[core // 2, half * NTOK:(half + 1) * NTOK, :].T)}
    for k in WSHAPES:
        m[k] = P[k]
    return m


def fused_shared(P):
    sh = {}
    sh["g_ffn1"] = np.stack([col16(P["ffn1_norm"][l]) for l in range(DEPTH)])
    sh["g_mix"] = np.stack([col16(P["mix_norm"][l]) for l in range(DEPTH)])
    sh["g_ffn2"] = np.stack([col16(P["ffn2_norm"][l]) for l in range(DEPTH)])
    sh["g_fin"] = col16(P["final_norm"])
    sh["bf"] = np.ascontiguousarray(P["fox_b_f"].reshape(DEPTH, 8, 1)).astype(np.float32)
    sh["wst"] = np.ascontiguousarray(np.transpose(P["gmlp_w_s"], (0, 3, 1, 2)).reshape(DEPTH, 128, 512)).astype(np.float32)
    sh["lnb3"] = np.ascontiguousarray(np.stack([np.concatenate(
        [np.broadcast_to(P["gmlp_ln_g"][l][None, :], (128, 512)), np.broadcast_to(P["gmlp_ln_b"][l][None, :], (128, 512)),
         np.broadcast_to(P["gmlp_b_s"][l].reshape(1, 512), (128, 512))], axis=1) for l in range(DEPTH)])).astype(np.float32)
    sh["ong"] = np.ascontiguousarray(np.stack([P["out_norm"][l][1536:2048].reshape(4, 128).T for l in range(DEPTH)])).astype(np.float32)
    sh["ona"] = np.ascontiguousarray(np.stack([P["out_norm"][l][0:1536].reshape(12, 128).T for l in range(DEPTH)])).astype(np.float32)
    return sh


_FUSED = {}


def kernel(**inputs):
    P = {k: np.asarray(v) for k, v in inputs.items()}
    if "nc" not in _FUSED:
        _FUSED["nc"] = build_fused()
    sh = fused_shared(P)
    maps = []
    for c in range(NCORES):
        half = c % 2
        m = fused_inputs(P, c)
        m.update(sh)
        hc = host_consts(half)
        m.update({"tab": hc["tab"], "maskr": hc["maskr"], "triu": hc["triu"], "ident": hc["ident"]})
        m.update(mixb_consts(half))
        maps.append(m)
    res = run_bass_kernel_spmd(_FUSED["nc"], maps, core_ids=list(range(NCORES)))
    x = P["x"]
    outp = np.empty_like(x)
    for c in range(NCORES):
        outp[c // 2, (c % 2) * NTOK:(c % 2 + 1) * NTOK, :] = res.results[c]["out"].T
    return outp
```

```python
import contextlib
import numpy as np
import concourse.bass as bass
import concourse.mybir as mybir
from concourse.bass_utils import run_bass_kernel_spmd

F32 = mybir.dt.float32
BF16 = mybir.dt.bfloat16
AF = mybir.ActivationFunctionType
ALU = mybir.AluOpType

D = 2048
NTOK = 2048
DFF = 5632
NCORES = 8
DEPTH = 4
EPS = 1e-6
KC = D // 128
TT = 1024
NTT = NTOK // TT
FH = 22
INCOLS = 6152


class Cnt:
    def __init__(self, h):
        self.h = h
        self.v = 0


class Prog:
    ENGS = ("sync", "scalar", "vector", "gpsimd", "tensor")

    def __init__(self, nc, stack):
        self.nc = nc
        self.stack = stack
        self.q = {e: [] for e in self.ENGS}
        self.waited = {e: {} for e in self.ENGS}
        self.cache = {}

    def sem(self, name):
        if name not in self.cache:
            self.cache[name] = Cnt(self.stack.enter_context(self.nc.semaphore(name)))
        return self.cache[name]

    def barrier(self):
        for eng in self.ENGS:
            self.op(eng, None, waits=[(c, c.v) for c in self.cache.values()])

    def sems(self, name, n):
        return [self.sem("%s%d" % (name, i)) for i in range(n)]

    def op(self, eng, fn, waits=(), inc=None, k=1):
        ws = []
        for (c, v) in waits:
            if v <= 0:
                continue
            key = id(c)
            if self.waited[eng].get(key, 0) >= v:
                continue
            self.waited[eng][key] = v
            ws.append((c.h, v))
        tgt = None
        if inc is not None:
            inc.v += k
            tgt = inc.v
        self.q[eng].append((ws, fn, inc.h if inc is not None else None, k))
        return tgt

    def dma(self, eng, out, in_, waits=(), inc=None):
        return self.op(eng, lambda e: e.dma_start(out=out, in_=in_), waits, inc, 16)

    def wait_only(self, eng, waits):
        self.q[eng].append(([(c.h, v) for (c, v) in waits if v > 0], None, None, 0))

    def emit(self):
        with self.nc.Block() as block:
            for name in self.ENGS:
                q = self.q[name]

                def body(e, q=q):
                    for ws, fn, inc, k in q:
                        for (h, v) in ws:
                            e.wait_ge(h, v)
                        if fn is None:
                            continue
                        ins = fn(e)
                        if inc is not None:
                            ins.then_inc(inc, k)

                getattr(block, name)(body)
        self.q = {e: [] for e in self.ENGS}


class Ctx:
    pass


_UID = [0]


def _uid():
    _UID[0] += 1
    return _UID[0]


def alloc_common(nc, stack, p, tt=TT, nps=8, stat_bank=6):
    c = Ctx()
    uid = _uid()
    c.nc = nc
    c.p = p
    c.TT = tt
    c.NSEG = tt // 512
    c.stat_bank = stat_bank
    sb = lambda name, shape, dt: stack.enter_context(nc.sbuf_tensor("sb%d_%s" % (uid, name), shape, dt))
    c.sb = sb
    c.uid = uid
    c.stack = stack
    c.ones = sb("ones", [128, 128], F32)
    c.xs = sb("xs", [128, 3, tt], F32)
    c.sq = sb("sq", [128, 2, tt], F32)
    c.rstd = sb("rstd", [128, tt], F32)
    c.h = sb("h", [128, KC, tt], BF16)
    c.ps = stack.enter_context(nc.psum_tensor("ps%d" % uid, [128, nps, 512], F32))
    c.xs_full = p.sems("xsfull", 3)
    c.xs_st = p.sems("xsst", 3)
    c.xs_cond = [[], [], []]
    c.xs_n = 0
    c.act_sq = p.sem("actsq")
    c.pe_st = p.sem("pest")
    c.dve_m = p.sem("dvem")
    c.act_m = p.sem("actm")
    c.dve_h = p.sem("dveh")
    c.setup_v = p.sem("setupv")
    c.setup_d = p.sem("setupd")
    p.op("vector", lambda e: e.memset(c.ones[:], 1.0), inc=c.setup_v)
    c.sq_n = 0
    c.h_free = []
    c.ps_free_waits = []
    return c


def xs_acquire(c):
    s = c.xs_n % 3
    c.xs_n += 1
    return s, list(c.xs_cond[s])


def norm_stats_and_h(c, xsrc, gcol, tt, out_fn=None):
    p = c.p
    TT = c.TT
    t0 = tt * TT
    SB = c.stat_bank
    for kc in range(KC):
        s, w = xs_acquire(c)
        full = p.dma("sync", c.xs[:, s, :], xsrc[kc * 128:(kc + 1) * 128, t0:t0 + TT], waits=w, inc=c.xs_full[s])
        q = c.sq_n % 2
        c.sq_n += 1
        a = p.op("scalar",
                 lambda e, s=s, q=q: e.activation(out=c.sq[:, q, :], in_=c.xs[:, s, :], func=AF.Square),
                 waits=[(c.xs_full[s], full), (c.pe_st, c.pe_st.v - 1)], inc=c.act_sq)
        c.xs_cond[s] = [(c.act_sq, a)]
        extra = list(c.ps_free_waits) if kc == 0 else []
        for sg_ in range(c.NSEG):
            p.op("tensor",
                 lambda e, q=q, kc=kc, sg_=sg_: e.matmul(c.ps[:, SB + sg_, :], lhsT=c.ones[:],
                                                        rhs=c.sq[:, q, sg_ * 512:(sg_ + 1) * 512],
                                                        start=(kc == 0), stop=(kc == KC - 1)),
                 waits=([(c.act_sq, a), (c.setup_v, c.setup_v.v)] + extra) if sg_ == 0 else [],
                 inc=(c.pe_st if sg_ == c.NSEG - 1 else None))
    st_done = c.pe_st.v
    psv = c.ps[:, SB:SB + c.NSEG, :]
    rv = c.rstd[:].rearrange("p (a b) -> p a b", a=c.NSEG)
    d1 = p.op("vector",
              lambda e: e.tensor_scalar(out=rv, in0=psv, scalar1=1.0 / D, scalar2=EPS, op0=ALU.mult, op1=ALU.add),
              waits=[(c.pe_st, st_done), (c.dve_h, c.dve_h.v)], inc=c.dve_m)
    c.ps_free_waits = [(c.dve_m, d1)]
    a1 = p.op("scalar", lambda e: e.activation(out=c.rstd[:], in_=c.rstd[:], func=AF.Sqrt),
              waits=[(c.dve_m, d1)], inc=c.act_m)
    d2 = p.op("vector", lambda e: e.reciprocal(out=c.rstd[:], in_=c.rstd[:]),
              waits=[(c.act_m, a1)], inc=c.dve_m)
    for kc in range(KC):
        s, w = xs_acquire(c)
        full = p.dma("sync", c.xs[:, s, :], xsrc[kc * 128:(kc + 1) * 128, t0:t0 + TT], waits=w, inc=c.xs_full[s])
        if out_fn is None:
            waits = [(c.xs_full[s], full), (c.dve_m, d2), (c.setup_d, c.setup_d.v)]
            if kc == 0:
                waits += c.h_free
            hv = p.op("vector",
                 lambda e, s=s, kc=kc: e.scalar_tensor_tensor(out=c.h[:, kc, :], in0=c.xs[:, s, :],
                                                              scalar=gcol[:, kc:kc + 1], in1=c.rstd[:],
                                                              op0=ALU.mult, op1=ALU.mult),
                 waits=waits, inc=c.dve_h)
            c.xs_cond[s] = [(c.dve_h, hv)]
        else:
            out_fn(kc, s, [(c.xs_full[s], full), (c.dve_m, d2), (c.setup_d, c.setup_d.v)])
    return c.dve_h.v


def alloc_ffn(c):
    p = c.p
    sb = c.sb
    c.hid = sb("hid", [128, FH, TT], BF16)
    c.sg = sb("sg", [128, 2, 512], F32)
    c.wgu = sb("wgu", [128, 2, 2, KC, 256], BF16)
    c.wd = sb("wd", [128, 2, FH, 256], BF16)
    c.wgu_full = p.sems("wgufull", 2)
    c.wd_full = p.sems("wdfull", 2)
    c.pe_gu = p.sem("pegu")
    c.act_sg = p.sem("actsg")
    c.dve_hid = p.sem("dvehid")
    c.pe_dn = p.sem("pedn")
    c.dve_res = p.sem("dveres")
    c.n_panel = 0
    c.n_gu = c.pe_gu.v
    c.n_dpanel = 0
    c.n_dn = c.pe_dn.v
    c.panel_done = {}
    c.dpanel_done = {}
    c.hid_free = []


def ffn_body(c, xin, xout, gcol, wg, wu, wd):
    p = c.p
    wgv = wg.rearrange("(kc p) f -> p kc f", p=128)
    wuv = wu.rearrange("(kc p) f -> p kc f", p=128)
    wdv = wd.rearrange("(fc p) d -> p fc d", p=128)
    for tt in range(NTT):
        t0 = tt * TT
        h_ready = norm_stats_and_h(c, xin, gcol, tt)
        for hf in range(2):
            for pn in range(FH // 2):
                col0 = (hf * FH + pn * 2) * 128
                npn = c.n_panel
                b = npn % 2
                c.n_panel += 1
                wfree = [(c.pe_gu, c.panel_done[npn - 2])] if npn >= 2 else []
                p.dma("gpsimd", c.wgu[:, b, 0, :, :], wgv[:, :, col0:col0 + 256], waits=wfree, inc=c.wgu_full[b])
                wl = p.dma("gpsimd", c.wgu[:, b, 1, :, :], wuv[:, :, col0:col0 + 256], waits=wfree, inc=c.wgu_full[b])
                for jj in range(2):
                    j = pn * 2 + jj
                    for th in range(2):
                        n = c.n_gu
                        c.n_gu += 1
                        gb = n % 2
                        ub = 2 + n % 2
                        for kc in range(KC):
                            waits = []
                            if kc == 0:
                                waits = [(c.wgu_full[b], wl), (c.dve_h, h_ready), (c.act_sg, n - 1)]
                            p.op("tensor",
                                 lambda e, b=b, kc=kc, jj=jj, th=th, gb=gb: e.matmul(
                                     c.ps[:, gb, :], lhsT=c.wgu[:, b, 0, kc, jj * 128:(jj + 1) * 128],
                                     rhs=c.h[:, kc, th * 512:(th + 1) * 512], start=(kc == 0), stop=(kc == KC - 1)),
                                 waits=waits)
                        for kc in range(KC):
                            waits = []
                            if kc == 0:
                                waits = [(c.dve_hid, n - 1)]
                            last = (kc == KC - 1)
                            p.op("tensor",
                                 lambda e, b=b, kc=kc, jj=jj, th=th, ub=ub: e.matmul(
                                     c.ps[:, ub, :], lhsT=c.wgu[:, b, 1, kc, jj * 128:(jj + 1) * 128],
                                     rhs=c.h[:, kc, th * 512:(th + 1) * 512], start=(kc == 0), stop=(kc == KC - 1)),
                                 waits=waits, inc=(c.pe_gu if last else None))
                        gu = c.pe_gu.v
                        a = p.op("scalar",
                                 lambda e, n=n, gb=gb: e.activation(out=c.sg[:, n % 2, :], in_=c.ps[:, gb, :], func=AF.Silu),
                                 waits=[(c.pe_gu, gu), (c.dve_hid, n - 1)], inc=c.act_sg)
                        waits = [(c.act_sg, a), (c.pe_gu, gu)]
                        if j == 0 and th == 0:
                            waits += c.hid_free
                        p.op("vector",
                             lambda e, n=n, ub=ub, j=j, th=th: e.tensor_tensor(
                                 out=c.hid[:, j, th * 512:(th + 1) * 512], in0=c.sg[:, n % 2, :], in1=c.ps[:, ub, :],
                                 op=ALU.mult),
                             waits=waits, inc=c.dve_hid)
                c.panel_done[npn] = c.pe_gu.v
            if hf == 1:
                c.h_free = [(c.pe_gu, c.pe_gu.v)]
            hid_ready = c.dve_hid.v
            xsrc = xin if hf == 0 else xout
            for pd in range(8):
                col0 = pd * 256
                npd = c.n_dpanel
                b = npd % 2
                c.n_dpanel += 1
                wl = p.dma("gpsimd", c.wd[:, b, :, :], wdv[:, hf * FH:(hf + 1) * FH, col0:col0 + 256],
                           waits=([(c.pe_dn, c.dpanel_done[npd - 2])] if npd >= 2 else []), inc=c.wd_full[b])
                for ii in range(2):
                    i = pd * 2 + ii
                    s, w = xs_acquire(c)
                    full = p.dma("sync", c.xs[:, s, :], xsrc[i * 128:(i + 1) * 128, t0:t0 + TT], waits=w,
                                 inc=c.xs_full[s])
                    for th in range(2):
                        n = c.n_dn
                        c.n_dn += 1
                        ob = 4 + n % 2
                        for f in range(FH):
                            waits = []
                            if f == 0:
                                waits = [(c.wd_full[b], wl), (c.dve_hid, hid_ready), (c.dve_res, n - 1)]
                            last = (f == FH - 1)
                            p.op("tensor",
                                 lambda e, b=b, f=f, ii=ii, th=th, ob=ob: e.matmul(
                                     c.ps[:, ob, :], lhsT=c.wd[:, b, f, ii * 128:(ii + 1) * 128],
                                     rhs=c.hid[:, f, th * 512:(th + 1) * 512], start=(f == 0), stop=(f == FH - 1)),
                                 waits=waits, inc=(c.pe_dn if last else None))
                        dn = c.pe_dn.v
                        r = p.op("vector",
                                 lambda e, s=s, th=th, ob=ob: e.scalar_tensor_tensor(
                                     out=c.xs[:, s, th * 512:(th + 1) * 512], in0=c.ps[:, ob, :], scalar=0.5,
                                     in1=c.xs[:, s, th * 512:(th + 1) * 512], op0=ALU.mult, op1=ALU.add),
                                 waits=[(c.pe_dn, dn), (c.xs_full[s], full)], inc=c.dve_res)
                    sv = p.dma("sync", xout[i * 128:(i + 1) * 128, t0:t0 + TT], c.xs[:, s, :],
                               waits=[(c.dve_res, r)], inc=c.xs_st[s])
                    c.xs_cond[s] = [(c.xs_st[s], sv)]
                c.dpanel_done[npd] = c.pe_dn.v
            c.hid_free = [(c.pe_dn, c.pe_dn.v)]


def finish(c):
    p = c.p
    p.wait_only("sync", [(c.xs_st[s], c.xs_st[s].v) for s in range(3)])


def build_ffn():
    nc = bass.Bass("TRN2", target_bir_lowering=False)
    xin = nc.dram_tensor("xin", [D, NTOK], F32, kind="ExternalInput").ap()
    g = nc.dram_tensor("g", [128, KC], F32, kind="ExternalInput").ap()
    wg = nc.dram_tensor("wg", [D, DFF], F32, kind="ExternalInput").ap()
    wu = nc.dram_tensor("wu", [D, DFF], F32, kind="ExternalInput").ap()
    wd = nc.dram_tensor("wd", [DFF, D], F32, kind="ExternalInput").ap()
    xout = nc.dram_tensor("xout", [D, NTOK], F32, kind="ExternalOutput").ap()
    with contextlib.ExitStack() as stack:
        p = Prog(nc, stack)
        c = alloc_common(nc, stack, p)
        alloc_ffn(c)
        gcol = c.sb("gcol", [128, KC], F32)
        p.dma("sync", gcol[:], g[:, :], inc=c.setup_d)
        ffn_body(c, xin, xout, gcol, wg, wu, wd)
        finish(c)
        p.emit()
    return nc


FOX_SCALE = 128.0 ** -0.5
GAM = [1.0 - 2.0 ** -(5 + h) for h in range(4)]
GAM64 = [g ** 64 for g in GAM]
TA = 512
NBLK = TA // 128
AX = mybir.AxisListType


class RowSplit:
    def __init__(self, parts):
        self.parts = parts
        self.h = parts[0].shape[0]

    def __getitem__(self, key):
        rs, cs = key
        i = rs.start // self.h
        assert (rs.stop - 1) // self.h == i
        return self.parts[i][rs.start - i * self.h:rs.stop - i * self.h, cs]


def _parts(x):
    return x.parts if isinstance(x, RowSplit) else [x]


class Slots:
    def __init__(self, p, name, n):
        self.sem = p.sems(name, n)
        self.cond = [[] for _ in range(n)]
        self.i = 0
        self.n = n

    def acquire(self):
        s = self.i % self.n
        self.i += 1
        return s, list(self.cond[s])


def mixa_body(c, xin, gcol, w_in, io):
    p = c.p
    nc = c.nc
    sb = c.sb
    winv = w_in.rearrange("(kc p) f -> p kc f", p=128)
    tabv = io["tab"].rearrange("(b p) f -> p b f", p=128)
    wp = sb("wp", [128, 2, 8192], BF16)
    wps = Slots(p, "wps", 2)
    wpfm = lambda b: wp[:, b, 0:4096].rearrange("p (k c) -> p k c", c=256)
    wptm = lambda b: wp[:, b, :].rearrange("p (k c) -> p k c", c=512)
    wpfz = lambda b: wp[:, b, 0:128].rearrange("p (k c) -> p k c", c=8)
    tabt = sb("tabt", [128, 2, NBLK, 268], F32)
    tabs = Slots(p, "tabs", 2)
    stg16 = sb("stg16", [128, 4, 512], BF16)
    st16 = Slots(p, "st16", 4)
    stg32 = sb("stg32", [128, 2, 512], F32)
    st32 = Slots(p, "st32", 2)
    u = sb("u", [128, 4, TA], F32)
    rv = sb("rv", [128, NBLK, 512], BF16)
    kr = sb("kr", [128, NBLK, 512], BF16)
    kz = sb("kz", [128, NBLK, 512], BF16)
    qx = sb("qx", [128, NBLK, 512], BF16)
    qg = sb("qg", [128, NBLK, 512], BF16)
    vln = sb("vln", [128, NBLK, 512], BF16)
    rot = sb("rot", [128, 2, 2, 512], F32)
    rots = Slots(p, "rots", 2)
    st = sb("lnst", [128, 24], F32)
    fzt = sb("fzt", [8, 512], F32)
    onesr = sb("onesr", [8, 512], F32)
    cneg = sb("cneg", [8, NTOK], F32)
    S32 = sb("S32", [128, 512], F32)
    Sb = sb("Sb", [128, 2 * NBLK, 512], BF16)
    krT = sb("krT", [128, 512], BF16)
    qxT = sb("qxT", [128, 512], BF16)
    sm = sb("sm", [128, 512], BF16)
    y1 = sb("y1", [128, 512], F32)
    y2 = sb("y2", [128, 512], F32)
    pst = c.stack.enter_context(nc.psum_tensor("pst%d" % c.uid, [128, 2, 1024], BF16))
    c.pe = p.sem("pe")
    c.act = p.sem("act")
    c.dve = p.sem("dve")
    misc = p.sem("miscst")
    pj_cond = [[], []]
    kv_cond = [[], []]
    b4_cond = []
    pst_cond = [[], []]
    pj_n = [0]
    kv_n = [0]
    u_free = []
    ret_free = []
    vln_free = []
    p.op("vector", lambda e: e.memset(onesr[:], 1.0), inc=c.setup_v)
    p.op("vector", lambda e: e.memset(S32[:], 0.0), inc=c.setup_v)
    setupw = [(c.setup_d, c.setup_d.v), (c.setup_v, c.setup_v.v)]

    def V4(ap):
        return ap.rearrange("p (a b) -> p a b", a=4)

    def bc(ap4):
        return ap4.unsqueeze(2).to_broadcast([128, 4, 128])

    def bh(ap128):
        return ap128.unsqueeze(1).to_broadcast([128, 4, 128])

    def bh64(ap64):
        return ap64.unsqueeze(1).to_broadcast([128, 4, 64])

    def proj(b, waits, lhs_fn, rhs_fn, out_fn):
        n = pj_n[0]
        pj_n[0] += 1
        bank = n % 2
        for kc in range(KC):
            p.op("tensor",
                 lambda e, kc=kc: e.matmul(out_fn(bank), lhsT=lhs_fn(kc), rhs=rhs_fn(kc), start=(kc == 0),
                                           stop=(kc == KC - 1)),
                 waits=(list(waits) + pj_cond[bank]) if kc == 0 else [], inc=(c.pe if kc == KC - 1 else None))
        return bank, c.pe.v

    def store16(src_fn, dst_ap, waits, eng_op):
        s, w = st16.acquire()
        a = p.op("scalar", lambda e: eng_op(e, stg16[:, s, :]), waits=list(waits) + w, inc=c.act)
        sv = p.dma("sync", dst_ap, src_fn(stg16[:, s, :]), waits=[(c.act, a)], inc=st16.sem[s])
        st16.cond[s] = [(st16.sem[s], sv)]
        return a

    for tt in range(NTOK // TA):
        t0 = tt * TA
        h_ready = norm_stats_and_h(c, xin, gcol, tt)
        hw = [(c.dve_h, h_ready)]
        ts_, w = tabs.acquire()
        tk = p.dma("sync", tabt[:, ts_], tabv[:, tt * NBLK:(tt + 1) * NBLK, :], waits=w, inc=tabs.sem[ts_])
        tabw = [(tabs.sem[ts_], tk)]
        for name, cbase, ncol in (("fq", 0, 1024), ("fk", 1024, 1024), ("rg", 4616, 512), ("gu", 5128, 512)):
            for pn in range(ncol // 256):
                col0 = cbase + pn * 256
                b, w = wps.acquire()
                t = p.dma("gpsimd", wpfm(b), winv[:, :, col0:col0 + 256], waits=w, inc=wps.sem[b])
                for jj in range(2):
                    ch = pn * 2 + jj
                    bank, pt = proj(b, [(wps.sem[b], t)] + hw,
                                    lambda kc, b=b, jj=jj: wpfm(b)[:, kc, jj * 128:(jj + 1) * 128],
                                    lambda kc: c.h[:, kc, :], lambda bank: c.ps[:, bank, :])
                    pw = [(c.pe, pt)]
                    if name == "fq":
                        a = store16(lambda s_: s_, io["qT"][ch * 128:(ch + 1) * 128, t0:t0 + TA], pw,
                                    lambda e, o, bank=bank: e.mul(out=o, in_=c.ps[:, bank, :], mul=FOX_SCALE))
                    elif name == "fk":
                        a = store16(lambda s_: s_, io["kT"][ch * 128:(ch + 1) * 128, t0:t0 + TA], pw,
                                    lambda e, o, bank=bank: e.copy(out=o, in_=c.ps[:, bank, :]))
                    elif name == "rg":
                        s, w2 = st32.acquire()
                        a = p.op("scalar", lambda e, s=s, bank=bank: e.activation(out=stg32[:, s, :], in_=c.ps[:, bank, :],
                                                                                func=AF.Silu),
                                 waits=pw + w2, inc=c.act)
                        sv = p.dma("sync", io["rgs"][ch * 128:(ch + 1) * 128, t0:t0 + TA], stg32[:, s, :],
                                   waits=[(c.act, a)], inc=st32.sem[s])
                        st32.cond[s] = [(st32.sem[s], sv)]
                    else:
                        a = p.op("scalar", lambda e, ch=ch, bank=bank: e.activation(out=u[:, ch, :], in_=c.ps[:, bank, :],
                                                                                  func=AF.Gelu_apprx_tanh),
                                 waits=pw + (u_free if ch == 0 else []), inc=c.act)
                    pj_cond[bank] = [(c.act, a)]
                wps.cond[b] = [(c.pe, pt)]
        b, w = wps.acquire()
        t = p.dma("gpsimd", wpfz(b), winv[:, :, 3072:3080], waits=w, inc=wps.sem[b])
        bank, pt = proj(b, [(wps.sem[b], t)] + hw, lambda kc, b=b: wpfz(b)[:, kc, :], lambda kc: c.h[:, kc, :],
                        lambda bank: c.ps[0:8, bank, :])
        wps.cond[b] = [(c.pe, pt)]
        a = p.op("scalar", lambda e, bank=bank: e.activation(out=fzt[:], in_=c.ps[0:8, bank, :], func=AF.Exp,
                                                             bias=io["negb"][:, 0:1], scale=-1.0),
                 waits=[(c.pe, pt), (c.dve, c.dve.v)] + setupw, inc=c.act)
        pj_cond[bank] = [(c.act, a)]
        a = p.op("scalar", lambda e: e.activation(out=fzt[:], in_=fzt[:], func=AF.Ln, bias=1.0), waits=[(c.act, a)],
                 inc=c.act)
        init = 0.0 if tt == 0 else cneg[:, t0 - 1:t0]
        p.op("vector", lambda e, init=init, t0=t0: e.tensor_tensor_scan(out=cneg[:, t0:t0 + TA], data0=onesr[:], data1=fzt[:],
                                                                 initial=init, op0=ALU.mult, op1=ALU.add),
             waits=[(c.act, a), (c.dve, c.dve.v)] + setupw, inc=c.dve)
        ready = {}
        for name, col0 in (("fv0", 2048), ("fv1", 2560), ("rv", 4104), ("rk", 3592), ("rq", 3080), ("gv", 5640)):
            b, w = wps.acquire()
            t = p.dma("gpsimd", wptm(b), winv[:, :, col0:col0 + 512], waits=w, inc=wps.sem[b])
            for tb in range(NBLK):
                bank, pt = proj(b, [(wps.sem[b], t)] + hw,
                                lambda kc, tb=tb: c.h[:, kc, tb * 128:(tb + 1) * 128],
                                lambda kc, b=b: wptm(b)[:, kc, :], lambda bank: c.ps[:, bank, :])
                pw = [(c.pe, pt)]
                psb = c.ps[:, bank, :]
                psv = V4(psb)
                r0 = t0 + tb * 128
                if name in ("fv0", "fv1"):
                    hc = 0 if name == "fv0" else 512
                    a = store16(lambda s_: s_, io["v"][r0:r0 + 128, hc:hc + 512], pw,
                                lambda e, o, psb=psb: e.copy(out=o, in_=psb))
                    pj_cond[bank] = [(c.act, a)]
                elif name == "rv":
                    a = p.op("scalar", lambda e, tb=tb, psb=psb: e.copy(out=rv[:, tb, :], in_=psb),
                             waits=pw + (ret_free if tb == 0 else []), inc=c.act)
                    pj_cond[bank] = [(c.act, a)]
                    ready[("rv", tb)] = [(c.act, a)]
                elif name in ("rk", "rq"):
                    rs, w2 = rots.acquire()
                    r1 = rot[:, rs, 0, :]
                    r2 = rot[:, rs, 1, :]
                    cosb = bh(tabt[:, ts_, tb, 0:128])
                    sina = bh64(tabt[:, ts_, tb, 128:192])
                    sinb = bh64(tabt[:, ts_, tb, 192:256])
                    p.op("vector", lambda e, psv=psv, r1=r1, cosb=cosb: e.tensor_tensor(out=V4(r1), in0=psv, in1=cosb,
                                                                                      op=ALU.mult),
                         waits=pw + w2 + tabw, inc=c.dve)
                    p.op("vector", lambda e, psv=psv, r2=r2, sina=sina: e.tensor_tensor(
                        out=V4(r2)[:, :, 0:64], in0=psv[:, :, 64:128], in1=sina, op=ALU.mult), inc=c.dve)
                    d = p.op("vector", lambda e, psv=psv, r2=r2, sinb=sinb: e.tensor_tensor(
                        out=V4(r2)[:, :, 64:128], in0=psv[:, :, 0:64], in1=sinb, op=ALU.mult), inc=c.dve)
                    pj_cond[bank] = [(c.dve, d)]
                    d = p.op("vector", lambda e, r1=r1, r2=r2: e.tensor_tensor(out=r1, in0=r1, in1=r2, op=ALU.add),
                             waits=[(c.dve, d)], inc=c.dve)
                    fw = ret_free if tb == 0 else []
                    if name == "rk":
                        a = p.op("scalar", lambda e, tb=tb, r1=r1: e.copy(out=kr[:, tb, :], in_=r1),
                                 waits=[(c.dve, d)] + fw, inc=c.act)
                        zb = bc(tabt[:, ts_, tb, 264:268])
                        d2 = p.op("vector", lambda e, tb=tb, r1=r1, zb=zb: e.tensor_tensor(out=V4(kz[:, tb, :]), in0=V4(r1),
                                                                                         in1=zb, op=ALU.mult),
                                  waits=[(c.dve, d)] + fw, inc=c.dve)
                        rots.cond[rs] = [(c.act, a), (c.dve, d2)]
                        ready[("kr", tb)] = [(c.act, a)]
                        ready[("kz", tb)] = [(c.dve, d2)]
                    else:
                        xb = bc(tabt[:, ts_, tb, 256:260])
                        gb_ = bc(tabt[:, ts_, tb, 260:264])
                        p.op("vector", lambda e, tb=tb, r1=r1, xb=xb: e.tensor_tensor(out=V4(qx[:, tb, :]), in0=V4(r1),
                                                                                    in1=xb, op=ALU.mult),
                             waits=[(c.dve, d)] + fw, inc=c.dve)
                        d2 = p.op("vector", lambda e, tb=tb, r1=r1, gb_=gb_: e.tensor_tensor(out=V4(qg[:, tb, :]),
                                                                                           in0=V4(r1), in1=gb_,
                                                                                           op=ALU.mult), inc=c.dve)
                        rots.cond[rs] = [(c.dve, d2)]
                        ready[("q", tb)] = [(c.dve, d2)]
                else:
                    rs, w2 = rots.acquire()
                    r1 = rot[:, rs, 0, :]
                    r2 = rot[:, rs, 1, :]
                    a = p.op("scalar", lambda e, psb=psb, r1=r1: e.activation(out=r1, in_=psb, func=AF.Gelu_apprx_tanh),
                             waits=pw + w2, inc=c.act)
                    pj_cond[bank] = [(c.act, a)]
                    a2 = p.op("scalar", lambda e, r1=r1, r2=r2: e.activation(out=r2, in_=r1, func=AF.Square),
                              waits=[(c.act, a)], inc=c.act)
                    d = p.op("vector", lambda e, r1=r1: e.tensor_reduce(out=st[:, 0:4], in_=V4(r1), axis=AX.X, op=ALU.add),
                             waits=[(c.act, a), (c.dve, c.dve.v)], inc=c.dve)
                    d = p.op("vector", lambda e, r2=r2: e.tensor_reduce(out=st[:, 4:8], in_=V4(r2), axis=AX.X, op=ALU.add),
                             waits=[(c.act, a2)], inc=c.dve)
                    d = p.op("vector", lambda e: e.tensor_scalar(out=st[:, 8:12], in0=st[:, 0:4], scalar1=1.0 / 128,
                                                                 scalar2=None, op0=ALU.mult),
                             waits=[(c.dve, d)], inc=c.dve)
                    d = p.op("vector", lambda e: e.tensor_tensor(out=st[:, 12:16], in0=st[:, 8:12], in1=st[:, 8:12],
                                                                 op=ALU.mult), waits=[(c.dve, d)], inc=c.dve)
                    d = p.op("vector", lambda e: e.scalar_tensor_tensor(out=st[:, 16:20], in0=st[:, 4:8], scalar=1.0 / 128,
                                                                        in1=st[:, 12:16], op0=ALU.mult,
                                                                        op1=ALU.subtract), waits=[(c.dve, d)], inc=c.dve)
                    d = p.op("vector", lambda e: e.tensor_scalar(out=st[:, 16:20], in0=st[:, 16:20], scalar1=EPS,
                                                                 scalar2=None, op0=ALU.add), waits=[(c.dve, d)], inc=c.dve)
                    a3 = p.op("scalar", lambda e: e.activation(out=st[:, 16:20], in_=st[:, 16:20], func=AF.Sqrt),
                              waits=[(c.dve, d)], inc=c.act)
                    d = p.op("vector", lambda e: e.reciprocal(out=st[:, 20:24], in_=st[:, 16:20]), waits=[(c.act, a3)],
                             inc=c.dve)
                    d = p.op("vector", lambda e, r1=r1: e.tensor_tensor(out=V4(r1), in0=V4(r1), in1=bc(st[:, 8:12]),
                                                                      op=ALU.subtract), waits=[(c.dve, d)], inc=c.dve)
                    d = p.op("vector", lambda e, r1=r1: e.tensor_tensor(out=V4(r1), in0=V4(r1), in1=bc(st[:, 20:24]),
                                                                      op=ALU.mult), waits=[(c.dve, d)], inc=c.dve)
                    d = p.op("vector", lambda e, r1=r1: e.tensor_tensor(out=r1, in0=r1, in1=io["lnb3"][:, 0, :],
                                                                      op=ALU.mult), waits=[(c.dve, d)] + setupw,
                             inc=c.dve)
                    d = p.op("vector", lambda e, r1=r1, tb=tb: e.tensor_tensor(out=vln[:, tb, :], in0=r1,
                                                                             in1=io["lnb3"][:, 1, :], op=ALU.add),
                             waits=[(c.dve, d)] + (vln_free if tb == 0 else []), inc=c.dve)
                    rots.cond[rs] = [(c.dve, d)]
                    ready[("vln", tb)] = [(c.dve, d)]
            wps.cond[b] = [(c.pe, pt)]
        for n in range(2 * NBLK):
            tb, a_ = n // 2, n % 2
            kb = 2 + kv_n[0] % 2
            ci = kv_n[0] % 2
            kv_n[0] += 1
            for hh in range(4):
                sl = slice(hh * 128, (hh + 1) * 128)
                p.op("tensor", lambda e, kb=kb, sl=sl, a_=a_, tb=tb: e.matmul(
                    c.ps[:, kb, sl], lhsT=kz[a_ * 64:(a_ + 1) * 64, tb, sl], rhs=rv[a_ * 64:(a_ + 1) * 64, tb, sl],
                    start=True, stop=True),
                     waits=(ready[("kz", tb)] + ready[("rv", tb)] + kv_cond[ci]) if hh == 0 else [],
                     inc=(c.pe if hh == 3 else None))
            pt = c.pe.v
            a = p.op("scalar", lambda e, n=n: e.copy(out=Sb[:, n, :], in_=S32[:]),
                     waits=[(c.dve, c.dve.v)] + (ret_free if n == 0 else []) + setupw, inc=c.act)
            ready[("Sb", n)] = [(c.act, a)]
            for hh in range(4):
                sl = slice(hh * 128, (hh + 1) * 128)
                d = p.op("vector", lambda e, kb=kb, sl=sl, hh=hh: e.scalar_tensor_tensor(
                    out=S32[:, sl], in0=S32[:, sl], scalar=GAM64[hh], in1=c.ps[:, kb, sl], op0=ALU.mult, op1=ALU.add),
                         waits=[(c.pe, pt), (c.act, a)] if hh == 0 else [], inc=c.dve)
            kv_cond[ci] = [(c.dve, d)]
        s32_done = [(c.dve, d)]
        for tb in range(NBLK):
            r0 = t0 + tb * 128
            for gg in range(4):
                sl = slice(gg * 128, (gg + 1) * 128)
                p.op("tensor", lambda e, sl=sl, tb=tb, gg=gg: e.matmul(c.ps[:, 4, sl], lhsT=vln[:, tb, sl],
                                                                     rhs=io["wsb"][:, gg, :], start=True, stop=True),
                     waits=(ready[("vln", tb)] + b4_cond + setupw) if gg == 0 else [], inc=(c.pe if gg == 3 else None))
            pt = c.pe.v
            d = p.op("vector", lambda e: e.tensor_tensor(out=y1[:], in0=c.ps[:, 4, :], in1=io["lnb3"][:, 2, :], op=ALU.add),
                     waits=[(c.pe, pt), (c.act, c.act.v), (c.dve, c.dve.v)], inc=c.dve)
            d = p.op("vector", lambda e, tb=tb: e.tensor_tensor(out=V4(y1[:]), in0=V4(y1[:]),
                                                               in1=u[:, :, tb * 128:(tb + 1) * 128], op=ALU.mult),
                     waits=[(c.dve, d)], inc=c.dve)
            a = p.op("scalar", lambda e: e.activation(out=y2[:], in_=y1[:], func=AF.Square), waits=[(c.dve, d)], inc=c.act)
            p.op("tensor", lambda e: e.matmul(c.ps[:, 4, :], lhsT=c.ones[:], rhs=y2[:], start=True, stop=True),
                 waits=[(c.act, a), (c.dve, d)], inc=c.pe)
            pt = c.pe.v
            d = p.op("vector", lambda e: e.tensor_scalar(out=y2[:], in0=c.ps[:, 4, :], scalar1=1.0 / 128, scalar2=EPS,
                                                         op0=ALU.mult, op1=ALU.add), waits=[(c.pe, pt)], inc=c.dve)
            b4_cond = [(c.dve, d)]
            a = p.op("scalar", lambda e: e.activation(out=y2[:], in_=y2[:], func=AF.Sqrt), waits=[(c.dve, d)], inc=c.act)
            d = p.op("vector", lambda e: e.reciprocal(out=y2[:], in_=y2[:]), waits=[(c.act, a)], inc=c.dve)
            d = p.op("vector", lambda e: e.tensor_tensor(out=y1[:], in0=y1[:], in1=y2[:], op=ALU.mult),
                     waits=[(c.dve, d)], inc=c.dve)
            s, w = st16.acquire()
            for gg in range(4):
                sl = slice(gg * 128, (gg + 1) * 128)
                a = p.op("vector", lambda e, s=s, sl=sl, gg=gg: e.tensor_scalar(out=stg16[:, s, sl], in0=y1[:, sl],
                                                                              scalar1=io["ong"][:, gg:gg + 1],
                                                                              scalar2=None, op0=ALU.mult),
                         waits=([(c.dve, d)] + w + setupw) if gg == 0 else [], inc=c.dve)
            sv = p.dma("sync", io["ytg"].rearrange("(g c) t -> c g t", c=128)[:, :, r0:r0 + 128], V4(stg16[:, s, :]),
                       waits=[(c.dve, a)], inc=st16.sem[s])
            st16.cond[s] = [(st16.sem[s], sv)]
        u_free = [(c.dve, c.dve.v)]
        vln_free = [(c.pe, c.pe.v)]
        for tb in range(NBLK):
            r0 = t0 + tb * 128
            for src, key, dstT, pb in ((kr, "kr", krT, 0), (qx, "q", qxT, 1), (qg, "q", None, 0)):
                for hh in range(4):
                    sl = slice(hh * 128, (hh + 1) * 128)
                    p.op("tensor", lambda e, src=src, sl=sl, tb=tb, pb=pb: e.transpose(out=pst[:, pb, sl], in_=src[:, tb, sl],
                                                                                     identity=io["identb"][:]),
                         waits=(ready[(key, tb)] + pst_cond[pb] + setupw) if hh == 0 else [],
                         inc=(c.pe if hh == 3 else None))
                pt = c.pe.v
                if dstT is not None:
                    d = p.op("vector", lambda e, dstT=dstT, pb=pb: e.tensor_copy(out=dstT[:], in_=pst[:, pb, 0:512]),
                             waits=[(c.pe, pt), (c.pe, c.pe.v)], inc=c.dve)
                    pst_cond[pb] = [(c.dve, d)]
                    ready[(id(dstT), tb)] = [(c.dve, d)]
                else:
                    s, w = st16.acquire()
                    a = p.op("scalar", lambda e, s=s, pb=pb: e.copy(out=stg16[:, s, :], in_=pst[:, pb, 0:512]),
                             waits=[(c.pe, pt)] + w, inc=c.act)
                    pst_cond[pb] = [(c.act, a)]
                    sv = p.dma("sync", io["ret_qg"].rearrange("(h d) t -> d h t", d=128)[:, :, r0:r0 + 128],
                               V4(stg16[:, s, :]), waits=[(c.act, a)], inc=st16.sem[s])
                    st16.cond[s] = [(st16.sem[s], sv)]
            for hh in range(4):
                sl = slice(hh * 128, (hh + 1) * 128)
                p.op("tensor", lambda e, sl=sl: e.matmul(c.ps[:, 4, sl], lhsT=krT[:, sl], rhs=qxT[:, sl], start=True,
                                                         stop=True),
                     waits=(ready[(id(krT), tb)] + ready[(id(qxT), tb)] + b4_cond) if hh == 0 else [],
                     inc=(c.pe if hh == 3 else None))
            pt = c.pe.v
            d = p.op("vector", lambda e: e.tensor_tensor(out=sm[:], in0=c.ps[:, 4, :], in1=io["maskr"][:], op=ALU.mult),
                     waits=[(c.pe, pt), (c.pe, c.pe.v)] + setupw, inc=c.dve)
            b4_cond = [(c.dve, d)]
            for hh in range(4):
                sl = slice(hh * 128, (hh + 1) * 128)
                p.op("tensor", lambda e, sl=sl, tb=tb: e.matmul(c.ps[:, 5, sl], lhsT=rv[:, tb, sl], rhs=sm[:, sl], start=True,
                                                               stop=False),
                     waits=([(c.dve, d)] + c.ps_free_waits + ready[("Sb", 2 * tb)] + ready[("Sb", 2 * tb + 1)])
                     if hh == 0 else [])
                for a_ in range(2):
                    cs = slice(hh * 128 + a_ * 64, hh * 128 + (a_ + 1) * 64)
                    p.op("tensor", lambda e, sl=sl, cs=cs, tb=tb, a_=a_: e.matmul(
                        c.ps[:, 5, cs], lhsT=Sb[:, 2 * tb + a_, sl], rhs=qxT[:, cs], start=False, stop=(a_ == 1)),
                         inc=(c.pe if (hh == 3 and a_ == 1) else None))
            pt = c.pe.v
            s, w = st32.acquire()
            a = p.op("scalar", lambda e, s=s: e.copy(out=stg32[:, s, :], in_=c.ps[:, 5, :]), waits=[(c.pe, pt)] + w,
                     inc=c.act)
            c.ps_free_waits = c.ps_free_waits + [(c.act, a)]
            sv = p.dma("sync", io["ret_o"].rearrange("(h e) t -> e h t", e=128)[:, :, r0:r0 + 128], V4(stg32[:, s, :]),
                       waits=[(c.act, a)], inc=st32.sem[s])
            st32.cond[s] = [(st32.sem[s], sv)]
        ret_free = [(c.pe, c.pe.v)]
    f1 = p.dma("sync", io["cneg_o"][:, :], cneg[:], waits=[(c.dve, c.dve.v)], inc=misc)
    f2 = p.dma("sync", io["ret_S"][:, :], S32[:], waits=s32_done, inc=misc)
    p.wait_only("sync", [(misc, f2)] + [(st16.sem[s], st16.sem[s].v) for s in range(4)] +
                [(st32.sem[s], st32.sem[s].v) for s in range(2)])


def mixa_setup(c, din, io):
    p = c.p
    sb = c.sb
    gcol = sb("gcol", [128, KC], F32)
    negb = sb("negb", [8, 1], F32)
    maskr = sb("maskr", [128, 512], F32)
    wst = sb("wst", [128, 512], F32)
    triu = sb("triu", [128, 128], F32)
    wsb = sb("wsb", [128, 4, 128], BF16)
    lnb3 = sb("lnb3", [128, 3, 512], F32)
    ong = sb("ong", [128, 4], F32)
    identb = sb("identb", [128, 128], BF16)
    io.update({"negb": negb, "maskr": maskr, "wsb": wsb, "lnb3": lnb3, "ong": ong, "identb": identb})
    for dst, src in ((gcol[:], din["g"]), (negb[:], din["bf"]), (maskr[:], din["maskr"]), (wst[:], din["wst"]),
                     (triu[:], din["triu"]), (lnb3[:].rearrange("p a b -> p (a b)"), din["lnb3"]),
                     (ong[:], din["ong"])):
        p.dma("sync", dst, src, inc=c.setup_d)
    p.dma("gpsimd", identb[:], din["ident"], inc=c.setup_d)
    dl = [(c.setup_d, c.setup_d.v)]
    p.op("vector", lambda e: e.tensor_scalar(out=negb[:], in0=negb[:], scalar1=-1.0, scalar2=None, op0=ALU.mult),
         waits=dl, inc=c.setup_v)
    p.op("vector", lambda e: e.tensor_tensor(out=wsb[:], in0=wst[:].rearrange("p (a b) -> p a b", a=4),
                                             in1=triu[:].unsqueeze(1).to_broadcast([128, 4, 128]), op=ALU.mult),
         inc=c.setup_v)
    return gcol


def build_mixa():
    nc = bass.Bass("TRN2", target_bir_lowering=False)
    di = lambda name, shape: nc.dram_tensor(name, shape, F32, kind="ExternalInput").ap()
    do = lambda name, shape, dt: nc.dram_tensor(name, shape, dt, kind="ExternalOutput").ap()
    xin = di("xin", [D, NTOK])
    w_in = di("w_in", [D, INCOLS])
    din = {"g": di("g", [128, KC])[:, :], "bf": di("bf", [8, 1])[:, :], "maskr": di("maskr", [128, 512])[:, :],
           "wst": di("wst", [128, 512])[:, :], "triu": di("triu", [128, 128])[:, :],
           "lnb3": di("lnb3", [128, 3 * 512])[:, :], "ong": di("ong", [128, 4])[:, :],
           "ident": di("ident", [128, 128])[:, :]}
    io = {
        "tab": di("tab", [NTOK, 268]),
        "qT": do("qT", [1024, NTOK], BF16), "kT": do("kT", [1024, NTOK], BF16), "v": do("v", [NTOK, 1024], BF16),
        "cneg_o": do("cneg", [8, NTOK], F32), "ret_o": do("ret_o", [512, NTOK], F32),
        "ret_qg": do("ret_qg", [512, NTOK], BF16), "ret_S": do("ret_S", [128, 512], F32),
        "rgs": do("rgs", [512, NTOK], F32), "ytg": do("ytg", [512, NTOK], BF16),
    }
    with contextlib.ExitStack() as stack:
        p = Prog(nc, stack)
        c = alloc_common(nc, stack, p, tt=TA, nps=6, stat_bank=5)
        c.stack = stack
        gcol = mixa_setup(c, din, io)
        mixa_body(c, xin, gcol, w_in, io)
        p.emit()
    return nc


def host_consts(half):
    t = np.arange(NTOK, dtype=np.float64)
    pos = (half * NTOK + np.arange(NTOK)).astype(np.float32)
    inv_freq = (np.float32(10000.0) ** (-np.arange(64, dtype=np.float32) / np.float32(64))).astype(np.float32)
    ang = (pos[:, None] * inv_freq[None, :]).astype(np.float32).astype(np.float64)
    cos, sin = np.cos(ang), np.sin(ang)
    gam = np.array(GAM, dtype=np.float64)
    cidx = (np.arange(NTOK) % 64).astype(np.float64)
    xi = gam[None, :] ** (cidx[:, None] + 1.0)
    gm = gam[None, :] ** (t[:, None] + 1.0)
    zeta = gam[None, :] ** (63.0 - cidx[:, None]) * (128.0 ** -0.5)
    tab = np.concatenate([cos, cos, -sin, sin, xi, gm, zeta], axis=1).astype(np.float32)
    s = np.arange(128)
    cc = np.arange(128)
    same = (s[:, None] // 64 == cc[None, :] // 64) & (cc[None, :] >= s[:, None])
    maskr = np.zeros((128, 4, 128), np.float64)
    for h in range(4):
        maskr[:, h, :] = np.where(same, gam[h] ** (-(s[:, None] % 64 + 1.0)), 0.0) * (128.0 ** -0.5)
    triu = (s[:, None] <= cc[None, :]).astype(np.float32)
    return {"tab": np.ascontiguousarray(tab), "maskr": maskr.reshape(128, 512).astype(np.float32), "triu": triu,
            "ident": np.eye(128, dtype=np.float32)}


def col16(vec):
    return np.ascontiguousarray(np.asarray(vec, np.float32).reshape(KC, 128).T)


def mixa_inputs(xT, half, l, P):
    hc = host_consts(half)
    wst = np.ascontiguousarray(np.transpose(P["gmlp_w_s"][l], (2, 0, 1)).reshape(128, 512))
    lnb3 = np.concatenate([np.broadcast_to(P["gmlp_ln_g"][l][None, :], (128, 512)),
                           np.broadcast_to(P["gmlp_ln_b"][l][None, :], (128, 512)),
                           np.broadcast_to(P["gmlp_b_s"][l].reshape(1, 512), (128, 512))], axis=1)
    ong = np.ascontiguousarray(P["out_norm"][l][1536:2048].reshape(4, 128).T)
    return {"xin": xT, "g": col16(P["mix_norm"][l]), "w_in": P["w_in"][l],
            "bf": np.ascontiguousarray(P["fox_b_f"][l].reshape(8, 1)), "tab": hc["tab"], "maskr": hc["maskr"],
            "wst": wst.astype(np.float32), "triu": hc["triu"], "lnb3": np.ascontiguousarray(lnb3, dtype=np.float32),
            "ong": ong.astype(np.float32), "ident": hc["ident"]}


QT = 512
NQT = NTOK // QT
MASKNEG = -30000.0


def mixb_body(nc, stack, p, io):
    uid = _uid()
    sb = lambda name, shape, dt: stack.enter_context(nc.sbuf_tensor("sb%d_%s" % (uid, name), shape, dt))
    ps = stack.enter_context(nc.psum_tensor("psb%d" % uid, [128, 8, 512], F32))
    onesf = sb("onesf", [128, 128], F32)
    onesb = sb("onesb", [128, 128], BF16)
    cmask = sb("cmask", [128, 896], F32)
    selh = sb("selh", [8, 8, 128], F32)
    ident8 = sb("ident8", [8, 8], F32)
    pmask = sb("pmask", [128, 1], F32)
    sflag = sb("sflag", [128, 1], F32)
    ona = sb("ona", [128, 12], F32)
    sinit = sb("sinit", [128, 512], F32)
    sinb = sb("sinb", [128, 512], BF16)
    cn = sb("cn", [8, 2 * NTOK], F32)
    ncl = sb("ncl", [8, NTOK], F32)
    ncp = sb("ncp", [8, NTOK], F32)
    biasT = sb("biasT", [128, 32, 8], F32)
    setup_d = p.sem("bsetupd")
    dve = p.sem("bdve")
    act = p.sem("bact")
    pe = p.sem("bpe")
    for dst, src in ((cmask[:], io["cmask"][:, :]), (selh[:].rearrange("k h m -> k (h m)"), io["selh"][:, :]),
                     (ident8[:], io["ident8"][:, :]), (pmask[:], io["pmask"][:, :]), (sflag[:], io["sflag"][:, :]),
                     (ona[:], io["ona"][:, :]), (sinit[:], io["s_init"][:, :]), (cn[:, 0:NTOK], io["cneg_prev"][:, :]),
                     (cn[:, NTOK:2 * NTOK], io["cneg_loc"][:, :])):
        p.dma("sync", dst, src, inc=setup_d)
    sd = [(setup_d, setup_d.v)]
    p.op("vector", lambda e: e.memset(onesf[:], 1.0), inc=dve)
    p.op("vector", lambda e: e.memset(onesb[:], 1.0), inc=dve)
    p.op("vector", lambda e: e.tensor_scalar(out=sinb[:], in0=sinit[:], scalar1=sflag[:, 0:1], scalar2=None, op0=ALU.mult),
         waits=sd, inc=dve)
    p.op("vector", lambda e: e.tensor_scalar(out=ncl[:], in0=cn[:, NTOK:2 * NTOK], scalar1=-1.0, scalar2=None, op0=ALU.mult),
         inc=dve)
    d = p.op("vector", lambda e: e.tensor_scalar(out=ncp[:], in0=ncl[:], scalar1=cn[:, NTOK - 1:NTOK], scalar2=None,
                                                 op0=ALU.subtract), waits=[(dve, dve.v)], inc=dve)
    for blk in range(32):
        p.op("tensor", lambda e, blk=blk: e.transpose(out=ps[:, 6, blk * 8:(blk + 1) * 8],
                                                      in_=cn[0:8, blk * 128:(blk + 1) * 128], identity=ident8[:]),
             waits=sd if blk == 0 else [], inc=(pe if blk == 31 else None))
    p.op("vector", lambda e: e.tensor_scalar(out=biasT[:, 0:16, :].rearrange("p a b -> p (a b)"), in0=ps[:, 6, 0:128],
                                             scalar1=pmask[:, 0:1], scalar2=None, op0=ALU.add),
         waits=[(pe, pe.v)] + sd, inc=dve)
    d = p.op("vector", lambda e: e.tensor_copy(out=biasT[:, 16:32, :].rearrange("p a b -> p (a b)"), in_=ps[:, 6, 128:256]),
             inc=dve)
    b6_cond = [(dve, d)]
    selb = sb("selb", [128, 8, 128], BF16)
    maskb = sb("maskb", [128, 896], BF16)
    identb = sb("identb", [128, 128], BF16)
    setup_d2 = p.sem("bsetupd2")
    p.dma("gpsimd", selb[:].rearrange("k h m -> k (h m)"), io["selh72"][:, :], inc=setup_d2)
    p.dma("gpsimd", maskb[:], io["cmask"][:, :], inc=setup_d2)
    p.dma("gpsimd", identb[:], io["ident"][:, :], inc=setup_d2)
    sd = sd + [(setup_d2, setup_d2.v)]
    cst = sb("cst", [128, 2, NTOK], BF16)
    csr = sb("csr", [8, NTOK], F32)
    csm = sb("csm", [8, NTOK], BF16)
    d = p.op("vector", lambda e: e.memset(cst[:], 0.0), inc=dve)
    for vi, srcc in ((0, ncp), (1, ncl)):
        d = p.op("vector", lambda e, vi=vi, srcc=srcc: e.tensor_copy(out=cst[0:8, vi, :], in_=srcc[:]), waits=[(dve, d)], inc=dve)
        d = p.op("vector", lambda e, vi=vi, srcc=srcc: e.tensor_tensor(out=csr[:], in0=srcc[:], in1=cst[0:8, vi, :],
                                                                     op=ALU.subtract), waits=[(dve, d)], inc=dve)
        d = p.op("vector", lambda e: e.tensor_copy(out=csm[:], in_=csr[:]), waits=[(dve, d)], inc=dve)
        d = p.op("vector", lambda e, vi=vi: e.tensor_copy(out=cst[32:40, vi, :], in_=csm[:]), waits=[(dve, d)], inc=dve)
        d = p.op("vector", lambda e: e.tensor_tensor(out=csr[:], in0=csr[:], in1=csm[:], op=ALU.subtract),
                 waits=[(dve, d)], inc=dve)
        d = p.op("vector", lambda e, vi=vi: e.tensor_copy(out=cst[64:72, vi, :], in_=csr[:]), waits=[(dve, d)], inc=dve)
    setup_done = [(dve, d)] + sd

    qh = sb("qh", [128, 2, NTOK], BF16)
    kh = sb("kh", [128, 2, 2 * NTOK], BF16)
    vh = sb("vh", [128, 2, 32, 128], BF16)
    hs = Slots(p, "hs", 2)
    cb = sb("cb", [128, 2, 2, QT], F32)
    cbs = Slots(p, "cbs", 2)
    tmp = sb("tmp", [128, 4, QT], F32)
    tmps = Slots(p, "tmps", 4)
    pt = sb("pt", [128, 4, QT], BF16)
    pts = Slots(p, "pts", 4)
    ya = sb("ya", [128, 2, QT], F32)
    yas = Slots(p, "yas", 2)
    yq = sb("yq", [128, QT], F32)
    stg = sb("stg", [128, 2, QT], BF16)
    stgs = Slots(p, "stgs", 2)
    ro = sb("ro", [128, 2, QT], F32)
    rg = sb("rg", [128, 2, QT], F32)
    qgt = sb("qgt", [128, 2, QT], BF16)
    rls = Slots(p, "rls", 2)
    S_BANKS = (0, 1, 7)
    LOOK = 2
    s_cond = {0: [], 1: [], 7: []}
    acc_cond = [[], []]
    s_n = [0]
    acc_n = [0]
    y_stores = []

    def headnorm_store(yslot, yready, gain_col, dst_ap, mul_tile=None, mul_wait=()):
        nonlocal b6_cond
        a = p.op("scalar", lambda e: e.activation(out=yq[:], in_=ya[:, yslot, :], func=AF.Square),
                 waits=list(yready) + [(dve, dve.v)], inc=act)
        p.op("tensor", lambda e: e.matmul(ps[:, 6, :], lhsT=onesf[:], rhs=yq[:], start=True, stop=True),
             waits=[(act, a)] + b6_cond, inc=pe)
        d = p.op("vector", lambda e: e.tensor_scalar(out=yq[:], in0=ps[:, 6, :], scalar1=1.0 / 128, scalar2=EPS,
                                                     op0=ALU.mult, op1=ALU.add), waits=[(pe, pe.v)], inc=dve)
        b6_cond = [(dve, d)]
        a = p.op("scalar", lambda e: e.activation(out=yq[:], in_=yq[:], func=AF.Sqrt), waits=[(dve, d)], inc=act)
        d = p.op("vector", lambda e: e.reciprocal(out=yq[:], in_=yq[:]), waits=[(act, a)], inc=dve)
        s, w = stgs.acquire()
        if mul_tile is None:
            d = p.op("vector", lambda e: e.scalar_tensor_tensor(out=stg[:, s, :], in0=ya[:, yslot, :], scalar=gain_col,
                                                                in1=yq[:], op0=ALU.mult, op1=ALU.mult),
                     waits=[(dve, d)] + w, inc=dve)
        else:
            d = p.op("vector", lambda e: e.scalar_tensor_tensor(out=ya[:, yslot, :], in0=ya[:, yslot, :], scalar=gain_col,
                                                                in1=yq[:], op0=ALU.mult, op1=ALU.mult),
                     waits=[(dve, d)], inc=dve)
            d = p.op("vector", lambda e: e.tensor_tensor(out=stg[:, s, :], in0=ya[:, yslot, :], in1=mul_tile, op=ALU.mult),
                     waits=[(dve, d)] + w + list(mul_wait), inc=dve)
        sv = p.dma("sync", dst_ap, stg[:, s, :], waits=[(dve, d)], inc=stgs.sem[s])
        stgs.cond[s] = [(stgs.sem[s], sv)]
        y_stores.append((stgs.sem[s], sv))
        return [(dve, d)]

    pending = [None]

    def run_pending():
        if pending[0] is not None:
            ys_, rdy_, gain_, dst_, mt_, rs_ = pending[0]
            pending[0] = None
            yw_ = headnorm_store(ys_, rdy_, gain_, dst_, mul_tile=mt_)
            yas.cond[ys_] = yw_
            if rs_ is not None:
                rls.cond[rs_] = yw_

    vparts = []
    blk0 = 0
    for key in ("v_prev", "v_loc"):
        for part in _parts(io[key]):
            nb = part.shape[0] // 128
            vparts.append((blk0, nb, part.rearrange("(b p) c -> p b c", p=128)))
            blk0 += nb
    assert blk0 == 32
    for h in range(8):
        hsl, w = hs.acquire()
        rows = slice(h * 128, (h + 1) * 128)
        p.dma("sync", qh[:, hsl, :], io["qT"][rows, :], waits=w, inc=hs.sem[hsl])
        p.dma("sync", kh[:, hsl, 0:NTOK], io["kT_prev"][rows, :], inc=hs.sem[hsl])
        p.dma("sync", kh[:, hsl, NTOK:2 * NTOK], io["kT_loc"][rows, :], inc=hs.sem[hsl])
        for (b0_, nb_, vw_) in vparts:
            hl = p.dma("sync", vh[:, hsl, b0_:b0_ + nb_, :], vw_[:, :, rows], inc=hs.sem[hsl])
        hw = [(hs.sem[hsl], hl)]
        for qt in range(NQT):
            qs = slice(qt * QT, (qt + 1) * QT)
            blocks = [(j, 0, None) for j in range(16)] + [(16 + j, 1, (j - 4 * qt) if j >= 4 * qt else None)
                                                          for j in range(4 * qt + 4)]
            an = acc_n[0]
            acc_n[0] += 1
            ob, db = 2 + an % 2, 4 + an % 2
            pend = None
            nblk = len(blocks)

            def emit_pv(pend, first, last):
                j, slot, pw_ = pend
                p.op("tensor", lambda e, j=j, slot=slot, ob=ob, hsl=hsl: e.matmul(ps[:, ob, :], lhsT=vh[:, hsl, j, :],
                                                                                 rhs=pt[:, slot, :], start=first, stop=last),
                     waits=pw_ + (acc_cond[an % 2] if first else []))
                p.op("tensor", lambda e, slot=slot, db=db: e.matmul(ps[:, db, :], lhsT=onesb[:], rhs=pt[:, slot, :],
                                                                   start=first, stop=last), inc=pe)
                pts.cond[slot] = [(pe, pe.v)]

            npv = 0
            queue = []
            for bi, (j, vi, dg) in enumerate(blocks):
                if bi == 6:
                    run_pending()
                sn = s_n[0]
                s_n[0] += 1
                sbk = S_BANKS[sn % 3]
                p.op("tensor", lambda e, j=j, sbk=sbk, qs=qs, hsl=hsl: e.matmul(ps[:, sbk, :], lhsT=kh[:, hsl, j * 128:(j + 1) * 128],
                                                                      rhs=qh[:, hsl, qs], start=True, stop=False),
                     waits=hw + s_cond[sbk] + setup_done)
                p.op("tensor", lambda e, sbk=sbk, qs=qs, h=h, vi=vi, dg=dg: e.matmul(ps[:, sbk, :], lhsT=selb[:, h, :],
                                                                                   rhs=cst[:, vi, qs], start=False,
                                                                                   stop=(dg is None)),
                     inc=(pe if dg is None else None))
                if dg is not None:
                    off = 384 - dg * 128
                    p.op("tensor", lambda e, sbk=sbk, off=off: e.matmul(ps[:, sbk, :], lhsT=identb[:], rhs=maskb[:, off:off + QT],
                                                                       start=False, stop=True), inc=pe)
                st_ = pe.v
                if len(queue) >= LOOK:
                    emit_pv(queue.pop(0), npv == 0, False)
                    npv += 1
                slot, w = pts.acquire()
                a = p.op("scalar", lambda e, sbk=sbk, slot=slot, j=j, h=h: e.activation(
                    out=pt[:, slot, :], in_=ps[:, sbk, :], func=AF.Exp, bias=biasT[:, j, h:h + 1], scale=1.0),
                         waits=[(pe, st_)] + w + setup_done, inc=act)
                s_cond[sbk] = [(act, a)]
                queue.append((j, slot, [(act, a)]))
            while queue:
                pend = queue.pop(0)
                emit_pv(pend, npv == 0, len(queue) == 0)
                npv += 1
            acc_done = pe.v
            ys, w = yas.acquire()
            d = p.op("vector", lambda e, db=db: e.reciprocal(out=yq[:], in_=ps[:, db, :]), waits=[(pe, acc_done), (dve, dve.v),
                                                                                          (act, act.v)], inc=dve)
            d = p.op("vector", lambda e, ys=ys, ob=ob: e.tensor_tensor(out=ya[:, ys, :], in0=ps[:, ob, :], in1=yq[:], op=ALU.mult),
                     waits=[(dve, d)] + w, inc=dve)
            acc_cond[an % 2] = [(dve, d)]
            run_pending()
            pending[0] = (ys, [(dve, d)], ona[:, h:h + 1], io["yT"][rows, qs], None, None)
        hs.cond[hsl] = [(pe, pe.v)]
    for hh in range(4):
        rows = slice(hh * 128, (hh + 1) * 128)
        for qt in range(NQT):
            qs = slice(qt * QT, (qt + 1) * QT)
            rs, w = rls.acquire()
            p.dma("sync", ro[:, rs, :], io["ret_o"][rows, qs], waits=w, inc=rls.sem[rs])
            p.dma("sync", rg[:, rs, :], io["rgs"][rows, qs], inc=rls.sem[rs])
            rl = p.dma("sync", qgt[:, rs, :], io["ret_qg"][rows, qs], inc=rls.sem[rs])
            p.op("tensor", lambda e, rs=rs, rows=rows: e.matmul(ps[:, 6, :], lhsT=sinb[:, rows], rhs=qgt[:, rs, :], start=True,
                                                               stop=True),
                 waits=[(rls.sem[rs], rl)] + b6_cond + setup_done, inc=pe)
            ys, w = yas.acquire()
            d = p.op("vector", lambda e, ys=ys, rs=rs: e.tensor_tensor(out=ya[:, ys, :], in0=ps[:, 6, :], in1=ro[:, rs, :],
                                                                      op=ALU.add), waits=[(pe, pe.v)] + w, inc=dve)
            b6_cond = [(dve, d)]
            run_pending()
            pending[0] = (ys, [(dve, d)], ona[:, 8 + hh:9 + hh],
                          io["yT"][1024 + hh * 128:1024 + (hh + 1) * 128, qs], rg[:, rs, :], rs)
    run_pending()
    hb = sb("hb", [128, KC, TT], BF16)
    wo = sb("wo", [128, 2, KC, 256], BF16)
    wos = Slots(p, "wos", 2)
    xs = sb("xsb", [128, 3, TT], F32)
    xss = Slots(p, "xss", 3)
    hbl = p.sem("hbl")
    wov = io["w_out"].rearrange("(kc p) f -> p kc f", p=128)
    ytv = io["yT"].rearrange("(kc p) t -> p kc t", p=128)
    ygv = io["ytg"].rearrange("(kc p) t -> p kc t", p=128)
    hb_free = []
    on = 0
    ob_cond = [[], []]
    for tt in range(NTT):
        t0 = tt * TT
        p.dma("sync", hb[:, 0:12, :], ytv[:, :, t0:t0 + TT], waits=list(y_stores) + hb_free, inc=hbl)
        hl = p.dma("sync", hb[:, 12:16, :], ygv[:, :, t0:t0 + TT], inc=hbl)
        for pd in range(8):
            col0 = pd * 256
            b, w = wos.acquire()
            wl = p.dma("gpsimd", wo[:, b, :, :], wov[:, :, col0:col0 + 256], waits=w, inc=wos.sem[b])
            for ii in range(2):
                i = pd * 2 + ii
                s, w = xss.acquire()
                full = p.dma("sync", xs[:, s, :], io["xin"][i * 128:(i + 1) * 128, t0:t0 + TT], waits=w, inc=xss.sem[s])
                for th in range(2):
                    obk = on % 2
                    on += 1
                    for kc in range(KC):
                        p.op("tensor", lambda e, b=b, kc=kc, ii=ii, th=th, obk=obk: e.matmul(
                            ps[:, obk, :], lhsT=wo[:, b, kc, ii * 128:(ii + 1) * 128], rhs=hb[:, kc, th * 512:(th + 1) * 512],
                            start=(kc == 0), stop=(kc == KC - 1)),
                             waits=([(wos.sem[b], wl), (hbl, hl)] + ob_cond[obk]) if kc == 0 else [],
                             inc=(pe if kc == KC - 1 else None))
                    r = p.op("vector", lambda e, s=s, th=th, obk=obk: e.tensor_tensor(
                        out=xs[:, s, th * 512:(th + 1) * 512], in0=ps[:, obk, :], in1=xs[:, s, th * 512:(th + 1) * 512],
                        op=ALU.add), waits=[(pe, pe.v), (xss.sem[s], full)], inc=dve)
                    ob_cond[obk] = [(dve, r)]
                sv = p.dma("sync", io["xout"][i * 128:(i + 1) * 128, t0:t0 + TT], xs[:, s, :], waits=[(dve, r)],
                           inc=xss.sem[s])
                xss.cond[s] = [(xss.sem[s], sv)]
            wos.cond[b] = [(pe, pe.v)]
        hb_free = [(pe, pe.v)]
    p.wait_only("sync", [(xss.sem[s], xss.sem[s].v) for s in range(3)])


def build_mixb():
    nc = bass.Bass("TRN2", target_bir_lowering=False)
    di = lambda name, shape, dt=F32: nc.dram_tensor(name, shape, dt, kind="ExternalInput").ap()
    io = {
        "qT": di("qT", [1024, NTOK], BF16), "kT_loc": di("kT_loc", [1024, NTOK], BF16),
        "kT_prev": di("kT_prev", [1024, NTOK], BF16), "v_loc": di("v_loc", [NTOK, 1024], BF16),
        "v_prev": di("v_prev", [NTOK, 1024], BF16), "cneg_loc": di("cneg_loc", [8, NTOK]),
        "cneg_prev": di("cneg_prev", [8, NTOK]), "s_init": di("s_init", [128, 512]),
        "ret_o": di("ret_o", [512, NTOK]), "ret_qg": di("ret_qg", [512, NTOK], BF16), "rgs": di("rgs", [512, NTOK]),
        "ytg": di("ytg", [512, NTOK], BF16), "cmask": di("cmask", [128, 896]), "selh": di("selh", [8, 1024]),
        "ident8": di("ident8", [8, 8]), "pmask": di("pmask", [128, 1]), "sflag": di("sflag", [128, 1]),
        "selh72": di("selh72", [128, 1024]), "ident": di("ident", [128, 128]),
        "ona": di("ona", [128, 12]), "w_out": di("w_out", [D, D]), "xin": di("xin", [D, NTOK]),
        "yT": nc.dram_tensor("yT", [1536, NTOK], BF16, kind="Internal").ap(),
        "xout": nc.dram_tensor("xout", [D, NTOK], F32, kind="ExternalOutput").ap(),
    }
    with contextlib.ExitStack() as stack:
        p = Prog(nc, stack)
        mixb_body(nc, stack, p, io)
        p.emit()
    return nc


def mixb_consts(half):
    s = np.arange(128)[:, None]
    u = np.arange(896)[None, :]
    cmask = np.where((u - 384) >= s, 0.0, MASKNEG).astype(np.float32)
    selh = np.zeros((8, 8, 128), np.float32)
    for h in range(8):
        selh[h, h, :] = 1.0
    selh72 = np.zeros((128, 8, 128), np.float32)
    for h in range(8):
        selh72[h, h, :] = 1.0
        selh72[32 + h, h, :] = 1.0
        selh72[64 + h, h, :] = 1.0
    return {"cmask": cmask, "selh": selh.reshape(8, 1024), "ident8": np.eye(8, dtype=np.float32),
            "selh72": selh72.reshape(128, 1024), "ident": np.eye(128, dtype=np.float32),
            "pmask": np.full((128, 1), 0.0 if half == 1 else MASKNEG, np.float32),
            "sflag": np.full((128, 1), 1.0 if half == 1 else 0.0, np.float32)}


def build_norm():
    nc = bass.Bass("TRN2", target_bir_lowering=False)
    xin = nc.dram_tensor("xin", [D, NTOK], F32, kind="ExternalInput").ap()
    g = nc.dram_tensor("g", [128, KC], F32, kind="ExternalInput").ap()
    xout = nc.dram_tensor("xout", [D, NTOK], F32, kind="ExternalOutput").ap()
    with contextlib.ExitStack() as stack:
        p = Prog(nc, stack)
        c = alloc_common(nc, stack, p)
        gcol = c.sb("gcol", [128, KC], F32)
        p.dma("sync", gcol[:], g[:, :], inc=c.setup_d)
        for tt in range(NTT):
            t0 = tt * TT

            def out_fn(kc, s, waits, t0=t0):
                d = p.op("vector", lambda e: e.scalar_tensor_tensor(out=c.xs[:, s, :], in0=c.xs[:, s, :],
                                                                    scalar=gcol[:, kc:kc + 1], in1=c.rstd[:],
                                                                    op0=ALU.mult, op1=ALU.mult), waits=waits, inc=c.dve_h)
                sv = p.dma("sync", xout[kc * 128:(kc + 1) * 128, t0:t0 + TT], c.xs[:, s, :], waits=[(c.dve_h, d)],
                           inc=c.xs_st[s])
                c.xs_cond[s] = [(c.xs_st[s], sv)]

            norm_stats_and_h(c, xin, gcol, tt, out_fn=out_fn)
        finish(c)
        p.emit()
    return nc


_PROGS = {}


def _prog(name):
    if name not in _PROGS:
        _PROGS[name] = {"ffn": build_ffn, "mixa": build_mixa, "mixb": build_mixb, "norm": build_norm}[name]()
    return _PROGS[name]


def _run(name, in_maps):
    res = run_bass_kernel_spmd(_prog(name), in_maps, core_ids=list(range(NCORES)))
    return res.results


def run_ffn(xTs, l, P, pre):
    g = col16(P[pre + "_norm"][l])
    maps = [{"xin": xTs[c], "g": g, "wg": P[pre + "_w_gate"][l], "wu": P[pre + "_w_up"][l], "wd": P[pre + "_w_down"][l]}
            for c in range(NCORES)]
    return [r["xout"] for r in _run("ffn", maps)]


def run_mixer(xTs, l, P):
    ra = _run("mixa", [mixa_inputs(xTs[c], c % 2, l, P) for c in range(NCORES)])
    ona = np.ascontiguousarray(P["out_norm"][l][0:1536].reshape(12, 128).T).astype(np.float32)
    maps = []
    for c in range(NCORES):
        half = c % 2
        pc = c - 1 if half == 1 else c
        m = {"qT": ra[c]["qT"], "kT_loc": ra[c]["kT"], "kT_prev": ra[pc]["kT"], "v_loc": ra[c]["v"], "v_prev": ra[pc]["v"],
             "cneg_loc": ra[c]["cneg"], "cneg_prev": ra[pc]["cneg"], "s_init": ra[pc]["ret_S"], "ret_o": ra[c]["ret_o"],
             "ret_qg": ra[c]["ret_qg"], "rgs": ra[c]["rgs"], "ytg": ra[c]["ytg"], "ona": ona, "w_out": P["w_out"][l],
             "xin": xTs[c]}
        m.update(mixb_consts(half))
        maps.append(m)
    return [r["xout"] for r in _run("mixb", maps)]


def kernel_unfused(**inputs):
    P = {k: np.asarray(v) for k, v in inputs.items()}
    x = P["x"]
    xTs = [np.ascontiguousarray(x[c // 2, (c % 2) * NTOK:(c % 2 + 1) * NTOK, :].T) for c in range(NCORES)]
    for l in range(DEPTH):
        xTs = run_ffn(xTs, l, P, "ffn1")
        xTs = run_mixer(xTs, l, P)
        xTs = run_ffn(xTs, l, P, "ffn2")
    g = col16(P["final_norm"])
    outs = [r["xout"] for r in _run("norm", [{"xin": xTs[c], "g": g} for c in range(NCORES)])]
    out = np.empty_like(x)
    for c in range(NCORES):
        out[c // 2, (c % 2) * NTOK:(c % 2 + 1) * NTOK, :] = outs[c].T
    return out


PAIRS = [[0, 1], [2, 3], [4, 5], [6, 7]]
WSHAPES = {"ffn1_w_gate": [DEPTH, D, DFF], "ffn1_w_up": [DEPTH, D, DFF], "ffn1_w_down": [DEPTH, DFF, D],
           "w_in": [DEPTH, D, INCOLS], "w_out": [DEPTH, D, D],
           "ffn2_w_gate": [DEPTH, D, DFF], "ffn2_w_up": [DEPTH, D, DFF], "ffn2_w_down": [DEPTH, DFF, D]}
SMALL = {"g_ffn1": [DEPTH, 128, KC], "g_mix": [DEPTH, 128, KC], "g_ffn2": [DEPTH, 128, KC], "g_fin": [128, KC],
         "bf": [DEPTH, 8, 1], "wst": [DEPTH, 128, 512], "lnb3": [DEPTH, 128, 1536], "ong": [DEPTH, 128, 4],
         "ona": [DEPTH, 128, 12], "tab": [NTOK, 268], "maskr": [128, 512], "triu": [128, 128], "ident": [128, 128],
         "cmask": [128, 896], "selh": [8, 1024], "ident8": [8, 8], "pmask": [128, 1], "sflag": [128, 1],
         "selh72": [128, 1024]}


def build_fused(depth=DEPTH, phases="fmxbF"):
    nc = bass.Bass("TRN2", target_bir_lowering=False)
    di = lambda name, shape: nc.dram_tensor(name, shape, F32, kind="ExternalInput").ap()
    it = lambda name, shape, dt: nc.dram_tensor(name, shape, dt, kind="Internal").ap()
    x_in = di("x", [D, NTOK])
    W = {k: di(k, [depth] + s[1:]) for k, s in WSHAPES.items()}
    S = {k: di(k, ([depth] + s[1:]) if len(s) == 3 else s) for k, s in SMALL.items()}
    out = nc.dram_tensor("out", [D, NTOK], F32, kind="ExternalOutput").ap()
    xres = it("xres", [D, NTOK], F32)
    qT = it("qT", [1024, NTOK], BF16)
    xk = [it("xk%d" % i, [512, NTOK], BF16) for i in range(2)]
    xv = [it("xv%d" % i, [1024, 1024], BF16) for i in range(2)]
    xc = it("xc", [8, NTOK], F32)
    xs_ = it("xs_", [128, 512], F32)
    gk = [it("gk%d" % i, [1024, NTOK], BF16) for i in range(2)]
    gv_ = [it("gv%d" % i, [2048, 1024], BF16) for i in range(2)]
    gc = it("gc", [16, NTOK], F32)
    gs = it("gs", [256, 512], F32)
    ret_o = it("ret_o", [512, NTOK], F32)
    ret_qg = it("ret_qg", [512, NTOK], BF16)
    rgs = it("rgs", [512, NTOK], F32)
    ytg = it("ytg", [512, NTOK], BF16)
    yT = it("yT", [1536, NTOK], BF16)
    with contextlib.ExitStack() as gstack:
        p = Prog(nc, gstack)

        def ffn_phase(xin, xout, g_ap, wg, wu, wd):
            with contextlib.ExitStack() as st:
                c = alloc_common(nc, st, p)
                alloc_ffn(c)
                gcol = c.sb("gcol", [128, KC], F32)
                p.dma("sync", gcol[:], g_ap, inc=c.setup_d)
                ffn_body(c, xin, xout, gcol, wg, wu, wd)
                finish(c)
                p.barrier()
                p.emit()

        def mixa_phase(l):
            with contextlib.ExitStack() as st:
                c = alloc_common(nc, st, p, tt=TA, nps=6, stat_bank=5)
                din = {"g": S["g_mix"][l], "bf": S["bf"][l], "maskr": S["maskr"][:, :], "wst": S["wst"][l],
                       "triu": S["triu"][:, :], "lnb3": S["lnb3"][l], "ong": S["ong"][l], "ident": S["ident"][:, :]}
                io = {"tab": S["tab"], "qT": qT, "kT": RowSplit(xk), "v": RowSplit(xv), "cneg_o": xc,
                      "ret_o": ret_o, "ret_qg": ret_qg, "ret_S": xs_, "rgs": rgs, "ytg": ytg}
                gcol = mixa_setup(c, din, io)
                mixa_body(c, xres, gcol, W["w_in"][l], io)
                p.barrier()
                p.emit()

        def exchange_phase():
            cc = p.sem("ccsem")
            for a_, b_ in ((xk[0], gk[0]), (xk[1], gk[1]), (xv[0], gv_[0]), (xv[1], gv_[1]), (xc, gc), (xs_, gs)):
                p.op("gpsimd", lambda e, a_=a_, b_=b_: e.collective_compute("AllGather", ALU.bypass, replica_groups=PAIRS,
                                                                            ins=[a_], outs=[b_]), inc=cc, k=1)
            p.barrier()
            p.emit()

        def mixb_phase(l):
            with contextlib.ExitStack() as st:
                io = {"qT": qT, "kT_loc": RowSplit(xk), "kT_prev": RowSplit([gk[0][0:512, :], gk[1][0:512, :]]),
                      "v_loc": RowSplit(xv), "v_prev": RowSplit([gv_[0][0:1024, :], gv_[1][0:1024, :]]),
                      "cneg_loc": xc, "cneg_prev": gc[0:8, :], "s_init": gs[0:128, :], "ret_o": ret_o,
                      "ret_qg": ret_qg, "rgs": rgs, "ytg": ytg, "cmask": S["cmask"], "selh": S["selh"],
                      "ident8": S["ident8"], "pmask": S["pmask"], "sflag": S["sflag"], "ona": S["ona"][l],
                      "selh72": S["selh72"], "ident": S["ident"],
                      "w_out": W["w_out"][l], "xin": xres, "yT": yT, "xout": xres}
                mixb_body(nc, st, p, io)
                p.barrier()
                p.emit()

        def norm_phase():
            with contextlib.ExitStack() as st:
                c = alloc_common(nc, st, p)
                gcol = c.sb("gcol", [128, KC], F32)
                p.dma("sync", gcol[:], S["g_fin"][:, :], inc=c.setup_d)
                for tt in range(NTT):
                    t0 = tt * TT

                    def out_fn(kc, s, waits, t0=t0):
                        d = p.op("vector", lambda e: e.scalar_tensor_tensor(out=c.xs[:, s, :], in0=c.xs[:, s, :],
                                                                            scalar=gcol[:, kc:kc + 1], in1=c.rstd[:],
                                                                            op0=ALU.mult, op1=ALU.mult), waits=waits,
                                 inc=c.dve_h)
                        sv = p.dma("sync", out[kc * 128:(kc + 1) * 128, t0:t0 + TT], c.xs[:, s, :], waits=[(c.dve_h, d)],
                                   inc=c.xs_st[s])
                        c.xs_cond[s] = [(c.xs_st[s], sv)]

                    norm_stats_and_h(c, xres, gcol, tt, out_fn=out_fn)
                finish(c)
                p.barrier()
                p.emit()

        for l in range(depth):
            if "f" in phases:
                ffn_phase(x_in if l == 0 else xres, xres, S["g_ffn1"][l], W["ffn1_w_gate"][l], W["ffn1_w_up"][l],
                          W["ffn1_w_down"][l])
            if "m" in phases:
                mixa_phase(l)
            if "x" in phases:
                exchange_phase()
            if "b" in phases:
                mixb_phase(l)
            if "F" in phases:
                ffn_phase(xres, xres, S["g_ffn2"][l], W["ffn2_w_gate"][l], W["ffn2_w_up"][l], W["ffn2_w_down"][l])
        norm_phase()
    return nc


def fused_inputs(P, core):
    half = core % 2
    x = P["x"]
    m = {"x": np.ascontiguousarray(x[core // 2, half * NTOK:(half + 1) * NTOK, :].T)}
    for k in WSHAPES:
        m[k] = P[k]
    return m


def fused_shared(P):
    sh = {}
    sh["g_ffn1"] = np.stack([col16(P["ffn1_norm"][l]) for l in range(DEPTH)])
    sh["g_mix"] = np.stack([col16(P["mix_norm"][l]) for l in range(DEPTH)])
    sh["g_ffn2"] = np.stack([col16(P["ffn2_norm"][l]) for l in range(DEPTH)])
    sh["g_fin"] = col16(P["final_norm"])
    sh["bf"] = np.ascontiguousarray(P["fox_b_f"].reshape(DEPTH, 8, 1)).astype(np.float32)
    sh["wst"] = np.ascontiguousarray(np.transpose(P["gmlp_w_s"], (0, 3, 1, 2)).reshape(DEPTH, 128, 512)).astype(np.float32)
    sh["lnb3"] = np.ascontiguousarray(np.stack([np.concatenate(
        [np.broadcast_to(P["gmlp_ln_g"][l][None, :], (128, 512)), np.broadcast_to(P["gmlp_ln_b"][l][None, :], (128, 512)),
         np.broadcast_to(P["gmlp_b_s"][l].reshape(1, 512), (128, 512))], axis=1) for l in range(DEPTH)])).astype(np.float32)
    sh["ong"] = np.ascontiguousarray(np.stack([P["out_norm"][l][1536:2048].reshape(4, 128).T for l in range(DEPTH)])).astype(np.float32)
    sh["ona"] = np.ascontiguousarray(np.stack([P["out_norm"][l][0:1536].reshape(12, 128).T for l in range(DEPTH)])).astype(np.float32)
    return sh


_FUSED = {}


def kernel(**inputs):
    P = {k: np.asarray(v) for k, v in inputs.items()}
    if "nc" not in _FUSED:
        _FUSED["nc"] = build_fused()
    sh = fused_shared(P)
    maps = []
    for c in range(NCORES):
        half = c % 2
        m = fused_inputs(P, c)
        m.update(sh)
        hc = host_consts(half)
        m.update({"tab": hc["tab"], "maskr": hc["maskr"], "triu": hc["triu"], "ident": hc["ident"]})
        m.update(mixb_consts(half))
        maps.append(m)
    res = run_bass_kernel_spmd(_FUSED["nc"], maps, core_ids=list(range(NCORES)))
    x = P["x"]
    outp = np.empty_like(x)
    for c in range(NCORES):
        outp[c // 2, (c % 2) * NTOK:(c % 2 + 1) * NTOK, :] = res.results[c]["out"].T
    return outp
```

```python
import contextlib
import numpy as np
import concourse.bass as bass
import concourse.mybir as mybir
from concourse.bass_utils import run_bass_kernel_spmd

F32 = mybir.dt.float32
BF16 = mybir.dt.bfloat16
AF = mybir.ActivationFunctionType
ALU = mybir.AluOpType

D = 2048
NTOK = 2048
DFF = 5632
NCORES = 8
DEPTH = 4
EPS = 1e-6
KC = D // 128
TT = 1024
NTT = NTOK // TT
FH = 22
INCOLS = 6152


class Cnt:
    def __init__(self, h):
        self.h = h
        self.v = 0


class Prog:
    ENGS = ("sync", "scalar", "vector", "gpsimd", "tensor")

    def __init__(self, nc, stack):
        self.nc = nc
        self.stack = stack
        self.q = {e: [] for e in self.ENGS}
        self.waited = {e: {} for e in self.ENGS}
        self.cache = {}

    def sem(self, name):
        if name not in self.cache:
            self.cache[name] = Cnt(self.stack.enter_context(self.nc.semaphore(name)))
        return self.cache[name]

    def barrier(self):
        for eng in self.ENGS:
            self.op(eng, None, waits=[(c, c.v) for c in self.cache.values()])

    def sems(self, name, n):
        return [self.sem("%s%d" % (name, i)) for i in range(n)]

    def op(self, eng, fn, waits=(), inc=None, k=1):
        ws = []
        for (c, v) in waits:
            if v <= 0:
                continue
            key = id(c)
            if self.waited[eng].get(key, 0) >= v:
                continue
            self.waited[eng][key] = v
            ws.append((c.h, v))
        tgt = None
        if inc is not None:
            inc.v += k
            tgt = inc.v
        self.q[eng].append((ws, fn, inc.h if inc is not None else None, k))
        return tgt

    def dma(self, eng, out, in_, waits=(), inc=None):
        return self.op(eng, lambda e: e.dma_start(out=out, in_=in_), waits, inc, 16)

    def wait_only(self, eng, waits):
        self.q[eng].append(([(c.h, v) for (c, v) in waits if v > 0], None, None, 0))

    def emit(self):
        with self.nc.Block() as block:
            for name in self.ENGS:
                q = self.q[name]

                def body(e, q=q):
                    for ws, fn, inc, k in q:
                        for (h, v) in ws:
                            e.wait_ge(h, v)
                        if fn is None:
                            continue
                        ins = fn(e)
                        if inc is not None:
                            ins.then_inc(inc, k)

                getattr(block, name)(body)
        self.q = {e: [] for e in self.ENGS}


class Ctx:
    pass


_UID = [0]


def _uid():
    _UID[0] += 1
    return _UID[0]


def alloc_common(nc, stack, p, tt=TT, nps=8, stat_bank=6):
    c = Ctx()
    uid = _uid()
    c.nc = nc
    c.p = p
    c.TT = tt
    c.NSEG = tt // 512
    c.stat_bank = stat_bank
    sb = lambda name, shape, dt: stack.enter_context(nc.sbuf_tensor("sb%d_%s" % (uid, name), shape, dt))
    c.sb = sb
    c.uid = uid
    c.stack = stack
    c.ones = sb("ones", [128, 128], F32)
    c.xs = sb("xs", [128, 3, tt], F32)
    c.sq = sb("sq", [128, 2, tt], F32)
    c.rstd = sb("rstd", [128, tt], F32)
    c.h = sb("h", [128, KC, tt], BF16)
    c.ps = stack.enter_context(nc.psum_tensor("ps%d" % uid, [128, nps, 512], F32))
    c.xs_full = p.sems("xsfull", 3)
    c.xs_st = p.sems("xsst", 3)
    c.xs_cond = [[], [], []]
    c.xs_n = 0
    c.act_sq = p.sem("actsq")
    c.pe_st = p.sem("pest")
    c.dve_m = p.sem("dvem")
    c.act_m = p.sem("actm")
    c.dve_h = p.sem("dveh")
    c.setup_v = p.sem("setupv")
    c.setup_d = p.sem("setupd")
    p.op("vector", lambda e: e.memset(c.ones[:], 1.0), inc=c.setup_v)
    c.sq_n = 0
    c.h_free = []
    c.ps_free_waits = []
    return c


def xs_acquire(c):
    s = c.xs_n % 3
    c.xs_n += 1
    return s, list(c.xs_cond[s])


def norm_stats_and_h(c, xsrc, gcol, tt, out_fn=None):
    p = c.p
    TT = c.TT
    t0 = tt * TT
    SB = c.stat_bank
    for kc in range(KC):
        s, w = xs_acquire(c)
        full = p.dma("sync", c.xs[:, s, :], xsrc[kc * 128:(kc + 1) * 128, t0:t0 + TT], waits=w, inc=c.xs_full[s])
        q = c.sq_n % 2
        c.sq_n += 1
        a = p.op("scalar",
                 lambda e, s=s, q=q: e.activation(out=c.sq[:, q, :], in_=c.xs[:, s, :], func=AF.Square),
                 waits=[(c.xs_full[s], full), (c.pe_st, c.pe_st.v - 1)], inc=c.act_sq)
        c.xs_cond[s] = [(c.act_sq, a)]
        extra = list(c.ps_free_waits) if kc == 0 else []
        for sg_ in range(c.NSEG):
            p.op("tensor",
                 lambda e, q=q, kc=kc, sg_=sg_: e.matmul(c.ps[:, SB + sg_, :], lhsT=c.ones[:],
                                                        rhs=c.sq[:, q, sg_ * 512:(sg_ + 1) * 512],
                                                        start=(kc == 0), stop=(kc == KC - 1)),
                 waits=([(c.act_sq, a), (c.setup_v, c.setup_v.v)] + extra) if sg_ == 0 else [],
                 inc=(c.pe_st if sg_ == c.NSEG - 1 else None))
    st_done = c.pe_st.v
    psv = c.ps[:, SB:SB + c.NSEG, :]
    rv = c.rstd[:].rearrange("p (a b) -> p a b", a=c.NSEG)
    d1 = p.op("vector",
              lambda e: e.tensor_scalar(out=rv, in0=psv, scalar1=1.0 / D, scalar2=EPS, op0=ALU.mult, op1=ALU.add),
              waits=[(c.pe_st, st_done), (c.dve_h, c.dve_h.v)], inc=c.dve_m)
    c.ps_free_waits = [(c.dve_m, d1)]
    a1 = p.op("scalar", lambda e: e.activation(out=c.rstd[:], in_=c.rstd[:], func=AF.Sqrt),
              waits=[(c.dve_m, d1)], inc=c.act_m)
    d2 = p.op("vector", lambda e: e.reciprocal(out=c.rstd[:], in_=c.rstd[:]),
              waits=[(c.act_m, a1)], inc=c.dve_m)
    for kc in range(KC):
        s, w = xs_acquire(c)
        full = p.dma("sync", c.xs[:, s, :], xsrc[kc * 128:(kc + 1) * 128, t0:t0 + TT], waits=w, inc=c.xs_full[s])
        if out_fn is None:
            waits = [(c.xs_full[s], full), (c.dve_m, d2), (c.setup_d, c.setup_d.v)]
            if kc == 0:
                waits += c.h_free
            hv = p.op("vector",
                 lambda e, s=s, kc=kc: e.scalar_tensor_tensor(out=c.h[:, kc, :], in0=c.xs[:, s, :],
                                                              scalar=gcol[:, kc:kc + 1], in1=c.rstd[:],
                                                              op0=ALU.mult, op1=ALU.mult),
                 waits=waits, inc=c.dve_h)
            c.xs_cond[s] = [(c.dve_h, hv)]
        else:
            out_fn(kc, s, [(c.xs_full[s], full), (c.dve_m, d2), (c.setup_d, c.setup_d.v)])
    return c.dve_h.v


def alloc_ffn(c):
    p = c.p
    sb = c.sb
    c.hid = sb("hid", [128, FH, TT], BF16)
    c.sg = sb("sg", [128, 2, 512], F32)
    c.wgu = sb("wgu", [128, 2, 2, KC, 256], BF16)
    c.wd = sb("wd", [128, 2, FH, 256], BF16)
    c.wgu_full = p.sems("wgufull", 2)
    c.wd_full = p.sems("wdfull", 2)
    c.pe_gu = p.sem("pegu")
    c.act_sg = p.sem("actsg")
    c.dve_hid = p.sem("dvehid")
    c.pe_dn = p.sem("pedn")
    c.dve_res = p.sem("dveres")
    c.n_panel = 0
    c.n_gu = c.pe_gu.v
    c.n_dpanel = 0
    c.n_dn = c.pe_dn.v
    c.panel_done = {}
    c.dpanel_done = {}
    c.hid_free = []


def ffn_body(c, xin, xout, gcol, wg, wu, wd):
    p = c.p
    wgv = wg.rearrange("(kc p) f -> p kc f", p=128)
    wuv = wu.rearrange("(kc p) f -> p kc f", p=128)
    wdv = wd.rearrange("(fc p) d -> p fc d", p=128)
    for tt in range(NTT):
        t0 = tt * TT
        h_ready = norm_stats_and_h(c, xin, gcol, tt)
        for hf in range(2):
            for pn in range(FH // 2):
                col0 = (hf * FH + pn * 2) * 128
                npn = c.n_panel
                b = npn % 2
                c.n_panel += 1
                wfree = [(c.pe_gu, c.panel_done[npn - 2])] if npn >= 2 else []
                p.dma("gpsimd", c.wgu[:, b, 0, :, :], wgv[:, :, col0:col0 + 256], waits=wfree, inc=c.wgu_full[b])
                wl = p.dma("gpsimd", c.wgu[:, b, 1, :, :], wuv[:, :, col0:col0 + 256], waits=wfree, inc=c.wgu_full[b])
                for jj in range(2):
                    j = pn * 2 + jj
                    for th in range(2):
                        n = c.n_gu
                        c.n_gu += 1
                        gb = n % 2
                        ub = 2 + n % 2
                        for kc in range(KC):
                            waits = []
                            if kc == 0:
                                waits = [(c.wgu_full[b], wl), (c.dve_h, h_ready), (c.act_sg, n - 1)]
                            p.op("tensor",
                                 lambda e, b=b, kc=kc, jj=jj, th=th, gb=gb: e.matmul(
                                     c.ps[:, gb, :], lhsT=c.wgu[:, b, 0, kc, jj * 128:(jj + 1) * 128],
                                     rhs=c.h[:, kc, th * 512:(th + 1) * 512], start=(kc == 0), stop=(kc == KC - 1)),
                                 waits=waits)
                        for kc in range(KC):
                            waits = []
                            if kc == 0:
                                waits = [(c.dve_hid, n - 1)]
                            last = (kc == KC - 1)
                            p.op("tensor",
                                 lambda e, b=b, kc=kc, jj=jj, th=th, ub=ub: e.matmul(
                                     c.ps[:, ub, :], lhsT=c.wgu[:, b, 1, kc, jj * 128:(jj + 1) * 128],
                                     rhs=c.h[:, kc, th * 512:(th + 1) * 512], start=(kc == 0), stop=(kc == KC - 1)),
                                 waits=waits, inc=(c.pe_gu if last else None))
                        gu = c.pe_gu.v
                        a = p.op("scalar",
                                 lambda e, n=n, gb=gb: e.activation(out=c.sg[:, n % 2, :], in_=c.ps[:, gb, :], func=AF.Silu),
                                 waits=[(c.pe_gu, gu), (c.dve_hid, n - 1)], inc=c.act_sg)
                        waits = [(c.act_sg, a), (c.pe_gu, gu)]
                        if j == 0 and th == 0:
                            waits += c.hid_free
                        p.op("vector",
                             lambda e, n=n, ub=ub, j=j, th=th: e.tensor_tensor(
                                 out=c.hid[:, j, th * 512:(th + 1) * 512], in0=c.sg[:, n % 2, :], in1=c.ps[:, ub, :],
                                 op=ALU.mult),
                             waits=waits, inc=c.dve_hid)
                c.panel_done[npn] = c.pe_gu.v
            if hf == 1:
                c.h_free = [(c.pe_gu, c.pe_gu.v)]
            hid_ready = c.dve_hid.v
            xsrc = xin if hf == 0 else xout
            for pd in range(8):
                col0 = pd * 256
                npd = c.n_dpanel
                b = npd % 2
                c.n_dpanel += 1
                wl = p.dma("gpsimd", c.wd[:, b, :, :], wdv[:, hf * FH:(hf + 1) * FH, col0:col0 + 256],
                           waits=([(c.pe_dn, c.dpanel_done[npd - 2])] if npd >= 2 else []), inc=c.wd_full[b])
                for ii in range(2):
                    i = pd * 2 + ii
                    s, w = xs_acquire(c)
                    full = p.dma("sync", c.xs[:, s, :], xsrc[i * 128:(i + 1) * 128, t0:t0 + TT], waits=w,
                                 inc=c.xs_full[s])
                    for th in range(2):
                        n = c.n_dn
                        c.n_dn += 1
                        ob = 4 + n % 2
                        for f in range(FH):
                            waits = []
                            if f == 0:
                                waits = [(c.wd_full[b], wl), (c.dve_hid, hid_ready), (c.dve_res, n - 1)]
                            last = (f == FH - 1)
                            p.op("tensor",
                                 lambda e, b=b, f=f, ii=ii, th=th, ob=ob: e.matmul(
                                     c.ps[:, ob, :], lhsT=c.wd[:, b, f, ii * 128:(ii + 1) * 128],
                                     rhs=c.hid[:, f, th * 512:(th + 1) * 512], start=(f == 0), stop=(f == FH - 1)),
                                 waits=waits, inc=(c.pe_dn if last else None))
                        dn = c.pe_dn.v
                        r = p.op("vector",
                                 lambda e, s=s, th=th, ob=ob: e.scalar_tensor_tensor(
                                     out=c.xs[:, s, th * 512:(th + 1) * 512], in0=c.ps[:, ob, :], scalar=0.5,
                                     in1=c.xs[:, s, th * 512:(th + 1) * 512], op0=ALU.mult, op1=ALU.add),
                                 waits=[(c.pe_dn, dn), (c.xs_full[s], full)], inc=c.dve_res)
                    sv = p.dma("sync", xout[i * 128:(i + 1) * 128, t0:t0 + TT], c.xs[:, s, :],
                               waits=[(c.dve_res, r)], inc=c.xs_st[s])
                    c.xs_cond[s] = [(c.xs_st[s], sv)]
                c.dpanel_done[npd] = c.pe_dn.v
            c.hid_free = [(c.pe_dn, c.pe_dn.v)]


def finish(c):
    p = c.p
    p.wait_only("sync", [(c.xs_st[s], c.xs_st[s].v) for s in range(3)])


def build_ffn():
    nc = bass.Bass("TRN2", target_bir_lowering=False)
    xin = nc.dram_tensor("xin", [D, NTOK], F32, kind="ExternalInput").ap()
    g = nc.dram_tensor("g", [128, KC], F32, kind="ExternalInput").ap()
    wg = nc.dram_tensor("wg", [D, DFF], F32, kind="ExternalInput").ap()
    wu = nc.dram_tensor("wu", [D, DFF], F32, kind="ExternalInput").ap()
    wd = nc.dram_tensor("wd", [DFF, D], F32, kind="ExternalInput").ap()
    xout = nc.dram_tensor("xout", [D, NTOK], F32, kind="ExternalOutput").ap()
    with contextlib.ExitStack() as stack:
        p = Prog(nc, stack)
        c = alloc_common(nc, stack, p)
        alloc_ffn(c)
        gcol = c.sb("gcol", [128, KC], F32)
        p.dma("sync", gcol[:], g[:, :], inc=c.setup_d)
        ffn_body(c, xin, xout, gcol, wg, wu, wd)
        finish(c)
        p.emit()
    return nc


FOX_SCALE = 128.0 ** -0.5
GAM = [1.0 - 2.0 ** -(5 + h) for h in range(4)]
GAM64 = [g ** 64 for g in GAM]
TA = 512
NBLK = TA // 128
AX = mybir.AxisListType


class RowSplit:
    def __init__(self, parts):
        self.parts = parts
        self.h = parts[0].shape[0]

    def __getitem__(self, key):
        rs, cs = key
        i = rs.start // self.h
        assert (rs.stop - 1) // self.h == i
        return self.parts[i][rs.start - i * self.h:rs.stop - i * self.h, cs]


def _parts(x):
    return x.parts if isinstance(x, RowSplit) else [x]


class Slots:
    def __init__(self, p, name, n):
        self.sem = p.sems(name, n)
        self.cond = [[] for _ in range(n)]
        self.i = 0
        self.n = n

    def acquire(self):
        s = self.i % self.n
        self.i += 1
        return s, list(self.cond[s])


def mixa_body(c, xin, gcol, w_in, io):
    p = c.p
    nc = c.nc
    sb = c.sb
    winv = w_in.rearrange("(kc p) f -> p kc f", p=128)
    tabv = io["tab"].rearrange("(b p) f -> p b f", p=128)
    wp = sb("wp", [128, 2, 8192], BF16)
    wps = Slots(p, "wps", 2)
    wpfm = lambda b: wp[:, b, 0:4096].rearrange("p (k c) -> p k c", c=256)
    wptm = lambda b: wp[:, b, :].rearrange("p (k c) -> p k c", c=512)
    wpfz = lambda b: wp[:, b, 0:128].rearrange("p (k c) -> p k c", c=8)
    tabt = sb("tabt", [128, 2, NBLK, 268], F32)
    tabs = Slots(p, "tabs", 2)
    stg16 = sb("stg16", [128, 4, 512], BF16)
    st16 = Slots(p, "st16", 4)
    stg32 = sb("stg32", [128, 2, 512], F32)
    st32 = Slots(p, "st32", 2)
    u = sb("u", [128, 4, TA], F32)
    rv = sb("rv", [128, NBLK, 512], BF16)
    kr = sb("kr", [128, NBLK, 512], BF16)
    kz = sb("kz", [128, NBLK, 512], BF16)
    qx = sb("qx", [128, NBLK, 512], BF16)
    qg = sb("qg", [128, NBLK, 512], BF16)
    vln = sb("vln", [128, NBLK, 512], BF16)
    rot = sb("rot", [128, 2, 2, 512], F32)
    rots = Slots(p, "rots", 2)
    st = sb("lnst", [128, 24], F32)
    fzt = sb("fzt", [8, 512], F32)
    onesr = sb("onesr", [8, 512], F32)
    cneg = sb("cneg", [8, NTOK], F32)
    S32 = sb("S32", [128, 512], F32)
    Sb = sb("Sb", [128, 2 * NBLK, 512], BF16)
    krT = sb("krT", [128, 512], BF16)
    qxT = sb("qxT", [128, 512], BF16)
    sm = sb("sm", [128, 512], BF16)
    y1 = sb("y1", [128, 512], F32)
    y2 = sb("y2", [128, 512], F32)
    pst = c.stack.enter_context(nc.psum_tensor("pst%d" % c.uid, [128, 2, 1024], BF16))
    c.pe = p.sem("pe")
    c.act = p.sem("act")
    c.dve = p.sem("dve")
    misc = p.sem("miscst")
    pj_cond = [[], []]
    kv_cond = [[], []]
    b4_cond = []
    pst_cond = [[], []]
    pj_n = [0]
    kv_n = [0]
    u_free = []
    ret_free = []
    vln_free = []
    p.op("vector", lambda e: e.memset(onesr[:], 1.0), inc=c.setup_v)
    p.op("vector", lambda e: e.memset(S32[:], 0.0), inc=c.setup_v)
    setupw = [(c.setup_d, c.setup_d.v), (c.setup_v, c.setup_v.v)]

    def V4(ap):
        return ap.rearrange("p (a b) -> p a b", a=4)

    def bc(ap4):
        return ap4.unsqueeze(2).to_broadcast([128, 4, 128])

    def bh(ap128):
        return ap128.unsqueeze(1).to_broadcast([128, 4, 128])

    def bh64(ap64):
        return ap64.unsqueeze(1).to_broadcast([128, 4, 64])

    def proj(b, waits, lhs_fn, rhs_fn, out_fn):
        n = pj_n[0]
        pj_n[0] += 1
        bank = n % 2
        for kc in range(KC):
            p.op("tensor",
                 lambda e, kc=kc: e.matmul(out_fn(bank), lhsT=lhs_fn(kc), rhs=rhs_fn(kc), start=(kc == 0),
                                           stop=(kc == KC - 1)),
                 waits=(list(waits) + pj_cond[bank]) if kc == 0 else [], inc=(c.pe if kc == KC - 1 else None))
        return bank, c.pe.v

    def store16(src_fn, dst_ap, waits, eng_op):
        s, w = st16.acquire()
        a = p.op("scalar", lambda e: eng_op(e, stg16[:, s, :]), waits=list(waits) + w, inc=c.act)
        sv = p.dma("sync", dst_ap, src_fn(stg16[:, s, :]), waits=[(c.act, a)], inc=st16.sem[s])
        st16.cond[s] = [(st16.sem[s], sv)]
        return a

    for tt in range(NTOK // TA):
        t0 = tt * TA
        h_ready = norm_stats_and_h(c, xin, gcol, tt)
        hw = [(c.dve_h, h_ready)]
        ts_, w = tabs.acquire()
        tk = p.dma("sync", tabt[:, ts_], tabv[:, tt * NBLK:(tt + 1) * NBLK, :], waits=w, inc=tabs.sem[ts_])
        tabw = [(tabs.sem[ts_], tk)]
        for name, cbase, ncol in (("fq", 0, 1024), ("fk", 1024, 1024), ("rg", 4616, 512), ("gu", 5128, 512)):
            for pn in range(ncol // 256):
                col0 = cbase + pn * 256
                b, w = wps.acquire()
                t = p.dma("gpsimd", wpfm(b), winv[:, :, col0:col0 + 256], waits=w, inc=wps.sem[b])
                for jj in range(2):
                    ch = pn * 2 + jj
                    bank, pt = proj(b, [(wps.sem[b], t)] + hw,
                                    lambda kc, b=b, jj=jj: wpfm(b)[:, kc, jj * 128:(jj + 1) * 128],
                                    lambda kc: c.h[:, kc, :], lambda bank: c.ps[:, bank, :])
                    pw = [(c.pe, pt)]
                    if name == "fq":
                        a = store16(lambda s_: s_, io["qT"][ch * 128:(ch + 1) * 128, t0:t0 + TA], pw,
                                    lambda e, o, bank=bank: e.mul(out=o, in_=c.ps[:, bank, :], mul=FOX_SCALE))
                    elif name == "fk":
                        a = store16(lambda s_: s_, io["kT"][ch * 128:(ch + 1) * 128, t0:t0 + TA], pw,
                                    lambda e, o, bank=bank: e.copy(out=o, in_=c.ps[:, bank, :]))
                    elif name == "rg":
                        s, w2 = st32.acquire()
                        a = p.op("scalar", lambda e, s=s, bank=bank: e.activation(out=stg32[:, s, :], in_=c.ps[:, bank, :],
                                                                                func=AF.Silu),
                                 waits=pw + w2, inc=c.act)
                        sv = p.dma("sync", io["rgs"][ch * 128:(ch + 1) * 128, t0:t0 + TA], stg32[:, s, :],
                                   waits=[(c.act, a)], inc=st32.sem[s])
                        st32.cond[s] = [(st32.sem[s], sv)]
                    else:
                        a = p.op("scalar", lambda e, ch=ch, bank=bank: e.activation(out=u[:, ch, :], in_=c.ps[:, bank, :],
                                                                                  func=AF.Gelu_apprx_tanh),
                                 waits=pw + (u_free if ch == 0 else []), inc=c.act)
                    pj_cond[bank] = [(c.act, a)]
                wps.cond[b] = [(c.pe, pt)]
        b, w = wps.acquire()
        t = p.dma("gpsimd", wpfz(b), winv[:, :, 3072:3080], waits=w, inc=wps.sem[b])
        bank, pt = proj(b, [(wps.sem[b], t)] + hw, lambda kc, b=b: wpfz(b)[:, kc, :], lambda kc: c.h[:, kc, :],
                        lambda bank: c.ps[0:8, bank, :])
        wps.cond[b] = [(c.pe, pt)]
        a = p.op("scalar", lambda e, bank=bank: e.activation(out=fzt[:], in_=c.ps[0:8, bank, :], func=AF.Exp,
                                                             bias=io["negb"][:, 0:1], scale=-1.0),
                 waits=[(c.pe, pt), (c.dve, c.dve.v)] + setupw, inc=c.act)
        pj_cond[bank] = [(c.act, a)]
        a = p.op("scalar", lambda e: e.activation(out=fzt[:], in_=fzt[:], func=AF.Ln, bias=1.0), waits=[(c.act, a)],
                 inc=c.act)
        init = 0.0 if tt == 0 else cneg[:, t0 - 1:t0]
        p.op("vector", lambda e, init=init, t0=t0: e.tensor_tensor_scan(out=cneg[:, t0:t0 + TA], data0=onesr[:], data1=fzt[:],
                                                                 initial=init, op0=ALU.mult, op1=ALU.add),
             waits=[(c.act, a), (c.dve, c.dve.v)] + setupw, inc=c.dve)
        ready = {}
        s32_last = [None]

        def emit_kv(n):
            tb, a_ = n // 2, n % 2
            kb = 2 + kv_n[0] % 2
            ci = kv_n[0] % 2
            kv_n[0] += 1
            for hh in range(4):
                sl = slice(hh * 128, (hh + 1) * 128)
                p.op("tensor", lambda e, kb=kb, sl=sl, a_=a_, tb=tb: e.matmul(
                    c.ps[:, kb, sl], lhsT=kz[a_ * 64:(a_ + 1) * 64, tb, sl], rhs=rv[a_ * 64:(a_ + 1) * 64, tb, sl],
                    start=True, stop=True),
                     waits=(ready[("kz", tb)] + ready[("rv", tb)] + kv_cond[ci]) if hh == 0 else [],
                     inc=(c.pe if hh == 3 else None))
            pt_ = c.pe.v
            a = p.op("scalar", lambda e, n=n: e.copy(out=Sb[:, n, :], in_=S32[:]),
                     waits=[(c.dve, c.dve.v)] + (ret_free if n == 0 else []) + setupw, inc=c.act)
            ready[("Sb", n)] = [(c.act, a)]
            for hh in range(4):
                sl = slice(hh * 128, (hh + 1) * 128)
                d = p.op("vector", lambda e, kb=kb, sl=sl, hh=hh: e.scalar_tensor_tensor(
                    out=S32[:, sl], in0=S32[:, sl], scalar=GAM64[hh], in1=c.ps[:, kb, sl], op0=ALU.mult, op1=ALU.add),
                         waits=[(c.pe, pt_), (c.act, a)] if hh == 0 else [], inc=c.dve)
            kv_cond[ci] = [(c.dve, d)]
            s32_last[0] = [(c.dve, d)]

        for name, col0 in (("fv0", 2048), ("fv1", 2560), ("rv", 4104), ("rk", 3592), ("rq", 3080), ("gv", 5640)):
            b, w = wps.acquire()
            t = p.dma("gpsimd", wptm(b), winv[:, :, col0:col0 + 512], waits=w, inc=wps.sem[b])
            for tb in range(NBLK):
                bank, pt = proj(b, [(wps.sem[b], t)] + hw,
                                lambda kc, tb=tb: c.h[:, kc, tb * 128:(tb + 1) * 128],
                                lambda kc, b=b: wptm(b)[:, kc, :], lambda bank: c.ps[:, bank, :])
                pw = [(c.pe, pt)]
                psb = c.ps[:, bank, :]
                psv = V4(psb)
                r0 = t0 + tb * 128
                if name in ("fv0", "fv1"):
                    hc = 0 if name == "fv0" else 512
                    a = store16(lambda s_: s_, io["v"][r0:r0 + 128, hc:hc + 512], pw,
                                lambda e, o, psb=psb: e.copy(out=o, in_=psb))
                    pj_cond[bank] = [(c.act, a)]
                elif name == "rv":
                    a = p.op("scalar", lambda e, tb=tb, psb=psb: e.copy(out=rv[:, tb, :], in_=psb),
                             waits=pw + (ret_free if tb == 0 else []), inc=c.act)
                    pj_cond[bank] = [(c.act, a)]
                    ready[("rv", tb)] = [(c.act, a)]
                elif name in ("rk", "rq"):
                    rs, w2 = rots.acquire()
                    r1 = rot[:, rs, 0, :]
                    r2 = rot[:, rs, 1, :]
                    cosb = bh(tabt[:, ts_, tb, 0:128])
                    sina = bh64(tabt[:, ts_, tb, 128:192])
                    sinb = bh64(tabt[:, ts_, tb, 192:256])
                    p.op("vector", lambda e, psv=psv, r1=r1, cosb=cosb: e.tensor_tensor(out=V4(r1), in0=psv, in1=cosb,
                                                                                      op=ALU.mult),
                         waits=pw + w2 + tabw, inc=c.dve)
                    p.op("vector", lambda e, psv=psv, r2=r2, sina=sina: e.tensor_tensor(
                        out=V4(r2)[:, :, 0:64], in0=psv[:, :, 64:128], in1=sina, op=ALU.mult), inc=c.dve)
                    d = p.op("vector", lambda e, psv=psv, r2=r2, sinb=sinb: e.tensor_tensor(
                        out=V4(r2)[:, :, 64:128], in0=psv[:, :, 0:64], in1=sinb, op=ALU.mult), inc=c.dve)
                    pj_cond[bank] = [(c.dve, d)]
                    d = p.op("vector", lambda e, r1=r1, r2=r2: e.tensor_tensor(out=r1, in0=r1, in1=r2, op=ALU.add),
                             waits=[(c.dve, d)], inc=c.dve)
                    fw = ret_free if tb == 0 else []
                    if name == "rk":
                        a = p.op("scalar", lambda e, tb=tb, r1=r1: e.copy(out=kr[:, tb, :], in_=r1),
                                 waits=[(c.dve, d)] + fw, inc=c.act)
                        zb = bc(tabt[:, ts_, tb, 264:268])
                        d2 = p.op("vector", lambda e, tb=tb, r1=r1, zb=zb: e.tensor_tensor(out=V4(kz[:, tb, :]), in0=V4(r1),
                                                                                         in1=zb, op=ALU.mult),
                                  waits=[(c.dve, d)] + fw, inc=c.dve)
                        rots.cond[rs] = [(c.act, a), (c.dve, d2)]
                        ready[("kr", tb)] = [(c.act, a)]
                        ready[("kz", tb)] = [(c.dve, d2)]
                    else:
                        xb = bc(tabt[:, ts_, tb, 256:260])
                        gb_ = bc(tabt[:, ts_, tb, 260:264])
                        p.op("vector", lambda e, tb=tb, r1=r1, xb=xb: e.tensor_tensor(out=V4(qx[:, tb, :]), in0=V4(r1),
                                                                                    in1=xb, op=ALU.mult),
                             waits=[(c.dve, d)] + fw, inc=c.dve)
                        d2 = p.op("vector", lambda e, tb=tb, r1=r1, gb_=gb_: e.tensor_tensor(out=V4(qg[:, tb, :]),
                                                                                           in0=V4(r1), in1=gb_,
                                                                                           op=ALU.mult), inc=c.dve)
                        rots.cond[rs] = [(c.dve, d2)]
                        ready[("q", tb)] = [(c.dve, d2)]
                        emit_kv(tb)
                else:
                    rs, w2 = rots.acquire()
                    r1 = rot[:, rs, 0, :]
                    r2 = rot[:, rs, 1, :]
                    a = p.op("scalar", lambda e, psb=psb, r1=r1: e.activation(out=r1, in_=psb, func=AF.Gelu_apprx_tanh),
                             waits=pw + w2, inc=c.act)
                    pj_cond[bank] = [(c.act, a)]
                    a2 = p.op("scalar", lambda e, r1=r1, r2=r2: e.activation(out=r2, in_=r1, func=AF.Square),
                              waits=[(c.act, a)], inc=c.act)
                    d = p.op("vector", lambda e, r1=r1: e.tensor_reduce(out=st[:, 0:4], in_=V4(r1), axis=AX.X, op=ALU.add),
                             waits=[(c.act, a), (c.dve, c.dve.v)], inc=c.dve)
                    d = p.op("vector", lambda e, r2=r2: e.tensor_reduce(out=st[:, 4:8], in_=V4(r2), axis=AX.X, op=ALU.add),
                             waits=[(c.act, a2)], inc=c.dve)
                    d = p.op("vector", lambda e: e.tensor_scalar(out=st[:, 8:12], in0=st[:, 0:4], scalar1=1.0 / 128,
                                                                 scalar2=None, op0=ALU.mult),
                             waits=[(c.dve, d)], inc=c.dve)
                    d = p.op("vector", lambda e: e.tensor_tensor(out=st[:, 12:16], in0=st[:, 8:12], in1=st[:, 8:12],
                                                                 op=ALU.mult), waits=[(c.dve, d)], inc=c.dve)
                    d = p.op("vector", lambda e: e.scalar_tensor_tensor(out=st[:, 16:20], in0=st[:, 4:8], scalar=1.0 / 128,
                                                                        in1=st[:, 12:16], op0=ALU.mult,
                                                                        op1=ALU.subtract), waits=[(c.dve, d)], inc=c.dve)
                    d = p.op("vector", lambda e: e.tensor_scalar(out=st[:, 16:20], in0=st[:, 16:20], scalar1=EPS,
                                                                 scalar2=None, op0=ALU.add), waits=[(c.dve, d)], inc=c.dve)
                    a3 = p.op("scalar", lambda e: e.activation(out=st[:, 16:20], in_=st[:, 16:20], func=AF.Sqrt),
                              waits=[(c.dve, d)], inc=c.act)
                    d = p.op("vector", lambda e: e.reciprocal(out=st[:, 20:24], in_=st[:, 16:20]), waits=[(c.act, a3)],
                             inc=c.dve)
                    d = p.op("vector", lambda e, r1=r1: e.tensor_tensor(out=V4(r1), in0=V4(r1), in1=bc(st[:, 8:12]),
                                                                      op=ALU.subtract), waits=[(c.dve, d)], inc=c.dve)
                    d = p.op("vector", lambda e, r1=r1: e.tensor_tensor(out=V4(r1), in0=V4(r1), in1=bc(st[:, 20:24]),
                                                                      op=ALU.mult), waits=[(c.dve, d)], inc=c.dve)
                    d = p.op("vector", lambda e, r1=r1: e.tensor_tensor(out=r1, in0=r1, in1=io["lnb3"][:, 0, :],
                                                                      op=ALU.mult), waits=[(c.dve, d)] + setupw,
                             inc=c.dve)
                    d = p.op("vector", lambda e, r1=r1, tb=tb: e.tensor_tensor(out=vln[:, tb, :], in0=r1,
                                                                             in1=io["lnb3"][:, 1, :], op=ALU.add),
                             waits=[(c.dve, d)] + (vln_free if tb == 0 else []), inc=c.dve)
                    rots.cond[rs] = [(c.dve, d)]
                    ready[("vln", tb)] = [(c.dve, d)]
                    emit_kv(NBLK + tb)
            wps.cond[b] = [(c.pe, pt)]
        s32_done = s32_last[0]
        for tb in range(NBLK):
            r0 = t0 + tb * 128
            for gg in range(4):
                sl = slice(gg * 128, (gg + 1) * 128)
                p.op("tensor", lambda e, sl=sl, tb=tb, gg=gg: e.matmul(c.ps[:, 4, sl], lhsT=vln[:, tb, sl],
                                                                     rhs=io["wsb"][:, gg, :], start=True, stop=True),
                     waits=(ready[("vln", tb)] + b4_cond + setupw) if gg == 0 else [], inc=(c.pe if gg == 3 else None))
            pt = c.pe.v
            d = p.op("vector", lambda e: e.tensor_tensor(out=y1[:], in0=c.ps[:, 4, :], in1=io["lnb3"][:, 2, :], op=ALU.add),
                     waits=[(c.pe, pt), (c.act, c.act.v), (c.dve, c.dve.v)], inc=c.dve)
            d = p.op("vector", lambda e, tb=tb: e.tensor_tensor(out=V4(y1[:]), in0=V4(y1[:]),
                                                               in1=u[:, :, tb * 128:(tb + 1) * 128], op=ALU.mult),
                     waits=[(c.dve, d)], inc=c.dve)
            a = p.op("scalar", lambda e: e.activation(out=y2[:], in_=y1[:], func=AF.Square), waits=[(c.dve, d)], inc=c.act)
            p.op("tensor", lambda e: e.matmul(c.ps[:, 4, :], lhsT=c.ones[:], rhs=y2[:], start=True, stop=True),
                 waits=[(c.act, a), (c.dve, d)], inc=c.pe)
            pt = c.pe.v
            d = p.op("vector", lambda e: e.tensor_scalar(out=y2[:], in0=c.ps[:, 4, :], scalar1=1.0 / 128, scalar2=EPS,
                                                         op0=ALU.mult, op1=ALU.add), waits=[(c.pe, pt)], inc=c.dve)
            b4_cond = [(c.dve, d)]
            a = p.op("scalar", lambda e: e.activation(out=y2[:], in_=y2[:], func=AF.Sqrt), waits=[(c.dve, d)], inc=c.act)
            d = p.op("vector", lambda e: e.reciprocal(out=y2[:], in_=y2[:]), waits=[(c.act, a)], inc=c.dve)
            d = p.op("vector", lambda e: e.tensor_tensor(out=y1[:], in0=y1[:], in1=y2[:], op=ALU.mult),
                     waits=[(c.dve, d)], inc=c.dve)
            s, w = st16.acquire()
            for gg in range(4):
                sl = slice(gg * 128, (gg + 1) * 128)
                a = p.op("vector", lambda e, s=s, sl=sl, gg=gg: e.tensor_scalar(out=stg16[:, s, sl], in0=y1[:, sl],
                                                                              scalar1=io["ong"][:, gg:gg + 1],
                                                                              scalar2=None, op0=ALU.mult),
                         waits=([(c.dve, d)] + w + setupw) if gg == 0 else [], inc=c.dve)
            sv = p.dma("sync", io["ytg"].rearrange("(g c) t -> c g t", c=128)[:, :, r0:r0 + 128], V4(stg16[:, s, :]),
                       waits=[(c.dve, a)], inc=st16.sem[s])
            st16.cond[s] = [(st16.sem[s], sv)]
        u_free = [(c.dve, c.dve.v)]
        vln_free = [(c.pe, c.pe.v)]
        for tb in range(NBLK):
            r0 = t0 + tb * 128
            for src, key, dstT, pb in ((kr, "kr", krT, 0), (qx, "q", qxT, 1), (qg, "q", None, 0)):
                for hh in range(4):
                    sl = slice(hh * 128, (hh + 1) * 128)
                    p.op("tensor", lambda e, src=src, sl=sl, tb=tb, pb=pb: e.transpose(out=pst[:, pb, sl], in_=src[:, tb, sl],
                                                                                     identity=io["identb"][:]),
                         waits=(ready[(key, tb)] + pst_cond[pb] + setupw) if hh == 0 else [],
                         inc=(c.pe if hh == 3 else None))
                pt = c.pe.v
                if dstT is not None:
                    d = p.op("vector", lambda e, dstT=dstT, pb=pb: e.tensor_copy(out=dstT[:], in_=pst[:, pb, 0:512]),
                             waits=[(c.pe, pt), (c.pe, c.pe.v)], inc=c.dve)
                    pst_cond[pb] = [(c.dve, d)]
                    ready[(id(dstT), tb)] = [(c.dve, d)]
                else:
                    s, w = st16.acquire()
                    a = p.op("scalar", lambda e, s=s, pb=pb: e.copy(out=stg16[:, s, :], in_=pst[:, pb, 0:512]),
                             waits=[(c.pe, pt)] + w, inc=c.act)
                    pst_cond[pb] = [(c.act, a)]
                    sv = p.dma("sync", io["ret_qg"].rearrange("(h d) t -> d h t", d=128)[:, :, r0:r0 + 128],
                               V4(stg16[:, s, :]), waits=[(c.act, a)], inc=st16.sem[s])
                    st16.cond[s] = [(st16.sem[s], sv)]
            for hh in range(4):
                sl = slice(hh * 128, (hh + 1) * 128)
                p.op("tensor", lambda e, sl=sl: e.matmul(c.ps[:, 4, sl], lhsT=krT[:, sl], rhs=qxT[:, sl], start=True,
                                                         stop=True),
                     waits=(ready[(id(krT), tb)] + ready[(id(qxT), tb)] + b4_cond) if hh == 0 else [],
                     inc=(c.pe if hh == 3 else None))
            pt = c.pe.v
            d = p.op("vector", lambda e: e.tensor_tensor(out=sm[:], in0=c.ps[:, 4, :], in1=io["maskr"][:], op=ALU.mult),
                     waits=[(c.pe, pt), (c.pe, c.pe.v)] + setupw, inc=c.dve)
            b4_cond = [(c.dve, d)]
            for hh in range(4):
                sl = slice(hh * 128, (hh + 1) * 128)
                p.op("tensor", lambda e, sl=sl, tb=tb: e.matmul(c.ps[:, 5, sl], lhsT=rv[:, tb, sl], rhs=sm[:, sl], start=True,
                                                               stop=False),
                     waits=([(c.dve, d)] + c.ps_free_waits + ready[("Sb", 2 * tb)] + ready[("Sb", 2 * tb + 1)])
                     if hh == 0 else [])
                for a_ in range(2):
                    cs = slice(hh * 128 + a_ * 64, hh * 128 + (a_ + 1) * 64)
                    p.op("tensor", lambda e, sl=sl, cs=cs, tb=tb, a_=a_: e.matmul(
                        c.ps[:, 5, cs], lhsT=Sb[:, 2 * tb + a_, sl], rhs=qxT[:, cs], start=False, stop=(a_ == 1)),
                         inc=(c.pe if (hh == 3 and a_ == 1) else None))
            pt = c.pe.v
            s, w = st32.acquire()
            a = p.op("scalar", lambda e, s=s: e.copy(out=stg32[:, s, :], in_=c.ps[:, 5, :]), waits=[(c.pe, pt)] + w,
                     inc=c.act)
            c.ps_free_waits = c.ps_free_waits + [(c.act, a)]
            sv = p.dma("sync", io["ret_o"].rearrange("(h e) t -> e h t", e=128)[:, :, r0:r0 + 128], V4(stg32[:, s, :]),
                       waits=[(c.act, a)], inc=st32.sem[s])
            st32.cond[s] = [(st32.sem[s], sv)]
        ret_free = [(c.pe, c.pe.v)]
    f1 = p.dma("sync", io["cneg_o"][:, :], cneg[:], waits=[(c.dve, c.dve.v)], inc=misc)
    f2 = p.dma("sync", io["ret_S"][:, :], S32[:], waits=s32_done, inc=misc)
    p.wait_only("sync", [(misc, f2)] + [(st16.sem[s], st16.sem[s].v) for s in range(4)] +
                [(st32.sem[s], st32.sem[s].v) for s in range(2)])


def mixa_setup(c, din, io):
    p = c.p
    sb = c.sb
    gcol = sb("gcol", [128, KC], F32)
    negb = sb("negb", [8, 1], F32)
    maskr = sb("maskr", [128, 512], F32)
    wst = sb("wst", [128, 512], F32)
    triu = sb("triu", [128, 128], F32)
    wsb = sb("wsb", [128, 4, 128], BF16)
    lnb3 = sb("lnb3", [128, 3, 512], F32)
    ong = sb("ong", [128, 4], F32)
    identb = sb("identb", [128, 128], BF16)
    io.update({"negb": negb, "maskr": maskr, "wsb": wsb, "lnb3": lnb3, "ong": ong, "identb": identb})
    for dst, src in ((gcol[:], din["g"]), (negb[:], din["bf"]), (maskr[:], din["maskr"]), (wst[:], din["wst"]),
                     (triu[:], din["triu"]), (lnb3[:].rearrange("p a b -> p (a b)"), din["lnb3"]),
                     (ong[:], din["ong"])):
        p.dma("sync", dst, src, inc=c.setup_d)
    p.dma("gpsimd", identb[:], din["ident"], inc=c.setup_d)
    dl = [(c.setup_d, c.setup_d.v)]
    p.op("vector", lambda e: e.tensor_scalar(out=negb[:], in0=negb[:], scalar1=-1.0, scalar2=None, op0=ALU.mult),
         waits=dl, inc=c.setup_v)
    p.op("vector", lambda e: e.tensor_tensor(out=wsb[:], in0=wst[:].rearrange("p (a b) -> p a b", a=4),
                                             in1=triu[:].unsqueeze(1).to_broadcast([128, 4, 128]), op=ALU.mult),
         inc=c.setup_v)
    return gcol


def build_mixa():
    nc = bass.Bass("TRN2", target_bir_lowering=False)
    di = lambda name, shape: nc.dram_tensor(name, shape, F32, kind="ExternalInput").ap()
    do = lambda name, shape, dt: nc.dram_tensor(name, shape, dt, kind="ExternalOutput").ap()
    xin = di("xin", [D, NTOK])
    w_in = di("w_in", [D, INCOLS])
    din = {"g": di("g", [128, KC])[:, :], "bf": di("bf", [8, 1])[:, :], "maskr": di("maskr", [128, 512])[:, :],
           "wst": di("wst", [128, 512])[:, :], "triu": di("triu", [128, 128])[:, :],
           "lnb3": di("lnb3", [128, 3 * 512])[:, :], "ong": di("ong", [128, 4])[:, :],
           "ident": di("ident", [128, 128])[:, :]}
    io = {
        "tab": di("tab", [NTOK, 268]),
        "qT": do("qT", [1024, NTOK], BF16), "kT": do("kT", [1024, NTOK], BF16), "v": do("v", [NTOK, 1024], BF16),
        "cneg_o": do("cneg", [8, NTOK], F32), "ret_o": do("ret_o", [512, NTOK], F32),
        "ret_qg": do("ret_qg", [512, NTOK], BF16), "ret_S": do("ret_S", [128, 512], F32),
        "rgs": do("rgs", [512, NTOK], F32), "ytg": do("ytg", [512, NTOK], BF16),
    }
    with contextlib.ExitStack() as stack:
        p = Prog(nc, stack)
        c = alloc_common(nc, stack, p, tt=TA, nps=6, stat_bank=5)
        c.stack = stack
        gcol = mixa_setup(c, din, io)
        mixa_body(c, xin, gcol, w_in, io)
        p.emit()
    return nc


def host_consts(half):
    t = np.arange(NTOK, dtype=np.float64)
    pos = (half * NTOK + np.arange(NTOK)).astype(np.float32)
    inv_freq = (np.float32(10000.0) ** (-np.arange(64, dtype=np.float32) / np.float32(64))).astype(np.float32)
    ang = (pos[:, None] * inv_freq[None, :]).astype(np.float32).astype(np.float64)
    cos, sin = np.cos(ang), np.sin(ang)
    gam = np.array(GAM, dtype=np.float64)
    cidx = (np.arange(NTOK) % 64).astype(np.float64)
    xi = gam[None, :] ** (cidx[:, None] + 1.0)
    gm = gam[None, :] ** (t[:, None] + 1.0)
    zeta = gam[None, :] ** (63.0 - cidx[:, None]) * (128.0 ** -0.5)
    tab = np.concatenate([cos, cos, -sin, sin, xi, gm, zeta], axis=1).astype(np.float32)
    s = np.arange(128)
    cc = np.arange(128)
    same = (s[:, None] // 64 == cc[None, :] // 64) & (cc[None, :] >= s[:, None])
    maskr = np.zeros((128, 4, 128), np.float64)
    for h in range(4):
        maskr[:, h, :] = np.where(same, gam[h] ** (-(s[:, None] % 64 + 1.0)), 0.0) * (128.0 ** -0.5)
    triu = (s[:, None] <= cc[None, :]).astype(np.float32)
    return {"tab": np.ascontiguousarray(tab), "maskr": maskr.reshape(128, 512).astype(np.float32), "triu": triu,
            "ident": np.eye(128, dtype=np.float32)}


def col16(vec):
    return np.ascontiguousarray(np.asarray(vec, np.float32).reshape(KC, 128).T)


def mixa_inputs(xT, half, l, P):
    hc = host_consts(half)
    wst = np.ascontiguousarray(np.transpose(P["gmlp_w_s"][l], (2, 0, 1)).reshape(128, 512))
    lnb3 = np.concatenate([np.broadcast_to(P["gmlp_ln_g"][l][None, :], (128, 512)),
                           np.broadcast_to(P["gmlp_ln_b"][l][None, :], (128, 512)),
                           np.broadcast_to(P["gmlp_b_s"][l].reshape(1, 512), (128, 512))], axis=1)
    ong = np.ascontiguousarray(P["out_norm"][l][1536:2048].reshape(4, 128).T)
    return {"xin": xT, "g": col16(P["mix_norm"][l]), "w_in": P["w_in"][l],
            "bf": np.ascontiguousarray(P["fox_b_f"][l].reshape(8, 1)), "tab": hc["tab"], "maskr": hc["maskr"],
            "wst": wst.astype(np.float32), "triu": hc["triu"], "lnb3": np.ascontiguousarray(lnb3, dtype=np.float32),
            "ong": ong.astype(np.float32), "ident": hc["ident"]}


QT = 512
NQT = NTOK // QT
MASKNEG = -30000.0


def mixb_body(nc, stack, p, io):
    uid = _uid()
    sb = lambda name, shape, dt: stack.enter_context(nc.sbuf_tensor("sb%d_%s" % (uid, name), shape, dt))
    ps = stack.enter_context(nc.psum_tensor("psb%d" % uid, [128, 8, 512], F32))
    onesf = sb("onesf", [128, 128], F32)
    onesb = sb("onesb", [128, 128], BF16)
    cmask = sb("cmask", [128, 896], F32)
    selh = sb("selh", [8, 8, 128], F32)
    ident8 = sb("ident8", [8, 8], F32)
    pmask = sb("pmask", [128, 1], F32)
    sflag = sb("sflag", [128, 1], F32)
    ona = sb("ona", [128, 12], F32)
    sinit = sb("sinit", [128, 512], F32)
    sinb = sb("sinb", [128, 512], BF16)
    cn = sb("cn", [8, 2 * NTOK], F32)
    ncl = sb("ncl", [8, NTOK], F32)
    ncp = sb("ncp", [8, NTOK], F32)
    biasT = sb("biasT", [128, 32, 8], F32)
    setup_d = p.sem("bsetupd")
    dve = p.sem("bdve")
    act = p.sem("bact")
    pe = p.sem("bpe")
    for dst, src in ((cmask[:], io["cmask"][:, :]), (selh[:].rearrange("k h m -> k (h m)"), io["selh"][:, :]),
                     (ident8[:], io["ident8"][:, :]), (pmask[:], io["pmask"][:, :]), (sflag[:], io["sflag"][:, :]),
                     (ona[:], io["ona"][:, :]), (sinit[:], io["s_init"][:, :]), (cn[:, 0:NTOK], io["cneg_prev"][:, :]),
                     (cn[:, NTOK:2 * NTOK], io["cneg_loc"][:, :])):
        p.dma("sync", dst, src, inc=setup_d)
    sd = [(setup_d, setup_d.v)]
    p.op("vector", lambda e: e.memset(onesf[:], 1.0), inc=dve)
    p.op("vector", lambda e: e.memset(onesb[:], 1.0), inc=dve)
    p.op("vector", lambda e: e.tensor_scalar(out=sinb[:], in0=sinit[:], scalar1=sflag[:, 0:1], scalar2=None, op0=ALU.mult),
         waits=sd, inc=dve)
    p.op("vector", lambda e: e.tensor_scalar(out=ncl[:], in0=cn[:, NTOK:2 * NTOK], scalar1=-1.0, scalar2=None, op0=ALU.mult),
         inc=dve)
    d = p.op("vector", lambda e: e.tensor_scalar(out=ncp[:], in0=ncl[:], scalar1=cn[:, NTOK - 1:NTOK], scalar2=None,
                                                 op0=ALU.subtract), waits=[(dve, dve.v)], inc=dve)
    for blk in range(32):
        p.op("tensor", lambda e, blk=blk: e.transpose(out=ps[:, 6, blk * 8:(blk + 1) * 8],
                                                      in_=cn[0:8, blk * 128:(blk + 1) * 128], identity=ident8[:]),
             waits=sd if blk == 0 else [], inc=(pe if blk == 31 else None))
    p.op("vector", lambda e: e.tensor_scalar(out=biasT[:, 0:16, :].rearrange("p a b -> p (a b)"), in0=ps[:, 6, 0:128],
                                             scalar1=pmask[:, 0:1], scalar2=None, op0=ALU.add),
         waits=[(pe, pe.v)] + sd, inc=dve)
    d = p.op("vector", lambda e: e.tensor_copy(out=biasT[:, 16:32, :].rearrange("p a b -> p (a b)"), in_=ps[:, 6, 128:256]),
             inc=dve)
    b6_cond = [(dve, d)]
    selb = sb("selb", [128, 8, 128], BF16)
    maskb = sb("maskb", [128, 896], BF16)
    identb = sb("identb", [128, 128], BF16)
    setup_d2 = p.sem("bsetupd2")
    p.dma("gpsimd", selb[:].rearrange("k h m -> k (h m)"), io["selh72"][:, :], inc=setup_d2)
    p.dma("gpsimd", maskb[:], io["cmask"][:, :], inc=setup_d2)
    p.dma("gpsimd", identb[:], io["ident"][:, :], inc=setup_d2)
    sd = sd + [(setup_d2, setup_d2.v)]
    cst = sb("cst", [128, 2, NTOK], BF16)
    csr = sb("csr", [8, NTOK], F32)
    csm = sb("csm", [8, NTOK], BF16)
    d = p.op("vector", lambda e: e.memset(cst[:], 0.0), inc=dve)
    for vi, srcc in ((0, ncp), (1, ncl)):
        d = p.op("vector", lambda e, vi=vi, srcc=srcc: e.tensor_copy(out=cst[0:8, vi, :], in_=srcc[:]), waits=[(dve, d)], inc=dve)
        d = p.op("vector", lambda e, vi=vi, srcc=srcc: e.tensor_tensor(out=csr[:], in0=srcc[:], in1=cst[0:8, vi, :],
                                                                     op=ALU.subtract), waits=[(dve, d)], inc=dve)
        d = p.op("vector", lambda e: e.tensor_copy(out=csm[:], in_=csr[:]), waits=[(dve, d)], inc=dve)
        d = p.op("vector", lambda e, vi=vi: e.tensor_copy(out=cst[32:40, vi, :], in_=csm[:]), waits=[(dve, d)], inc=dve)
        d = p.op("vector", lambda e: e.tensor_tensor(out=csr[:], in0=csr[:], in1=csm[:], op=ALU.subtract),
                 waits=[(dve, d)], inc=dve)
        d = p.op("vector", lambda e, vi=vi: e.tensor_copy(out=cst[64:72, vi, :], in_=csr[:]), waits=[(dve, d)], inc=dve)
    setup_done = [(dve, d)] + sd

    qh = sb("qh", [128, 2, NTOK], BF16)
    kh = sb("kh", [128, 2, 2 * NTOK], BF16)
    vh = sb("vh", [128, 2, 32, 128], BF16)
    hs = Slots(p, "hs", 2)
    cb = sb("cb", [128, 2, 2, QT], F32)
    cbs = Slots(p, "cbs", 2)
    tmp = sb("tmp", [128, 4, QT], F32)
    tmps = Slots(p, "tmps", 4)
    pt = sb("pt", [128, 4, QT], BF16)
    pts = Slots(p, "pts", 4)
    ya = sb("ya", [128, 2, QT], F32)
    yas = Slots(p, "yas", 2)
    yq = sb("yq", [128, QT], F32)
    stg = sb("stg", [128, 2, QT], BF16)
    stgs = Slots(p, "stgs", 2)
    ro = sb("ro", [128, 2, QT], F32)
    rg = sb("rg", [128, 2, QT], F32)
    qgt = sb("qgt", [128, 2, QT], BF16)
    rls = Slots(p, "rls", 2)
    S_BANKS = (0, 1, 7)
    LOOK = 2
    s_cond = {0: [], 1: [], 7: []}
    acc_cond = [[], []]
    s_n = [0]
    acc_n = [0]
    y_stores = []

    def headnorm_store(yslot, yready, gain_col, dst_ap, mul_tile=None, mul_wait=()):
        nonlocal b6_cond
        a = p.op("scalar", lambda e: e.activation(out=yq[:], in_=ya[:, yslot, :], func=AF.Square),
                 waits=list(yready) + [(dve, dve.v)], inc=act)
        p.op("tensor", lambda e: e.matmul(ps[:, 6, :], lhsT=onesf[:], rhs=yq[:], start=True, stop=True),
             waits=[(act, a)] + b6_cond, inc=pe)
        d = p.op("vector", lambda e: e.tensor_scalar(out=yq[:], in0=ps[:, 6, :], scalar1=1.0 / 128, scalar2=EPS,
                                                     op0=ALU.mult, op1=ALU.add), waits=[(pe, pe.v)], inc=dve)
        b6_cond = [(dve, d)]
        a = p.op("scalar", lambda e: e.activation(out=yq[:], in_=yq[:], func=AF.Sqrt), waits=[(dve, d)], inc=act)
        d = p.op("vector", lambda e: e.reciprocal(out=yq[:], in_=yq[:]), waits=[(act, a)], inc=dve)
        s, w = stgs.acquire()
        if mul_tile is None:
            d = p.op("vector", lambda e: e.scalar_tensor_tensor(out=stg[:, s, :], in0=ya[:, yslot, :], scalar=gain_col,
                                                                in1=yq[:], op0=ALU.mult, op1=ALU.mult),
                     waits=[(dve, d)] + w, inc=dve)
        else:
            d = p.op("vector", lambda e: e.scalar_tensor_tensor(out=ya[:, yslot, :], in0=ya[:, yslot, :], scalar=gain_col,
                                                                in1=yq[:], op0=ALU.mult, op1=ALU.mult),
                     waits=[(dve, d)], inc=dve)
            d = p.op("vector", lambda e: e.tensor_tensor(out=stg[:, s, :], in0=ya[:, yslot, :], in1=mul_tile, op=ALU.mult),
                     waits=[(dve, d)] + w + list(mul_wait), inc=dve)
        sv = p.dma("sync", dst_ap, stg[:, s, :], waits=[(dve, d)], inc=stgs.sem[s])
        stgs.cond[s] = [(stgs.sem[s], sv)]
        y_stores.append((stgs.sem[s], sv))
        return [(dve, d)]

    pending = [None]

    def run_pending():
        if pending[0] is not None:
            ys_, rdy_, gain_, dst_, mt_, rs_ = pending[0]
            pending[0] = None
            yw_ = headnorm_store(ys_, rdy_, gain_, dst_, mul_tile=mt_)
            yas.cond[ys_] = yw_
            if rs_ is not None:
                rls.cond[rs_] = yw_

    vparts = []
    blk0 = 0
    for key in ("v_prev", "v_loc"):
        for part in _parts(io[key]):
            nb = part.shape[0] // 128
            vparts.append((blk0, nb, part.rearrange("(b p) c -> p b c", p=128)))
            blk0 += nb
    assert blk0 == 32
    for h in range(8):
        hsl, w = hs.acquire()
        rows = slice(h * 128, (h + 1) * 128)
        p.dma("sync", qh[:, hsl, :], io["qT"][rows, :], waits=w, inc=hs.sem[hsl])
        p.dma("sync", kh[:, hsl, 0:NTOK], io["kT_prev"][rows, :], inc=hs.sem[hsl])
        p.dma("sync", kh[:, hsl, NTOK:2 * NTOK], io["kT_loc"][rows, :], inc=hs.sem[hsl])
        for (b0_, nb_, vw_) in vparts:
            hl = p.dma("sync", vh[:, hsl, b0_:b0_ + nb_, :], vw_[:, :, rows], inc=hs.sem[hsl])
        hw = [(hs.sem[hsl], hl)]
        for qt in range(NQT):
            qs = slice(qt * QT, (qt + 1) * QT)
            blocks = [(j, 0, None) for j in range(16)] + [(16 + j, 1, (j - 4 * qt) if j >= 4 * qt else None)
                                                          for j in range(4 * qt + 4)]
            an = acc_n[0]
            acc_n[0] += 1
            ob, db = 2 + an % 2, 4 + an % 2
            pend = None
            nblk = len(blocks)

            def emit_pv(pend, first, last):
                j, slot, pw_ = pend
                p.op("tensor", lambda e, j=j, slot=slot, ob=ob, hsl=hsl: e.matmul(ps[:, ob, :], lhsT=vh[:, hsl, j, :],
                                                                                 rhs=pt[:, slot, :], start=first, stop=last),
                     waits=pw_ + (acc_cond[an % 2] if first else []))
                p.op("tensor", lambda e, slot=slot, db=db: e.matmul(ps[:, db, :], lhsT=onesb[:], rhs=pt[:, slot, :],
                                                                   start=first, stop=last), inc=pe)
                pts.cond[slot] = [(pe, pe.v)]

            npv = 0
            queue = []
            for bi, (j, vi, dg) in enumerate(blocks):
                if bi == 6:
                    run_pending()
                sn = s_n[0]
                s_n[0] += 1
                sbk = S_BANKS[sn % 3]
                p.op("tensor", lambda e, j=j, sbk=sbk, qs=qs, hsl=hsl: e.matmul(ps[:, sbk, :], lhsT=kh[:, hsl, j * 128:(j + 1) * 128],
                                                                      rhs=qh[:, hsl, qs], start=True, stop=False),
                     waits=hw + s_cond[sbk] + setup_done)
                p.op("tensor", lambda e, sbk=sbk, qs=qs, h=h, vi=vi, dg=dg: e.matmul(ps[:, sbk, :], lhsT=selb[:, h, :],
                                                                                   rhs=cst[:, vi, qs], start=False,
                                                                                   stop=(dg is None)),
                     inc=(pe if dg is None else None))
                if dg is not None:
                    off = 384 - dg * 128
                    p.op("tensor", lambda e, sbk=sbk, off=off: e.matmul(ps[:, sbk, :], lhsT=identb[:], rhs=maskb[:, off:off + QT],
                                                                       start=False, stop=True), inc=pe)
                st_ = pe.v
                if len(queue) >= LOOK:
                    emit_pv(queue.pop(0), npv == 0, False)
                    npv += 1
                slot, w = pts.acquire()
                a = p.op("scalar", lambda e, sbk=sbk, slot=slot, j=j, h=h: e.activation(
                    out=pt[:, slot, :], in_=ps[:, sbk, :], func=AF.Exp, bias=biasT[:, j, h:h + 1], scale=1.0),
                         waits=[(pe, st_)] + w + setup_done, inc=act)
                s_cond[sbk] = [(act, a)]
                queue.append((j, slot, [(act, a)]))
            while queue:
                pend = queue.pop(0)
                emit_pv(pend, npv == 0, len(queue) == 0)
                npv += 1
            acc_done = pe.v
            ys, w = yas.acquire()
            d = p.op("vector", lambda e, db=db: e.reciprocal(out=yq[:], in_=ps[:, db, :]), waits=[(pe, acc_done), (dve, dve.v),
                                                                                          (act, act.v)], inc=dve)
            d = p.op("vector", lambda e, ys=ys, ob=ob: e.tensor_tensor(out=ya[:, ys, :], in0=ps[:, ob, :], in1=yq[:], op=ALU.mult),
                     waits=[(dve, d)] + w, inc=dve)
            acc_cond[an % 2] = [(dve, d)]
            run_pending()
            pending[0] = (ys, [(dve, d)], ona[:, h:h + 1], io["yT"][rows, qs], None, None)
        hs.cond[hsl] = [(pe, pe.v)]
    for hh in range(4):
        rows = slice(hh * 128, (hh + 1) * 128)
        for qt in range(NQT):
            qs = slice(qt * QT, (qt + 1) * QT)
            rs, w = rls.acquire()
            p.dma("sync", ro[:, rs, :], io["ret_o"][rows, qs], waits=w, inc=rls.sem[rs])
            p.dma("sync", rg[:, rs, :], io["rgs"][rows, qs], inc=rls.sem[rs])
            rl = p.dma("sync", qgt[:, rs, :], io["ret_qg"][rows, qs], inc=rls.sem[rs])
            p.op("tensor", lambda e, rs=rs, rows=rows: e.matmul(ps[:, 6, :], lhsT=sinb[:, rows], rhs=qgt[:, rs, :], start=True,
                                                               stop=True),
                 waits=[(rls.sem[rs], rl)] + b6_cond + setup_done, inc=pe)
            ys, w = yas.acquire()
            d = p.op("vector", lambda e, ys=ys, rs=rs: e.tensor_tensor(out=ya[:, ys, :], in0=ps[:, 6, :], in1=ro[:, rs, :],
                                                                      op=ALU.add), waits=[(pe, pe.v)] + w, inc=dve)
            b6_cond = [(dve, d)]
            run_pending()
            pending[0] = (ys, [(dve, d)], ona[:, 8 + hh:9 + hh],
                          io["yT"][1024 + hh * 128:1024 + (hh + 1) * 128, qs], rg[:, rs, :], rs)
    run_pending()
    hb = sb("hb", [128, KC, TT], BF16)
    wo = sb("wo", [128, 2, KC, 256], BF16)
    wos = Slots(p, "wos", 2)
    xs = sb("xsb", [128, 3, TT], F32)
    xss = Slots(p, "xss", 3)
    hbl = p.sem("hbl")
    wov = io["w_out"].rearrange("(kc p) f -> p kc f", p=128)
    ytv = io["yT"].rearrange("(kc p) t -> p kc t", p=128)
    ygv = io["ytg"].rearrange("(kc p) t -> p kc t", p=128)
    hb_free = []
    on = 0
    ob_cond = [[], []]
    for tt in range(NTT):
        t0 = tt * TT
        p.dma("sync", hb[:, 0:12, :], ytv[:, :, t0:t0 + TT], waits=list(y_stores) + hb_free, inc=hbl)
        hl = p.dma("sync", hb[:, 12:16, :], ygv[:, :, t0:t0 + TT], inc=hbl)
        for pd in range(8):
            col0 = pd * 256
            b, w = wos.acquire()
            wl = p.dma("gpsimd", wo[:, b, :, :], wov[:, :, col0:col0 + 256], waits=w, inc=wos.sem[b])
            for ii in range(2):
                i = pd * 2 + ii
                s, w = xss.acquire()
                full = p.dma("sync", xs[:, s, :], io["xin"][i * 128:(i + 1) * 128, t0:t0 + TT], waits=w, inc=xss.sem[s])
                for th in range(2):
                    obk = on % 2
                    on += 1
                    for kc in range(KC):
                        p.op("tensor", lambda e, b=b, kc=kc, ii=ii, th=th, obk=obk: e.matmul(
                            ps[:, obk, :], lhsT=wo[:, b, kc, ii * 128:(ii + 1) * 128], rhs=hb[:, kc, th * 512:(th + 1) * 512],
                            start=(kc == 0), stop=(kc == KC - 1)),
                             waits=([(wos.sem[b], wl), (hbl, hl)] + ob_cond[obk]) if kc == 0 else [],
                             inc=(pe if kc == KC - 1 else None))
                    r = p.op("vector", lambda e, s=s, th=th, obk=obk: e.tensor_tensor(
                        out=xs[:, s, th * 512:(th + 1) * 512], in0=ps[:, obk, :], in1=xs[:, s, th * 512:(th + 1) * 512],
                        op=ALU.add), waits=[(pe, pe.v), (xss.sem[s], full)], inc=dve)
                    ob_cond[obk] = [(dve, r)]
                sv = p.dma("sync", io["xout"][i * 128:(i + 1) * 128, t0:t0 + TT], xs[:, s, :], waits=[(dve, r)],
                           inc=xss.sem[s])
                xss.cond[s] = [(xss.sem[s], sv)]
            wos.cond[b] = [(pe, pe.v)]
        hb_free = [(pe, pe.v)]
    p.wait_only("sync", [(xss.sem[s], xss.sem[s].v) for s in range(3)])


def build_mixb():
    nc = bass.Bass("TRN2", target_bir_lowering=False)
    di = lambda name, shape, dt=F32: nc.dram_tensor(name, shape, dt, kind="ExternalInput").ap()
    io = {
        "qT": di("qT", [1024, NTOK], BF16), "kT_loc": di("kT_loc", [1024, NTOK], BF16),
        "kT_prev": di("kT_prev", [1024, NTOK], BF16), "v_loc": di("v_loc", [NTOK, 1024], BF16),
        "v_prev": di("v_prev", [NTOK, 1024], BF16), "cneg_loc": di("cneg_loc", [8, NTOK]),
        "cneg_prev": di("cneg_prev", [8, NTOK]), "s_init": di("s_init", [128, 512]),
        "ret_o": di("ret_o", [512, NTOK]), "ret_qg": di("ret_qg", [512, NTOK], BF16), "rgs": di("rgs", [512, NTOK]),
        "ytg": di("ytg", [512, NTOK], BF16), "cmask": di("cmask", [128, 896]), "selh": di("selh", [8, 1024]),
        "ident8": di("ident8", [8, 8]), "pmask": di("pmask", [128, 1]), "sflag": di("sflag", [128, 1]),
        "selh72": di("selh72", [128, 1024]), "ident": di("ident", [128, 128]),
        "ona": di("ona", [128, 12]), "w_out": di("w_out", [D, D]), "xin": di("xin", [D, NTOK]),
        "yT": nc.dram_tensor("yT", [1536, NTOK], BF16, kind="Internal").ap(),
        "xout": nc.dram_tensor("xout", [D, NTOK], F32, kind="ExternalOutput").ap(),
    }
    with contextlib.ExitStack() as stack:
        p = Prog(nc, stack)
        mixb_body(nc, stack, p, io)
        p.emit()
    return nc


def mixb_consts(half):
    s = np.arange(128)[:, None]
    u = np.arange(896)[None, :]
    cmask = np.where((u - 384) >= s, 0.0, MASKNEG).astype(np.float32)
    selh = np.zeros((8, 8, 128), np.float32)
    for h in range(8):
        selh[h, h, :] = 1.0
    selh72 = np.zeros((128, 8, 128), np.float32)
    for h in range(8):
        selh72[h, h, :] = 1.0
        selh72[32 + h, h, :] = 1.0
        selh72[64 + h, h, :] = 1.0
    return {"cmask": cmask, "selh": selh.reshape(8, 1024), "ident8": np.eye(8, dtype=np.float32),
            "selh72": selh72.reshape(128, 1024), "ident": np.eye(128, dtype=np.float32),
            "pmask": np.full((128, 1), 0.0 if half == 1 else MASKNEG, np.float32),
            "sflag": np.full((128, 1), 1.0 if half == 1 else 0.0, np.float32)}


def build_norm():
    nc = bass.Bass("TRN2", target_bir_lowering=False)
    xin = nc.dram_tensor("xin", [D, NTOK], F32, kind="ExternalInput").ap()
    g = nc.dram_tensor("g", [128, KC], F32, kind="ExternalInput").ap()
    xout = nc.dram_tensor("xout", [D, NTOK], F32, kind="ExternalOutput").ap()
    with contextlib.ExitStack() as stack:
        p = Prog(nc, stack)
        c = alloc_common(nc, stack, p)
        gcol = c.sb("gcol", [128, KC], F32)
        p.dma("sync", gcol[:], g[:, :], inc=c.setup_d)
        for tt in range(NTT):
            t0 = tt * TT

            def out_fn(kc, s, waits, t0=t0):
                d = p.op("vector", lambda e: e.scalar_tensor_tensor(out=c.xs[:, s, :], in0=c.xs[:, s, :],
                                                                    scalar=gcol[:, kc:kc + 1], in1=c.rstd[:],
                                                                    op0=ALU.mult, op1=ALU.mult), waits=waits, inc=c.dve_h)
                sv = p.dma("sync", xout[kc * 128:(kc + 1) * 128, t0:t0 + TT], c.xs[:, s, :], waits=[(c.dve_h, d)],
                           inc=c.xs_st[s])
                c.xs_cond[s] = [(c.xs_st[s], sv)]

            norm_stats_and_h(c, xin, gcol, tt, out_fn=out_fn)
        finish(c)
        p.emit()
    return nc


_PROGS = {}


def _prog(name):
    if name not in _PROGS:
        _PROGS[name] = {"ffn": build_ffn, "mixa": build_mixa, "mixb": build_mixb, "norm": build_norm}[name]()
    return _PROGS[name]


def _run(name, in_maps):
    res = run_bass_kernel_spmd(_prog(name), in_maps, core_ids=list(range(NCORES)))
    return res.results


def run_ffn(xTs, l, P, pre):
    g = col16(P[pre + "_norm"][l])
    maps = [{"xin": xTs[c], "g": g, "wg": P[pre + "_w_gate"][l], "wu": P[pre + "_w_up"][l], "wd": P[pre + "_w_down"][l]}
            for c in range(NCORES)]
    return [r["xout"] for r in _run("ffn", maps)]


def run_mixer(xTs, l, P):
    ra = _run("mixa", [mixa_inputs(xTs[c], c % 2, l, P) for c in range(NCORES)])
    ona = np.ascontiguousarray(P["out_norm"][l][0:1536].reshape(12, 128).T).astype(np.float32)
    maps = []
    for c in range(NCORES):
        half = c % 2
        pc = c - 1 if half == 1 else c
        m = {"qT": ra[c]["qT"], "kT_loc": ra[c]["kT"], "kT_prev": ra[pc]["kT"], "v_loc": ra[c]["v"], "v_prev": ra[pc]["v"],
             "cneg_loc": ra[c]["cneg"], "cneg_prev": ra[pc]["cneg"], "s_init": ra[pc]["ret_S"], "ret_o": ra[c]["ret_o"],
             "ret_qg": ra[c]["ret_qg"], "rgs": ra[c]["rgs"], "ytg": ra[c]["ytg"], "ona": ona, "w_out": P["w_out"][l],
             "xin": xTs[c]}
        m.update(mixb_consts(half))
        maps.append(m)
    return [r["xout"] for r in _run("mixb", maps)]


def kernel_unfused(**inputs):
    P = {k: np.asarray(v) for k, v in inputs.items()}
    x = P["x"]
    xTs = [np.ascontiguousarray(x[c // 2, (c % 2) * NTOK:(c % 2 + 1) * NTOK, :].T) for c in range(NCORES)]
    for l in range(DEPTH):
        xTs = run_ffn(xTs, l, P, "ffn1")
        xTs = run_mixer(xTs, l, P)
        xTs = run_ffn(xTs, l, P, "ffn2")
    g = col16(P["final_norm"])
    outs = [r["xout"] for r in _run("norm", [{"xin": xTs[c], "g": g} for c in range(NCORES)])]
    out = np.empty_like(x)
    for c in range(NCORES):
        out[c // 2, (c % 2) * NTOK:(c % 2 + 1) * NTOK, :] = outs[c].T
    return out


PAIRS = [[0, 1], [2, 3], [4, 5], [6, 7]]
WSHAPES = {"ffn1_w_gate": [DEPTH, D, DFF], "ffn1_w_up": [DEPTH, D, DFF], "ffn1_w_down": [DEPTH, DFF, D],
           "w_in": [DEPTH, D, INCOLS], "w_out": [DEPTH, D, D],
           "ffn2_w_gate": [DEPTH, D, DFF], "ffn2_w_up": [DEPTH, D, DFF], "ffn2_w_down": [DEPTH, DFF, D]}
SMALL = {"g_ffn1": [DEPTH, 128, KC], "g_mix": [DEPTH, 128, KC], "g_ffn2": [DEPTH, 128, KC], "g_fin": [128, KC],
         "bf": [DEPTH, 8, 1], "wst": [DEPTH, 128, 512], "lnb3": [DEPTH, 128, 1536], "ong": [DEPTH, 128, 4],
         "ona": [DEPTH, 128, 12], "tab": [NTOK, 268], "maskr": [128, 512], "triu": [128, 128], "ident": [128, 128],
         "cmask": [128, 896], "selh": [8, 1024], "ident8": [8, 8], "pmask": [128, 1], "sflag": [128, 1],
         "selh72": [128, 1024]}


def build_fused(depth=DEPTH, phases="fmxbF"):
    nc = bass.Bass("TRN2", target_bir_lowering=False)
    di = lambda name, shape: nc.dram_tensor(name, shape, F32, kind="ExternalInput").ap()
    it = lambda name, shape, dt: nc.dram_tensor(name, shape, dt, kind="Internal").ap()
    x_in = di("x", [D, NTOK])
    W = {k: di(k, [depth] + s[1:]) for k, s in WSHAPES.items()}
    S = {k: di(k, ([depth] + s[1:]) if len(s) == 3 else s) for k, s in SMALL.items()}
    out = nc.dram_tensor("out", [D, NTOK], F32, kind="ExternalOutput").ap()
    xres = it("xres", [D, NTOK], F32)
    qT = it("qT", [1024, NTOK], BF16)
    xk = [it("xk%d" % i, [512, NTOK], BF16) for i in range(2)]
    xv = [it("xv%d" % i, [1024, 1024], BF16) for i in range(2)]
    xc = it("xc", [8, NTOK], F32)
    xs_ = it("xs_", [128, 512], F32)
    gk = [it("gk%d" % i, [1024, NTOK], BF16) for i in range(2)]
    gv_ = [it("gv%d" % i, [2048, 1024], BF16) for i in range(2)]
    gc = it("gc", [16, NTOK], F32)
    gs = it("gs", [256, 512], F32)
    ret_o = it("ret_o", [512, NTOK], F32)
    ret_qg = it("ret_qg", [512, NTOK], BF16)
    rgs = it("rgs", [512, NTOK], F32)
    ytg = it("ytg", [512, NTOK], BF16)
    yT = it("yT", [1536, NTOK], BF16)
    with contextlib.ExitStack() as gstack:
        p = Prog(nc, gstack)

        def ffn_phase(xin, xout, g_ap, wg, wu, wd):
            with contextlib.ExitStack() as st:
                c = alloc_common(nc, st, p)
                alloc_ffn(c)
                gcol = c.sb("gcol", [128, KC], F32)
                p.dma("sync", gcol[:], g_ap, inc=c.setup_d)
                ffn_body(c, xin, xout, gcol, wg, wu, wd)
                finish(c)
                p.barrier()
                p.emit()

        def mixa_phase(l):
            with contextlib.ExitStack() as st:
                c = alloc_common(nc, st, p, tt=TA, nps=6, stat_bank=5)
                din = {"g": S["g_mix"][l], "bf": S["bf"][l], "maskr": S["maskr"][:, :], "wst": S["wst"][l],
                       "triu": S["triu"][:, :], "lnb3": S["lnb3"][l], "ong": S["ong"][l], "ident": S["ident"][:, :]}
                io = {"tab": S["tab"], "qT": qT, "kT": RowSplit(xk), "v": RowSplit(xv), "cneg_o": xc,
                      "ret_o": ret_o, "ret_qg": ret_qg, "ret_S": xs_, "rgs": rgs, "ytg": ytg}
                gcol = mixa_setup(c, din, io)
                mixa_body(c, xres, gcol, W["w_in"][l], io)
                p.barrier()
                p.emit()

        def exchange_phase():
            cc = p.sem("ccsem")
            for a_, b_ in ((xk[0], gk[0]), (xk[1], gk[1]), (xv[0], gv_[0]), (xv[1], gv_[1]), (xc, gc), (xs_, gs)):
                p.op("gpsimd", lambda e, a_=a_, b_=b_: e.collective_compute("AllGather", ALU.bypass, replica_groups=PAIRS,
                                                                            ins=[a_], outs=[b_]), inc=cc, k=1)
            p.barrier()
            p.emit()

        def mixb_phase(l):
            with contextlib.ExitStack() as st:
                io = {"qT": qT, "kT_loc": RowSplit(xk), "kT_prev": RowSplit([gk[0][0:512, :], gk[1][0:512, :]]),
                      "v_loc": RowSplit(xv), "v_prev": RowSplit([gv_[0][0:1024, :], gv_[1][0:1024, :]]),
                      "cneg_loc": xc, "cneg_prev": gc[0:8, :], "s_init": gs[0:128, :], "ret_o": ret_o,
                      "ret_qg": ret_qg, "rgs": rgs, "ytg": ytg, "cmask": S["cmask"], "selh": S["selh"],
                      "ident8": S["ident8"], "pmask": S["pmask"], "sflag": S["sflag"], "ona": S["ona"][l],
                      "selh72": S["selh72"], "ident": S["ident"],
                      "w_out": W["w_out"][l], "xin": xres, "yT": yT, "xout": xres}
                mixb_body(nc, st, p, io)
                p.barrier()
                p.emit()

        def norm_phase():
            with contextlib.ExitStack() as st:
                c = alloc_common(nc, st, p)
                gcol = c.sb("gcol", [128, KC], F32)
                p.dma("sync", gcol[:], S["g_fin"][:, :], inc=c.setup_d)
                for tt in range(NTT):
                    t0 = tt * TT

                    def out_fn(kc, s, waits, t0=t0):
                        d = p.op("vector", lambda e: e.scalar_tensor_tensor(out=c.xs[:, s, :], in0=c.xs[:, s, :],
                                                                            scalar=gcol[:, kc:kc + 1], in1=c.rstd[:],
                                                                            op0=ALU.mult, op1=ALU.mult), waits=waits,
                                 inc=c.dve_h)
                        sv = p.dma("sync", out[kc * 128:(kc + 1) * 128, t0:t0 + TT], c.xs[:, s, :], waits=[(c.dve_h, d)],
                                   inc=c.xs_st[s])
                        c.xs_cond[s] = [(c.xs_st[s], sv)]

                    norm_stats_and_h(c, xres, gcol, tt, out_fn=out_fn)
                finish(c)
                p.barrier()
                p.emit()

        for l in range(depth):
            if "f" in phases:
                ffn_phase(x_in if l == 0 else xres, xres, S["g_ffn1"][l], W["ffn1_w_gate"][l], W["ffn1_w_up"][l],
                          W["ffn1_w_down"][l])
            if "m" in phases:
                mixa_phase(l)
            if "x" in phases:
                exchange_phase()
            if "b" in phases:
                mixb_phase(l)
            if "F" in phases:
                ffn_phase(xres, xres, S["g_ffn2"][l], W["ffn2_w_gate"][l], W["ffn2_w_up"][l], W["ffn2_w_down"][l])
        norm_phase()
    return nc


def fused_inputs(P, core):
    half = core % 2
    x = P["x"]
    m = {"x": np.ascontiguousarray(x[core // 2, half * NTOK:(half + 1) * NTOK, :].T)}
    for k in WSHAPES:
        m[k] = P[k]
    return m


def fused_shared(P):
    sh = {}
    sh["g_ffn1"] = np.stack([col16(P["ffn1_norm"][l]) for l in range(DEPTH)])
    sh["g_mix"] = np.stack([col16(P["mix_norm"][l]) for l in range(DEPTH)])
    sh["g_ffn2"] = np.stack([col16(P["ffn2_norm"][l]) for l in range(DEPTH)])
    sh["g_fin"] = col16(P["final_norm"])
    sh["bf"] = np.ascontiguousarray(P["fox_b_f"].reshape(DEPTH, 8, 1)).astype(np.float32)
    sh["wst"] = np.ascontiguousarray(np.transpose(P["gmlp_w_s"], (0, 3, 1, 2)).reshape(DEPTH, 128, 512)).astype(np.float32)
    sh["lnb3"] = np.ascontiguousarray(np.stack([np.concatenate(
        [np.broadcast_to(P["gmlp_ln_g"][l][None, :], (128, 512)), np.broadcast_to(P["gmlp_ln_b"][l][None, :], (128, 512)),
         np.broadcast_to(P["gmlp_b_s"][l].reshape(1, 512), (128, 512))], axis=1) for l in range(DEPTH)])).astype(np.float32)
    sh["ong"] = np.ascontiguousarray(np.stack([P["out_norm"][l][1536:2048].reshape(4, 128).T for l in range(DEPTH)])).astype(np.float32)
    sh["ona"] = np.ascontiguousarray(np.stack([P["out_norm"][l][0:1536].reshape(12, 128).T for l in range(DEPTH)])).astype(np.float32)
    return sh


_FUSED = {}


def kernel(**inputs):
    P = {k: np.asarray(v) for k, v in inputs.items()}
    if "nc" not in _FUSED:
        _FUSED["nc"] = build_fused()
    sh = fused_shared(P)
    maps = []
    for c in range(NCORES):
        half = c % 2
        m = fused_inputs(P, c)
        m.update(sh)
        hc = host_consts(half)
        m.update({"tab": hc["tab"], "maskr": hc["maskr"], "triu": hc["triu"], "ident": hc["ident"]})
        m.update(mixb_consts(half))
        maps.append(m)
    res = run_bass_kernel_spmd(_FUSED["nc"], maps, core_ids=list(range(NCORES)))
    x = P["x"]
    outp = np.empty_like(x)
    for c in range(NCORES):
        outp[c // 2, (c % 2) * NTOK:(c % 2 + 1) * NTOK, :] = res.results[c]["out"].T
    return outp
```

```python
import contextlib
import numpy as np
import concourse.bass as bass
import concourse.mybir as mybir
from concourse.bass_utils import run_bass_kernel_spmd

F32 = mybir.dt.float32
BF16 = mybir.dt.bfloat16
AF = mybir.ActivationFunctionType
ALU = mybir.AluOpType

D = 2048
NTOK = 2048
DFF = 5632
NCORES = 8
DEPTH = 4
EPS = 1e-6
KC = D // 128
TT = 1024
NTT = NTOK // TT
FH = 22
INCOLS = 6152


class Cnt:
    def __init__(self, h):
        self.h = h
        self.v = 0


class Prog:
    ENGS = ("sync", "scalar", "vector", "gpsimd", "tensor")

    def __init__(self, nc, stack):
        self.nc = nc
        self.stack = stack
        self.q = {e: [] for e in self.ENGS}
        self.waited = {e: {} for e in self.ENGS}
        self.cache = {}

    def sem(self, name):
        if name not in self.cache:
            self.cache[name] = Cnt(self.stack.enter_context(self.nc.semaphore(name)))
        return self.cache[name]

    def barrier(self):
        for eng in self.ENGS:
            self.op(eng, None, waits=[(c, c.v) for c in self.cache.values()])

    def sems(self, name, n):
        return [self.sem("%s%d" % (name, i)) for i in range(n)]

    def op(self, eng, fn, waits=(), inc=None, k=1):
        ws = []
        for (c, v) in waits:
            if v <= 0:
                continue
            key = id(c)
            if self.waited[eng].get(key, 0) >= v:
                continue
            self.waited[eng][key] = v
            ws.append((c.h, v))
        tgt = None
        if inc is not None:
            inc.v += k
            tgt = inc.v
        self.q[eng].append((ws, fn, inc.h if inc is not None else None, k))
        return tgt

    def dma(self, eng, out, in_, waits=(), inc=None):
        return self.op(eng, lambda e: e.dma_start(out=out, in_=in_), waits, inc, 16)

    def wait_only(self, eng, waits):
        self.q[eng].append(([(c.h, v) for (c, v) in waits if v > 0], None, None, 0))

    def emit(self):
        with self.nc.Block() as block:
            for name in self.ENGS:
                q = self.q[name]

                def body(e, q=q):
                    for ws, fn, inc, k in q:
                        for (h, v) in ws:
                            e.wait_ge(h, v)
                        if fn is None:
                            continue
                        ins = fn(e)
                        if inc is not None:
                            ins.then_inc(inc, k)

                getattr(block, name)(body)
        self.q = {e: [] for e in self.ENGS}


class Ctx:
    pass


_UID = [0]


def _uid():
    _UID[0] += 1
    return _UID[0]


def alloc_common(nc, stack, p, tt=TT, nps=8, stat_bank=6):
    c = Ctx()
    uid = _uid()
    c.nc = nc
    c.p = p
    c.TT = tt
    c.NSEG = tt // 512
    c.stat_bank = stat_bank
    sb = lambda name, shape, dt: stack.enter_context(nc.sbuf_tensor("sb%d_%s" % (uid, name), shape, dt))
    c.sb = sb
    c.uid = uid
    c.stack = stack
    c.ones = sb("ones", [128, 128], F32)
    c.xs = sb("xs", [128, 3, tt], F32)
    c.sq = sb("sq", [128, 2, tt], F32)
    c.rstd = sb("rstd", [128, tt], F32)
    c.h = sb("h", [128, KC, tt], BF16)
    c.ps = stack.enter_context(nc.psum_tensor("ps%d" % uid, [128, nps, 512], F32))
    c.xs_full = p.sems("xsfull", 3)
    c.xs_st = p.sems("xsst", 3)
    c.xs_cond = [[], [], []]
    c.xs_n = 0
    c.act_sq = p.sem("actsq")
    c.pe_st = p.sem("pest")
    c.dve_m = p.sem("dvem")
    c.act_m = p.sem("actm")
    c.dve_h = p.sem("dveh")
    c.setup_v = p.sem("setupv")
    c.setup_d = p.sem("setupd")
    p.op("vector", lambda e: e.memset(c.ones[:], 1.0), inc=c.setup_v)
    c.sq_n = 0
    c.h_free = []
    c.ps_free_waits = []
    return c


def xs_acquire(c):
    s = c.xs_n % 3
    c.xs_n += 1
    return s, list(c.xs_cond[s])


def norm_stats_and_h(c, xsrc, gcol, tt, out_fn=None):
    p = c.p
    TT = c.TT
    t0 = tt * TT
    SB = c.stat_bank
    for kc in range(KC):
        s, w = xs_acquire(c)
        full = p.dma("sync", c.xs[:, s, :], xsrc[kc * 128:(kc + 1) * 128, t0:t0 + TT], waits=w, inc=c.xs_full[s])
        q = c.sq_n % 2
        c.sq_n += 1
        a = p.op("scalar",
                 lambda e, s=s, q=q: e.activation(out=c.sq[:, q, :], in_=c.xs[:, s, :], func=AF.Square),
                 waits=[(c.xs_full[s], full), (c.pe_st, c.pe_st.v - 1)], inc=c.act_sq)
        c.xs_cond[s] = [(c.act_sq, a)]
        extra = list(c.ps_free_waits) if kc == 0 else []
        for sg_ in range(c.NSEG):
            p.op("tensor",
                 lambda e, q=q, kc=kc, sg_=sg_: e.matmul(c.ps[:, SB + sg_, :], lhsT=c.ones[:],
                                                        rhs=c.sq[:, q, sg_ * 512:(sg_ + 1) * 512],
                                                        start=(kc == 0), stop=(kc == KC - 1)),
                 waits=([(c.act_sq, a), (c.setup_v, c.setup_v.v)] + extra) if sg_ == 0 else [],
                 inc=(c.pe_st if sg_ == c.NSEG - 1 else None))
    st_done = c.pe_st.v
    psv = c.ps[:, SB:SB + c.NSEG, :]
    rv = c.rstd[:].rearrange("p (a b) -> p a b", a=c.NSEG)
    d1 = p.op("vector",
              lambda e: e.tensor_scalar(out=rv, in0=psv, scalar1=1.0 / D, scalar2=EPS, op0=ALU.mult, op1=ALU.add),
              waits=[(c.pe_st, st_done), (c.dve_h, c.dve_h.v)], inc=c.dve_m)
    c.ps_free_waits = [(c.dve_m, d1)]
    a1 = p.op("scalar", lambda e: e.activation(out=c.rstd[:], in_=c.rstd[:], func=AF.Sqrt),
              waits=[(c.dve_m, d1)], inc=c.act_m)
    d2 = p.op("vector", lambda e: e.reciprocal(out=c.rstd[:], in_=c.rstd[:]),
              waits=[(c.act_m, a1)], inc=c.dve_m)
    for kc in range(KC):
        s, w = xs_acquire(c)
        full = p.dma("sync", c.xs[:, s, :], xsrc[kc * 128:(kc + 1) * 128, t0:t0 + TT], waits=w, inc=c.xs_full[s])
        if out_fn is None:
            waits = [(c.xs_full[s], full), (c.dve_m, d2), (c.setup_d, c.setup_d.v)]
            if kc == 0:
                waits += c.h_free
            hv = p.op("vector",
                 lambda e, s=s, kc=kc: e.scalar_tensor_tensor(out=c.h[:, kc, :], in0=c.xs[:, s, :],
                                                              scalar=gcol[:, kc:kc + 1], in1=c.rstd[:],
                                                              op0=ALU.mult, op1=ALU.mult),
                 waits=waits, inc=c.dve_h)
            c.xs_cond[s] = [(c.dve_h, hv)]
        else:
            out_fn(kc, s, [(c.xs_full[s], full), (c.dve_m, d2), (c.setup_d, c.setup_d.v)])
    return c.dve_h.v


def alloc_ffn(c):
    p = c.p
    sb = c.sb
    c.hid = sb("hid", [128, FH, TT], BF16)
    c.sg = sb("sg", [128, 2, 512], F32)
    c.wgu = sb("wgu", [128, 2, 2, KC, 256], BF16)
    c.wd = sb("wd", [128, 2, FH, 256], BF16)
    c.wgu_full = p.sems("wgufull", 2)
    c.wd_full = p.sems("wdfull", 2)
    c.pe_gu = p.sem("pegu")
    c.act_sg = p.sem("actsg")
    c.dve_hid = p.sem("dvehid")
    c.pe_dn = p.sem("pedn")
    c.dve_res = p.sem("dveres")
    c.n_panel = 0
    c.n_gu = c.pe_gu.v
    c.n_dpanel = 0
    c.n_dn = c.pe_dn.v
    c.panel_done = {}
    c.dpanel_done = {}
    c.hid_free = []


def ffn_body(c, xin, xout, gcol, wg, wu, wd):
    p = c.p
    wgv = wg.rearrange("(kc p) f -> p kc f", p=128)
    wuv = wu.rearrange("(kc p) f -> p kc f", p=128)
    wdv = wd.rearrange("(fc p) d -> p fc d", p=128)
    for tt in range(NTT):
        t0 = tt * TT
        h_ready = norm_stats_and_h(c, xin, gcol, tt)
        for hf in range(2):
            for pn in range(FH // 2):
                col0 = (hf * FH + pn * 2) * 128
                npn = c.n_panel
                b = npn % 2
                c.n_panel += 1
                wfree = [(c.pe_gu, c.panel_done[npn - 2])] if npn >= 2 else []
                p.dma("gpsimd", c.wgu[:, b, 0, :, :], wgv[:, :, col0:col0 + 256], waits=wfree, inc=c.wgu_full[b])
                wl = p.dma("gpsimd", c.wgu[:, b, 1, :, :], wuv[:, :, col0:col0 + 256], waits=wfree, inc=c.wgu_full[b])
                for jj in range(2):
                    j = pn * 2 + jj
                    for th in range(2):
                        n = c.n_gu
                        c.n_gu += 1
                        gb = n % 2
                        ub = 2 + n % 2
                        for kc in range(KC):
                            waits = []
                            if kc == 0:
                                waits = [(c.wgu_full[b], wl), (c.dve_h, h_ready), (c.act_sg, n - 1)]
                            p.op("tensor",
                                 lambda e, b=b, kc=kc, jj=jj, th=th, gb=gb: e.matmul(
                                     c.ps[:, gb, :], lhsT=c.wgu[:, b, 0, kc, jj * 128:(jj + 1) * 128],
                                     rhs=c.h[:, kc, th * 512:(th + 1) * 512], start=(kc == 0), stop=(kc == KC - 1)),
                                 waits=waits)
                        for kc in range(KC):
                            waits = []
                            if kc == 0:
                                waits = [(c.dve_hid, n - 1)]
                            last = (kc == KC - 1)
                            p.op("tensor",
                                 lambda e, b=b, kc=kc, jj=jj, th=th, ub=ub: e.matmul(
                                     c.ps[:, ub, :], lhsT=c.wgu[:, b, 1, kc, jj * 128:(jj + 1) * 128],
                                     rhs=c.h[:, kc, th * 512:(th + 1) * 512], start=(kc == 0), stop=(kc == KC - 1)),
                                 waits=waits, inc=(c.pe_gu if last else None))
                        gu = c.pe_gu.v
                        a = p.op("scalar",
                                 lambda e, n=n, gb=gb: e.activation(out=c.sg[:, n % 2, :], in_=c.ps[:, gb, :], func=AF.Silu),
                                 waits=[(c.pe_gu, gu), (c.dve_hid, n - 1)], inc=c.act_sg)
                        waits = [(c.act_sg, a), (c.pe_gu, gu)]
                        if j == 0 and th == 0:
                            waits += c.hid_free
                        p.op("vector",
                             lambda e, n=n, ub=ub, j=j, th=th: e.tensor_tensor(
                                 out=c.hid[:, j, th * 512:(th + 1) * 512], in0=c.sg[:, n % 2, :], in1=c.ps[:, ub, :],
                                 op=ALU.mult),
                             waits=waits, inc=c.dve_hid)
                c.panel_done[npn] = c.pe_gu.v
            if hf == 1:
                c.h_free = [(c.pe_gu, c.pe_gu.v)]
            hid_ready = c.dve_hid.v
            xsrc = xin if hf == 0 else xout
            for pd in range(8):
                col0 = pd * 256
                npd = c.n_dpanel
                b = npd % 2
                c.n_dpanel += 1
                wl = p.dma("gpsimd", c.wd[:, b, :, :], wdv[:, hf * FH:(hf + 1) * FH, col0:col0 + 256],
                           waits=([(c.pe_dn, c.dpanel_done[npd - 2])] if npd >= 2 else []), inc=c.wd_full[b])
                for ii in range(2):
                    i = pd * 2 + ii
                    s, w = xs_acquire(c)
                    full = p.dma("sync", c.xs[:, s, :], xsrc[i * 128:(i + 1) * 128, t0:t0 + TT], waits=w,
                                 inc=c.xs_full[s])
                    for th in range(2):
                        n = c.n_dn
                        c.n_dn += 1
                        ob = 4 + n % 2
                        for f in range(FH):
                            waits = []
                            if f == 0:
                                waits = [(c.wd_full[b], wl), (c.dve_hid, hid_ready), (c.dve_res, n - 1)]
                            last = (f == FH - 1)
                            p.op("tensor",
                                 lambda e, b=b, f=f, ii=ii, th=th, ob=ob: e.matmul(
                                     c.ps[:, ob, :], lhsT=c.wd[:, b, f, ii * 128:(ii + 1) * 128],
                                     rhs=c.hid[:, f, th * 512:(th + 1) * 512], start=(f == 0), stop=(f == FH - 1)),
                                 waits=waits, inc=(c.pe_dn if last else None))
                        dn = c.pe_dn.v
                        r = p.op("vector",
                                 lambda e, s=s, th=th, ob=ob: e.scalar_tensor_tensor(
                                     out=c.xs[:, s, th * 512:(th + 1) * 512], in0=c.ps[:, ob, :], scalar=0.5,
                                     in1=c.xs[:, s, th * 512:(th + 1) * 512], op0=ALU.mult, op1=ALU.add),
                                 waits=[(c.pe_dn, dn), (c.xs_full[s], full)], inc=c.dve_res)
                    sv = p.dma("sync", xout[i * 128:(i + 1) * 128, t0:t0 + TT], c.xs[:, s, :],
                               waits=[(c.dve_res, r)], inc=c.xs_st[s])
                    c.xs_cond[s] = [(c.xs_st[s], sv)]
                c.dpanel_done[npd] = c.pe_dn.v
            c.hid_free = [(c.pe_dn, c.pe_dn.v)]


def finish(c):
    p = c.p
    p.wait_only("sync", [(c.xs_st[s], c.xs_st[s].v) for s in range(3)])


def build_ffn():
    nc = bass.Bass("TRN2", target_bir_lowering=False)
    xin = nc.dram_tensor("xin", [D, NTOK], F32, kind="ExternalInput").ap()
    g = nc.dram_tensor("g", [128, KC], F32, kind="ExternalInput").ap()
    wg = nc.dram_tensor("wg", [D, DFF], F32, kind="ExternalInput").ap()
    wu = nc.dram_tensor("wu", [D, DFF], F32, kind="ExternalInput").ap()
    wd = nc.dram_tensor("wd", [DFF, D], F32, kind="ExternalInput").ap()
    xout = nc.dram_tensor("xout", [D, NTOK], F32, kind="ExternalOutput").ap()
    with contextlib.ExitStack() as stack:
        p = Prog(nc, stack)
        c = alloc_common(nc, stack, p)
        alloc_ffn(c)
        gcol = c.sb("gcol", [128, KC], F32)
        p.dma("sync", gcol[:], g[:, :], inc=c.setup_d)
        ffn_body(c, xin, xout, gcol, wg, wu, wd)
        finish(c)
        p.emit()
    return nc


FOX_SCALE = 128.0 ** -0.5
GAM = [1.0 - 2.0 ** -(5 + h) for h in range(4)]
GAM64 = [g ** 64 for g in GAM]
TA = 512
NBLK = TA // 128
AX = mybir.AxisListType


class RowSplit:
    def __init__(self, parts):
        self.parts = parts
        self.h = parts[0].shape[0]

    def __getitem__(self, key):
        rs, cs = key
        i = rs.start // self.h
        assert (rs.stop - 1) // self.h == i
        return self.parts[i][rs.start - i * self.h:rs.stop - i * self.h, cs]


def _parts(x):
    return x.parts if isinstance(x, RowSplit) else [x]


class Slots:
    def __init__(self, p, name, n):
        self.sem = p.sems(name, n)
        self.cond = [[] for _ in range(n)]
        self.i = 0
        self.n = n

    def acquire(self):
        s = self.i % self.n
        self.i += 1
        return s, list(self.cond[s])


def mixa_body(c, xin, gcol, w_in, io):
    p = c.p
    nc = c.nc
    sb = c.sb
    winv = w_in.rearrange("(kc p) f -> p kc f", p=128)
    tabv = io["tab"].rearrange("(b p) f -> p b f", p=128)
    wp = sb("wp", [128, 2, 8192], BF16)
    wps = Slots(p, "wps", 2)
    wpfm = lambda b: wp[:, b, 0:4096].rearrange("p (k c) -> p k c", c=256)
    wptm = lambda b: wp[:, b, :].rearrange("p (k c) -> p k c", c=512)
    wpfz = lambda b: wp[:, b, 0:128].rearrange("p (k c) -> p k c", c=8)
    tabt = sb("tabt", [128, 2, NBLK, 268], F32)
    tabs = Slots(p, "tabs", 2)
    stg16 = sb("stg16", [128, 4, 512], BF16)
    st16 = Slots(p, "st16", 4)
    stg32 = sb("stg32", [128, 2, 512], F32)
    st32 = Slots(p, "st32", 2)
    u = sb("u", [128, 4, TA], F32)
    rv = sb("rv", [128, NBLK, 512], BF16)
    kr = sb("kr", [128, NBLK, 512], BF16)
    kz = sb("kz", [128, NBLK, 512], BF16)
    qx = sb("qx", [128, NBLK, 512], BF16)
    qg = sb("qg", [128, NBLK, 512], BF16)
    vln = sb("vln", [128, NBLK, 512], BF16)
    rot = sb("rot", [128, 2, 2, 512], F32)
    rots = Slots(p, "rots", 2)
    st = sb("lnst", [128, 24], F32)
    fzt = sb("fzt", [8, 512], F32)
    onesr = sb("onesr", [8, 512], F32)
    cneg = sb("cneg", [8, NTOK], F32)
    S32 = sb("S32", [128, 512], F32)
    Sb = sb("Sb", [128, 2 * NBLK, 512], BF16)
    krT = sb("krT", [128, 512], BF16)
    qxT = sb("qxT", [128, 512], BF16)
    sm = sb("sm", [128, 512], BF16)
    y1 = sb("y1", [128, 512], F32)
    y2 = sb("y2", [128, 512], F32)
    pst = c.stack.enter_context(nc.psum_tensor("pst%d" % c.uid, [128, 2, 1024], BF16))
    c.pe = p.sem("pe")
    c.act = p.sem("act")
    c.dve = p.sem("dve")
    misc = p.sem("miscst")
    pj_cond = [[], []]
    kv_cond = [[], []]
    b4_cond = []
    pst_cond = [[], []]
    pj_n = [0]
    kv_n = [0]
    u_free = []
    ret_free = []
    vln_free = []
    p.op("vector", lambda e: e.memset(onesr[:], 1.0), inc=c.setup_v)
    p.op("vector", lambda e: e.memset(S32[:], 0.0), inc=c.setup_v)
    setupw = [(c.setup_d, c.setup_d.v), (c.setup_v, c.setup_v.v)]

    def V4(ap):
        return ap.rearrange("p (a b) -> p a b", a=4)

    def bc(ap4):
        return ap4.unsqueeze(2).to_broadcast([128, 4, 128])

    def bh(ap128):
        return ap128.unsqueeze(1).to_broadcast([128, 4, 128])

    def bh64(ap64):
        return ap64.unsqueeze(1).to_broadcast([128, 4, 64])

    def proj(b, waits, lhs_fn, rhs_fn, out_fn):
        n = pj_n[0]
        pj_n[0] += 1
        bank = n % 2
        for kc in range(KC):
            p.op("tensor",
                 lambda e, kc=kc: e.matmul(out_fn(bank), lhsT=lhs_fn(kc), rhs=rhs_fn(kc), start=(kc == 0),
                                           stop=(kc == KC - 1)),
                 waits=(list(waits) + pj_cond[bank]) if kc == 0 else [], inc=(c.pe if kc == KC - 1 else None))
        return bank, c.pe.v

    def store16(src_fn, dst_ap, waits, eng_op):
        s, w = st16.acquire()
        a = p.op("scalar", lambda e: eng_op(e, stg16[:, s, :]), waits=list(waits) + w, inc=c.act)
        sv = p.dma("sync", dst_ap, src_fn(stg16[:, s, :]), waits=[(c.act, a)], inc=st16.sem[s])
        st16.cond[s] = [(st16.sem[s], sv)]
        return a

    for tt in range(NTOK // TA):
        t0 = tt * TA
        h_ready = norm_stats_and_h(c, xin, gcol, tt)
        hw = [(c.dve_h, h_ready)]
        ts_, w = tabs.acquire()
        tk = p.dma("sync", tabt[:, ts_], tabv[:, tt * NBLK:(tt + 1) * NBLK, :], waits=w, inc=tabs.sem[ts_])
        tabw = [(tabs.sem[ts_], tk)]
        for name, cbase, ncol in (("fq", 0, 1024), ("fk", 1024, 1024), ("rg", 4616, 512), ("gu", 5128, 512)):
            for pn in range(ncol // 256):
                col0 = cbase + pn * 256
                b, w = wps.acquire()
                t = p.dma("gpsimd", wpfm(b), winv[:, :, col0:col0 + 256], waits=w, inc=wps.sem[b])
                for jj in range(2):
                    ch = pn * 2 + jj
                    bank, pt = proj(b, [(wps.sem[b], t)] + hw,
                                    lambda kc, b=b, jj=jj: wpfm(b)[:, kc, jj * 128:(jj + 1) * 128],
                                    lambda kc: c.h[:, kc, :], lambda bank: c.ps[:, bank, :])
                    pw = [(c.pe, pt)]
                    if name == "fq":
                        a = store16(lambda s_: s_, io["qT"][ch * 128:(ch + 1) * 128, t0:t0 + TA], pw,
                                    lambda e, o, bank=bank: e.mul(out=o, in_=c.ps[:, bank, :], mul=FOX_SCALE))
                    elif name == "fk":
                        a = store16(lambda s_: s_, io["kT"][ch * 128:(ch + 1) * 128, t0:t0 + TA], pw,
                                    lambda e, o, bank=bank: e.copy(out=o, in_=c.ps[:, bank, :]))
                    elif name == "rg":
                        s, w2 = st32.acquire()
                        a = p.op("scalar", lambda e, s=s, bank=bank: e.activation(out=stg32[:, s, :], in_=c.ps[:, bank, :],
                                                                                func=AF.Silu),
                                 waits=pw + w2, inc=c.act)
                        sv = p.dma("sync", io["rgs"][ch * 128:(ch + 1) * 128, t0:t0 + TA], stg32[:, s, :],
                                   waits=[(c.act, a)], inc=st32.sem[s])
                        st32.cond[s] = [(st32.sem[s], sv)]
                    else:
                        a = p.op("scalar", lambda e, ch=ch, bank=bank: e.activation(out=u[:, ch, :], in_=c.ps[:, bank, :],
                                                                                  func=AF.Gelu_apprx_tanh),
                                 waits=pw + (u_free if ch == 0 else []), inc=c.act)
                    pj_cond[bank] = [(c.act, a)]
                wps.cond[b] = [(c.pe, pt)]
        b, w = wps.acquire()
        t = p.dma("gpsimd", wpfz(b), winv[:, :, 3072:3080], waits=w, inc=wps.sem[b])
        bank, pt = proj(b, [(wps.sem[b], t)] + hw, lambda kc, b=b: wpfz(b)[:, kc, :], lambda kc: c.h[:, kc, :],
                        lambda bank: c.ps[0:8, bank, :])
        wps.cond[b] = [(c.pe, pt)]
        a = p.op("scalar", lambda e, bank=bank: e.activation(out=fzt[:], in_=c.ps[0:8, bank, :], func=AF.Exp,
                                                             bias=io["negb"][:, 0:1], scale=-1.0),
                 waits=[(c.pe, pt), (c.dve, c.dve.v)] + setupw, inc=c.act)
        pj_cond[bank] = [(c.act, a)]
        a = p.op("scalar", lambda e: e.activation(out=fzt[:], in_=fzt[:], func=AF.Ln, bias=1.0), waits=[(c.act, a)],
                 inc=c.act)
        init = 0.0 if tt == 0 else cneg[:, t0 - 1:t0]
        p.op("vector", lambda e, init=init, t0=t0: e.tensor_tensor_scan(out=cneg[:, t0:t0 + TA], data0=onesr[:], data1=fzt[:],
                                                                 initial=init, op0=ALU.mult, op1=ALU.add),
             waits=[(c.act, a), (c.dve, c.dve.v)] + setupw, inc=c.dve)
        ready = {}
        s32_last = [None]

        def emit_kv(n):
            tb, a_ = n // 2, n % 2
            kb = 2 + kv_n[0] % 2
            ci = kv_n[0] % 2
            kv_n[0] += 1
            for hh in range(4):
                sl = slice(hh * 128, (hh + 1) * 128)
                p.op("tensor", lambda e, kb=kb, sl=sl, a_=a_, tb=tb: e.matmul(
                    c.ps[:, kb, sl], lhsT=kz[a_ * 64:(a_ + 1) * 64, tb, sl], rhs=rv[a_ * 64:(a_ + 1) * 64, tb, sl],
                    start=True, stop=True),
                     waits=(ready[("kz", tb)] + ready[("rv", tb)] + kv_cond[ci]) if hh == 0 else [],
                     inc=(c.pe if hh == 3 else None))
            pt_ = c.pe.v
            a = p.op("scalar", lambda e, n=n: e.copy(out=Sb[:, n, :], in_=S32[:]),
                     waits=[(c.dve, c.dve.v)] + (ret_free if n == 0 else []) + setupw, inc=c.act)
            ready[("Sb", n)] = [(c.act, a)]
            for hh in range(4):
                sl = slice(hh * 128, (hh + 1) * 128)
                d = p.op("vector", lambda e, kb=kb, sl=sl, hh=hh: e.scalar_tensor_tensor(
                    out=S32[:, sl], in0=S32[:, sl], scalar=GAM64[hh], in1=c.ps[:, kb, sl], op0=ALU.mult, op1=ALU.add),
                         waits=[(c.pe, pt_), (c.act, a)] if hh == 0 else [], inc=c.dve)
            kv_cond[ci] = [(c.dve, d)]
            s32_last[0] = [(c.dve, d)]

        b4h = [b4_cond]

        def emit_gmlp(tb):
            r0 = t0 + tb * 128
            for gg in range(4):
                sl = slice(gg * 128, (gg + 1) * 128)
                p.op("tensor", lambda e, sl=sl, tb=tb, gg=gg: e.matmul(c.ps[:, 4, sl], lhsT=vln[:, tb, sl],
                                                                     rhs=io["wsb"][:, gg, :], start=True, stop=True),
                     waits=(ready[("vln", tb)] + b4h[0] + setupw) if gg == 0 else [], inc=(c.pe if gg == 3 else None))
            pt = c.pe.v
            d = p.op("vector", lambda e: e.tensor_tensor(out=y1[:], in0=c.ps[:, 4, :], in1=io["lnb3"][:, 2, :], op=ALU.add),
                     waits=[(c.pe, pt), (c.act, c.act.v), (c.dve, c.dve.v)], inc=c.dve)
            d = p.op("vector", lambda e, tb=tb: e.tensor_tensor(out=V4(y1[:]), in0=V4(y1[:]),
                                                               in1=u[:, :, tb * 128:(tb + 1) * 128], op=ALU.mult),
                     waits=[(c.dve, d)], inc=c.dve)
            a = p.op("scalar", lambda e: e.activation(out=y2[:], in_=y1[:], func=AF.Square), waits=[(c.dve, d)], inc=c.act)
            p.op("tensor", lambda e: e.matmul(c.ps[:, 4, :], lhsT=c.ones[:], rhs=y2[:], start=True, stop=True),
                 waits=[(c.act, a), (c.dve, d)], inc=c.pe)
            pt = c.pe.v
            d = p.op("vector", lambda e: e.tensor_scalar(out=y2[:], in0=c.ps[:, 4, :], scalar1=1.0 / 128, scalar2=EPS,
                                                         op0=ALU.mult, op1=ALU.add), waits=[(c.pe, pt)], inc=c.dve)
            b4h[0] = [(c.dve, d)]
            a = p.op("scalar", lambda e: e.activation(out=y2[:], in_=y2[:], func=AF.Sqrt), waits=[(c.dve, d)], inc=c.act)
            d = p.op("vector", lambda e: e.reciprocal(out=y2[:], in_=y2[:]), waits=[(c.act, a)], inc=c.dve)
            d = p.op("vector", lambda e: e.tensor_tensor(out=y1[:], in0=y1[:], in1=y2[:], op=ALU.mult),
                     waits=[(c.dve, d)], inc=c.dve)
            s, w = st16.acquire()
            for gg in range(4):
                sl = slice(gg * 128, (gg + 1) * 128)
                a = p.op("vector", lambda e, s=s, sl=sl, gg=gg: e.tensor_scalar(out=stg16[:, s, sl], in0=y1[:, sl],
                                                                              scalar1=io["ong"][:, gg:gg + 1],
                                                                              scalar2=None, op0=ALU.mult),
                         waits=([(c.dve, d)] + w + setupw) if gg == 0 else [], inc=c.dve)
            sv = p.dma("sync", io["ytg"].rearrange("(g c) t -> c g t", c=128)[:, :, r0:r0 + 128], V4(stg16[:, s, :]),
                       waits=[(c.dve, a)], inc=st16.sem[s])
            st16.cond[s] = [(st16.sem[s], sv)]

        for name, col0 in (("fv0", 2048), ("fv1", 2560), ("rv", 4104), ("rk", 3592), ("rq", 3080), ("gv", 5640)):
            b, w = wps.acquire()
            t = p.dma("gpsimd", wptm(b), winv[:, :, col0:col0 + 512], waits=w, inc=wps.sem[b])
            for tb in range(NBLK):
                bank, pt = proj(b, [(wps.sem[b], t)] + hw,
                                lambda kc, tb=tb: c.h[:, kc, tb * 128:(tb + 1) * 128],
                                lambda kc, b=b: wptm(b)[:, kc, :], lambda bank: c.ps[:, bank, :])
                pw = [(c.pe, pt)]
                psb = c.ps[:, bank, :]
                psv = V4(psb)
                r0 = t0 + tb * 128
                if name in ("fv0", "fv1"):
                    hc = 0 if name == "fv0" else 512
                    a = store16(lambda s_: s_, io["v"][r0:r0 + 128, hc:hc + 512], pw,
                                lambda e, o, psb=psb: e.copy(out=o, in_=psb))
                    pj_cond[bank] = [(c.act, a)]
                elif name == "rv":
                    a = p.op("scalar", lambda e, tb=tb, psb=psb: e.copy(out=rv[:, tb, :], in_=psb),
                             waits=pw + (ret_free if tb == 0 else []), inc=c.act)
                    pj_cond[bank] = [(c.act, a)]
                    ready[("rv", tb)] = [(c.act, a)]
                elif name in ("rk", "rq"):
                    rs, w2 = rots.acquire()
                    r1 = rot[:, rs, 0, :]
                    r2 = rot[:, rs, 1, :]
                    cosb = bh(tabt[:, ts_, tb, 0:128])
                    sina = bh64(tabt[:, ts_, tb, 128:192])
                    sinb = bh64(tabt[:, ts_, tb, 192:256])
                    p.op("vector", lambda e, psv=psv, r1=r1, cosb=cosb: e.tensor_tensor(out=V4(r1), in0=psv, in1=cosb,
                                                                                      op=ALU.mult),
                         waits=pw + w2 + tabw, inc=c.dve)
                    p.op("vector", lambda e, psv=psv, r2=r2, sina=sina: e.tensor_tensor(
                        out=V4(r2)[:, :, 0:64], in0=psv[:, :, 64:128], in1=sina, op=ALU.mult), inc=c.dve)
                    d = p.op("vector", lambda e, psv=psv, r2=r2, sinb=sinb: e.tensor_tensor(
                        out=V4(r2)[:, :, 64:128], in0=psv[:, :, 0:64], in1=sinb, op=ALU.mult), inc=c.dve)
                    pj_cond[bank] = [(c.dve, d)]
                    d = p.op("vector", lambda e, r1=r1, r2=r2: e.tensor_tensor(out=r1, in0=r1, in1=r2, op=ALU.add),
                             waits=[(c.dve, d)], inc=c.dve)
                    fw = ret_free if tb == 0 else []
                    if name == "rk":
                        a = p.op("scalar", lambda e, tb=tb, r1=r1: e.copy(out=kr[:, tb, :], in_=r1),
                                 waits=[(c.dve, d)] + fw, inc=c.act)
                        zb = bc(tabt[:, ts_, tb, 264:268])
                        d2 = p.op("vector", lambda e, tb=tb, r1=r1, zb=zb: e.tensor_tensor(out=V4(kz[:, tb, :]), in0=V4(r1),
                                                                                         in1=zb, op=ALU.mult),
                                  waits=[(c.dve, d)] + fw, inc=c.dve)
                        rots.cond[rs] = [(c.act, a), (c.dve, d2)]
                        ready[("kr", tb)] = [(c.act, a)]
                        ready[("kz", tb)] = [(c.dve, d2)]
                    else:
                        xb = bc(tabt[:, ts_, tb, 256:260])
                        gb_ = bc(tabt[:, ts_, tb, 260:264])
                        p.op("vector", lambda e, tb=tb, r1=r1, xb=xb: e.tensor_tensor(out=V4(qx[:, tb, :]), in0=V4(r1),
                                                                                    in1=xb, op=ALU.mult),
                             waits=[(c.dve, d)] + fw, inc=c.dve)
                        d2 = p.op("vector", lambda e, tb=tb, r1=r1, gb_=gb_: e.tensor_tensor(out=V4(qg[:, tb, :]),
                                                                                           in0=V4(r1), in1=gb_,
                                                                                           op=ALU.mult), inc=c.dve)
                        rots.cond[rs] = [(c.dve, d2)]
                        ready[("q", tb)] = [(c.dve, d2)]
                        emit_kv(tb)
                else:
                    rs, w2 = rots.acquire()
                    r1 = rot[:, rs, 0, :]
                    r2 = rot[:, rs, 1, :]
                    a = p.op("scalar", lambda e, psb=psb, r1=r1: e.activation(out=r1, in_=psb, func=AF.Gelu_apprx_tanh),
                             waits=pw + w2, inc=c.act)
                    pj_cond[bank] = [(c.act, a)]
                    a2 = p.op("scalar", lambda e, r1=r1, r2=r2: e.activation(out=r2, in_=r1, func=AF.Square),
                              waits=[(c.act, a)], inc=c.act)
                    d = p.op("vector", lambda e, r1=r1: e.tensor_reduce(out=st[:, 0:4], in_=V4(r1), axis=AX.X, op=ALU.add),
                             waits=[(c.act, a), (c.dve, c.dve.v)], inc=c.dve)
                    d = p.op("vector", lambda e, r2=r2: e.tensor_reduce(out=st[:, 4:8], in_=V4(r2), axis=AX.X, op=ALU.add),
                             waits=[(c.act, a2)], inc=c.dve)
                    d = p.op("vector", lambda e: e.tensor_scalar(out=st[:, 8:12], in0=st[:, 0:4], scalar1=1.0 / 128,
                                                                 scalar2=None, op0=ALU.mult),
                             waits=[(c.dve, d)], inc=c.dve)
                    d = p.op("vector", lambda e: e.tensor_tensor(out=st[:, 12:16], in0=st[:, 8:12], in1=st[:, 8:12],
                                                                 op=ALU.mult), waits=[(c.dve, d)], inc=c.dve)
                    d = p.op("vector", lambda e: e.scalar_tensor_tensor(out=st[:, 16:20], in0=st[:, 4:8], scalar=1.0 / 128,
                                                                        in1=st[:, 12:16], op0=ALU.mult,
                                                                        op1=ALU.subtract), waits=[(c.dve, d)], inc=c.dve)
                    d = p.op("vector", lambda e: e.tensor_scalar(out=st[:, 16:20], in0=st[:, 16:20], scalar1=EPS,
                                                                 scalar2=None, op0=ALU.add), waits=[(c.dve, d)], inc=c.dve)
                    a3 = p.op("scalar", lambda e: e.activation(out=st[:, 16:20], in_=st[:, 16:20], func=AF.Sqrt),
                              waits=[(c.dve, d)], inc=c.act)
                    d = p.op("vector", lambda e: e.reciprocal(out=st[:, 20:24], in_=st[:, 16:20]), waits=[(c.act, a3)],
                             inc=c.dve)
                    d = p.op("vector", lambda e, r1=r1: e.tensor_tensor(out=V4(r1), in0=V4(r1), in1=bc(st[:, 8:12]),
                                                                      op=ALU.subtract), waits=[(c.dve, d)], inc=c.dve)
                    d = p.op("vector", lambda e, r1=r1: e.tensor_tensor(out=V4(r1), in0=V4(r1), in1=bc(st[:, 20:24]),
                                                                      op=ALU.mult), waits=[(c.dve, d)], inc=c.dve)
                    d = p.op("vector", lambda e, r1=r1: e.tensor_tensor(out=r1, in0=r1, in1=io["lnb3"][:, 0, :],
                                                                      op=ALU.mult), waits=[(c.dve, d)] + setupw,
                             inc=c.dve)
                    d = p.op("vector", lambda e, r1=r1, tb=tb: e.tensor_tensor(out=vln[:, tb, :], in0=r1,
                                                                             in1=io["lnb3"][:, 1, :], op=ALU.add),
                             waits=[(c.dve, d)] + (vln_free if tb == 0 else []), inc=c.dve)
                    rots.cond[rs] = [(c.dve, d)]
                    ready[("vln", tb)] = [(c.dve, d)]
                    emit_kv(NBLK + tb)
                    if tb >= 1:
                        emit_gmlp(tb - 1)
            wps.cond[b] = [(c.pe, pt)]
        s32_done = s32_last[0]
        emit_gmlp(NBLK - 1)
        b4_cond = b4h[0]
        u_free = [(c.dve, c.dve.v)]
        vln_free = [(c.pe, c.pe.v)]
        for tb in range(NBLK):
            r0 = t0 + tb * 128
            for src, key, dstT, pb in ((kr, "kr", krT, 0), (qx, "q", qxT, 1), (qg, "q", None, 0)):
                for hh in range(4):
                    sl = slice(hh * 128, (hh + 1) * 128)
                    p.op("tensor", lambda e, src=src, sl=sl, tb=tb, pb=pb: e.transpose(out=pst[:, pb, sl], in_=src[:, tb, sl],
                                                                                     identity=io["identb"][:]),
                         waits=(ready[(key, tb)] + pst_cond[pb] + setupw) if hh == 0 else [],
                         inc=(c.pe if hh == 3 else None))
                pt = c.pe.v
                if dstT is not None:
                    d = p.op("vector", lambda e, dstT=dstT, pb=pb: e.tensor_copy(out=dstT[:], in_=pst[:, pb, 0:512]),
                             waits=[(c.pe, pt), (c.pe, c.pe.v)], inc=c.dve)
                    pst_cond[pb] = [(c.dve, d)]
                    ready[(id(dstT), tb)] = [(c.dve, d)]
                else:
                    s, w = st16.acquire()
                    a = p.op("scalar", lambda e, s=s, pb=pb: e.copy(out=stg16[:, s, :], in_=pst[:, pb, 0:512]),
                             waits=[(c.pe, pt)] + w, inc=c.act)
                    pst_cond[pb] = [(c.act, a)]
                    sv = p.dma("sync", io["ret_qg"].rearrange("(h d) t -> d h t", d=128)[:, :, r0:r0 + 128],
                               V4(stg16[:, s, :]), waits=[(c.act, a)], inc=st16.sem[s])
                    st16.cond[s] = [(st16.sem[s], sv)]
            for hh in range(4):
                sl = slice(hh * 128, (hh + 1) * 128)
                p.op("tensor", lambda e, sl=sl: e.matmul(c.ps[:, 4, sl], lhsT=krT[:, sl], rhs=qxT[:, sl], start=True,
                                                         stop=True),
                     waits=(ready[(id(krT), tb)] + ready[(id(qxT), tb)] + b4_cond) if hh == 0 else [],
                     inc=(c.pe if hh == 3 else None))
            pt = c.pe.v
            d = p.op("vector", lambda e: e.tensor_tensor(out=sm[:], in0=c.ps[:, 4, :], in1=io["maskr"][:], op=ALU.mult),
                     waits=[(c.pe, pt), (c.pe, c.pe.v)] + setupw, inc=c.dve)
            b4_cond = [(c.dve, d)]
            for hh in range(4):
                sl = slice(hh * 128, (hh + 1) * 128)
                p.op("tensor", lambda e, sl=sl, tb=tb: e.matmul(c.ps[:, 5, sl], lhsT=rv[:, tb, sl], rhs=sm[:, sl], start=True,
                                                               stop=False),
                     waits=([(c.dve, d)] + c.ps_free_waits + ready[("Sb", 2 * tb)] + ready[("Sb", 2 * tb + 1)])
                     if hh == 0 else [])
                for a_ in range(2):
                    cs = slice(hh * 128 + a_ * 64, hh * 128 + (a_ + 1) * 64)
                    p.op("tensor", lambda e, sl=sl, cs=cs, tb=tb, a_=a_: e.matmul(
                        c.ps[:, 5, cs], lhsT=Sb[:, 2 * tb + a_, sl], rhs=qxT[:, cs], start=False, stop=(a_ == 1)),
                         inc=(c.pe if (hh == 3 and a_ == 1) else None))
            pt = c.pe.v
            s, w = st32.acquire()
            a = p.op("scalar", lambda e, s=s: e.copy(out=stg32[:, s, :], in_=c.ps[:, 5, :]), waits=[(c.pe, pt)] + w,
                     inc=c.act)
            c.ps_free_waits = c.ps_free_waits + [(c.act, a)]
            sv = p.dma("sync", io["ret_o"].rearrange("(h e) t -> e h t", e=128)[:, :, r0:r0 + 128], V4(stg32[:, s, :]),
                       waits=[(c.act, a)], inc=st32.sem[s])
            st32.cond[s] = [(st32.sem[s], sv)]
        ret_free = [(c.pe, c.pe.v)]
    f1 = p.dma("sync", io["cneg_o"][:, :], cneg[:], waits=[(c.dve, c.dve.v)], inc=misc)
    f2 = p.dma("sync", io["ret_S"][:, :], S32[:], waits=s32_done, inc=misc)
    p.wait_only("sync", [(misc, f2)] + [(st16.sem[s], st16.sem[s].v) for s in range(4)] +
                [(st32.sem[s], st32.sem[s].v) for s in range(2)])


def mixa_setup(c, din, io):
    p = c.p
    sb = c.sb
    gcol = sb("gcol", [128, KC], F32)
    negb = sb("negb", [8, 1], F32)
    maskr = sb("maskr", [128, 512], F32)
    wst = sb("wst", [128, 512], F32)
    triu = sb("triu", [128, 128], F32)
    wsb = sb("wsb", [128, 4, 128], BF16)
    lnb3 = sb("lnb3", [128, 3, 512], F32)
    ong = sb("ong", [128, 4], F32)
    identb = sb("identb", [128, 128], BF16)
    io.update({"negb": negb, "maskr": maskr, "wsb": wsb, "lnb3": lnb3, "ong": ong, "identb": identb})
    for dst, src in ((gcol[:], din["g"]), (negb[:], din["bf"]), (maskr[:], din["maskr"]), (wst[:], din["wst"]),
                     (triu[:], din["triu"]), (lnb3[:].rearrange("p a b -> p (a b)"), din["lnb3"]),
                     (ong[:], din["ong"])):
        p.dma("sync", dst, src, inc=c.setup_d)
    p.dma("gpsimd", identb[:], din["ident"], inc=c.setup_d)
    dl = [(c.setup_d, c.setup_d.v)]
    p.op("vector", lambda e: e.tensor_scalar(out=negb[:], in0=negb[:], scalar1=-1.0, scalar2=None, op0=ALU.mult),
         waits=dl, inc=c.setup_v)
    p.op("vector", lambda e: e.tensor_tensor(out=wsb[:], in0=wst[:].rearrange("p (a b) -> p a b", a=4),
                                             in1=triu[:].unsqueeze(1).to_broadcast([128, 4, 128]), op=ALU.mult),
         inc=c.setup_v)
    return gcol


def build_mixa():
    nc = bass.Bass("TRN2", target_bir_lowering=False)
    di = lambda name, shape: nc.dram_tensor(name, shape, F32, kind="ExternalInput").ap()
    do = lambda name, shape, dt: nc.dram_tensor(name, shape, dt, kind="ExternalOutput").ap()
    xin = di("xin", [D, NTOK])
    w_in = di("w_in", [D, INCOLS])
    din = {"g": di("g", [128, KC])[:, :], "bf": di("bf", [8, 1])[:, :], "maskr": di("maskr", [128, 512])[:, :],
           "wst": di("wst", [128, 512])[:, :], "triu": di("triu", [128, 128])[:, :],
           "lnb3": di("lnb3", [128, 3 * 512])[:, :], "ong": di("ong", [128, 4])[:, :],
           "ident": di("ident", [128, 128])[:, :]}
    io = {
        "tab": di("tab", [NTOK, 268]),
        "qT": do("qT", [1024, NTOK], BF16), "kT": do("kT", [1024, NTOK], BF16), "v": do("v", [NTOK, 1024], BF16),
        "cneg_o": do("cneg", [8, NTOK], F32), "ret_o": do("ret_o", [512, NTOK], F32),
        "ret_qg": do("ret_qg", [512, NTOK], BF16), "ret_S": do("ret_S", [128, 512], F32),
        "rgs": do("rgs", [512, NTOK], F32), "ytg": do("ytg", [512, NTOK], BF16),
    }
    with contextlib.ExitStack() as stack:
        p = Prog(nc, stack)
        c = alloc_common(nc, stack, p, tt=TA, nps=6, stat_bank=5)
        c.stack = stack
        gcol = mixa_setup(c, din, io)
        mixa_body(c, xin, gcol, w_in, io)
        p.emit()
    return nc


def host_consts(half):
    t = np.arange(NTOK, dtype=np.float64)
    pos = (half * NTOK + np.arange(NTOK)).astype(np.float32)
    inv_freq = (np.float32(10000.0) ** (-np.arange(64, dtype=np.float32) / np.float32(64))).astype(np.float32)
    ang = (pos[:, None] * inv_freq[None, :]).astype(np.float32).astype(np.float64)
    cos, sin = np.cos(ang), np.sin(ang)
    gam = np.array(GAM, dtype=np.float64)
    cidx = (np.arange(NTOK) % 64).astype(np.float64)
    xi = gam[None, :] ** (cidx[:, None] + 1.0)
    gm = gam[None, :] ** (t[:, None] + 1.0)
    zeta = gam[None, :] ** (63.0 - cidx[:, None]) * (128.0 ** -0.5)
    tab = np.concatenate([cos, cos, -sin, sin, xi, gm, zeta], axis=1).astype(np.float32)
    s = np.arange(128)
    cc = np.arange(128)
    same = (s[:, None] // 64 == cc[None, :] // 64) & (cc[None, :] >= s[:, None])
    maskr = np.zeros((128, 4, 128), np.float64)
    for h in range(4):
        maskr[:, h, :] = np.where(same, gam[h] ** (-(s[:, None] % 64 + 1.0)), 0.0) * (128.0 ** -0.5)
    triu = (s[:, None] <= cc[None, :]).astype(np.float32)
    return {"tab": np.ascontiguousarray(tab), "maskr": maskr.reshape(128, 512).astype(np.float32), "triu": triu,
            "ident": np.eye(128, dtype=np.float32)}


def col16(vec):
    return np.ascontiguousarray(np.asarray(vec, np.float32).reshape(KC, 128).T)


def mixa_inputs(xT, half, l, P):
    hc = host_consts(half)
    wst = np.ascontiguousarray(np.transpose(P["gmlp_w_s"][l], (2, 0, 1)).reshape(128, 512))
    lnb3 = np.concatenate([np.broadcast_to(P["gmlp_ln_g"][l][None, :], (128, 512)),
                           np.broadcast_to(P["gmlp_ln_b"][l][None, :], (128, 512)),
                           np.broadcast_to(P["gmlp_b_s"][l].reshape(1, 512), (128, 512))], axis=1)
    ong = np.ascontiguousarray(P["out_norm"][l][1536:2048].reshape(4, 128).T)
    return {"xin": xT, "g": col16(P["mix_norm"][l]), "w_in": P["w_in"][l],
            "bf": np.ascontiguousarray(P["fox_b_f"][l].reshape(8, 1)), "tab": hc["tab"], "maskr": hc["maskr"],
            "wst": wst.astype(np.float32), "triu": hc["triu"], "lnb3": np.ascontiguousarray(lnb3, dtype=np.float32),
            "ong": ong.astype(np.float32), "ident": hc["ident"]}


QT = 512
NQT = NTOK // QT
MASKNEG = -30000.0


def mixb_body(nc, stack, p, io):
    uid = _uid()
    sb = lambda name, shape, dt: stack.enter_context(nc.sbuf_tensor("sb%d_%s" % (uid, name), shape, dt))
    ps = stack.enter_context(nc.psum_tensor("psb%d" % uid, [128, 8, 512], F32))
    onesf = sb("onesf", [128, 128], F32)
    onesb = sb("onesb", [128, 128], BF16)
    cmask = sb("cmask", [128, 896], F32)
    selh = sb("selh", [8, 8, 128], F32)
    ident8 = sb("ident8", [8, 8], F32)
    pmask = sb("pmask", [128, 1], F32)
    sflag = sb("sflag", [128, 1], F32)
    ona = sb("ona", [128, 12], F32)
    sinit = sb("sinit", [128, 512], F32)
    sinb = sb("sinb", [128, 512], BF16)
    cn = sb("cn", [8, 2 * NTOK], F32)
    ncl = sb("ncl", [8, NTOK], F32)
    ncp = sb("ncp", [8, NTOK], F32)
    biasT = sb("biasT", [128, 32, 8], F32)
    setup_d = p.sem("bsetupd")
    dve = p.sem("bdve")
    act = p.sem("bact")
    pe = p.sem("bpe")
    for dst, src in ((cmask[:], io["cmask"][:, :]), (selh[:].rearrange("k h m -> k (h m)"), io["selh"][:, :]),
                     (ident8[:], io["ident8"][:, :]), (pmask[:], io["pmask"][:, :]), (sflag[:], io["sflag"][:, :]),
                     (ona[:], io["ona"][:, :]), (sinit[:], io["s_init"][:, :]), (cn[:, 0:NTOK], io["cneg_prev"][:, :]),
                     (cn[:, NTOK:2 * NTOK], io["cneg_loc"][:, :])):
        p.dma("sync", dst, src, inc=setup_d)
    sd = [(setup_d, setup_d.v)]
    p.op("vector", lambda e: e.memset(onesf[:], 1.0), inc=dve)
    p.op("vector", lambda e: e.memset(onesb[:], 1.0), inc=dve)
    p.op("vector", lambda e: e.tensor_scalar(out=sinb[:], in0=sinit[:], scalar1=sflag[:, 0:1], scalar2=None, op0=ALU.mult),
         waits=sd, inc=dve)
    p.op("vector", lambda e: e.tensor_scalar(out=ncl[:], in0=cn[:, NTOK:2 * NTOK], scalar1=-1.0, scalar2=None, op0=ALU.mult),
         inc=dve)
    d = p.op("vector", lambda e: e.tensor_scalar(out=ncp[:], in0=ncl[:], scalar1=cn[:, NTOK - 1:NTOK], scalar2=None,
                                                 op0=ALU.subtract), waits=[(dve, dve.v)], inc=dve)
    for blk in range(32):
        p.op("tensor", lambda e, blk=blk: e.transpose(out=ps[:, 6, blk * 8:(blk + 1) * 8],
                                                      in_=cn[0:8, blk * 128:(blk + 1) * 128], identity=ident8[:]),
             waits=sd if blk == 0 else [], inc=(pe if blk == 31 else None))
    p.op("vector", lambda e: e.tensor_scalar(out=biasT[:, 0:16, :].rearrange("p a b -> p (a b)"), in0=ps[:, 6, 0:128],
                                             scalar1=pmask[:, 0:1], scalar2=None, op0=ALU.add),
         waits=[(pe, pe.v)] + sd, inc=dve)
    d = p.op("vector", lambda e: e.tensor_copy(out=biasT[:, 16:32, :].rearrange("p a b -> p (a b)"), in_=ps[:, 6, 128:256]),
             inc=dve)
    b6_cond = [(dve, d)]
    selb = sb("selb", [128, 8, 128], BF16)
    maskb = sb("maskb", [128, 896], BF16)
    identb = sb("identb", [128, 128], BF16)
    setup_d2 = p.sem("bsetupd2")
    p.dma("gpsimd", selb[:].rearrange("k h m -> k (h m)"), io["selh72"][:, :], inc=setup_d2)
    p.dma("gpsimd", maskb[:], io["cmask"][:, :], inc=setup_d2)
    p.dma("gpsimd", identb[:], io["ident"][:, :], inc=setup_d2)
    sd = sd + [(setup_d2, setup_d2.v)]
    cst = sb("cst", [128, 2, NTOK], BF16)
    csr = sb("csr", [8, NTOK], F32)
    csm = sb("csm", [8, NTOK], BF16)
    d = p.op("vector", lambda e: e.memset(cst[:], 0.0), inc=dve)
    for vi, srcc in ((0, ncp), (1, ncl)):
        d = p.op("vector", lambda e, vi=vi, srcc=srcc: e.tensor_copy(out=cst[0:8, vi, :], in_=srcc[:]), waits=[(dve, d)], inc=dve)
        d = p.op("vector", lambda e, vi=vi, srcc=srcc: e.tensor_tensor(out=csr[:], in0=srcc[:], in1=cst[0:8, vi, :],
                                                                     op=ALU.subtract), waits=[(dve, d)], inc=dve)
        d = p.op("vector", lambda e: e.tensor_copy(out=csm[:], in_=csr[:]), waits=[(dve, d)], inc=dve)
        d = p.op("vector", lambda e, vi=vi: e.tensor_copy(out=cst[32:40, vi, :], in_=csm[:]), waits=[(dve, d)], inc=dve)
        d = p.op("vector", lambda e: e.tensor_tensor(out=csr[:], in0=csr[:], in1=csm[:], op=ALU.subtract),
                 waits=[(dve, d)], inc=dve)
        d = p.op("vector", lambda e, vi=vi: e.tensor_copy(out=cst[64:72, vi, :], in_=csr[:]), waits=[(dve, d)], inc=dve)
    setup_done = [(dve, d)] + sd

    qh = sb("qh", [128, 2, NTOK], BF16)
    kh = sb("kh", [128, 2, 2 * NTOK], BF16)
    vh = sb("vh", [128, 2, 32, 128], BF16)
    hs = Slots(p, "hs", 2)
    cb = sb("cb", [128, 2, 2, QT], F32)
    cbs = Slots(p, "cbs", 2)
    tmp = sb("tmp", [128, 4, QT], F32)
    tmps = Slots(p, "tmps", 4)
    pt = sb("pt", [128, 4, QT], BF16)
    pts = Slots(p, "pts", 4)
    ya = sb("ya", [128, 2, QT], F32)
    yas = Slots(p, "yas", 2)
    yq = sb("yq", [128, QT], F32)
    stg = sb("stg", [128, 2, QT], BF16)
    stgs = Slots(p, "stgs", 2)
    ro = sb("ro", [128, 2, QT], F32)
    rg = sb("rg", [128, 2, QT], F32)
    qgt = sb("qgt", [128, 2, QT], BF16)
    rls = Slots(p, "rls", 2)
    S_BANKS = (0, 1, 7)
    LOOK = 2
    s_cond = {0: [], 1: [], 7: []}
    acc_cond = [[], []]
    s_n = [0]
    acc_n = [0]
    y_stores = []

    def headnorm_store(yslot, yready, gain_col, dst_ap, mul_tile=None, mul_wait=()):
        nonlocal b6_cond
        a = p.op("scalar", lambda e: e.activation(out=yq[:], in_=ya[:, yslot, :], func=AF.Square),
                 waits=list(yready) + [(dve, dve.v)], inc=act)
        p.op("tensor", lambda e: e.matmul(ps[:, 6, :], lhsT=onesf[:], rhs=yq[:], start=True, stop=True),
             waits=[(act, a)] + b6_cond, inc=pe)
        d = p.op("vector", lambda e: e.tensor_scalar(out=yq[:], in0=ps[:, 6, :], scalar1=1.0 / 128, scalar2=EPS,
                                                     op0=ALU.mult, op1=ALU.add), waits=[(pe, pe.v)], inc=dve)
        b6_cond = [(dve, d)]
        a = p.op("scalar", lambda e: e.activation(out=yq[:], in_=yq[:], func=AF.Sqrt), waits=[(dve, d)], inc=act)
        d = p.op("vector", lambda e: e.reciprocal(out=yq[:], in_=yq[:]), waits=[(act, a)], inc=dve)
        s, w = stgs.acquire()
        if mul_tile is None:
            d = p.op("vector", lambda e: e.scalar_tensor_tensor(out=stg[:, s, :], in0=ya[:, yslot, :], scalar=gain_col,
                                                                in1=yq[:], op0=ALU.mult, op1=ALU.mult),
                     waits=[(dve, d)] + w, inc=dve)
        else:
            d = p.op("vector", lambda e: e.scalar_tensor_tensor(out=ya[:, yslot, :], in0=ya[:, yslot, :], scalar=gain_col,
                                                                in1=yq[:], op0=ALU.mult, op1=ALU.mult),
                     waits=[(dve, d)], inc=dve)
            d = p.op("vector", lambda e: e.tensor_tensor(out=stg[:, s, :], in0=ya[:, yslot, :], in1=mul_tile, op=ALU.mult),
                     waits=[(dve, d)] + w + list(mul_wait), inc=dve)
        sv = p.dma("sync", dst_ap, stg[:, s, :], waits=[(dve, d)], inc=stgs.sem[s])
        stgs.cond[s] = [(stgs.sem[s], sv)]
        y_stores.append((stgs.sem[s], sv))
        return [(dve, d)]

    pending = [None]

    def run_pending():
        if pending[0] is not None:
            ys_, rdy_, gain_, dst_, mt_, rs_ = pending[0]
            pending[0] = None
            yw_ = headnorm_store(ys_, rdy_, gain_, dst_, mul_tile=mt_)
            yas.cond[ys_] = yw_
            if rs_ is not None:
                rls.cond[rs_] = yw_

    vparts = []
    blk0 = 0
    for key in ("v_prev", "v_loc"):
        for part in _parts(io[key]):
            nb = part.shape[0] // 128
            vparts.append((blk0, nb, part.rearrange("(b p) c -> p b c", p=128)))
            blk0 += nb
    assert blk0 == 32
    for h in range(8):
        hsl, w = hs.acquire()
        rows = slice(h * 128, (h + 1) * 128)
        p.dma("sync", qh[:, hsl, :], io["qT"][rows, :], waits=w, inc=hs.sem[hsl])
        p.dma("sync", kh[:, hsl, 0:NTOK], io["kT_prev"][rows, :], inc=hs.sem[hsl])
        p.dma("sync", kh[:, hsl, NTOK:2 * NTOK], io["kT_loc"][rows, :], inc=hs.sem[hsl])
        for (b0_, nb_, vw_) in vparts:
            hl = p.dma("sync", vh[:, hsl, b0_:b0_ + nb_, :], vw_[:, :, rows], inc=hs.sem[hsl])
        hw = [(hs.sem[hsl], hl)]
        for qt in range(NQT):
            qs = slice(qt * QT, (qt + 1) * QT)
            blocks = [(j, 0, None) for j in range(16)] + [(16 + j, 1, (j - 4 * qt) if j >= 4 * qt else None)
                                                          for j in range(4 * qt + 4)]
            an = acc_n[0]
            acc_n[0] += 1
            ob, db = 2 + an % 2, 4 + an % 2
            pend = None
            nblk = len(blocks)

            def emit_pv(pend, first, last):
                j, slot, pw_ = pend
                p.op("tensor", lambda e, j=j, slot=slot, ob=ob, hsl=hsl: e.matmul(ps[:, ob, :], lhsT=vh[:, hsl, j, :],
                                                                                 rhs=pt[:, slot, :], start=first, stop=last),
                     waits=pw_ + (acc_cond[an % 2] if first else []))
                p.op("tensor", lambda e, slot=slot, db=db: e.matmul(ps[:, db, :], lhsT=onesb[:], rhs=pt[:, slot, :],
                                                                   start=first, stop=last), inc=pe)
                pts.cond[slot] = [(pe, pe.v)]

            npv = 0
            queue = []
            for bi, (j, vi, dg) in enumerate(blocks):
                if bi == 6:
                    run_pending()
                sn = s_n[0]
                s_n[0] += 1
                sbk = S_BANKS[sn % 3]
                p.op("tensor", lambda e, j=j, sbk=sbk, qs=qs, hsl=hsl: e.matmul(ps[:, sbk, :], lhsT=kh[:, hsl, j * 128:(j + 1) * 128],
                                                                      rhs=qh[:, hsl, qs], start=True, stop=False),
                     waits=hw + s_cond[sbk] + setup_done)
                p.op("tensor", lambda e, sbk=sbk, qs=qs, h=h, vi=vi, dg=dg: e.matmul(ps[:, sbk, :], lhsT=selb[:, h, :],
                                                                                   rhs=cst[:, vi, qs], start=False,
                                                                                   stop=(dg is None)),
                     inc=(pe if dg is None else None))
                if dg is not None:
                    off = 384 - dg * 128
                    p.op("tensor", lambda e, sbk=sbk, off=off: e.matmul(ps[:, sbk, :], lhsT=identb[:], rhs=maskb[:, off:off + QT],
                                                                       start=False, stop=True), inc=pe)
                st_ = pe.v
                if len(queue) >= LOOK:
                    emit_pv(queue.pop(0), npv == 0, False)
                    npv += 1
                slot, w = pts.acquire()
                a = p.op("scalar", lambda e, sbk=sbk, slot=slot, j=j, h=h: e.activation(
                    out=pt[:, slot, :], in_=ps[:, sbk, :], func=AF.Exp, bias=biasT[:, j, h:h + 1], scale=1.0),
                         waits=[(pe, st_)] + w + setup_done, inc=act)
                s_cond[sbk] = [(act, a)]
                queue.append((j, slot, [(act, a)]))
            while queue:
                pend = queue.pop(0)
                emit_pv(pend, npv == 0, len(queue) == 0)
                npv += 1
            acc_done = pe.v
            ys, w = yas.acquire()
            d = p.op("vector", lambda e, db=db: e.reciprocal(out=yq[:], in_=ps[:, db, :]), waits=[(pe, acc_done), (dve, dve.v),
                                                                                          (act, act.v)], inc=dve)
            d = p.op("vector", lambda e, ys=ys, ob=ob: e.tensor_tensor(out=ya[:, ys, :], in0=ps[:, ob, :], in1=yq[:], op=ALU.mult),
                     waits=[(dve, d)] + w, inc=dve)
            acc_cond[an % 2] = [(dve, d)]
            run_pending()
            pending[0] = (ys, [(dve, d)], ona[:, h:h + 1], io["yT"][rows, qs], None, None)
        hs.cond[hsl] = [(pe, pe.v)]
    for hh in range(4):
        rows = slice(hh * 128, (hh + 1) * 128)
        for qt in range(NQT):
            qs = slice(qt * QT, (qt + 1) * QT)
            rs, w = rls.acquire()
            p.dma("sync", ro[:, rs, :], io["ret_o"][rows, qs], waits=w, inc=rls.sem[rs])
            p.dma("sync", rg[:, rs, :], io["rgs"][rows, qs], inc=rls.sem[rs])
            rl = p.dma("sync", qgt[:, rs, :], io["ret_qg"][rows, qs], inc=rls.sem[rs])
            p.op("tensor", lambda e, rs=rs, rows=rows: e.matmul(ps[:, 6, :], lhsT=sinb[:, rows], rhs=qgt[:, rs, :], start=True,
                                                               stop=True),
                 waits=[(rls.sem[rs], rl)] + b6_cond + setup_done, inc=pe)
            ys, w = yas.acquire()
            d = p.op("vector", lambda e, ys=ys, rs=rs: e.tensor_tensor(out=ya[:, ys, :], in0=ps[:, 6, :], in1=ro[:, rs, :],
                                                                      op=ALU.add), waits=[(pe, pe.v)] + w, inc=dve)
            b6_cond = [(dve, d)]
            run_pending()
            pending[0] = (ys, [(dve, d)], ona[:, 8 + hh:9 + hh],
                          io["yT"][1024 + hh * 128:1024 + (hh + 1) * 128, qs], rg[:, rs, :], rs)
    run_pending()
    hb = sb("hb", [128, KC, TT], BF16)
    wo = sb("wo", [128, 2, KC, 256], BF16)
    wos = Slots(p, "wos", 2)
    xs = sb("xsb", [128, 3, TT], F32)
    xss = Slots(p, "xss", 3)
    hbl = p.sem("hbl")
    wov = io["w_out"].rearrange("(kc p) f -> p kc f", p=128)
    ytv = io["yT"].rearrange("(kc p) t -> p kc t", p=128)
    ygv = io["ytg"].rearrange("(kc p) t -> p kc t", p=128)
    hb_free = []
    on = 0
    ob_cond = [[], []]
    for tt in range(NTT):
        t0 = tt * TT
        p.dma("sync", hb[:, 0:12, :], ytv[:, :, t0:t0 + TT], waits=list(y_stores) + hb_free, inc=hbl)
        hl = p.dma("sync", hb[:, 12:16, :], ygv[:, :, t0:t0 + TT], inc=hbl)
        for pd in range(8):
            col0 = pd * 256
            b, w = wos.acquire()
            wl = p.dma("gpsimd", wo[:, b, :, :], wov[:, :, col0:col0 + 256], waits=w, inc=wos.sem[b])
            for ii in range(2):
                i = pd * 2 + ii
                s, w = xss.acquire()
                full = p.dma("sync", xs[:, s, :], io["xin"][i * 128:(i + 1) * 128, t0:t0 + TT], waits=w, inc=xss.sem[s])
                for th in range(2):
                    obk = on % 2
                    on += 1
                    for kc in range(KC):
                        p.op("tensor", lambda e, b=b, kc=kc, ii=ii, th=th, obk=obk: e.matmul(
                            ps[:, obk, :], lhsT=wo[:, b, kc, ii * 128:(ii + 1) * 128], rhs=hb[:, kc, th * 512:(th + 1) * 512],
                            start=(kc == 0), stop=(kc == KC - 1)),
                             waits=([(wos.sem[b], wl), (hbl, hl)] + ob_cond[obk]) if kc == 0 else [],
                             inc=(pe if kc == KC - 1 else None))
                    r = p.op("vector", lambda e, s=s, th=th, obk=obk: e.tensor_tensor(
                        out=xs[:, s, th * 512:(th + 1) * 512], in0=ps[:, obk, :], in1=xs[:, s, th * 512:(th + 1) * 512],
                        op=ALU.add), waits=[(pe, pe.v), (xss.sem[s], full)], inc=dve)
                    ob_cond[obk] = [(dve, r)]
                sv = p.dma("sync", io["xout"][i * 128:(i + 1) * 128, t0:t0 + TT], xs[:, s, :], waits=[(dve, r)],
                           inc=xss.sem[s])
                xss.cond[s] = [(xss.sem[s], sv)]
            wos.cond[b] = [(pe, pe.v)]
        hb_free = [(pe, pe.v)]
    p.wait_only("sync", [(xss.sem[s], xss.sem[s].v) for s in range(3)])


def build_mixb():
    nc = bass.Bass("TRN2", target_bir_lowering=False)
    di = lambda name, shape, dt=F32: nc.dram_tensor(name, shape, dt, kind="ExternalInput").ap()
    io = {
        "qT": di("qT", [1024, NTOK], BF16), "kT_loc": di("kT_loc", [1024, NTOK], BF16),
        "kT_prev": di("kT_prev", [1024, NTOK], BF16), "v_loc": di("v_loc", [NTOK, 1024], BF16),
        "v_prev": di("v_prev", [NTOK, 1024], BF16), "cneg_loc": di("cneg_loc", [8, NTOK]),
        "cneg_prev": di("cneg_prev", [8, NTOK]), "s_init": di("s_init", [128, 512]),
        "ret_o": di("ret_o", [512, NTOK]), "ret_qg": di("ret_qg", [512, NTOK], BF16), "rgs": di("rgs", [512, NTOK]),
        "ytg": di("ytg", [512, NTOK], BF16), "cmask": di("cmask", [128, 896]), "selh": di("selh", [8, 1024]),
        "ident8": di("ident8", [8, 8]), "pmask": di("pmask", [128, 1]), "sflag": di("sflag", [128, 1]),
        "selh72": di("selh72", [128, 1024]), "ident": di("ident", [128, 128]),
        "ona": di("ona", [128, 12]), "w_out": di("w_out", [D, D]), "xin": di("xin", [D, NTOK]),
        "yT": nc.dram_tensor("yT", [1536, NTOK], BF16, kind="Internal").ap(),
        "xout": nc.dram_tensor("xout", [D, NTOK], F32, kind="ExternalOutput").ap(),
    }
    with contextlib.ExitStack() as stack:
        p = Prog(nc, stack)
        mixb_body(nc, stack, p, io)
        p.emit()
    return nc


def mixb_consts(half):
    s = np.arange(128)[:, None]
    u = np.arange(896)[None, :]
    cmask = np.where((u - 384) >= s, 0.0, MASKNEG).astype(np.float32)
    selh = np.zeros((8, 8, 128), np.float32)
    for h in range(8):
        selh[h, h, :] = 1.0
    selh72 = np.zeros((128, 8, 128), np.float32)
    for h in range(8):
        selh72[h, h, :] = 1.0
        selh72[32 + h, h, :] = 1.0
        selh72[64 + h, h, :] = 1.0
    return {"cmask": cmask, "selh": selh.reshape(8, 1024), "ident8": np.eye(8, dtype=np.float32),
            "selh72": selh72.reshape(128, 1024), "ident": np.eye(128, dtype=np.float32),
            "pmask": np.full((128, 1), 0.0 if half == 1 else MASKNEG, np.float32),
            "sflag": np.full((128, 1), 1.0 if half == 1 else 0.0, np.float32)}


def build_norm():
    nc = bass.Bass("TRN2", target_bir_lowering=False)
    xin = nc.dram_tensor("xin", [D, NTOK], F32, kind="ExternalInput").ap()
    g = nc.dram_tensor("g", [128, KC], F32, kind="ExternalInput").ap()
    xout = nc.dram_tensor("xout", [D, NTOK], F32, kind="ExternalOutput").ap()
    with contextlib.ExitStack() as stack:
        p = Prog(nc, stack)
        c = alloc_common(nc, stack, p)
        gcol = c.sb("gcol", [128, KC], F32)
        p.dma("sync", gcol[:], g[:, :], inc=c.setup_d)
        for tt in range(NTT):
            t0 = tt * TT

            def out_fn(kc, s, waits, t0=t0):
                d = p.op("vector", lambda e: e.scalar_tensor_tensor(out=c.xs[:, s, :], in0=c.xs[:, s, :],
                                                                    scalar=gcol[:, kc:kc + 1], in1=c.rstd[:],
                                                                    op0=ALU.mult, op1=ALU.mult), waits=waits, inc=c.dve_h)
                sv = p.dma("sync", xout[kc * 128:(kc + 1) * 128, t0:t0 + TT], c.xs[:, s, :], waits=[(c.dve_h, d)],
                           inc=c.xs_st[s])
                c.xs_cond[s] = [(c.xs_st[s], sv)]

            norm_stats_and_h(c, xin, gcol, tt, out_fn=out_fn)
        finish(c)
        p.emit()
    return nc


_PROGS = {}


def _prog(name):
    if name not in _PROGS:
        _PROGS[name] = {"ffn": build_ffn, "mixa": build_mixa, "mixb": build_mixb, "norm": build_norm}[name]()
    return _PROGS[name]


def _run(name, in_maps):
    res = run_bass_kernel_spmd(_prog(name), in_maps, core_ids=list(range(NCORES)))
    return res.results


def run_ffn(xTs, l, P, pre):
    g = col16(P[pre + "_norm"][l])
    maps = [{"xin": xTs[c], "g": g, "wg": P[pre + "_w_gate"][l], "wu": P[pre + "_w_up"][l], "wd": P[pre + "_w_down"][l]}
            for c in range(NCORES)]
    return [r["xout"] for r in _run("ffn", maps)]


def run_mixer(xTs, l, P):
    ra = _run("mixa", [mixa_inputs(xTs[c], c % 2, l, P) for c in range(NCORES)])
    ona = np.ascontiguousarray(P["out_norm"][l][0:1536].reshape(12, 128).T).astype(np.float32)
    maps = []
    for c in range(NCORES):
        half = c % 2
        pc = c - 1 if half == 1 else c
        m = {"qT": ra[c]["qT"], "kT_loc": ra[c]["kT"], "kT_prev": ra[pc]["kT"], "v_loc": ra[c]["v"], "v_prev": ra[pc]["v"],
             "cneg_loc": ra[c]["cneg"], "cneg_prev": ra[pc]["cneg"], "s_init": ra[pc]["ret_S"], "ret_o": ra[c]["ret_o"],
             "ret_qg": ra[c]["ret_qg"], "rgs": ra[c]["rgs"], "ytg": ra[c]["ytg"], "ona": ona, "w_out": P["w_out"][l],
             "xin": xTs[c]}
        m.update(mixb_consts(half))
        maps.append(m)
    return [r["xout"] for r in _run("mixb", maps)]


def kernel_unfused(**inputs):
    P = {k: np.asarray(v) for k, v in inputs.items()}
    x = P["x"]
    xTs = [np.ascontiguousarray(x[c // 2, (c % 2) * NTOK:(c % 2 + 1) * NTOK, :].T) for c in range(NCORES)]
    for l in range(DEPTH):
        xTs = run_ffn(xTs, l, P, "ffn1")
        xTs = run_mixer(xTs, l, P)
        xTs = run_ffn(xTs, l, P, "ffn2")
    g = col16(P["final_norm"])
    outs = [r["xout"] for r in _run("norm", [{"xin": xTs[c], "g": g} for c in range(NCORES)])]
    out = np.empty_like(x)
    for c in range(NCORES):
        out[c // 2, (c % 2) * NTOK:(c % 2 + 1) * NTOK, :] = outs[c].T
    return out


PAIRS = [[0, 1], [2, 3], [4, 5], [6, 7]]
WSHAPES = {"ffn1_w_gate": [DEPTH, D, DFF], "ffn1_w_up": [DEPTH, D, DFF], "ffn1_w_down": [DEPTH, DFF, D],
           "w_in": [DEPTH, D, INCOLS], "w_out": [DEPTH, D, D],
           "ffn2_w_gate": [DEPTH, D, DFF], "ffn2_w_up": [DEPTH, D, DFF], "ffn2_w_down": [DEPTH, DFF, D]}
SMALL = {"g_ffn1": [DEPTH, 128, KC], "g_mix": [DEPTH, 128, KC], "g_ffn2": [DEPTH, 128, KC], "g_fin": [128, KC],
         "bf": [DEPTH, 8, 1], "wst": [DEPTH, 128, 512], "lnb3": [DEPTH, 128, 1536], "ong": [DEPTH, 128, 4],
         "ona": [DEPTH, 128, 12], "tab": [NTOK, 268], "maskr": [128, 512], "triu": [128, 128], "ident": [128, 128],
         "cmask": [128, 896], "selh": [8, 1024], "ident8": [8, 8], "pmask": [128, 1], "sflag": [128, 1],
         "selh72": [128, 1024]}


def build_fused(depth=DEPTH, phases="fmxbF"):
    nc = bass.Bass("TRN2", target_bir_lowering=False)
    di = lambda name, shape: nc.dram_tensor(name, shape, F32, kind="ExternalInput").ap()
    it = lambda name, shape, dt: nc.dram_tensor(name, shape, dt, kind="Internal").ap()
    x_in = di("x", [D, NTOK])
    W = {k: di(k, [depth] + s[1:]) for k, s in WSHAPES.items()}
    S = {k: di(k, ([depth] + s[1:]) if len(s) == 3 else s) for k, s in SMALL.items()}
    out = nc.dram_tensor("out", [D, NTOK], F32, kind="ExternalOutput").ap()
    xres = it("xres", [D, NTOK], F32)
    qT = it("qT", [1024, NTOK], BF16)
    xk = [it("xk%d" % i, [512, NTOK], BF16) for i in range(2)]
    xv = [it("xv%d" % i, [1024, 1024], BF16) for i in range(2)]
    xc = it("xc", [8, NTOK], F32)
    xs_ = it("xs_", [128, 512], F32)
    gk = [it("gk%d" % i, [1024, NTOK], BF16) for i in range(2)]
    gv_ = [it("gv%d" % i, [2048, 1024], BF16) for i in range(2)]
    gc = it("gc", [16, NTOK], F32)
    gs = it("gs", [256, 512], F32)
    ret_o = it("ret_o", [512, NTOK], F32)
    ret_qg = it("ret_qg", [512, NTOK], BF16)
    rgs = it("rgs", [512, NTOK], F32)
    ytg = it("ytg", [512, NTOK], BF16)
    yT = it("yT", [1536, NTOK], BF16)
    with contextlib.ExitStack() as gstack:
        p = Prog(nc, gstack)

        def ffn_phase(xin, xout, g_ap, wg, wu, wd):
            with contextlib.ExitStack() as st:
                c = alloc_common(nc, st, p)
                alloc_ffn(c)
                gcol = c.sb("gcol", [128, KC], F32)
                p.dma("sync", gcol[:], g_ap, inc=c.setup_d)
                ffn_body(c, xin, xout, gcol, wg, wu, wd)
                finish(c)
                p.barrier()
                p.emit()

        def mixa_phase(l):
            with contextlib.ExitStack() as st:
                c = alloc_common(nc, st, p, tt=TA, nps=6, stat_bank=5)
                din = {"g": S["g_mix"][l], "bf": S["bf"][l], "maskr": S["maskr"][:, :], "wst": S["wst"][l],
                       "triu": S["triu"][:, :], "lnb3": S["lnb3"][l], "ong": S["ong"][l], "ident": S["ident"][:, :]}
                io = {"tab": S["tab"], "qT": qT, "kT": RowSplit(xk), "v": RowSplit(xv), "cneg_o": xc,
                      "ret_o": ret_o, "ret_qg": ret_qg, "ret_S": xs_, "rgs": rgs, "ytg": ytg}
                gcol = mixa_setup(c, din, io)
                mixa_body(c, xres, gcol, W["w_in"][l], io)
                p.barrier()
                p.emit()

        def exchange_phase():
            cc = p.sem("ccsem")
            for a_, b_ in ((xk[0], gk[0]), (xk[1], gk[1]), (xv[0], gv_[0]), (xv[1], gv_[1]), (xc, gc), (xs_, gs)):
                p.op("gpsimd", lambda e, a_=a_, b_=b_: e.collective_compute("AllGather", ALU.bypass, replica_groups=PAIRS,
                                                                            ins=[a_], outs=[b_]), inc=cc, k=1)
            p.barrier()
            p.emit()

        def mixb_phase(l):
            with contextlib.ExitStack() as st:
                io = {"qT": qT, "kT_loc": RowSplit(xk), "kT_prev": RowSplit([gk[0][0:512, :], gk[1][0:512, :]]),
                      "v_loc": RowSplit(xv), "v_prev": RowSplit([gv_[0][0:1024, :], gv_[1][0:1024, :]]),
                      "cneg_loc": xc, "cneg_prev": gc[0:8, :], "s_init": gs[0:128, :], "ret_o": ret_o,
                      "ret_qg": ret_qg, "rgs": rgs, "ytg": ytg, "cmask": S["cmask"], "selh": S["selh"],
                      "ident8": S["ident8"], "pmask": S["pmask"], "sflag": S["sflag"], "ona": S["ona"][l],
                      "selh72": S["selh72"], "ident": S["ident"],
                      "w_out": W["w_out"][l], "xin": xres, "yT": yT, "xout": xres}
                mixb_body(nc, st, p, io)
                p.barrier()
                p.emit()

        def norm_phase():
            with contextlib.ExitStack() as st:
                c = alloc_common(nc, st, p)
                gcol = c.sb("gcol", [128, KC], F32)
                p.dma("sync", gcol[:], S["g_fin"][:, :], inc=c.setup_d)
                for tt in range(NTT):
                    t0 = tt * TT

                    def out_fn(kc, s, waits, t0=t0):
                        d = p.op("vector", lambda e: e.scalar_tensor_tensor(out=c.xs[:, s, :], in0=c.xs[:, s, :],
                                                                            scalar=gcol[:, kc:kc + 1], in1=c.rstd[:],
                                                                            op0=ALU.mult, op1=ALU.mult), waits=waits,
                                 inc=c.dve_h)
                        sv = p.dma("sync", out[kc * 128:(kc + 1) * 128, t0:t0 + TT], c.xs[:, s, :], waits=[(c.dve_h, d)],
                                   inc=c.xs_st[s])
                        c.xs_cond[s] = [(c.xs_st[s], sv)]

                    norm_stats_and_h(c, xres, gcol, tt, out_fn=out_fn)
                finish(c)
                p.barrier()
                p.emit()

        for l in range(depth):
            if "f" in phases:
                ffn_phase(x_in if l == 0 else xres, xres, S["g_ffn1"][l], W["ffn1_w_gate"][l], W["ffn1_w_up"][l],
                          W["ffn1_w_down"][l])
            if "m" in phases:
                mixa_phase(l)
            if "x" in phases:
                exchange_phase()
            if "b" in phases:
                mixb_phase(l)
            if "F" in phases:
                ffn_phase(xres, xres, S["g_ffn2"][l], W["ffn2_w_gate"][l], W["ffn2_w_up"][l], W["ffn2_w_down"][l])
        norm_phase()
    return nc


def fused_inputs(P, core):
    half = core % 2
    x = P["x"]
    m = {"x": np.ascontiguousarray(x[core // 2, half * NTOK:(half + 1) * NTOK, :].T)}
    for k in WSHAPES:
        m[k] = P[k]
    return m


def fused_shared(P):
    sh = {}
    sh["g_ffn1"] = np.stack([col16(P["ffn1_norm"][l]) for l in range(DEPTH)])
    sh["g_mix"] = np.stack([col16(P["mix_norm"][l]) for l in range(DEPTH)])
    sh["g_ffn2"] = np.stack([col16(P["ffn2_norm"][l]) for l in range(DEPTH)])
    sh["g_fin"] = col16(P["final_norm"])
    sh["bf"] = np.ascontiguousarray(P["fox_b_f"].reshape(DEPTH, 8, 1)).astype(np.float32)
    sh["wst"] = np.ascontiguousarray(np.transpose(P["gmlp_w_s"], (0, 3, 1, 2)).reshape(DEPTH, 128, 512)).astype(np.float32)
    sh["lnb3"] = np.ascontiguousarray(np.stack([np.concatenate(
        [np.broadcast_to(P["gmlp_ln_g"][l][None, :], (128, 512)), np.broadcast_to(P["gmlp_ln_b"][l][None, :], (128, 512)),
         np.broadcast_to(P["gmlp_b_s"][l].reshape(1, 512), (128, 512))], axis=1) for l in range(DEPTH)])).astype(np.float32)
    sh["ong"] = np.ascontiguousarray(np.stack([P["out_norm"][l][1536:2048].reshape(4, 128).T for l in range(DEPTH)])).astype(np.float32)
    sh["ona"] = np.ascontiguousarray(np.stack([P["out_norm"][l][0:1536].reshape(12, 128).T for l in range(DEPTH)])).astype(np.float32)
    return sh


_FUSED = {}


def kernel(**inputs):
    P = {k: np.asarray(v) for k, v in inputs.items()}
    if "nc" not in _FUSED:
        _FUSED["nc"] = build_fused()
    sh = fused_shared(P)
    maps = []
    for c in range(NCORES):
        half = c % 2
        m = fused_inputs(P, c)
        m.update(sh)
        hc = host_consts(half)
        m.update({"tab": hc["tab"], "maskr": hc["maskr"], "triu": hc["triu"], "ident": hc["ident"]})
        m.update(mixb_consts(half))
        maps.append(m)
    res = run_bass_kernel_spmd(_FUSED["nc"], maps, core_ids=list(range(NCORES)))
    x = P["x"]
    outp = np.empty_like(x)
    for c in range(NCORES):
        outp[c // 2, (c % 2) * NTOK:(c % 2 + 1) * NTOK, :] = res.results[c]["out"].T
    return outp
```

```python
import contextlib
import numpy as np
import concourse.bass as bass
import concourse.mybir as mybir
from concourse.bass_utils import run_bass_kernel_spmd

F32 = mybir.dt.float32
BF16 = mybir.dt.bfloat16
AF = mybir.ActivationFunctionType
ALU = mybir.AluOpType

D = 2048
NTOK = 2048
DFF = 5632
NCORES = 8
DEPTH = 4
EPS = 1e-6
KC = D // 128
TT = 1024
NTT = NTOK // TT
FH = 22
INCOLS = 6152


class Cnt:
    def __init__(self, h):
        self.h = h
        self.v = 0


class Prog:
    ENGS = ("sync", "scalar", "vector", "gpsimd", "tensor")

    def __init__(self, nc, stack):
        self.nc = nc
        self.stack = stack
        self.q = {e: [] for e in self.ENGS}
        self.waited = {e: {} for e in self.ENGS}
        self.cache = {}

    def sem(self, name):
        if name not in self.cache:
            self.cache[name] = Cnt(self.stack.enter_context(self.nc.semaphore(name)))
        return self.cache[name]

    def barrier(self):
        for eng in self.ENGS:
            self.op(eng, None, waits=[(c, c.v) for c in self.cache.values()])

    def sems(self, name, n):
        return [self.sem("%s%d" % (name, i)) for i in range(n)]

    def op(self, eng, fn, waits=(), inc=None, k=1):
        ws = []
        for (c, v) in waits:
            if v <= 0:
                continue
            key = id(c)
            if self.waited[eng].get(key, 0) >= v:
                continue
            self.waited[eng][key] = v
            ws.append((c.h, v))
        tgt = None
        if inc is not None:
            inc.v += k
            tgt = inc.v
        self.q[eng].append((ws, fn, inc.h if inc is not None else None, k))
        return tgt

    def dma(self, eng, out, in_, waits=(), inc=None):
        return self.op(eng, lambda e: e.dma_start(out=out, in_=in_), waits, inc, 16)

    def wait_only(self, eng, waits):
        self.q[eng].append(([(c.h, v) for (c, v) in waits if v > 0], None, None, 0))

    def emit(self):
        with self.nc.Block() as block:
            for name in self.ENGS:
                q = self.q[name]

                def body(e, q=q):
                    for ws, fn, inc, k in q:
                        for (h, v) in ws:
                            e.wait_ge(h, v)
                        if fn is None:
                            continue
                        ins = fn(e)
                        if inc is not None:
                            ins.then_inc(inc, k)

                getattr(block, name)(body)
        self.q = {e: [] for e in self.ENGS}


class Ctx:
    pass


_UID = [0]


def _uid():
    _UID[0] += 1
    return _UID[0]


def alloc_common(nc, stack, p, tt=TT, nps=8, stat_bank=6):
    c = Ctx()
    uid = _uid()
    c.nc = nc
    c.p = p
    c.TT = tt
    c.NSEG = tt // 512
    c.stat_bank = stat_bank
    sb = lambda name, shape, dt: stack.enter_context(nc.sbuf_tensor("sb%d_%s" % (uid, name), shape, dt))
    c.sb = sb
    c.uid = uid
    c.stack = stack
    c.ones = sb("ones", [128, 128], F32)
    c.xs = sb("xs", [128, 3, tt], F32)
    c.sq = sb("sq", [128, 2, tt], F32)
    c.rstd = sb("rstd", [128, tt], F32)
    c.h = sb("h", [128, KC, tt], BF16)
    c.ps = stack.enter_context(nc.psum_tensor("ps%d" % uid, [128, nps, 512], F32))
    c.xs_full = p.sems("xsfull", 3)
    c.xs_st = p.sems("xsst", 3)
    c.xs_cond = [[], [], []]
    c.xs_n = 0
    c.act_sq = p.sem("actsq")
    c.pe_st = p.sem("pest")
    c.dve_m = p.sem("dvem")
    c.act_m = p.sem("actm")
    c.dve_h = p.sem("dveh")
    c.setup_v = p.sem("setupv")
    c.setup_d = p.sem("setupd")
    p.op("vector", lambda e: e.memset(c.ones[:], 1.0), inc=c.setup_v)
    c.sq_n = 0
    c.h_free = []
    c.ps_free_waits = []
    return c


def xs_acquire(c):
    s = c.xs_n % 3
    c.xs_n += 1
    return s, list(c.xs_cond[s])


def norm_stats_and_h(c, xsrc, gcol, tt, out_fn=None):
    p = c.p
    TT = c.TT
    t0 = tt * TT
    SB = c.stat_bank
    for kc in range(KC):
        s, w = xs_acquire(c)
        full = p.dma("sync", c.xs[:, s, :], xsrc[kc * 128:(kc + 1) * 128, t0:t0 + TT], waits=w, inc=c.xs_full[s])
        q = c.sq_n % 2
        c.sq_n += 1
        a = p.op("scalar",
                 lambda e, s=s, q=q: e.activation(out=c.sq[:, q, :], in_=c.xs[:, s, :], func=AF.Square),
                 waits=[(c.xs_full[s], full), (c.pe_st, c.pe_st.v - 1)], inc=c.act_sq)
        c.xs_cond[s] = [(c.act_sq, a)]
        extra = list(c.ps_free_waits) if kc == 0 else []
        for sg_ in range(c.NSEG):
            p.op("tensor",
                 lambda e, q=q, kc=kc, sg_=sg_: e.matmul(c.ps[:, SB + sg_, :], lhsT=c.ones[:],
                                                        rhs=c.sq[:, q, sg_ * 512:(sg_ + 1) * 512],
                                                        start=(kc == 0), stop=(kc == KC - 1)),
                 waits=([(c.act_sq, a), (c.setup_v, c.setup_v.v)] + extra) if sg_ == 0 else [],
                 inc=(c.pe_st if sg_ == c.NSEG - 1 else None))
    st_done = c.pe_st.v
    psv = c.ps[:, SB:SB + c.NSEG, :]
    rv = c.rstd[:].rearrange("p (a b) -> p a b", a=c.NSEG)
    d1 = p.op("vector",
              lambda e: e.tensor_scalar(out=rv, in0=psv, scalar1=1.0 / D, scalar2=EPS, op0=ALU.mult, op1=ALU.add),
              waits=[(c.pe_st, st_done), (c.dve_h, c.dve_h.v)], inc=c.dve_m)
    c.ps_free_waits = [(c.dve_m, d1)]
    a1 = p.op("scalar", lambda e: e.activation(out=c.rstd[:], in_=c.rstd[:], func=AF.Sqrt),
              waits=[(c.dve_m, d1)], inc=c.act_m)
    d2 = p.op("vector", lambda e: e.reciprocal(out=c.rstd[:], in_=c.rstd[:]),
              waits=[(c.act_m, a1)], inc=c.dve_m)
    for kc in range(KC):
        s, w = xs_acquire(c)
        full = p.dma("sync", c.xs[:, s, :], xsrc[kc * 128:(kc + 1) * 128, t0:t0 + TT], waits=w, inc=c.xs_full[s])
        if out_fn is None:
            waits = [(c.xs_full[s], full), (c.dve_m, d2), (c.setup_d, c.setup_d.v)]
            if kc == 0:
                waits += c.h_free
            hv = p.op("vector",
                 lambda e, s=s, kc=kc: e.scalar_tensor_tensor(out=c.h[:, kc, :], in0=c.xs[:, s, :],
                                                              scalar=gcol[:, kc:kc + 1], in1=c.rstd[:],
                                                              op0=ALU.mult, op1=ALU.mult),
                 waits=waits, inc=c.dve_h)
            c.xs_cond[s] = [(c.dve_h, hv)]
        else:
            out_fn(kc, s, [(c.xs_full[s], full), (c.dve_m, d2), (c.setup_d, c.setup_d.v)])
    return c.dve_h.v


def alloc_ffn(c):
    p = c.p
    sb = c.sb
    c.hid = sb("hid", [128, FH, TT], BF16)
    c.sg = sb("sg", [128, 2, 512], F32)
    c.wgu = sb("wgu", [128, 2, 2, KC, 256], BF16)
    c.wd = sb("wd", [128, 2, FH, 256], BF16)
    c.wgu_full = p.sems("wgufull", 2)
    c.wd_full = p.sems("wdfull", 2)
    c.pe_gu = p.sem("pegu")
    c.act_sg = p.sem("actsg")
    c.dve_hid = p.sem("dvehid")
    c.pe_dn = p.sem("pedn")
    c.dve_res = p.sem("dveres")
    c.n_panel = 0
    c.n_gu = c.pe_gu.v
    c.n_dpanel = 0
    c.n_dn = c.pe_dn.v
    c.panel_done = {}
    c.dpanel_done = {}
    c.hid_free = []


def ffn_body(c, xin, xout, gcol, wg, wu, wd):
    p = c.p
    wgv = wg.rearrange("(kc p) f -> p kc f", p=128)
    wuv = wu.rearrange("(kc p) f -> p kc f", p=128)
    wdv = wd.rearrange("(fc p) d -> p fc d", p=128)
    for tt in range(NTT):
        t0 = tt * TT
        h_ready = norm_stats_and_h(c, xin, gcol, tt)
        for hf in range(2):
            for pn in range(FH // 2):
                col0 = (hf * FH + pn * 2) * 128
                npn = c.n_panel
                b = npn % 2
                c.n_panel += 1
                wfree = [(c.pe_gu, c.panel_done[npn - 2])] if npn >= 2 else []
                p.dma("gpsimd", c.wgu[:, b, 0, :, :], wgv[:, :, col0:col0 + 256], waits=wfree, inc=c.wgu_full[b])
                wl = p.dma("gpsimd", c.wgu[:, b, 1, :, :], wuv[:, :, col0:col0 + 256], waits=wfree, inc=c.wgu_full[b])
                for jj in range(2):
                    j = pn * 2 + jj
                    for th in range(2):
                        n = c.n_gu
                        c.n_gu += 1
                        gb = n % 2
                        ub = 2 + n % 2
                        for kc in range(KC):
                            waits = []
                            if kc == 0:
                                waits = [(c.wgu_full[b], wl), (c.dve_h, h_ready), (c.act_sg, n - 1)]
                            p.op("tensor",
                                 lambda e, b=b, kc=kc, jj=jj, th=th, gb=gb: e.matmul(
                                     c.ps[:, gb, :], lhsT=c.wgu[:, b, 0, kc, jj * 128:(jj + 1) * 128],
                                     rhs=c.h[:, kc, th * 512:(th + 1) * 512], start=(kc == 0), stop=(kc == KC - 1)),
                                 waits=waits)
                        for kc in range(KC):
                            waits = []
                            if kc == 0:
                                waits = [(c.dve_hid, n - 1)]
                            last = (kc == KC - 1)
                            p.op("tensor",
                                 lambda e, b=b, kc=kc, jj=jj, th=th, ub=ub: e.matmul(
                                     c.ps[:, ub, :], lhsT=c.wgu[:, b, 1, kc, jj * 128:(jj + 1) * 128],
                                     rhs=c.h[:, kc, th * 512:(th + 1) * 512], start=(kc == 0), stop=(kc == KC - 1)),
                                 waits=waits, inc=(c.pe_gu if last else None))
                        gu = c.pe_gu.v
                        a = p.op("scalar",
                                 lambda e, n=n, gb=gb: e.activation(out=c.sg[:, n % 2, :], in_=c.ps[:, gb, :], func=AF.Silu),
                                 waits=[(c.pe_gu, gu), (c.dve_hid, n - 1)], inc=c.act_sg)
                        waits = [(c.act_sg, a), (c.pe_gu, gu)]
                        if j == 0 and th == 0:
                            waits += c.hid_free
                        p.op("vector",
                             lambda e, n=n, ub=ub, j=j, th=th: e.tensor_tensor(
                                 out=c.hid[:, j, th * 512:(th + 1) * 512], in0=c.sg[:, n % 2, :], in1=c.ps[:, ub, :],
                                 op=ALU.mult),
                             waits=waits, inc=c.dve_hid)
                c.panel_done[npn] = c.pe_gu.v
            if hf == 1:
                c.h_free = [(c.pe_gu, c.pe_gu.v)]
            hid_ready = c.dve_hid.v
            xsrc = xin if hf == 0 else xout
            for pd in range(8):
                col0 = pd * 256
                npd = c.n_dpanel
                b = npd % 2
                c.n_dpanel += 1
                wl = p.dma("gpsimd", c.wd[:, b, :, :], wdv[:, hf * FH:(hf + 1) * FH, col0:col0 + 256],
                           waits=([(c.pe_dn, c.dpanel_done[npd - 2])] if npd >= 2 else []), inc=c.wd_full[b])
                for ii in range(2):
                    i = pd * 2 + ii
                    s, w = xs_acquire(c)
                    full = p.dma("sync", c.xs[:, s, :], xsrc[i * 128:(i + 1) * 128, t0:t0 + TT], waits=w,
                                 inc=c.xs_full[s])
                    for th in range(2):
                        n = c.n_dn
                        c.n_dn += 1
                        ob = 4 + n % 2
                        for f in range(FH):
                            waits = []
                            if f == 0:
                                waits = [(c.wd_full[b], wl), (c.dve_hid, hid_ready), (c.dve_res, n - 1)]
                            last = (f == FH - 1)
                            p.op("tensor",
                                 lambda e, b=b, f=f, ii=ii, th=th, ob=ob: e.matmul(
                                     c.ps[:, ob, :], lhsT=c.wd[:, b, f, ii * 128:(ii + 1) * 128],
                                     rhs=c.hid[:, f, th * 512:(th + 1) * 512], start=(f == 0), stop=(f == FH - 1)),
                                 waits=waits, inc=(c.pe_dn if last else None))
                        dn = c.pe_dn.v
                        r = p.op("vector",
                                 lambda e, s=s, th=th, ob=ob: e.scalar_tensor_tensor(
                                     out=c.xs[:, s, th * 512:(th + 1) * 512], in0=c.ps[:, ob, :], scalar=0.5,
                                     in1=c.xs[:, s, th * 512:(th + 1) * 512], op0=ALU.mult, op1=ALU.add),
                                 waits=[(c.pe_dn, dn), (c.xs_full[s], full)], inc=c.dve_res)
                    sv = p.dma("sync", xout[i * 128:(i + 1) * 128, t0:t0 + TT], c.xs[:, s, :],
                               waits=[(c.dve_res, r)], inc=c.xs_st[s])
                    c.xs_cond[s] = [(c.xs_st[s], sv)]
                c.dpanel_done[npd] = c.pe_dn.v
            c.hid_free = [(c.pe_dn, c.pe_dn.v)]


def finish(c):
    p = c.p
    p.wait_only("sync", [(c.xs_st[s], c.xs_st[s].v) for s in range(3)])


def build_ffn():
    nc = bass.Bass("TRN2", target_bir_lowering=False)
    xin = nc.dram_tensor("xin", [D, NTOK], F32, kind="ExternalInput").ap()
    g = nc.dram_tensor("g", [128, KC], F32, kind="ExternalInput").ap()
    wg = nc.dram_tensor("wg", [D, DFF], F32, kind="ExternalInput").ap()
    wu = nc.dram_tensor("wu", [D, DFF], F32, kind="ExternalInput").ap()
    wd = nc.dram_tensor("wd", [DFF, D], F32, kind="ExternalInput").ap()
    xout = nc.dram_tensor("xout", [D, NTOK], F32, kind="ExternalOutput").ap()
    with contextlib.ExitStack() as stack:
        p = Prog(nc, stack)
        c = alloc_common(nc, stack, p)
        alloc_ffn(c)
        gcol = c.sb("gcol", [128, KC], F32)
        p.dma("sync", gcol[:], g[:, :], inc=c.setup_d)
        ffn_body(c, xin, xout, gcol, wg, wu, wd)
        finish(c)
        p.emit()
    return nc


FOX_SCALE = 128.0 ** -0.5
GAM = [1.0 - 2.0 ** -(5 + h) for h in range(4)]
GAM64 = [g ** 64 for g in GAM]
TA = 512
NBLK = TA // 128
AX = mybir.AxisListType


class RowSplit:
    def __init__(self, parts):
        self.parts = parts
        self.h = parts[0].shape[0]

    def __getitem__(self, key):
        rs, cs = key
        i = rs.start // self.h
        assert (rs.stop - 1) // self.h == i
        return self.parts[i][rs.start - i * self.h:rs.stop - i * self.h, cs]


def _parts(x):
    return x.parts if isinstance(x, RowSplit) else [x]


class Slots:
    def __init__(self, p, name, n):
        self.sem = p.sems(name, n)
        self.cond = [[] for _ in range(n)]
        self.i = 0
        self.n = n

    def acquire(self):
        s = self.i % self.n
        self.i += 1
        return s, list(self.cond[s])


def mixa_body(c, xin, gcol, w_in, io):
    p = c.p
    nc = c.nc
    sb = c.sb
    winv = w_in.rearrange("(kc p) f -> p kc f", p=128)
    tabv = io["tab"].rearrange("(b p) f -> p b f", p=128)
    wp = sb("wp", [128, 2, 8192], BF16)
    wps = Slots(p, "wps", 2)
    wpfm = lambda b: wp[:, b, 0:4096].rearrange("p (k c) -> p k c", c=256)
    wptm = lambda b: wp[:, b, :].rearrange("p (k c) -> p k c", c=512)
    wpfz = lambda b: wp[:, b, 0:128].rearrange("p (k c) -> p k c", c=8)
    tabt = sb("tabt", [128, 2, NBLK, 268], F32)
    tabs = Slots(p, "tabs", 2)
    stg16 = sb("stg16", [128, 4, 512], BF16)
    st16 = Slots(p, "st16", 4)
    stg32 = sb("stg32", [128, 2, 512], F32)
    st32 = Slots(p, "st32", 2)
    u = sb("u", [128, 4, TA], F32)
    rv = sb("rv", [128, NBLK, 512], BF16)
    kr = sb("kr", [128, NBLK, 512], BF16)
    kz = sb("kz", [128, NBLK, 512], BF16)
    qx = sb("qx", [128, NBLK, 512], BF16)
    qg = sb("qg", [128, NBLK, 512], BF16)
    vln = sb("vln", [128, NBLK, 512], BF16)
    rot = sb("rot", [128, 2, 2, 512], F32)
    rots = Slots(p, "rots", 2)
    st = sb("lnst", [128, 24], F32)
    fzt = sb("fzt", [8, 512], F32)
    onesr = sb("onesr", [8, 512], F32)
    cneg = sb("cneg", [8, NTOK], F32)
    S32 = sb("S32", [128, 512], F32)
    Sb = sb("Sb", [128, 2 * NBLK, 512], BF16)
    krT = sb("krT", [128, 512], BF16)
    qxT = sb("qxT", [128, 512], BF16)
    sm = sb("sm", [128, 512], BF16)
    y1 = sb("y1", [128, 512], F32)
    y2 = sb("y2", [128, 512], F32)
    pst = c.stack.enter_context(nc.psum_tensor("pst%d" % c.uid, [128, 2, 1024], BF16))
    c.pe = p.sem("pe")
    c.act = p.sem("act")
    c.dve = p.sem("dve")
    misc = p.sem("miscst")
    pj_cond = [[], []]
    kv_cond = [[], []]
    s32_tick = [[]]
    b4_cond = []
    pst_cond = [[], []]
    pj_n = [0]
    kv_n = [0]
    u_free = []
    ret_free = []
    vln_free = []
    p.op("vector", lambda e: e.memset(onesr[:], 1.0), inc=c.setup_v)
    p.op("vector", lambda e: e.memset(S32[:], 0.0), inc=c.setup_v)
    setupw = [(c.setup_d, c.setup_d.v), (c.setup_v, c.setup_v.v)]

    def V4(ap):
        return ap.rearrange("p (a b) -> p a b", a=4)

    def bc(ap4):
        return ap4.unsqueeze(2).to_broadcast([128, 4, 128])

    def bh(ap128):
        return ap128.unsqueeze(1).to_broadcast([128, 4, 128])

    def bh64(ap64):
        return ap64.unsqueeze(1).to_broadcast([128, 4, 64])

    def proj(b, waits, lhs_fn, rhs_fn, out_fn):
        n = pj_n[0]
        pj_n[0] += 1
        bank = n % 2
        for kc in range(KC):
            p.op("tensor",
                 lambda e, kc=kc: e.matmul(out_fn(bank), lhsT=lhs_fn(kc), rhs=rhs_fn(kc), start=(kc == 0),
                                           stop=(kc == KC - 1)),
                 waits=(list(waits) + pj_cond[bank]) if kc == 0 else [], inc=(c.pe if kc == KC - 1 else None))
        return bank, c.pe.v

    def store16(src_fn, dst_ap, waits, eng_op):
        s, w = st16.acquire()
        a = p.op("scalar", lambda e: eng_op(e, stg16[:, s, :]), waits=list(waits) + w, inc=c.act)
        sv = p.dma("sync", dst_ap, src_fn(stg16[:, s, :]), waits=[(c.act, a)], inc=st16.sem[s])
        st16.cond[s] = [(st16.sem[s], sv)]
        return a

    for tt in range(NTOK // TA):
        t0 = tt * TA
        h_ready = norm_stats_and_h(c, xin, gcol, tt)
        hw = [(c.dve_h, h_ready)]
        ts_, w = tabs.acquire()
        tk = p.dma("sync", tabt[:, ts_], tabv[:, tt * NBLK:(tt + 1) * NBLK, :], waits=w, inc=tabs.sem[ts_])
        tabw = [(tabs.sem[ts_], tk)]
        for name, cbase, ncol in (("fq", 0, 1024), ("fk", 1024, 1024), ("rg", 4616, 512), ("gu", 5128, 512)):
            for pn in range(ncol // 256):
                col0 = cbase + pn * 256
                b, w = wps.acquire()
                t = p.dma("gpsimd", wpfm(b), winv[:, :, col0:col0 + 256], waits=w, inc=wps.sem[b])
                for jj in range(2):
                    ch = pn * 2 + jj
                    bank, pt = proj(b, [(wps.sem[b], t)] + hw,
                                    lambda kc, b=b, jj=jj: wpfm(b)[:, kc, jj * 128:(jj + 1) * 128],
                                    lambda kc: c.h[:, kc, :], lambda bank: c.ps[:, bank, :])
                    pw = [(c.pe, pt)]
                    if name == "fq":
                        a = store16(lambda s_: s_, io["qT"][ch * 128:(ch + 1) * 128, t0:t0 + TA], pw,
                                    lambda e, o, bank=bank: e.mul(out=o, in_=c.ps[:, bank, :], mul=FOX_SCALE))
                    elif name == "fk":
                        a = store16(lambda s_: s_, io["kT"][ch * 128:(ch + 1) * 128, t0:t0 + TA], pw,
                                    lambda e, o, bank=bank: e.copy(out=o, in_=c.ps[:, bank, :]))
                    elif name == "rg":
                        s, w2 = st32.acquire()
                        a = p.op("scalar", lambda e, s=s, bank=bank: e.activation(out=stg32[:, s, :], in_=c.ps[:, bank, :],
                                                                                func=AF.Silu),
                                 waits=pw + w2, inc=c.act)
                        sv = p.dma("sync", io["rgs"][ch * 128:(ch + 1) * 128, t0:t0 + TA], stg32[:, s, :],
                                   waits=[(c.act, a)], inc=st32.sem[s])
                        st32.cond[s] = [(st32.sem[s], sv)]
                    else:
                        a = p.op("scalar", lambda e, ch=ch, bank=bank: e.activation(out=u[:, ch, :], in_=c.ps[:, bank, :],
                                                                                  func=AF.Gelu_apprx_tanh),
                                 waits=pw + (u_free if ch == 0 else []), inc=c.act)
                    pj_cond[bank] = [(c.act, a)]
                wps.cond[b] = [(c.pe, pt)]
        b, w = wps.acquire()
        t = p.dma("gpsimd", wpfz(b), winv[:, :, 3072:3080], waits=w, inc=wps.sem[b])
        bank, pt = proj(b, [(wps.sem[b], t)] + hw, lambda kc, b=b: wpfz(b)[:, kc, :], lambda kc: c.h[:, kc, :],
                        lambda bank: c.ps[0:8, bank, :])
        wps.cond[b] = [(c.pe, pt)]
        a = p.op("scalar", lambda e, bank=bank: e.activation(out=fzt[:], in_=c.ps[0:8, bank, :], func=AF.Exp,
                                                             bias=io["negb"][:, 0:1], scale=-1.0),
                 waits=[(c.pe, pt), (c.dve, c.dve.v)] + setupw, inc=c.act)
        pj_cond[bank] = [(c.act, a)]
        a = p.op("scalar", lambda e: e.activation(out=fzt[:], in_=fzt[:], func=AF.Ln, bias=1.0), waits=[(c.act, a)],
                 inc=c.act)
        init = 0.0 if tt == 0 else cneg[:, t0 - 1:t0]
        p.op("vector", lambda e, init=init, t0=t0: e.tensor_tensor_scan(out=cneg[:, t0:t0 + TA], data0=onesr[:], data1=fzt[:],
                                                                 initial=init, op0=ALU.mult, op1=ALU.add),
             waits=[(c.act, a), (c.dve, c.dve.v)] + setupw, inc=c.dve)
        ready = {}
        s32_last = [None]

        def emit_kv(n):
            tb, a_ = n // 2, n % 2
            kb = 2 + kv_n[0] % 2
            ci = kv_n[0] % 2
            kv_n[0] += 1
            for hh in range(4):
                sl = slice(hh * 128, (hh + 1) * 128)
                p.op("tensor", lambda e, kb=kb, sl=sl, a_=a_, tb=tb: e.matmul(
                    c.ps[:, kb, sl], lhsT=kz[a_ * 64:(a_ + 1) * 64, tb, sl], rhs=rv[a_ * 64:(a_ + 1) * 64, tb, sl],
                    start=True, stop=True),
                     waits=(ready[("kz", tb)] + ready[("rv", tb)] + kv_cond[ci]) if hh == 0 else [],
                     inc=(c.pe if hh == 3 else None))
            pt_ = c.pe.v
            a = p.op("scalar", lambda e, n=n: e.copy(out=Sb[:, n, :], in_=S32[:]),
                     waits=s32_tick[0] + (ret_free if n == 0 else []) + setupw, inc=c.act)
            ready[("Sb", n)] = [(c.act, a)]
            for hh in range(4):
                sl = slice(hh * 128, (hh + 1) * 128)
                d = p.op("vector", lambda e, kb=kb, sl=sl, hh=hh: e.scalar_tensor_tensor(
                    out=S32[:, sl], in0=S32[:, sl], scalar=GAM64[hh], in1=c.ps[:, kb, sl], op0=ALU.mult, op1=ALU.add),
                         waits=[(c.pe, pt_), (c.act, a)] if hh == 0 else [], inc=c.dve)
            kv_cond[ci] = [(c.dve, d)]
            s32_last[0] = [(c.dve, d)]
            s32_tick[0] = [(c.dve, d)]

        for name, col0 in (("fv0", 2048), ("fv1", 2560), ("rv", 4104), ("rk", 3592), ("rq", 3080), ("gv", 5640)):
            b, w = wps.acquire()
            t = p.dma("gpsimd", wptm(b), winv[:, :, col0:col0 + 512], waits=w, inc=wps.sem[b])
            for tb in range(NBLK):
                bank, pt = proj(b, [(wps.sem[b], t)] + hw,
                                lambda kc, tb=tb: c.h[:, kc, tb * 128:(tb + 1) * 128],
                                lambda kc, b=b: wptm(b)[:, kc, :], lambda bank: c.ps[:, bank, :])
                pw = [(c.pe, pt)]
                psb = c.ps[:, bank, :]
                psv = V4(psb)
                r0 = t0 + tb * 128
                if name in ("fv0", "fv1"):
                    hc = 0 if name == "fv0" else 512
                    a = store16(lambda s_: s_, io["v"][r0:r0 + 128, hc:hc + 512], pw,
                                lambda e, o, psb=psb: e.copy(out=o, in_=psb))
                    pj_cond[bank] = [(c.act, a)]
                elif name == "rv":
                    a = p.op("scalar", lambda e, tb=tb, psb=psb: e.copy(out=rv[:, tb, :], in_=psb),
                             waits=pw + (ret_free if tb == 0 else []), inc=c.act)
                    pj_cond[bank] = [(c.act, a)]
                    ready[("rv", tb)] = [(c.act, a)]
                elif name in ("rk", "rq"):
                    rs, w2 = rots.acquire()
                    r1 = rot[:, rs, 0, :]
                    r2 = rot[:, rs, 1, :]
                    cosb = bh(tabt[:, ts_, tb, 0:128])
                    sina = bh64(tabt[:, ts_, tb, 128:192])
                    sinb = bh64(tabt[:, ts_, tb, 192:256])
                    p.op("vector", lambda e, psv=psv, r1=r1, cosb=cosb: e.tensor_tensor(out=V4(r1), in0=psv, in1=cosb,
                                                                                      op=ALU.mult),
                         waits=pw + w2 + tabw, inc=c.dve)
                    p.op("vector", lambda e, psv=psv, r2=r2, sina=sina: e.tensor_tensor(
                        out=V4(r2)[:, :, 0:64], in0=psv[:, :, 64:128], in1=sina, op=ALU.mult), inc=c.dve)
                    d = p.op("vector", lambda e, psv=psv, r2=r2, sinb=sinb: e.tensor_tensor(
                        out=V4(r2)[:, :, 64:128], in0=psv[:, :, 0:64], in1=sinb, op=ALU.mult), inc=c.dve)
                    pj_cond[bank] = [(c.dve, d)]
                    d = p.op("vector", lambda e, r1=r1, r2=r2: e.tensor_tensor(out=r1, in0=r1, in1=r2, op=ALU.add),
                             waits=[(c.dve, d)], inc=c.dve)
                    fw = ret_free if tb == 0 else []
                    if name == "rk":
                        a = p.op("scalar", lambda e, tb=tb, r1=r1: e.copy(out=kr[:, tb, :], in_=r1),
                                 waits=[(c.dve, d)] + fw, inc=c.act)
                        zb = bc(tabt[:, ts_, tb, 264:268])
                        d2 = p.op("vector", lambda e, tb=tb, r1=r1, zb=zb: e.tensor_tensor(out=V4(kz[:, tb, :]), in0=V4(r1),
                                                                                         in1=zb, op=ALU.mult),
                                  waits=[(c.dve, d)] + fw, inc=c.dve)
                        rots.cond[rs] = [(c.act, a), (c.dve, d2)]
                        ready[("kr", tb)] = [(c.act, a)]
                        ready[("kz", tb)] = [(c.dve, d2)]
                    else:
                        xb = bc(tabt[:, ts_, tb, 256:260])
                        gb_ = bc(tabt[:, ts_, tb, 260:264])
                        p.op("vector", lambda e, tb=tb, r1=r1, xb=xb: e.tensor_tensor(out=V4(qx[:, tb, :]), in0=V4(r1),
                                                                                    in1=xb, op=ALU.mult),
                             waits=[(c.dve, d)] + fw, inc=c.dve)
                        d2 = p.op("vector", lambda e, tb=tb, r1=r1, gb_=gb_: e.tensor_tensor(out=V4(qg[:, tb, :]),
                                                                                           in0=V4(r1), in1=gb_,
                                                                                           op=ALU.mult), inc=c.dve)
                        rots.cond[rs] = [(c.dve, d2)]
                        ready[("q", tb)] = [(c.dve, d2)]
                        emit_kv(tb)
                else:
                    rs, w2 = rots.acquire()
                    r1 = rot[:, rs, 0, :]
                    r2 = rot[:, rs, 1, :]
                    a = p.op("scalar", lambda e, psb=psb, r1=r1: e.activation(out=r1, in_=psb, func=AF.Gelu_apprx_tanh),
                             waits=pw + w2, inc=c.act)
                    pj_cond[bank] = [(c.act, a)]
                    a2 = p.op("scalar", lambda e, r1=r1, r2=r2: e.activation(out=r2, in_=r1, func=AF.Square),
                              waits=[(c.act, a)], inc=c.act)
                    d = p.op("vector", lambda e, r1=r1: e.tensor_reduce(out=st[:, 0:4], in_=V4(r1), axis=AX.X, op=ALU.add),
                             waits=[(c.act, a), (c.dve, c.dve.v)], inc=c.dve)
                    d = p.op("vector", lambda e, r2=r2: e.tensor_reduce(out=st[:, 4:8], in_=V4(r2), axis=AX.X, op=ALU.add),
                             waits=[(c.act, a2)], inc=c.dve)
                    d = p.op("vector", lambda e: e.tensor_scalar(out=st[:, 8:12], in0=st[:, 0:4], scalar1=1.0 / 128,
                                                                 scalar2=None, op0=ALU.mult),
                             waits=[(c.dve, d)], inc=c.dve)
                    d = p.op("vector", lambda e: e.tensor_tensor(out=st[:, 12:16], in0=st[:, 8:12], in1=st[:, 8:12],
                                                                 op=ALU.mult), waits=[(c.dve, d)], inc=c.dve)
                    d = p.op("vector", lambda e: e.scalar_tensor_tensor(out=st[:, 16:20], in0=st[:, 4:8], scalar=1.0 / 128,
                                                                        in1=st[:, 12:16], op0=ALU.mult,
                                                                        op1=ALU.subtract), waits=[(c.dve, d)], inc=c.dve)
                    d = p.op("vector", lambda e: e.tensor_scalar(out=st[:, 16:20], in0=st[:, 16:20], scalar1=EPS,
                                                                 scalar2=None, op0=ALU.add), waits=[(c.dve, d)], inc=c.dve)
                    a3 = p.op("scalar", lambda e: e.activation(out=st[:, 16:20], in_=st[:, 16:20], func=AF.Sqrt),
                              waits=[(c.dve, d)], inc=c.act)
                    d = p.op("vector", lambda e: e.reciprocal(out=st[:, 20:24], in_=st[:, 16:20]), waits=[(c.act, a3)],
                             inc=c.dve)
                    d = p.op("vector", lambda e, r1=r1: e.tensor_tensor(out=V4(r1), in0=V4(r1), in1=bc(st[:, 8:12]),
                                                                      op=ALU.subtract), waits=[(c.dve, d)], inc=c.dve)
                    d = p.op("vector", lambda e, r1=r1: e.tensor_tensor(out=V4(r1), in0=V4(r1), in1=bc(st[:, 20:24]),
                                                                      op=ALU.mult), waits=[(c.dve, d)], inc=c.dve)
                    d = p.op("vector", lambda e, r1=r1: e.tensor_tensor(out=r1, in0=r1, in1=io["lnb3"][:, 0, :],
                                                                      op=ALU.mult), waits=[(c.dve, d)] + setupw,
                             inc=c.dve)
                    d = p.op("vector", lambda e, r1=r1, tb=tb: e.tensor_tensor(out=vln[:, tb, :], in0=r1,
                                                                             in1=io["lnb3"][:, 1, :], op=ALU.add),
                             waits=[(c.dve, d)] + (vln_free if tb == 0 else []), inc=c.dve)
                    rots.cond[rs] = [(c.dve, d)]
                    ready[("vln", tb)] = [(c.dve, d)]
                    emit_kv(NBLK + tb)
            wps.cond[b] = [(c.pe, pt)]
        s32_done = s32_last[0]
        for tb in range(NBLK):
            r0 = t0 + tb * 128
            for gg in range(4):
                sl = slice(gg * 128, (gg + 1) * 128)
                p.op("tensor", lambda e, sl=sl, tb=tb, gg=gg: e.matmul(c.ps[:, 4, sl], lhsT=vln[:, tb, sl],
                                                                     rhs=io["wsb"][:, gg, :], start=True, stop=True),
                     waits=(ready[("vln", tb)] + b4_cond + setupw) if gg == 0 else [], inc=(c.pe if gg == 3 else None))
            pt = c.pe.v
            d = p.op("vector", lambda e: e.tensor_tensor(out=y1[:], in0=c.ps[:, 4, :], in1=io["lnb3"][:, 2, :], op=ALU.add),
                     waits=[(c.pe, pt), (c.act, c.act.v), (c.dve, c.dve.v)], inc=c.dve)
            d = p.op("vector", lambda e, tb=tb: e.tensor_tensor(out=V4(y1[:]), in0=V4(y1[:]),
                                                               in1=u[:, :, tb * 128:(tb + 1) * 128], op=ALU.mult),
                     waits=[(c.dve, d)], inc=c.dve)
            a = p.op("scalar", lambda e: e.activation(out=y2[:], in_=y1[:], func=AF.Square), waits=[(c.dve, d)], inc=c.act)
            p.op("tensor", lambda e: e.matmul(c.ps[:, 4, :], lhsT=c.ones[:], rhs=y2[:], start=True, stop=True),
                 waits=[(c.act, a), (c.dve, d)], inc=c.pe)
            pt = c.pe.v
            d = p.op("vector", lambda e: e.tensor_scalar(out=y2[:], in0=c.ps[:, 4, :], scalar1=1.0 / 128, scalar2=EPS,
                                                         op0=ALU.mult, op1=ALU.add), waits=[(c.pe, pt)], inc=c.dve)
            b4_cond = [(c.dve, d)]
            a = p.op("scalar", lambda e: e.activation(out=y2[:], in_=y2[:], func=AF.Sqrt), waits=[(c.dve, d)], inc=c.act)
            d = p.op("vector", lambda e: e.reciprocal(out=y2[:], in_=y2[:]), waits=[(c.act, a)], inc=c.dve)
            d = p.op("vector", lambda e: e.tensor_tensor(out=y1[:], in0=y1[:], in1=y2[:], op=ALU.mult),
                     waits=[(c.dve, d)], inc=c.dve)
            s, w = st16.acquire()
            for gg in range(4):
                sl = slice(gg * 128, (gg + 1) * 128)
                a = p.op("vector", lambda e, s=s, sl=sl, gg=gg: e.tensor_scalar(out=stg16[:, s, sl], in0=y1[:, sl],
                                                                              scalar1=io["ong"][:, gg:gg + 1],
                                                                              scalar2=None, op0=ALU.mult),
                         waits=([(c.dve, d)] + w + setupw) if gg == 0 else [], inc=c.dve)
            sv = p.dma("sync", io["ytg"].rearrange("(g c) t -> c g t", c=128)[:, :, r0:r0 + 128], V4(stg16[:, s, :]),
                       waits=[(c.dve, a)], inc=st16.sem[s])
            st16.cond[s] = [(st16.sem[s], sv)]
        u_free = [(c.dve, c.dve.v)]
        vln_free = [(c.pe, c.pe.v)]
        for tb in range(NBLK):
            r0 = t0 + tb * 128
            for src, key, dstT, pb in ((kr, "kr", krT, 0), (qx, "q", qxT, 1), (qg, "q", None, 0)):
                for hh in range(4):
                    sl = slice(hh * 128, (hh + 1) * 128)
                    p.op("tensor", lambda e, src=src, sl=sl, tb=tb, pb=pb: e.transpose(out=pst[:, pb, sl], in_=src[:, tb, sl],
                                                                                     identity=io["identb"][:]),
                         waits=(ready[(key, tb)] + pst_cond[pb] + setupw) if hh == 0 else [],
                         inc=(c.pe if hh == 3 else None))
                pt = c.pe.v
                if dstT is not None:
                    d = p.op("vector", lambda e, dstT=dstT, pb=pb: e.tensor_copy(out=dstT[:], in_=pst[:, pb, 0:512]),
                             waits=[(c.pe, pt), (c.pe, c.pe.v)], inc=c.dve)
                    pst_cond[pb] = [(c.dve, d)]
                    ready[(id(dstT), tb)] = [(c.dve, d)]
                else:
                    s, w = st16.acquire()
                    a = p.op("scalar", lambda e, s=s, pb=pb: e.copy(out=stg16[:, s, :], in_=pst[:, pb, 0:512]),
                             waits=[(c.pe, pt)] + w, inc=c.act)
                    pst_cond[pb] = [(c.act, a)]
                    sv = p.dma("sync", io["ret_qg"].rearrange("(h d) t -> d h t", d=128)[:, :, r0:r0 + 128],
                               V4(stg16[:, s, :]), waits=[(c.act, a)], inc=st16.sem[s])
                    st16.cond[s] = [(st16.sem[s], sv)]
            for hh in range(4):
                sl = slice(hh * 128, (hh + 1) * 128)
                p.op("tensor", lambda e, sl=sl: e.matmul(c.ps[:, 4, sl], lhsT=krT[:, sl], rhs=qxT[:, sl], start=True,
                                                         stop=True),
                     waits=(ready[(id(krT), tb)] + ready[(id(qxT), tb)] + b4_cond) if hh == 0 else [],
                     inc=(c.pe if hh == 3 else None))
            pt = c.pe.v
            d = p.op("vector", lambda e: e.tensor_tensor(out=sm[:], in0=c.ps[:, 4, :], in1=io["maskr"][:], op=ALU.mult),
                     waits=[(c.pe, pt), (c.pe, c.pe.v)] + setupw, inc=c.dve)
            b4_cond = [(c.dve, d)]
            for hh in range(4):
                sl = slice(hh * 128, (hh + 1) * 128)
                p.op("tensor", lambda e, sl=sl, tb=tb: e.matmul(c.ps[:, 5, sl], lhsT=rv[:, tb, sl], rhs=sm[:, sl], start=True,
                                                               stop=False),
                     waits=([(c.dve, d)] + c.ps_free_waits + ready[("Sb", 2 * tb)] + ready[("Sb", 2 * tb + 1)])
                     if hh == 0 else [])
                for a_ in range(2):
                    cs = slice(hh * 128 + a_ * 64, hh * 128 + (a_ + 1) * 64)
                    p.op("tensor", lambda e, sl=sl, cs=cs, tb=tb, a_=a_: e.matmul(
                        c.ps[:, 5, cs], lhsT=Sb[:, 2 * tb + a_, sl], rhs=qxT[:, cs], start=False, stop=(a_ == 1)),
                         inc=(c.pe if (hh == 3 and a_ == 1) else None))
            pt = c.pe.v
            s, w = st32.acquire()
            a = p.op("scalar", lambda e, s=s: e.copy(out=stg32[:, s, :], in_=c.ps[:, 5, :]), waits=[(c.pe, pt)] + w,
                     inc=c.act)
            c.ps_free_waits = c.ps_free_waits + [(c.act, a)]
            sv = p.dma("sync", io["ret_o"].rearrange("(h e) t -> e h t", e=128)[:, :, r0:r0 + 128], V4(stg32[:, s, :]),
                       waits=[(c.act, a)], inc=st32.sem[s])
            st32.cond[s] = [(st32.sem[s], sv)]
        ret_free = [(c.pe, c.pe.v)]
    f1 = p.dma("sync", io["cneg_o"][:, :], cneg[:], waits=[(c.dve, c.dve.v)], inc=misc)
    f2 = p.dma("sync", io["ret_S"][:, :], S32[:], waits=s32_done, inc=misc)
    p.wait_only("sync", [(misc, f2)] + [(st16.sem[s], st16.sem[s].v) for s in range(4)] +
                [(st32.sem[s], st32.sem[s].v) for s in range(2)])


def mixa_setup(c, din, io):
    p = c.p
    sb = c.sb
    gcol = sb("gcol", [128, KC], F32)
    negb = sb("negb", [8, 1], F32)
    maskr = sb("maskr", [128, 512], F32)
    wst = sb("wst", [128, 512], F32)
    triu = sb("triu", [128, 128], F32)
    wsb = sb("wsb", [128, 4, 128], BF16)
    lnb3 = sb("lnb3", [128, 3, 512], F32)
    ong = sb("ong", [128, 4], F32)
    identb = sb("identb", [128, 128], BF16)
    io.update({"negb": negb, "maskr": maskr, "wsb": wsb, "lnb3": lnb3, "ong": ong, "identb": identb})
    for dst, src in ((gcol[:], din["g"]), (negb[:], din["bf"]), (maskr[:], din["maskr"]), (wst[:], din["wst"]),
                     (triu[:], din["triu"]), (lnb3[:].rearrange("p a b -> p (a b)"), din["lnb3"]),
                     (ong[:], din["ong"])):
        p.dma("sync", dst, src, inc=c.setup_d)
    p.dma("gpsimd", identb[:], din["ident"], inc=c.setup_d)
    dl = [(c.setup_d, c.setup_d.v)]
    p.op("vector", lambda e: e.tensor_scalar(out=negb[:], in0=negb[:], scalar1=-1.0, scalar2=None, op0=ALU.mult),
         waits=dl, inc=c.setup_v)
    p.op("vector", lambda e: e.tensor_tensor(out=wsb[:], in0=wst[:].rearrange("p (a b) -> p a b", a=4),
                                             in1=triu[:].unsqueeze(1).to_broadcast([128, 4, 128]), op=ALU.mult),
         inc=c.setup_v)
    return gcol


def build_mixa():
    nc = bass.Bass("TRN2", target_bir_lowering=False)
    di = lambda name, shape: nc.dram_tensor(name, shape, F32, kind="ExternalInput").ap()
    do = lambda name, shape, dt: nc.dram_tensor(name, shape, dt, kind="ExternalOutput").ap()
    xin = di("xin", [D, NTOK])
    w_in = di("w_in", [D, INCOLS])
    din = {"g": di("g", [128, KC])[:, :], "bf": di("bf", [8, 1])[:, :], "maskr": di("maskr", [128, 512])[:, :],
           "wst": di("wst", [128, 512])[:, :], "triu": di("triu", [128, 128])[:, :],
           "lnb3": di("lnb3", [128, 3 * 512])[:, :], "ong": di("ong", [128, 4])[:, :],
           "ident": di("ident", [128, 128])[:, :]}
    io = {
        "tab": di("tab", [NTOK, 268]),
        "qT": do("qT", [1024, NTOK], BF16), "kT": do("kT", [1024, NTOK], BF16), "v": do("v", [NTOK, 1024], BF16),
        "cneg_o": do("cneg", [8, NTOK], F32), "ret_o": do("ret_o", [512, NTOK], F32),
        "ret_qg": do("ret_qg", [512, NTOK], BF16), "ret_S": do("ret_S", [128, 512], F32),
        "rgs": do("rgs", [512, NTOK], F32), "ytg": do("ytg", [512, NTOK], BF16),
    }
    with contextlib.ExitStack() as stack:
        p = Prog(nc, stack)
        c = alloc_common(nc, stack, p, tt=TA, nps=6, stat_bank=5)
        c.stack = stack
        gcol = mixa_setup(c, din, io)
        mixa_body(c, xin, gcol, w_in, io)
        p.emit()
    return nc


def host_consts(half):
    t = np.arange(NTOK, dtype=np.float64)
    pos = (half * NTOK + np.arange(NTOK)).astype(np.float32)
    inv_freq = (np.float32(10000.0) ** (-np.arange(64, dtype=np.float32) / np.float32(64))).astype(np.float32)
    ang = (pos[:, None] * inv_freq[None, :]).astype(np.float32).astype(np.float64)
    cos, sin = np.cos(ang), np.sin(ang)
    gam = np.array(GAM, dtype=np.float64)
    cidx = (np.arange(NTOK) % 64).astype(np.float64)
    xi = gam[None, :] ** (cidx[:, None] + 1.0)
    gm = gam[None, :] ** (t[:, None] + 1.0)
    zeta = gam[None, :] ** (63.0 - cidx[:, None]) * (128.0 ** -0.5)
    tab = np.concatenate([cos, cos, -sin, sin, xi, gm, zeta], axis=1).astype(np.float32)
    s = np.arange(128)
    cc = np.arange(128)
    same = (s[:, None] // 64 == cc[None, :] // 64) & (cc[None, :] >= s[:, None])
    maskr = np.zeros((128, 4, 128), np.float64)
    for h in range(4):
        maskr[:, h, :] = np.where(same, gam[h] ** (-(s[:, None] % 64 + 1.0)), 0.0) * (128.0 ** -0.5)
    triu = (s[:, None] <= cc[None, :]).astype(np.float32)
    return {"tab": np.ascontiguousarray(tab), "maskr": maskr.reshape(128, 512).astype(np.float32), "triu": triu,
            "ident": np.eye(128, dtype=np.float32)}


def col16(vec):
    return np.ascontiguousarray(np.asarray(vec, np.float32).reshape(KC, 128).T)


def mixa_inputs(xT, half, l, P):
    hc = host_consts(half)
    wst = np.ascontiguousarray(np.transpose(P["gmlp_w_s"][l], (2, 0, 1)).reshape(128, 512))
    lnb3 = np.concatenate([np.broadcast_to(P["gmlp_ln_g"][l][None, :], (128, 512)),
                           np.broadcast_to(P["gmlp_ln_b"][l][None, :], (128, 512)),
                           np.broadcast_to(P["gmlp_b_s"][l].reshape(1, 512), (128, 512))], axis=1)
    ong = np.ascontiguousarray(P["out_norm"][l][1536:2048].reshape(4, 128).T)
    return {"xin": xT, "g": col16(P["mix_norm"][l]), "w_in": P["w_in"][l],
            "bf": np.ascontiguousarray(P["fox_b_f"][l].reshape(8, 1)), "tab": hc["tab"], "maskr": hc["maskr"],
            "wst": wst.astype(np.float32), "triu": hc["triu"], "lnb3": np.ascontiguousarray(lnb3, dtype=np.float32),
            "ong": ong.astype(np.float32), "ident": hc["ident"]}


QT = 512
NQT = NTOK // QT
MASKNEG = -30000.0


def mixb_body(nc, stack, p, io):
    uid = _uid()
    sb = lambda name, shape, dt: stack.enter_context(nc.sbuf_tensor("sb%d_%s" % (uid, name), shape, dt))
    ps = stack.enter_context(nc.psum_tensor("psb%d" % uid, [128, 8, 512], F32))
    onesf = sb("onesf", [128, 128], F32)
    onesb = sb("onesb", [128, 128], BF16)
    cmask = sb("cmask", [128, 896], F32)
    selh = sb("selh", [8, 8, 128], F32)
    ident8 = sb("ident8", [8, 8], F32)
    pmask = sb("pmask", [128, 1], F32)
    sflag = sb("sflag", [128, 1], F32)
    ona = sb("ona", [128, 12], F32)
    sinit = sb("sinit", [128, 512], F32)
    sinb = sb("sinb", [128, 512], BF16)
    cn = sb("cn", [8, 2 * NTOK], F32)
    ncl = sb("ncl", [8, NTOK], F32)
    ncp = sb("ncp", [8, NTOK], F32)
    biasT = sb("biasT", [128, 32, 8], F32)
    setup_d = p.sem("bsetupd")
    dve = p.sem("bdve")
    act = p.sem("bact")
    pe = p.sem("bpe")
    for dst, src in ((cmask[:], io["cmask"][:, :]), (selh[:].rearrange("k h m -> k (h m)"), io["selh"][:, :]),
                     (ident8[:], io["ident8"][:, :]), (pmask[:], io["pmask"][:, :]), (sflag[:], io["sflag"][:, :]),
                     (ona[:], io["ona"][:, :]), (sinit[:], io["s_init"][:, :]), (cn[:, 0:NTOK], io["cneg_prev"][:, :]),
                     (cn[:, NTOK:2 * NTOK], io["cneg_loc"][:, :])):
        p.dma("sync", dst, src, inc=setup_d)
    sd = [(setup_d, setup_d.v)]
    p.op("vector", lambda e: e.memset(onesf[:], 1.0), inc=dve)
    p.op("vector", lambda e: e.memset(onesb[:], 1.0), inc=dve)
    p.op("vector", lambda e: e.tensor_scalar(out=sinb[:], in0=sinit[:], scalar1=sflag[:, 0:1], scalar2=None, op0=ALU.mult),
         waits=sd, inc=dve)
    p.op("vector", lambda e: e.tensor_scalar(out=ncl[:], in0=cn[:, NTOK:2 * NTOK], scalar1=-1.0, scalar2=None, op0=ALU.mult),
         inc=dve)
    d = p.op("vector", lambda e: e.tensor_scalar(out=ncp[:], in0=ncl[:], scalar1=cn[:, NTOK - 1:NTOK], scalar2=None,
                                                 op0=ALU.subtract), waits=[(dve, dve.v)], inc=dve)
    for blk in range(32):
        p.op("tensor", lambda e, blk=blk: e.transpose(out=ps[:, 6, blk * 8:(blk + 1) * 8],
                                                      in_=cn[0:8, blk * 128:(blk + 1) * 128], identity=ident8[:]),
             waits=sd if blk == 0 else [], inc=(pe if blk == 31 else None))
    p.op("vector", lambda e: e.tensor_scalar(out=biasT[:, 0:16, :].rearrange("p a b -> p (a b)"), in0=ps[:, 6, 0:128],
                                             scalar1=pmask[:, 0:1], scalar2=None, op0=ALU.add),
         waits=[(pe, pe.v)] + sd, inc=dve)
    d = p.op("vector", lambda e: e.tensor_copy(out=biasT[:, 16:32, :].rearrange("p a b -> p (a b)"), in_=ps[:, 6, 128:256]),
             inc=dve)
    b6_cond = [(dve, d)]
    selb = sb("selb", [128, 8, 128], BF16)
    maskb = sb("maskb", [128, 896], BF16)
    identb = sb("identb", [128, 128], BF16)
    setup_d2 = p.sem("bsetupd2")
    p.dma("gpsimd", selb[:].rearrange("k h m -> k (h m)"), io["selh72"][:, :], inc=setup_d2)
    p.dma("gpsimd", maskb[:], io["cmask"][:, :], inc=setup_d2)
    p.dma("gpsimd", identb[:], io["ident"][:, :], inc=setup_d2)
    sd = sd + [(setup_d2, setup_d2.v)]
    cst = sb("cst", [128, 2, NTOK], BF16)
    csr = sb("csr", [8, NTOK], F32)
    csm = sb("csm", [8, NTOK], BF16)
    d = p.op("vector", lambda e: e.memset(cst[:], 0.0), inc=dve)
    for vi, srcc in ((0, ncp), (1, ncl)):
        d = p.op("vector", lambda e, vi=vi, srcc=srcc: e.tensor_copy(out=cst[0:8, vi, :], in_=srcc[:]), waits=[(dve, d)], inc=dve)
        d = p.op("vector", lambda e, vi=vi, srcc=srcc: e.tensor_tensor(out=csr[:], in0=srcc[:], in1=cst[0:8, vi, :],
                                                                     op=ALU.subtract), waits=[(dve, d)], inc=dve)
        d = p.op("vector", lambda e: e.tensor_copy(out=csm[:], in_=csr[:]), waits=[(dve, d)], inc=dve)
        d = p.op("vector", lambda e, vi=vi: e.tensor_copy(out=cst[32:40, vi, :], in_=csm[:]), waits=[(dve, d)], inc=dve)
        d = p.op("vector", lambda e: e.tensor_tensor(out=csr[:], in0=csr[:], in1=csm[:], op=ALU.subtract),
                 waits=[(dve, d)], inc=dve)
        d = p.op("vector", lambda e, vi=vi: e.tensor_copy(out=cst[64:72, vi, :], in_=csr[:]), waits=[(dve, d)], inc=dve)
    setup_done = [(dve, d)] + sd

    qh = sb("qh", [128, 2, NTOK], BF16)
    kh = sb("kh", [128, 2, 2 * NTOK], BF16)
    vh = sb("vh", [128, 2, 32, 128], BF16)
    hs = Slots(p, "hs", 2)
    cb = sb("cb", [128, 2, 2, QT], F32)
    cbs = Slots(p, "cbs", 2)
    tmp = sb("tmp", [128, 4, QT], F32)
    tmps = Slots(p, "tmps", 4)
    pt = sb("pt", [128, 4, QT], BF16)
    pts = Slots(p, "pts", 4)
    ya = sb("ya", [128, 2, QT], F32)
    yas = Slots(p, "yas", 2)
    yq = sb("yq", [128, QT], F32)
    stg = sb("stg", [128, 2, QT], BF16)
    stgs = Slots(p, "stgs", 2)
    ro = sb("ro", [128, 2, QT], F32)
    rg = sb("rg", [128, 2, QT], F32)
    qgt = sb("qgt", [128, 2, QT], BF16)
    rls = Slots(p, "rls", 2)
    S_BANKS = (0, 1, 7)
    LOOK = 2
    s_cond = {0: [], 1: [], 7: []}
    acc_cond = [[], []]
    s_n = [0]
    acc_n = [0]
    y_stores = []

    def headnorm_store(yslot, yready, gain_col, dst_ap, mul_tile=None, mul_wait=()):
        nonlocal b6_cond
        a = p.op("scalar", lambda e: e.activation(out=yq[:], in_=ya[:, yslot, :], func=AF.Square),
                 waits=list(yready) + [(dve, dve.v)], inc=act)
        p.op("tensor", lambda e: e.matmul(ps[:, 6, :], lhsT=onesf[:], rhs=yq[:], start=True, stop=True),
             waits=[(act, a)] + b6_cond, inc=pe)
        d = p.op("vector", lambda e: e.tensor_scalar(out=yq[:], in0=ps[:, 6, :], scalar1=1.0 / 128, scalar2=EPS,
                                                     op0=ALU.mult, op1=ALU.add), waits=[(pe, pe.v)], inc=dve)
        b6_cond = [(dve, d)]
        a = p.op("scalar", lambda e: e.activation(out=yq[:], in_=yq[:], func=AF.Sqrt), waits=[(dve, d)], inc=act)
        d = p.op("vector", lambda e: e.reciprocal(out=yq[:], in_=yq[:]), waits=[(act, a)], inc=dve)
        s, w = stgs.acquire()
        if mul_tile is None:
            d = p.op("vector", lambda e: e.scalar_tensor_tensor(out=stg[:, s, :], in0=ya[:, yslot, :], scalar=gain_col,
                                                                in1=yq[:], op0=ALU.mult, op1=ALU.mult),
                     waits=[(dve, d)] + w, inc=dve)
        else:
            d = p.op("vector", lambda e: e.scalar_tensor_tensor(out=ya[:, yslot, :], in0=ya[:, yslot, :], scalar=gain_col,
                                                                in1=yq[:], op0=ALU.mult, op1=ALU.mult),
                     waits=[(dve, d)], inc=dve)
            d = p.op("vector", lambda e: e.tensor_tensor(out=stg[:, s, :], in0=ya[:, yslot, :], in1=mul_tile, op=ALU.mult),
                     waits=[(dve, d)] + w + list(mul_wait), inc=dve)
        sv = p.dma("sync", dst_ap, stg[:, s, :], waits=[(dve, d)], inc=stgs.sem[s])
        stgs.cond[s] = [(stgs.sem[s], sv)]
        y_stores.append((stgs.sem[s], sv))
        return [(dve, d)]

    pending = [None]

    def run_pending():
        if pending[0] is not None:
            ys_, rdy_, gain_, dst_, mt_, rs_ = pending[0]
            pending[0] = None
            yw_ = headnorm_store(ys_, rdy_, gain_, dst_, mul_tile=mt_)
            yas.cond[ys_] = yw_
            if rs_ is not None:
                rls.cond[rs_] = yw_

    vparts = []
    blk0 = 0
    for key in ("v_prev", "v_loc"):
        for part in _parts(io[key]):
            nb = part.shape[0] // 128
            vparts.append((blk0, nb, part.rearrange("(b p) c -> p b c", p=128)))
            blk0 += nb
    assert blk0 == 32
    for h in range(8):
        hsl, w = hs.acquire()
        rows = slice(h * 128, (h + 1) * 128)
        p.dma("sync", qh[:, hsl, :], io["qT"][rows, :], waits=w, inc=hs.sem[hsl])
        p.dma("sync", kh[:, hsl, 0:NTOK], io["kT_prev"][rows, :], inc=hs.sem[hsl])
        p.dma("sync", kh[:, hsl, NTOK:2 * NTOK], io["kT_loc"][rows, :], inc=hs.sem[hsl])
        for (b0_, nb_, vw_) in vparts:
            hl = p.dma("sync", vh[:, hsl, b0_:b0_ + nb_, :], vw_[:, :, rows], inc=hs.sem[hsl])
        hw = [(hs.sem[hsl], hl)]
        for qt in range(NQT):
            qs = slice(qt * QT, (qt + 1) * QT)
            blocks = [(j, 0, None) for j in range(16)] + [(16 + j, 1, (j - 4 * qt) if j >= 4 * qt else None)
                                                          for j in range(4 * qt + 4)]
            an = acc_n[0]
            acc_n[0] += 1
            ob, db = 2 + an % 2, 4 + an % 2
            pend = None
            nblk = len(blocks)

            def emit_pv(pend, first, last):
                j, slot, pw_ = pend
                p.op("tensor", lambda e, j=j, slot=slot, ob=ob, hsl=hsl: e.matmul(ps[:, ob, :], lhsT=vh[:, hsl, j, :],
                                                                                 rhs=pt[:, slot, :], start=first, stop=last),
                     waits=pw_ + (acc_cond[an % 2] if first else []))
                p.op("tensor", lambda e, slot=slot, db=db: e.matmul(ps[:, db, :], lhsT=onesb[:], rhs=pt[:, slot, :],
                                                                   start=first, stop=last), inc=pe)
                pts.cond[slot] = [(pe, pe.v)]

            npv = 0
            queue = []
            for bi, (j, vi, dg) in enumerate(blocks):
                if bi == 6:
                    run_pending()
                sn = s_n[0]
                s_n[0] += 1
                sbk = S_BANKS[sn % 3]
                p.op("tensor", lambda e, j=j, sbk=sbk, qs=qs, hsl=hsl: e.matmul(ps[:, sbk, :], lhsT=kh[:, hsl, j * 128:(j + 1) * 128],
                                                                      rhs=qh[:, hsl, qs], start=True, stop=False),
                     waits=hw + s_cond[sbk] + setup_done)
                p.op("tensor", lambda e, sbk=sbk, qs=qs, h=h, vi=vi, dg=dg: e.matmul(ps[:, sbk, :], lhsT=selb[:, h, :],
                                                                                   rhs=cst[:, vi, qs], start=False,
                                                                                   stop=(dg is None)),
                     inc=(pe if dg is None else None))
                if dg is not None:
                    off = 384 - dg * 128
                    p.op("tensor", lambda e, sbk=sbk, off=off: e.matmul(ps[:, sbk, :], lhsT=identb[:], rhs=maskb[:, off:off + QT],
                                                                       start=False, stop=True), inc=pe)
                st_ = pe.v
                if len(queue) >= LOOK:
                    emit_pv(queue.pop(0), npv == 0, False)
                    npv += 1
                slot, w = pts.acquire()
                a = p.op("scalar", lambda e, sbk=sbk, slot=slot, j=j, h=h: e.activation(
                    out=pt[:, slot, :], in_=ps[:, sbk, :], func=AF.Exp, bias=biasT[:, j, h:h + 1], scale=1.0),
                         waits=[(pe, st_)] + w + setup_done, inc=act)
                s_cond[sbk] = [(act, a)]
                queue.append((j, slot, [(act, a)]))
            while queue:
                pend = queue.pop(0)
                emit_pv(pend, npv == 0, len(queue) == 0)
                npv += 1
            acc_done = pe.v
            ys, w = yas.acquire()
            d = p.op("vector", lambda e, db=db: e.reciprocal(out=yq[:], in_=ps[:, db, :]), waits=[(pe, acc_done), (dve, dve.v),
                                                                                          (act, act.v)], inc=dve)
            d = p.op("vector", lambda e, ys=ys, ob=ob: e.tensor_tensor(out=ya[:, ys, :], in0=ps[:, ob, :], in1=yq[:], op=ALU.mult),
                     waits=[(dve, d)] + w, inc=dve)
            acc_cond[an % 2] = [(dve, d)]
            run_pending()
            pending[0] = (ys, [(dve, d)], ona[:, h:h + 1], io["yT"][rows, qs], None, None)
        hs.cond[hsl] = [(pe, pe.v)]
    for hh in range(4):
        rows = slice(hh * 128, (hh + 1) * 128)
        for qt in range(NQT):
            qs = slice(qt * QT, (qt + 1) * QT)
            rs, w = rls.acquire()
            p.dma("sync", ro[:, rs, :], io["ret_o"][rows, qs], waits=w, inc=rls.sem[rs])
            p.dma("sync", rg[:, rs, :], io["rgs"][rows, qs], inc=rls.sem[rs])
            rl = p.dma("sync", qgt[:, rs, :], io["ret_qg"][rows, qs], inc=rls.sem[rs])
            p.op("tensor", lambda e, rs=rs, rows=rows: e.matmul(ps[:, 6, :], lhsT=sinb[:, rows], rhs=qgt[:, rs, :], start=True,
                                                               stop=True),
                 waits=[(rls.sem[rs], rl)] + b6_cond + setup_done, inc=pe)
            ys, w = yas.acquire()
            d = p.op("vector", lambda e, ys=ys, rs=rs: e.tensor_tensor(out=ya[:, ys, :], in0=ps[:, 6, :], in1=ro[:, rs, :],
                                                                      op=ALU.add), waits=[(pe, pe.v)] + w, inc=dve)
            b6_cond = [(dve, d)]
            run_pending()
            pending[0] = (ys, [(dve, d)], ona[:, 8 + hh:9 + hh],
                          io["yT"][1024 + hh * 128:1024 + (hh + 1) * 128, qs], rg[:, rs, :], rs)
    run_pending()
    hb = sb("hb", [128, KC, TT], BF16)
    wo = sb("wo", [128, 2, KC, 256], BF16)
    wos = Slots(p, "wos", 2)
    xs = sb("xsb", [128, 3, TT], F32)
    xss = Slots(p, "xss", 3)
    hbl = p.sem("hbl")
    wov = io["w_out"].rearrange("(kc p) f -> p kc f", p=128)
    ytv = io["yT"].rearrange("(kc p) t -> p kc t", p=128)
    ygv = io["ytg"].rearrange("(kc p) t -> p kc t", p=128)
    hb_free = []
    on = 0
    ob_cond = [[], []]
    for tt in range(NTT):
        t0 = tt * TT
        p.dma("sync", hb[:, 0:12, :], ytv[:, :, t0:t0 + TT], waits=list(y_stores) + hb_free, inc=hbl)
        hl = p.dma("sync", hb[:, 12:16, :], ygv[:, :, t0:t0 + TT], inc=hbl)
        for pd in range(8):
            col0 = pd * 256
            b, w = wos.acquire()
            wl = p.dma("gpsimd", wo[:, b, :, :], wov[:, :, col0:col0 + 256], waits=w, inc=wos.sem[b])
            for ii in range(2):
                i = pd * 2 + ii
                s, w = xss.acquire()
                full = p.dma("sync", xs[:, s, :], io["xin"][i * 128:(i + 1) * 128, t0:t0 + TT], waits=w, inc=xss.sem[s])
                for th in range(2):
                    obk = on % 2
                    on += 1
                    for kc in range(KC):
                        p.op("tensor", lambda e, b=b, kc=kc, ii=ii, th=th, obk=obk: e.matmul(
                            ps[:, obk, :], lhsT=wo[:, b, kc, ii * 128:(ii + 1) * 128], rhs=hb[:, kc, th * 512:(th + 1) * 512],
                            start=(kc == 0), stop=(kc == KC - 1)),
                             waits=([(wos.sem[b], wl), (hbl, hl)] + ob_cond[obk]) if kc == 0 else [],
                             inc=(pe if kc == KC - 1 else None))
                    r = p.op("vector", lambda e, s=s, th=th, obk=obk: e.tensor_tensor(
                        out=xs[:, s, th * 512:(th + 1) * 512], in0=ps[:, obk, :], in1=xs[:, s, th * 512:(th + 1) * 512],
                        op=ALU.add), waits=[(pe, pe.v), (xss.sem[s], full)], inc=dve)
                    ob_cond[obk] = [(dve, r)]
                sv = p.dma("sync", io["xout"][i * 128:(i + 1) * 128, t0:t0 + TT], xs[:, s, :], waits=[(dve, r)],
                           inc=xss.sem[s])
                xss.cond[s] = [(xss.sem[s], sv)]
            wos.cond[b] = [(pe, pe.v)]
        hb_free = [(pe, pe.v)]
    p.wait_only("sync", [(xss.sem[s], xss.sem[s].v) for s in range(3)])


def build_mixb():
    nc = bass.Bass("TRN2", target_bir_lowering=False)
    di = lambda name, shape, dt=F32: nc.dram_tensor(name, shape, dt, kind="ExternalInput").ap()
    io = {
        "qT": di("qT", [1024, NTOK], BF16), "kT_loc": di("kT_loc", [1024, NTOK], BF16),
        "kT_prev": di("kT_prev", [1024, NTOK], BF16), "v_loc": di("v_loc", [NTOK, 1024], BF16),
        "v_prev": di("v_prev", [NTOK, 1024], BF16), "cneg_loc": di("cneg_loc", [8, NTOK]),
        "cneg_prev": di("cneg_prev", [8, NTOK]), "s_init": di("s_init", [128, 512]),
        "ret_o": di("ret_o", [512, NTOK]), "ret_qg": di("ret_qg", [512, NTOK], BF16), "rgs": di("rgs", [512, NTOK]),
        "ytg": di("ytg", [512, NTOK], BF16), "cmask": di("cmask", [128, 896]), "selh": di("selh", [8, 1024]),
        "ident8": di("ident8", [8, 8]), "pmask": di("pmask", [128, 1]), "sflag": di("sflag", [128, 1]),
        "selh72": di("selh72", [128, 1024]), "ident": di("ident", [128, 128]),
        "ona": di("ona", [128, 12]), "w_out": di("w_out", [D, D]), "xin": di("xin", [D, NTOK]),
        "yT": nc.dram_tensor("yT", [1536, NTOK], BF16, kind="Internal").ap(),
        "xout": nc.dram_tensor("xout", [D, NTOK], F32, kind="ExternalOutput").ap(),
    }
    with contextlib.ExitStack() as stack:
        p = Prog(nc, stack)
        mixb_body(nc, stack, p, io)
        p.emit()
    return nc


def mixb_consts(half):
    s = np.arange(128)[:, None]
    u = np.arange(896)[None, :]
    cmask = np.where((u - 384) >= s, 0.0, MASKNEG).astype(np.float32)
    selh = np.zeros((8, 8, 128), np.float32)
    for h in range(8):
        selh[h, h, :] = 1.0
    selh72 = np.zeros((128, 8, 128), np.float32)
    for h in range(8):
        selh72[h, h, :] = 1.0
        selh72[32 + h, h, :] = 1.0
        selh72[64 + h, h, :] = 1.0
    return {"cmask": cmask, "selh": selh.reshape(8, 1024), "ident8": np.eye(8, dtype=np.float32),
            "selh72": selh72.reshape(128, 1024), "ident": np.eye(128, dtype=np.float32),
            "pmask": np.full((128, 1), 0.0 if half == 1 else MASKNEG, np.float32),
            "sflag": np.full((128, 1), 1.0 if half == 1 else 0.0, np.float32)}


def build_norm():
    nc = bass.Bass("TRN2", target_bir_lowering=False)
    xin = nc.dram_tensor("xin", [D, NTOK], F32, kind="ExternalInput").ap()
    g = nc.dram_tensor("g", [128, KC], F32, kind="ExternalInput").ap()
    xout = nc.dram_tensor("xout", [D, NTOK], F32, kind="ExternalOutput").ap()
    with contextlib.ExitStack() as stack:
        p = Prog(nc, stack)
        c = alloc_common(nc, stack, p)
        gcol = c.sb("gcol", [128, KC], F32)
        p.dma("sync", gcol[:], g[:, :], inc=c.setup_d)
        for tt in range(NTT):
            t0 = tt * TT

            def out_fn(kc, s, waits, t0=t0):
                d = p.op("vector", lambda e: e.scalar_tensor_tensor(out=c.xs[:, s, :], in0=c.xs[:, s, :],
                                                                    scalar=gcol[:, kc:kc + 1], in1=c.rstd[:],
                                                                    op0=ALU.mult, op1=ALU.mult), waits=waits, inc=c.dve_h)
                sv = p.dma("sync", xout[kc * 128:(kc + 1) * 128, t0:t0 + TT], c.xs[:, s, :], waits=[(c.dve_h, d)],
                           inc=c.xs_st[s])
                c.xs_cond[s] = [(c.xs_st[s], sv)]

            norm_stats_and_h(c, xin, gcol, tt, out_fn=out_fn)
        finish(c)
        p.emit()
    return nc


_PROGS = {}


def _prog(name):
    if name not in _PROGS:
        _PROGS[name] = {"ffn": build_ffn, "mixa": build_mixa, "mixb": build_mixb, "norm": build_norm}[name]()
    return _PROGS[name]


def _run(name, in_maps):
    res = run_bass_kernel_spmd(_prog(name), in_maps, core_ids=list(range(NCORES)))
    return res.results


def run_ffn(xTs, l, P, pre):
    g = col16(P[pre + "_norm"][l])
    maps = [{"xin": xTs[c], "g": g, "wg": P[pre + "_w_gate"][l], "wu": P[pre + "_w_up"][l], "wd": P[pre + "_w_down"][l]}
            for c in range(NCORES)]
    return [r["xout"] for r in _run("ffn", maps)]


def run_mixer(xTs, l, P):
    ra = _run("mixa", [mixa_inputs(xTs[c], c % 2, l, P) for c in range(NCORES)])
    ona = np.ascontiguousarray(P["out_norm"][l][0:1536].reshape(12, 128).T).astype(np.float32)
    maps = []
    for c in range(NCORES):
        half = c % 2
        pc = c - 1 if half == 1 else c
        m = {"qT": ra[c]["qT"], "kT_loc": ra[c]["kT"], "kT_prev": ra[pc]["kT"], "v_loc": ra[c]["v"], "v_prev": ra[pc]["v"],
             "cneg_loc": ra[c]["cneg"], "cneg_prev": ra[pc]["cneg"], "s_init": ra[pc]["ret_S"], "ret_o": ra[c]["ret_o"],
             "ret_qg": ra[c]["ret_qg"], "rgs": ra[c]["rgs"], "ytg": ra[c]["ytg"], "ona": ona, "w_out": P["w_out"][l],
             "xin": xTs[c]}
        m.update(mixb_consts(half))
        maps.append(m)
    return [r["xout"] for r in _run("mixb", maps)]


def kernel_unfused(**inputs):
    P = {k: np.asarray(v) for k, v in inputs.items()}
    x = P["x"]
    xTs = [np.ascontiguousarray(x[c // 2, (c % 2) * NTOK:(c % 2 + 1) * NTOK, :].T) for c in range(NCORES)]
    for l in range(DEPTH):
        xTs = run_ffn(xTs, l, P, "ffn1")
        xTs = run_mixer(xTs, l, P)
        xTs = run_ffn(xTs, l, P, "ffn2")
    g = col16(P["final_norm"])
    outs = [r["xout"] for r in _run("norm", [{"xin": xTs[c], "g": g} for c in range(NCORES)])]
    out = np.empty_like(x)
    for c in range(NCORES):
        out[c // 2, (c % 2) * NTOK:(c % 2 + 1) * NTOK, :] = outs[c].T
    return out


PAIRS = [[0, 1], [2, 3], [4, 5], [6, 7]]
WSHAPES = {"ffn1_w_gate": [DEPTH, D, DFF], "ffn1_w_up": [DEPTH, D, DFF], "ffn1_w_down": [DEPTH, DFF, D],
           "w_in": [DEPTH, D, INCOLS], "w_out": [DEPTH, D, D],
           "ffn2_w_gate": [DEPTH, D, DFF], "ffn2_w_up": [DEPTH, D, DFF], "ffn2_w_down": [DEPTH, DFF, D]}
SMALL = {"g_ffn1": [DEPTH, 128, KC], "g_mix": [DEPTH, 128, KC], "g_ffn2": [DEPTH, 128, KC], "g_fin": [128, KC],
         "bf": [DEPTH, 8, 1], "wst": [DEPTH, 128, 512], "lnb3": [DEPTH, 128, 1536], "ong": [DEPTH, 128, 4],
         "ona": [DEPTH, 128, 12], "tab": [NTOK, 268], "maskr": [128, 512], "triu": [128, 128], "ident": [128, 128],
         "cmask": [128, 896], "selh": [8, 1024], "ident8": [8, 8], "pmask": [128, 1], "sflag": [128, 1],
         "selh72": [128, 1024]}


def build_fused(depth=DEPTH, phases="fmxbF"):
    nc = bass.Bass("TRN2", target_bir_lowering=False)
    di = lambda name, shape: nc.dram_tensor(name, shape, F32, kind="ExternalInput").ap()
    it = lambda name, shape, dt: nc.dram_tensor(name, shape, dt, kind="Internal").ap()
    x_in = di("x", [D, NTOK])
    W = {k: di(k, [depth] + s[1:]) for k, s in WSHAPES.items()}
    S = {k: di(k, ([depth] + s[1:]) if len(s) == 3 else s) for k, s in SMALL.items()}
    out = nc.dram_tensor("out", [D, NTOK], F32, kind="ExternalOutput").ap()
    xres = it("xres", [D, NTOK], F32)
    qT = it("qT", [1024, NTOK], BF16)
    xk = [it("xk%d" % i, [512, NTOK], BF16) for i in range(2)]
    xv = [it("xv%d" % i, [1024, 1024], BF16) for i in range(2)]
    xc = it("xc", [8, NTOK], F32)
    xs_ = it("xs_", [128, 512], F32)
    gk = [it("gk%d" % i, [1024, NTOK], BF16) for i in range(2)]
    gv_ = [it("gv%d" % i, [2048, 1024], BF16) for i in range(2)]
    gc = it("gc", [16, NTOK], F32)
    gs = it("gs", [256, 512], F32)
    ret_o = it("ret_o", [512, NTOK], F32)
    ret_qg = it("ret_qg", [512, NTOK], BF16)
    rgs = it("rgs", [512, NTOK], F32)
    ytg = it("ytg", [512, NTOK], BF16)
    yT = it("yT", [1536, NTOK], BF16)
    with contextlib.ExitStack() as gstack:
        p = Prog(nc, gstack)

        def ffn_phase(xin, xout, g_ap, wg, wu, wd):
            with contextlib.ExitStack() as st:
                c = alloc_common(nc, st, p)
                alloc_ffn(c)
                gcol = c.sb("gcol", [128, KC], F32)
                p.dma("sync", gcol[:], g_ap, inc=c.setup_d)
                ffn_body(c, xin, xout, gcol, wg, wu, wd)
                finish(c)
                p.barrier()
                p.emit()

        def mixa_phase(l):
            with contextlib.ExitStack() as st:
                c = alloc_common(nc, st, p, tt=TA, nps=6, stat_bank=5)
                din = {"g": S["g_mix"][l], "bf": S["bf"][l], "maskr": S["maskr"][:, :], "wst": S["wst"][l],
                       "triu": S["triu"][:, :], "lnb3": S["lnb3"][l], "ong": S["ong"][l], "ident": S["ident"][:, :]}
                io = {"tab": S["tab"], "qT": qT, "kT": RowSplit(xk), "v": RowSplit(xv), "cneg_o": xc,
                      "ret_o": ret_o, "ret_qg": ret_qg, "ret_S": xs_, "rgs": rgs, "ytg": ytg}
                gcol = mixa_setup(c, din, io)
                mixa_body(c, xres, gcol, W["w_in"][l], io)
                p.barrier()
                p.emit()

        def exchange_phase():
            cc = p.sem("ccsem")
            for a_, b_ in ((xk[0], gk[0]), (xk[1], gk[1]), (xv[0], gv_[0]), (xv[1], gv_[1]), (xc, gc), (xs_, gs)):
                p.op("gpsimd", lambda e, a_=a_, b_=b_: e.collective_compute("AllGather", ALU.bypass, replica_groups=PAIRS,
                                                                            ins=[a_], outs=[b_]), inc=cc, k=1)
            p.barrier()
            p.emit()

        def mixb_phase(l):
            with contextlib.ExitStack() as st:
                io = {"qT": qT, "kT_loc": RowSplit(xk), "kT_prev": RowSplit([gk[0][0:512, :], gk[1][0:512, :]]),
                      "v_loc": RowSplit(xv), "v_prev": RowSplit([gv_[0][0:1024, :], gv_[1][0:1024, :]]),
                      "cneg_loc": xc, "cneg_prev": gc[0:8, :], "s_init": gs[0:128, :], "ret_o": ret_o,
                      "ret_qg": ret_qg, "rgs": rgs, "ytg": ytg, "cmask": S["cmask"], "selh": S["selh"],
                      "ident8": S["ident8"], "pmask": S["pmask"], "sflag": S["sflag"], "ona": S["ona"][l],
                      "selh72": S["selh72"], "ident": S["ident"],
                      "w_out": W["w_out"][l], "xin": xres, "yT": yT, "xout": xres}
                mixb_body(nc, st, p, io)
                p.barrier()
                p.emit()

        def norm_phase():
            with contextlib.ExitStack() as st:
                c = alloc_common(nc, st, p)
                gcol = c.sb("gcol", [128, KC], F32)
                p.dma("sync", gcol[:], S["g_fin"][:, :], inc=c.setup_d)
                for tt in range(NTT):
                    t0 = tt * TT

                    def out_fn(kc, s, waits, t0=t0):
                        d = p.op("vector", lambda e: e.scalar_tensor_tensor(out=c.xs[:, s, :], in0=c.xs[:, s, :],
                                                                            scalar=gcol[:, kc:kc + 1], in1=c.rstd[:],
                                                                            op0=ALU.mult, op1=ALU.mult), waits=waits,
                                 inc=c.dve_h)
                        sv = p.dma("sync", out[kc * 128:(kc + 1) * 128, t0:t0 + TT], c.xs[:, s, :], waits=[(c.dve_h, d)],
                                   inc=c.xs_st[s])
                        c.xs_cond[s] = [(c.xs_st[s], sv)]

                    norm_stats_and_h(c, xres, gcol, tt, out_fn=out_fn)
                finish(c)
                p.barrier()
                p.emit()

        for l in range(depth):
            if "f" in phases:
                ffn_phase(x_in if l == 0 else xres, xres, S["g_ffn1"][l], W["ffn1_w_gate"][l], W["ffn1_w_up"][l],
                          W["ffn1_w_down"][l])
            if "m" in phases:
                mixa_phase(l)
            if "x" in phases:
                exchange_phase()
            if "b" in phases:
                mixb_phase(l)
            if "F" in phases:
                ffn_phase(xres, xres, S["g_ffn2"][l], W["ffn2_w_gate"][l], W["ffn2_w_up"][l], W["ffn2_w_down"][l])
        norm_phase()
    return nc


def fused_inputs(P, core):
    half = core % 2
    x = P["x"]
    m = {"x": np.ascontiguousarray(x[core // 2, half * NTOK:(half + 1) * NTOK, :].T)}
    for k in WSHAPES:
        m[k] = P[k]
    return m


def fused_shared(P):
    sh = {}
    sh["g_ffn1"] = np.stack([col16(P["ffn1_norm"][l]) for l in range(DEPTH)])
    sh["g_mix"] = np.stack([col16(P["mix_norm"][l]) for l in range(DEPTH)])
    sh["g_ffn2"] = np.stack([col16(P["ffn2_norm"][l]) for l in range(DEPTH)])
    sh["g_fin"] = col16(P["final_norm"])
    sh["bf"] = np.ascontiguousarray(P["fox_b_f"].reshape(DEPTH, 8, 1)).astype(np.float32)
    sh["wst"] = np.ascontiguousarray(np.transpose(P["gmlp_w_s"], (0, 3, 1, 2)).reshape(DEPTH, 128, 512)).astype(np.float32)
    sh["lnb3"] = np.ascontiguousarray(np.stack([np.concatenate(
        [np.broadcast_to(P["gmlp_ln_g"][l][None, :], (128, 512)), np.broadcast_to(P["gmlp_ln_b"][l][None, :], (128, 512)),
         np.broadcast_to(P["gmlp_b_s"][l].reshape(1, 512), (128, 512))], axis=1) for l in range(DEPTH)])).astype(np.float32)
    sh["ong"] = np.ascontiguousarray(np.stack([P["out_norm"][l][1536:2048].reshape(4, 128).T for l in range(DEPTH)])).astype(np.float32)
    sh["ona"] = np.ascontiguousarray(np.stack([P["out_norm"][l][0:1536].reshape(12, 128).T for l in range(DEPTH)])).astype(np.float32)
    return sh


_FUSED = {}


def kernel(**inputs):
    P = {k: np.asarray(v) for k, v in inputs.items()}
    if "nc" not in _FUSED:
        _FUSED["nc"] = build_fused()
    sh = fused_shared(P)
    maps = []
    for c in range(NCORES):
        half = c % 2
        m = fused_inputs(P, c)
        m.update(sh)
        hc = host_consts(half)
        m.update({"tab": hc["tab"], "maskr": hc["maskr"], "triu": hc["triu"], "ident": hc["ident"]})
        m.update(mixb_consts(half))
        maps.append(m)
    res = run_bass_kernel_spmd(_FUSED["nc"], maps, core_ids=list(range(NCORES)))
    x = P["x"]
    outp = np.empty_like(x)
    for c in range(NCORES):
        outp[c // 2, (c % 2) * NTOK:(c % 2 + 1) * NTOK, :] = res.results[c]["out"].T
    return outp
```
